# Optimizing a Trainium2 kernel written in Bass

```python
import math
import jax, jax.numpy as jnp
from jax import lax
import numpy as np

D_MODEL = 1024
BATCH = 8
SEQ = 4096
DEPTH = 1

MEM_LEN = 256
N_MLSTM_HEADS = 4
MLSTM_HEAD_DIM = 128
MLSTM_WIDTH = N_MLSTM_HEADS * MLSTM_HEAD_DIM
CONV_WIDTH = 4
MLSTM_CHUNK = 128
N_DIFF_HEADS = 4
DIFF_QK_DIM = 64
DIFF_V_DIM = 2 * DIFF_QK_DIM
DIFF_WIDTH = N_DIFF_HEADS * DIFF_V_DIM
Q_BLOCK = 128
N_MEM_HEADS = 4
MEM_HEAD_DIM = 128
MEM_WIDTH = N_MEM_HEADS * MEM_HEAD_DIM
MIX_WIDTH = MLSTM_WIDTH + DIFF_WIDTH + MEM_WIDTH
IN_WIDTHS = (MLSTM_WIDTH, MLSTM_WIDTH, MLSTM_WIDTH,
             2 * N_DIFF_HEADS * DIFF_QK_DIM, 2 * N_DIFF_HEADS * DIFF_QK_DIM, DIFF_WIDTH, DIFF_WIDTH,
             MEM_WIDTH, MEM_WIDTH)
IN_WIDTH = 4608
NORM_EPS = 1e-6

kernel_name = 'hybrid_mlstm_diffattn_memxattn'


def _split_cols(a, widths):
    offs = np.cumsum(widths)[:-1].tolist()
    return jnp.split(a, offs, axis=-1)


def rmsnorm(x, g):
    xf = x.astype(jnp.float32)
    y = xf * lax.rsqrt(jnp.mean(xf * xf, axis=-1, keepdims=True) + NORM_EPS)
    return (y * g.astype(jnp.float32)).astype(x.dtype)


def head_layernorm(h, g):
    mu = jnp.mean(h, axis=-1, keepdims=True)
    var = jnp.mean(jnp.square(h - mu), axis=-1, keepdims=True)
    y = (h - mu) * lax.rsqrt(var + NORM_EPS)
    y = y.reshape(h.shape[0], h.shape[1], -1)
    return y * g.astype(jnp.float32)


def causal_depthwise_conv(x, w, b):
    y = lax.conv_general_dilated(x, w[:, None, :], window_strides=(1,),
                                 padding=[(CONV_WIDTH - 1, 0)],
                                 dimension_numbers=('NWC', 'WIO', 'NWC'),
                                 feature_group_count=x.shape[-1])
    return y + b


def mlstm_chunkwise(q, k, v, i_pre, f_pre):
    B_, H_, S_, dh = q.shape
    L = MLSTM_CHUNK
    nc = S_ // L
    k = k * (dh ** -0.5)
    log_f = jax.nn.log_sigmoid(f_pre)

    def chunks(a):
        a = a.reshape((B_, H_, nc, L) + a.shape[3:])
        return jnp.moveaxis(a, 2, 0)

    qc, kc, vc, ic = chunks(q), chunks(k), chunks(v), chunks(i_pre)
    bc = jnp.cumsum(chunks(log_f), axis=-1)
    tri = jnp.tril(jnp.ones((L, L), dtype=bool))

    def step(carry, inp):
        C, n, m = carry
        qb, kb, vb, ib, bb = inp
        d = bb[..., :, None] - bb[..., None, :] + ib[..., None, :]
        d = jnp.where(tri, d, -jnp.inf)
        g = bb + m[..., None]
        m_t = jnp.maximum(g, jnp.max(d, axis=-1))
        s_qk = jnp.einsum('bhtk,bhsk->bhts', qb, kb) * jnp.exp(d - m_t[..., None])
        inter = jnp.exp(g - m_t)
        num = (jnp.einsum('bhts,bhsv->bhtv', s_qk, vb)
               + inter[..., None] * jnp.einsum('bhvk,bhtk->bhtv', C, qb))
        den = jnp.sum(s_qk, axis=-1) + inter * jnp.einsum('bhk,bhtk->bht', n, qb)
        h = num / jnp.maximum(jnp.abs(den), jnp.exp(-m_t))[..., None]
        b_end = bb[..., -1]
        a = b_end[..., None] - bb + ib
        m_new = jnp.maximum(b_end + m, jnp.max(a, axis=-1))
        wa = jnp.exp(a - m_new[..., None])
        decay = jnp.exp(b_end + m - m_new)
        C_new = decay[..., None, None] * C + jnp.einsum('bhs,bhsv,bhsk->bhvk', wa, vb, kb)
        n_new = decay[..., None] * n + jnp.einsum('bhs,bhsk->bhk', wa, kb)
        return (C_new, n_new, m_new), h

    init = (jnp.zeros((B_, H_, dh, dh), jnp.float32),
            jnp.zeros((B_, H_, dh), jnp.float32),
            jnp.zeros((B_, H_), jnp.float32))
    _, h = lax.scan(step, init, (qc, kc, vc, ic, bc))
    return jnp.moveaxis(h, 0, 2).reshape(B_, H_, S_, dh)


def diff_causal_attention(q, k, v, lam):
    S_ = q.shape[3]
    scale = DIFF_QK_DIM ** -0.5
    outs = []
    for j in range(S_ // Q_BLOCK):
        lo, hi = j * Q_BLOCK, (j + 1) * Q_BLOCK
        s = jnp.einsum('bhmqd,bhmkd->bhmqk', q[:, :, :, lo:hi], k[:, :, :, :hi]).astype(jnp.float32) * scale
        mask = jnp.arange(hi)[None, :] <= jnp.arange(lo, hi)[:, None]
        p = jax.nn.softmax(jnp.where(mask, s, -jnp.inf), axis=-1)
        pd = p[:, :, 0] - lam * p[:, :, 1]
        outs.append(jnp.einsum('bhqk,bhkd->bhqd', pd.astype(v.dtype), v[:, :, :hi]))
    return jnp.concatenate(outs, axis=2)


def setup_inputs(seed: int = 0) -> dict:
    key = jax.random.key(seed)
    ks = jax.random.split(key, 24)
    nrm = jax.random.normal
    f32 = jnp.float32
    H = N_MLSTM_HEADS
    b_i = 0.1 * nrm(ks[10], (DEPTH, H), f32)
    b_f = 3.0 + 3.0 * jax.random.uniform(ks[11], (DEPTH, H), f32)
    return {
        'x': nrm(ks[0], (BATCH, SEQ, D_MODEL), f32),
        'mem': nrm(ks[1], (BATCH, MEM_LEN, D_MODEL), f32),
        'norm_g': 1.0 + 0.02 * nrm(ks[2], (DEPTH, D_MODEL), f32),
        'w_in': nrm(ks[3], (DEPTH, D_MODEL, IN_WIDTH), f32) * D_MODEL ** -0.5,
        'conv_w': nrm(ks[4], (DEPTH, CONV_WIDTH, MLSTM_WIDTH), f32) * CONV_WIDTH ** -0.5,
        'conv_b': 0.01 * nrm(ks[5], (DEPTH, MLSTM_WIDTH), f32),
        'wq_m': nrm(ks[6], (DEPTH, H, MLSTM_HEAD_DIM, MLSTM_HEAD_DIM), f32) * MLSTM_HEAD_DIM ** -0.5,
        'wk_m': nrm(ks[7], (DEPTH, H, MLSTM_HEAD_DIM, MLSTM_HEAD_DIM), f32) * MLSTM_HEAD_DIM ** -0.5,
        'wv_m': nrm(ks[8], (DEPTH, H, MLSTM_HEAD_DIM, MLSTM_HEAD_DIM), f32) * MLSTM_HEAD_DIM ** -0.5,
        'w_if': nrm(ks[9], (DEPTH, 3 * MLSTM_WIDTH, 2 * H), f32) * (3 * MLSTM_WIDTH) ** -0.5,
        'b_if': jnp.concatenate([b_i, b_f], axis=-1),
        'mnorm_g': 1.0 + 0.02 * nrm(ks[12], (DEPTH, MLSTM_WIDTH), f32),
        'skip_m': 1.0 + 0.02 * nrm(ks[13], (DEPTH, MLSTM_WIDTH), f32),
        'lam_q1': 0.1 * nrm(ks[14], (DEPTH, DIFF_QK_DIM), f32),
        'lam_k1': 0.1 * nrm(ks[15], (DEPTH, DIFF_QK_DIM), f32),
        'lam_q2': 0.1 * nrm(ks[16], (DEPTH, DIFF_QK_DIM), f32),
        'lam_k2': 0.1 * nrm(ks[17], (DEPTH, DIFF_QK_DIM), f32),
        'dnorm_g': 1.0 + 0.02 * nrm(ks[18], (DEPTH, DIFF_WIDTH), f32),
        'mem_norm_g': 1.0 + 0.02 * nrm(ks[19], (DEPTH, D_MODEL), f32),
        'w_mem_kv': nrm(ks[20], (DEPTH, D_MODEL, 2 * MEM_WIDTH), f32) * D_MODEL ** -0.5,
        'w_out': nrm(ks[21], (DEPTH, MIX_WIDTH, D_MODEL), f32) * MIX_WIDTH ** -0.5,
        'final_g': 1.0 + 0.02 * nrm(ks[22], (D_MODEL,), f32),
    }


def reference(x, mem, norm_g, w_in, conv_w, conv_b, wq_m, wk_m, wv_m, w_if, b_if, mnorm_g, skip_m,
              lam_q1, lam_k1, lam_q2, lam_k2, dnorm_g, mem_norm_g, w_mem_kv, w_out, final_g):
    B_, S_, _ = x.shape
    Hm, dh = N_MLSTM_HEADS, MLSTM_HEAD_DIM
    Hd, Hc = N_DIFF_HEADS, N_MEM_HEADS
    for l in range(DEPTH):
        h = rmsnorm(x, norm_g[l])
        proj = h @ w_in[l]
        x_m, o_m, z_m, q_d, k_d, v_d, z_d, q_c, z_c = _split_cols(proj, IN_WIDTHS)

        x_cv = jax.nn.silu(causal_depthwise_conv(x_m, conv_w[l], conv_b[l]))
        xch = x_cv.reshape(B_, S_, Hm, dh)
        xmh = x_m.reshape(B_, S_, Hm, dh)
        q = jnp.einsum('bshd,hde->bshe', xch, wq_m[l])
        k = jnp.einsum('bshd,hde->bshe', xch, wk_m[l])
        v = jnp.einsum('bshd,hde->bshe', xmh, wv_m[l])
        qkv = jnp.concatenate([q, k, v], axis=2).reshape(B_, S_, 3 * MLSTM_WIDTH)
        gates = (qkv @ w_if[l] + b_if[l]).astype(jnp.float32)
        i_pre = jnp.transpose(gates[..., :Hm], (0, 2, 1))
        f_pre = jnp.transpose(gates[..., Hm:], (0, 2, 1))
        tr = lambda a: jnp.transpose(a.astype(jnp.float32), (0, 2, 1, 3))
        h_t = mlstm_chunkwise(tr(q), tr(k), tr(v), i_pre, f_pre)
        h_t = jnp.transpose(h_t, (0, 2, 1, 3))
        h_t = jax.nn.sigmoid(o_m.reshape(B_, S_, Hm, dh).astype(jnp.float32)) * h_t
        y_m = head_layernorm(h_t, mnorm_g[l]).astype(x.dtype)
        y_m = (y_m + skip_m[l] * x_cv) * jax.nn.silu(z_m)

        qd = jnp.transpose(q_d.reshape(B_, S_, Hd, 2, DIFF_QK_DIM), (0, 2, 3, 1, 4))
        kd = jnp.transpose(k_d.reshape(B_, S_, Hd, 2, DIFF_QK_DIM), (0, 2, 3, 1, 4))
        vd = jnp.transpose(v_d.reshape(B_, S_, Hd, DIFF_V_DIM), (0, 2, 1, 3))
        lam_init = 0.8 - 0.6 * math.exp(-0.3 * l)
        lam = (jnp.exp(jnp.sum(lam_q1[l].astype(jnp.float32) * lam_k1[l].astype(jnp.float32)))
               - jnp.exp(jnp.sum(lam_q2[l].astype(jnp.float32) * lam_k2[l].astype(jnp.float32)))
               + lam_init)
        od = jnp.transpose(diff_causal_attention(qd, kd, vd, lam), (0, 2, 1, 3))
        od = rmsnorm(od, dnorm_g[l].reshape(Hd, DIFF_V_DIM)) * (1.0 - lam_init)
        y_d = od.reshape(B_, S_, DIFF_WIDTH) * jax.nn.silu(z_d)

        kv = rmsnorm(mem, mem_norm_g[l]) @ w_mem_kv[l]
        km = kv[..., :MEM_WIDTH].reshape(B_, -1, Hc, MEM_HEAD_DIM)
        vm = kv[..., MEM_WIDTH:].reshape(B_, -1, Hc, MEM_HEAD_DIM)
        qc = q_c.reshape(B_, S_, Hc, MEM_HEAD_DIM)
        sc = jnp.einsum('bshd,bmhd->bhsm', qc, km).astype(jnp.float32) * MEM_HEAD_DIM ** -0.5
        pc = jax.nn.softmax(sc, axis=-1).astype(x.dtype)
        oc = jnp.einsum('bhsm,bmhd->bshd', pc, vm).reshape(B_, S_, MEM_WIDTH)
        y_c = oc * jax.nn.silu(z_c)

        y = jnp.concatenate([y_m, y_d, y_c], axis=-1) @ w_out[l]
        x = x + y
    return rmsnorm(x, final_g)
```

```python
import math
from contextlib import ExitStack

import numpy as np
import concourse.bass as bass
import concourse.mybir as mybir
from concourse.bass_utils import run_bass_kernel_spmd

F32 = mybir.dt.float32
BF16 = mybir.dt.bfloat16
AF = mybir.ActivationFunctionType
ALU = mybir.AluOpType

S_LEN = 4096
D = 1024
NT = 32
NB = 8
EPS = 1e-6
LN_SQRT_DH = 0.5 * math.log(128.0)


class Sched:
    ENG = ("pe", "act", "dve", "pool", "sp")

    def __init__(self, nc, sems, dsems):
        self.nc = nc
        self.sems = sems
        self.dsems = dsems
        self.q = {e: [] for e in self.ENG}
        self.cnt = {e: 0 for e in self.ENG}
        self.last_w = {}
        self.readers = {}
        self.seen = {e: {} for e in self.ENG}
        self.n_dma = len(dsems)
        self.dma_cnt = [0] * self.n_dma
        half = self.n_dma // 2
        self.dma_pool = {"sp": list(range(0, half)), "pool": list(range(half, self.n_dma))}
        self.dma_rr = {"sp": 0, "pool": 0}

    def _need(self, eng, tok, waits):
        src, val = tok
        if src == eng and eng in ("pe", "sp"):
            return
        if self.seen[eng].get(src, 0) >= val:
            return
        self.seen[eng][src] = val
        waits[src] = max(waits.get(src, 0), val)

    def op(self, eng, fn, r=(), w=(), dma=False):
        waits = {}
        for k in r:
            t = self.last_w.get(k)
            if t is not None:
                self._need(eng, t, waits)
        for k in w:
            t = self.last_w.get(k)
            if t is not None:
                self._need(eng, t, waits)
            for t in self.readers.get(k, ()):
                self._need(eng, t, waits)
        if dma:
            pool_ = self.dma_pool[eng]
            i = pool_[self.dma_rr[eng] % len(pool_)]
            self.dma_rr[eng] += 1
            if self.dma_cnt[i] > 0:
                self._need(eng, (("d", i), 16 * self.dma_cnt[i]), waits)
            self.dma_cnt[i] += 1
            tok = (("d", i), 16 * self.dma_cnt[i])
        else:
            self.cnt[eng] += 1
            tok = (eng, self.cnt[eng])
        self.q[eng].append((list(waits.items()), fn, tok))
        for k in w:
            self.last_w[k] = tok
            self.readers[k] = []
        for k in r:
            self.readers.setdefault(k, []).append(tok)
        return tok

    def barrier(self):
        toks = [(e, self.cnt[e]) for e in self.ENG if self.cnt[e] > 0]
        toks += [(("d", i), 16 * c) for i, c in enumerate(self.dma_cnt) if c > 0]
        for e in self.ENG:
            waits = {}
            for t in toks:
                if t[0] != e:
                    self._need(e, t, waits)
            if waits:
                self.q[e].append((list(waits.items()), None, None))

    def emit(self, block):
        sems, dsems = self.sems, self.dsems

        def run(e, engobj):
            for waits, fn, tok in self.q[e]:
                for src, val in waits:
                    s = dsems[src[1]] if isinstance(src, tuple) else sems[src]
                    engobj.wait_ge(s, val)
                if fn is None:
                    continue
                ins = fn(engobj)
                if isinstance(tok[0], tuple):
                    ins.then_inc(dsems[tok[0][1]], 16)
                else:
                    ins.then_inc(sems[tok[0]], 1)
            self.q[e] = []

        @block.tensor
        def _(eng):
            run("pe", eng)

        @block.scalar
        def _(eng):
            run("act", eng)

        @block.vector
        def _(eng):
            run("dve", eng)

        @block.gpsimd
        def _(eng):
            run("pool", eng)

        @block.sync
        def _(eng):
            run("sp", eng)


def build_nc(stop_after=99, dbg=None):
    nc = bass.Bass("TRN2", target_bir_lowering=False)
    din = lambda n, s: nc.dram_tensor(n, s, F32, kind="ExternalInput").ap()
    x_d = din("x", [S_LEN, D])
    mem_d = din("mem", [256, D])
    win_d = din("w_in", [D, 4608])
    wkv_d = din("w_kv", [D, 1024])
    wout_d = din("w_out", [1536, D])
    wq_d = din("wq", [4, 128, 128])
    wk_d = din("wk", [4, 128, 128])
    wv_d = din("wv", [4, 128, 128])
    wif_d = din("w_if", [1536, 8])
    cst_d = din("cst", [128, 512])
    par_d = din("par", [128, 320])
    fg_d = din("fg", [128, D])
    out_d = nc.dram_tensor("out", [S_LEN, D], F32, kind="ExternalOutput").ap()
    dbg_outs = []
    win_v = win_d.rearrange("(c p) f -> p c f", p=128)

    top = ExitStack()
    with top:
        sb = lambda es, name, shape, dt: es.enter_context(nc.sbuf_tensor(name, shape, dt))
        ps = [top.enter_context(nc.psum_tensor("ps%d" % i, [128, 512], F32)) for i in range(8)]
        sems = {e: top.enter_context(nc.semaphore("s_" + e)) for e in Sched.ENG}
        dsems = [top.enter_context(nc.semaphore("d%d" % i)) for i in range(32)]
        S = Sched(nc, sems, dsems)
        P = lambda i: ("ps", i)

        def MM(out, lhsT, rhs, start, stop, r, w):
            return S.op("pe", lambda e: e.matmul(out, lhsT=lhsT, rhs=rhs, start=start, stop=stop,
                                                  skip_group_check=True), r=r, w=w)

        def TR(out, in_, r, w):
            return S.op("pe", lambda e: e.transpose(out=out, in_=in_, identity=ident[:]), r=list(r) + ["cst"], w=w)

        def ACT(out, in_, func, r, w, bias=None, scale=None, accum=None, eng="act"):
            kw = {}
            if bias is not None:
                kw["bias"] = bias
            if scale is not None:
                kw["scale"] = scale
            if accum is not None:
                kw["accum_out"] = accum
            return S.op("act", lambda e: e.activation(out=out, in_=in_, func=func, **kw), r=r, w=w)

        def TS(eng, out, in0, s1, s2, op0, op1, r, w):
            if op1 is None:
                return S.op(eng, lambda e: e.tensor_scalar(out=out, in0=in0, scalar1=s1, scalar2=None, op0=op0), r=r, w=w)
            return S.op(eng, lambda e: e.tensor_scalar(out=out, in0=in0, scalar1=s1, scalar2=s2, op0=op0, op1=op1), r=r, w=w)

        def TT(eng, out, in0, in1, op, r, w):
            return S.op(eng, lambda e: e.tensor_tensor(out=out, in0=in0, in1=in1, op=op), r=r, w=w)

        def STT(out, in0, scalar, in1, op0, op1, r, w, accum=None):
            if accum is not None:
                return S.op("dve", lambda e: e.scalar_tensor_tensor(out=out, in0=in0, scalar=scalar, in1=in1, op0=op0, op1=op1, accum_out=accum), r=r, w=w)
            return S.op("dve", lambda e: e.scalar_tensor_tensor(out=out, in0=in0, scalar=scalar, in1=in1, op0=op0, op1=op1), r=r, w=w)

        def CP(eng, out, in_, r, w):
            if eng == "act":
                return S.op("act", lambda e: e.copy(out=out, in_=in_), r=r, w=w)
            return S.op(eng, lambda e: e.tensor_copy(out=out, in_=in_), r=r, w=w)

        def MSET(eng, ap, val, w):
            return S.op(eng, lambda e: e.memset(ap, val), w=w)

        def RCP(out, in_, r, w):
            return S.op("dve", lambda e: e.reciprocal(out=out, in_=in_), r=r, w=w)

        def SCAN(out, d0, init, op0, r, w):
            return S.op("dve", lambda e: e.tensor_tensor_scan(out=out, data0=d0, data1=d0, initial=init, op0=op0, op1=ALU.bypass), r=r, w=w)

        def DMA(eng, out, in_, r, w):
            return S.op(eng, lambda e: e.dma_start(out=out, in_=in_), r=r, w=w, dma=True)

        def dump(es, name, ap, shape, dt=F32):
            d = nc.dram_tensor("dbg_" + name, list(shape), dt, kind="ExternalOutput").ap()
            S.barrier()
            dbg_outs.append(DMA("sp", d, ap, r=[], w=[]))

        def finish(block):
            S.barrier()
            S.emit(block)

        cstt = sb(top, "cstt", [128, 512], F32)
        ident = cstt[:, 0:128]
        triu = cstt[:, 128:256]
        M1 = cstt[:, 256:384]
        ones_f = cstt[:, 384:512]
        part = sb(top, "part", [128, 320], F32)
        mask_bf = sb(top, "mask_bf", [128, 128], BF16)
        ones_bf = sb(top, "ones_bf", [128, 128], BF16)
        ident_bf = sb(top, "ident_bf", [128, 128], BF16)
        kmT = sb(top, "kmT", [128, 4, 256], BF16)
        vm1 = sb(top, "vm1", [128, 2, 4, 129], BF16)
        ymT = sb(top, "ymT", [128, 4, S_LEN], BF16)
        hT = sb(top, "hT", [128, 8, S_LEN], BF16)
        wm_cm = nc.sbuf_tensor("wm_bf", [128, 8, 1536], BF16)
        wm_bf = wm_cm.__enter__()

        def rms_tile_a(es_bufs, src_rows, i):
            xtb, junk, ssb, xnb = es_bufs
            xt = xtb[i % 4]
            ss = ssb[:, 2 * (i % 4):2 * (i % 4) + 1]
            rs = ssb[:, 2 * (i % 4) + 1:2 * (i % 4) + 2]
            kx, ks = ("xt", i % 4), ("ss", i % 4)
            DMA("sp", xt[:], src_rows, r=[], w=[kx])
            ACT(junk[:], xt[:], AF.Square, r=[kx], w=["junk", ks], accum=ss)
            ACT(rs, ss, AF.Sqrt, r=[ks], w=[ks], bias=EPS, scale=1.0 / D)
            RCP(rs, rs, r=[ks], w=[ks])

        def rms_tile_b(es_bufs, gcol, dstT, col0, i):
            xtb, junk, ssb, xnb = es_bufs
            xt = xtb[i % 4]
            xn = xnb[i % 3]
            rs = ssb[:, 2 * (i % 4) + 1:2 * (i % 4) + 2]
            kx, kn, ks = ("xt", i % 4), ("xn", i % 3), ("ss", i % 4)
            ACT(xn[:], xt[:], AF.Copy, r=[kx, ks], w=[kn], scale=rs)
            for half in range(2):
                b = 2 * (i % 4) + half
                psb = ps[b][:].bitcast(BF16)
                for j in range(4):
                    c = 4 * half + j
                    S.op("pe", (lambda o, a: (lambda e: e.transpose(out=o, in_=a, identity=ident_bf[:])))(
                        psb[:, j * 128:(j + 1) * 128], xn[:, c * 128:(c + 1) * 128]), r=[kn, "ident_bf"], w=[P(b)])
                TT("dve", dstT[:, 4 * half:4 * half + 4, col0:col0 + 128],
                   psb[:, 0:512].rearrange("p (j t) -> p j t", t=128),
                   part[:, gcol + 4 * half:gcol + 4 * half + 4].unsqueeze(2).to_broadcast([128, 4, 128]),
                   ALU.mult, r=[P(b), "par"], w=[("hT", col0 // 512) if dstT is hT else "memT"])

        with nc.Block() as blk0, ExitStack() as es:
            DMA("sp", cstt[:], cst_d[:, :], r=[], w=["cst"])
            DMA("sp", part[:], par_d[:, :], r=[], w=["par"])
            CP("dve", mask_bf[:], triu, r=["cst"], w=["mask_bf"])
            CP("dve", ones_bf[:], ones_f, r=["cst"], w=["ones_bf"])
            CP("dve", ident_bf[:], ident, r=["cst"], w=["ident_bf"])
            xtb = [sb(es, "xt%d" % i, [128, D], F32) for i in range(4)]
            xnb = [sb(es, "xn%d" % i, [128, D], BF16) for i in range(3)]
            junk = sb(es, "junk", [128, D], BF16)
            ssb = sb(es, "ssb", [128, 8], F32)
            bufs = (xtb, junk, ssb, xnb)
            memT = sb(es, "memT", [128, 8, 256], BF16)
            wkv_bf = sb(es, "wkv_bf", [128, 8, 1024], BF16)
            wkv_v = wkv_d.rearrange("(c p) f -> p c f", p=128)
            for c in range(8):
                DMA("pool", wkv_bf[:, c, :], wkv_v[:, c, :], r=[], w=["wkv"])
            for c in range(8):
                DMA("pool", wm_bf[:, c, :], win_v[:, c, 0:1536], r=[], w=[("wm", c)])
            seq = [(mem_d[i * 128:(i + 1) * 128, :], 8, memT, i * 128) for i in range(2)]
            seq += [(x_d[i * 128:(i + 1) * 128, :], 0, hT, i * 128) for i in range(NT)]
            rms_tile_a(bufs, seq[0][0], 0)
            for k, (src_rows, gcol, dstT, col0) in enumerate(seq):
                if k + 1 < len(seq):
                    rms_tile_a(bufs, seq[k + 1][0], k + 1)
                rms_tile_b(bufs, gcol, dstT, col0, k)
            for h in range(4):
                b = 4 + h % 2
                for c in range(8):
                    MM(ps[b][:, 0:256], wkv_bf[:, c, h * 128:(h + 1) * 128], memT[:, c, :], c == 0, c == 7,
                       r=["wkv", "memT"], w=[P(b)])
                ACT(kmT[:, h, :], ps[b][:, 0:256], AF.Identity, r=[P(b)], w=["kmT"], scale=128.0 ** -0.5)
            MSET("pool", vm1[:, :, :, 128:129], 1.0, w=["vm1"])
            for mt in range(2):
                b = 6 + mt
                for c in range(8):
                    MM(ps[b][:], memT[:, c, mt * 128:(mt + 1) * 128], wkv_bf[:, c, 512:1024], c == 0, c == 7,
                       r=["wkv", "memT"], w=[P(b)])
                CP("dve", vm1[:, mt, :, 0:128], ps[b][:].rearrange("p (h d) -> p h d", d=128), r=[P(b)], w=["vm1"])
            if dbg == 0:
                dump(es, "hT", hT[:, :, 0:512], [128, 8, 512], BF16)
                dump(es, "kmT", kmT[:], [128, 4, 256], BF16)
                dump(es, "vm1", vm1[:], [128, 2, 4, 129], BF16)
            finish(blk0)
        if stop_after == 0:
            return nc

        with nc.Block() as blk1, ExitStack() as es:
            wqkv = sb(es, "wqkv", [128, 3, 4, 128], BF16)
            for j, wd in enumerate((wq_d, wk_d, wv_d)):
                DMA("pool", wqkv[:, j, :, :], wd.rearrange("h d e -> d h e"), r=[], w=["wqkv"])
            wif_bf = sb(es, "wif_bf", [128, 12, 8], BF16)
            DMA("pool", wif_bf[:], wif_d.rearrange("(j p) g -> p j g", p=128), r=[], w=["wif"])
            ABf = sb(es, "ABf", [128, 2, 4, 8], BF16)
            wTs = [sb(es, "wTs%d" % i, [128, 128], BF16) for i in range(3)]
            psb4 = ps[4][:].bitcast(BF16)
            for h in range(4):
                for j in range(3):
                    S.op("pe", (lambda o, a: (lambda e: e.transpose(out=o, in_=a, identity=ident_bf[:])))(
                        psb4[:, j * 128:(j + 1) * 128], wqkv[:, j, h, :]), r=["wqkv", "ident_bf"], w=[P(4)])
                    CP("dve", wTs[j][:], psb4[:, j * 128:(j + 1) * 128], r=[P(4)], w=[("wTs", j)])
                MM(ps[5][:, h * 8:(h + 1) * 8], wTs[0][:], wif_bf[:, h, :], True, False, r=[("wTs", 0), "wif"], w=[P(5)])
                MM(ps[5][:, h * 8:(h + 1) * 8], wTs[1][:], wif_bf[:, 4 + h, :], False, True, r=[("wTs", 1), "wif"], w=[P(5)])
                MM(ps[5][:, 32 + h * 8:32 + (h + 1) * 8], wTs[2][:], wif_bf[:, 8 + h, :], True, True, r=[("wTs", 2), "wif"], w=[P(5)])
            CP("dve", ABf[:].rearrange("p a h g -> p (a h g)"), ps[5][:, 0:64], r=[P(5)], w=["ABf"])
            xmb = [sb(es, "xm%d" % i, [128, 4, 515], BF16) for i in range(2)]
            xcv = sb(es, "xcv", [128, 4, 512], BF16)
            xs = sb(es, "xs", [128, 4, 512], BF16)
            qT = sb(es, "qT", [128, 4, 512], BF16)
            kT = sb(es, "kT", [128, 4, 512], BF16)
            vT = None
            zs = sb(es, "zs", [128, 4, 512], BF16)
            sgo = sb(es, "sgo", [128, 4, 512], BF16)
            Dg = sb(es, "Dg", [128, 16, 128], BF16)
            for hj in range(16):
                TS("dve", Dg[:, hj, :], ident, part[:, 16 + hj:16 + hj + 1], None, ALU.mult, None, r=["cst", "par"], w=["Dg"])
            Gtok = sb(es, "Gtok", [128, 32, 8], F32)
            CONVW, CONVB, MG, SK = 16, 32, 36, 40
            rot = [0]

            def nextbank():
                rot[0] ^= 1
                return rot[0]

            def frontA(blk):
                xm = xmb[blk % 2]
                kxm = ("xm", blk % 2)
                cols = slice(blk * 512, (blk + 1) * 512)
                if blk == 0:
                    MSET("pool", xm[:, :, 0:3], 0.0, w=[kxm])
                else:
                    CP("pool", xm[:, :, 0:3], xmb[(blk - 1) % 2][:, :, 512:515],
                       r=[("xm", (blk - 1) % 2)] + [(("xm", (blk - 1) % 2), hh) for hh in range(4)], w=[kxm])
                for h in range(4):
                    b = nextbank()
                    for c in range(8):
                        MM(ps[b][:], wm_bf[:, c, h * 128:(h + 1) * 128], hT[:, c, cols], c == 0, c == 7,
                           r=[("wm", c), ("hT", blk)], w=[P(b)])
                    CP("act", xm[:, h, 3:515], ps[b][:], r=[P(b)], w=[(kxm, h)])
                return xm, kxm

            def frontBC(blk, need_v, do_qk=True):
                xm = xmb[blk % 2]
                kxm = ("xm", blk % 2)
                for h in range(4):
                    b = nextbank()
                    for j in range(4):
                        MM(ps[b][:], Dg[:, 4 * h + j, :], xm[:, h, j:j + 512], j == 0, j == 3,
                           r=["Dg", (kxm, h), kxm], w=[P(b)])
                    ACT(xcv[:, h, :], ps[b][:], AF.Silu, r=[P(b), "par"], w=[("xcv", h)],
                        bias=part[:, CONVB + h:CONVB + h + 1])
                for h in range(4 if do_qk else 0):
                    for j, dst, kd in ((0, qT, "qT"), (1, kT, "kT")) + (((2, vT, "vT"),) if need_v else ()):
                        b = nextbank()
                        src = xcv[:, h, :] if j < 2 else xm[:, h, 3:515]
                        MM(ps[b][:], wqkv[:, j, h, :], src, True, True, r=["wqkv", ("xcv", h) if j < 2 else (kxm, h)], w=[P(b)])
                        CP("act" if j == 0 else "dve", dst[:, h, :], ps[b][:], r=[P(b)], w=[kd])
                return xm, kxm

            for blk in range(NB):
                if blk == 0:
                    frontA(0)
                frontBC(blk, False, do_qk=False)
                if blk + 1 < NB:
                    frontA(blk + 1)
                xm_ = xmb[blk % 2]
                kxm_ = ("xm", blk % 2)
                for tt in range(4):
                    tc_ = slice(tt * 128, (tt + 1) * 128)
                    for h in range(4):
                        MM(ps[2][:, tt * 8:(tt + 1) * 8], xcv[:, h, tc_], ABf[:, 0, h, :], h == 0, False,
                           r=[("xcv", h), "ABf"], w=[P(2)])
                        MM(ps[2][:, tt * 8:(tt + 1) * 8], xm_[:, h, 3 + tt * 128:3 + (tt + 1) * 128], ABf[:, 1, h, :], False, h == 3,
                           r=[(kxm_, h), "ABf"], w=[P(2)])
                CP("dve", Gtok[:, blk * 4:(blk + 1) * 4, :], ps[2][:, 0:32].rearrange("p (t g) -> p t g", g=8),
                   r=[P(2)], w=["Gtok"])
            if dbg and dbg >= 1:
                dump(es, "Gtok", Gtok[:], [128, 32, 8])
                dump(es, "qT", qT[:], [128, 4, 512], BF16)
                dump(es, "xcv", xcv[:], [128, 4, 512], BF16)

            rows = sb(es, "rows", [128, 8, 128], F32)
            GIr, GFr, LF, Fp, U_, W_, W2_, TH_ = [rows[:, i, :] for i in range(8)]
            cols_ = sb(es, "colsb", [128, 16], F32)
            col = lambda i: cols_[:, i:i + 1]
            rowt = sb(es, "rowt", [1, 4, 128], F32)
            Ebc = sb(es, "Ebc", [128, 128], F32)
            Wt = sb(es, "Wt", [128, 3, 128], F32)
            BI, BFc = 48, 49
            Gv = Gtok[:].rearrange("p c g -> p (c g)")
            gsp = sb(es, "gsp", [128, 2, 128], F32)
            CP("dve", gsp[:, 0, :].rearrange("p (c h) -> p c h", h=4), Gtok[:, :, 0:4], r=["Gtok"], w=["gsp"])
            CP("dve", gsp[:, 1, :].rearrange("p (c h) -> p c h", h=4), Gtok[:, :, 4:8], r=["Gtok"], w=["gsp"])
            TR(ps[3][:, 0:128], gsp[:, 0, :], r=["gsp"], w=[P(3)])
            TR(ps[3][:, 128:256], gsp[:, 1, :], r=["gsp"], w=[P(3)])
            CP("dve", rows[:, 0:2, :], ps[3][:, 0:256].rearrange("p (a t) -> p a t", t=128), r=[P(3)], w=["rows"])
            TS("dve", col(0), part[:, BFc:BFc + 1], -1.0, None, ALU.mult, None, r=["par"], w=["cols"])
            ACT(LF, GFr, AF.Exp, r=["rows", "cols"], w=["rows"], bias=col(0), scale=-1.0)
            ACT(LF, LF, AF.Ln, r=["rows"], w=["rows"], bias=1.0)
            SCAN(Fp, LF, 0.0, ALU.add, r=["rows"], w=["rows"])
            MM(ps[3][:, 256:257], M1, rows[:, 3, 127:128], True, True, r=["cst", "rows"], w=[P(3)])
            CP("dve", col(1), ps[3][:, 256:257], r=[P(3)], w=["cols"])
            TS("dve", Fp, Fp, col(1), None, ALU.add, None, r=["rows", "cols"], w=["rows"])
            STT(GIr, GIr, part[:, BI:BI + 1], Fp, ALU.add, ALU.add, r=["rows", "par"], w=["rows"])
            SCAN(U_, GIr, 0.0, ALU.max, r=["rows"], w=["rows"])
            MM(ps[3][0:1, 384:512], rows[:, 4, 127:128], ident, True, True, r=["rows", "cst"], w=[P(3)])
            CP("dve", rowt[:, 0, :], ps[3][0:1, 384:512], r=[P(3)], w=["rowt"])
            for h in range(4):
                SCAN(rowt[:, 1, :].rearrange("p (c h) -> p h c", h=4)[:, h, :],
                     rowt[:, 0, :].rearrange("p (c h) -> p h c", h=4)[:, h, :], 0.0, ALU.max, r=["rowt"], w=["rowt"])
            MSET("dve", rowt[:, 2, 0:4], 0.0, w=["rowt"])
            CP("dve", rowt[:, 2, 4:128], rowt[:, 1, 0:124], r=["rowt"], w=["rowt"])
            TT("dve", rowt[:, 3, :], rowt[:, 2, :], rowt[:, 1, :], ALU.subtract, r=["rowt"], w=["rowt"])
            MM(ps[3][:, 257:258], rowt[:, 2, :], ones_f[0:1, 0:1], True, True, r=["rowt", "cst"], w=[P(3)])
            MM(ps[3][:, 258:259], rowt[:, 3, :], ones_f[0:1, 0:1], True, True, r=["rowt", "cst"], w=[P(3)])
            CP("dve", cols_[:, 2:4], ps[3][:, 257:259], r=[P(3)], w=["cols"])
            MM(ps[4][:, 0:128], ones_f[0:1, :], rowt[:, 3, :], True, True, r=["rowt", "cst"], w=[P(4)])
            ACT(Ebc[:], ps[4][:, 0:128], AF.Exp, r=[P(4)], w=["Ebc"])
            TS("dve", col(4), col(2), -1.0, -LN_SQRT_DH, ALU.mult, ALU.add, r=["cols"], w=["cols"])
            TS("dve", col(5), col(2), -1.0, None, ALU.mult, None, r=["cols"], w=["cols"])
            TT("dve", col(6), col(4), col(3), ALU.add, r=["cols"], w=["cols"])
            ACT(W_, GIr, AF.Exp, r=["rows", "cols"], w=["rows"], bias=col(4))
            ACT(W2_, GIr, AF.Exp, r=["rows", "cols"], w=["rows"], bias=col(6))
            ACT(TH_, Fp, AF.Exp, r=["rows", "cols"], w=["rows"], bias=col(5))
            for i in range(3):
                TR(ps[4][:, 128 * (i + 1):128 * (i + 2)], rows[:, 5 + i, :], r=["rows"], w=[P(4)])
            CP("dve", Wt[:], ps[4][:, 128:512].rearrange("p (a t) -> p a t", t=128), r=[P(4)], w=["Wt"])
            if dbg and dbg >= 2:
                dump(es, "rows", rows[:], [128, 8, 128])
                dump(es, "Wt", Wt[:], [128, 3, 128])
                dump(es, "Ebc", Ebc[:], [128, 128])
                dump(es, "rowt", rowt[:], [1, 4, 128])

            kwb = [sb(es, "kw%d" % i, [128, 4, 128], BF16) for i in range(2)]
            V1b = [sb(es, "V1_%d" % i, [128, 4, 129], BF16) for i in range(2)]
            sqkb = [sb(es, "sqk%d" % i, [128, 4, 128], BF16) for i in range(2)]
            ABsb = [sb(es, "ABs%d" % i, [128, 4, 129], F32) for i in range(2)]
            lnbb = [sb(es, "lnb%d" % i, [128, 4, 128], F32) for i in range(2)]
            Cn = sb(es, "Cn", [128, 4, 129], F32)
            Cnb = sb(es, "Cnb", [128, 4, 129], BF16)
            hsb = [sb(es, "hs%d" % i, [128, 4, 128], F32) for i in range(2)]
            smb = [sb(es, "sm%d" % i, [128, 64], F32) for i in range(2)]
            ymt = sb(es, "ymt", [128, 4, 128], F32)
            mhalf = sb(es, "mhalf", [128, 4], F32)
            MSET("pool", mhalf[:], -0.5, w=["mhalf"])
            for i in range(2):
                MSET("pool", V1b[i][:, :, 128:129], 1.0, w=[("V1", i)])
            MSET("pool", Cn[:], 0.0, w=["Cn"])
            MSET("pool", Cnb[:], 0.0, w=["Cnb"])

            def pre(c_, xm, kxm):
                tt, p = c_ % 4, c_ % 2
                tc_ = slice(tt * 128, (tt + 1) * 128)
                ch4 = slice(c_ * 4, c_ * 4 + 4)
                for h in range(4):
                    MM(ps[2][:, h * 128:(h + 1) * 128], xcv[:, h, tc_], wqkv[:, 1, h, :], True, True,
                       r=[("xcv", h), "wqkv"], w=[P(2)])
                    MM(ps[3][:, h * 128:(h + 1) * 128], xm[:, h, 3 + tt * 128:3 + (tt + 1) * 128], wqkv[:, 2, h, :],
                       True, True, r=[(kxm, h), "wqkv"], w=[P(3)])
                    MM(ps[4][:, h * 128:(h + 1) * 128], kT[:, h, tc_], qT[:, h, tc_], True, True,
                       r=["kT", "qT"], w=[P(4)])
                for h in range(4):
                    ACT(kwb[p][:, h, :], ps[2][:, h * 128:(h + 1) * 128], AF.Copy, r=[P(2), "Wt"], w=[("kw", p)],
                        scale=Wt[:, 1, c_ * 4 + h:c_ * 4 + h + 1])
                CP("act", V1b[p][:, :, 0:128], ps[3][:].rearrange("p (h d) -> p h d", d=128), r=[P(3)], w=[("V1", p)])
                for h in range(4):
                    STT(sqkb[p][:, h, :], ps[4][:, h * 128:(h + 1) * 128], Wt[:, 0, c_ * 4 + h:c_ * 4 + h + 1], triu,
                        ALU.mult, ALU.mult, r=[P(4), "Wt", "cst"], w=[("sqk", p)])

            def rec(c_, filler=None):
                tt, p = c_ % 4, c_ % 2
                tc_ = slice(tt * 128, (tt + 1) * 128)
                kw, V1, sqk = kwb[p], V1b[p], sqkb[p]
                for hp in range(2):
                    ab = ps[5 + hp]
                    for hh in range(2):
                        h = 2 * hp + hh
                        MM(ab[:, hh * 129:(hh + 1) * 129], sqk[:, h, :], V1[:, h, :], hh == 0, False,
                           r=[("sqk", p), ("V1", p)], w=[P(5 + hp)])
                        MM(ab[:, hh * 129:(hh + 1) * 129], qT[:, h, tc_], Cnb[:, h, :], False, True,
                           r=["qT", "Cnb"], w=[P(5 + hp)])
                    if hp == 1 and filler is not None:
                        filler()
                    for hh in range(2):
                        h = 2 * hp + hh
                        MM(ps[7][:, hh * 129:(hh + 1) * 129], kw[:, h, :], V1[:, h, :], hh == 0, True,
                           r=[("kw", p), ("V1", p)], w=[P(7)])
                    for hh in range(2):
                        h = 2 * hp + hh
                        STT(Cn[:, h, :], Cn[:, h, :], Ebc[:, c_ * 4 + h:c_ * 4 + h + 1], ps[7][:, hh * 129:(hh + 1) * 129],
                            ALU.mult, ALU.add, r=["Cn", "Ebc", P(7)], w=["Cn"])
                    CP("dve", Cnb[:, 2 * hp:2 * hp + 2, :], Cn[:, 2 * hp:2 * hp + 2, :], r=["Cn"], w=["Cnb"])
                    CP("act", ABsb[p][:, 2 * hp:2 * hp + 2, :], ab[:, 0:258].rearrange("p (h d) -> p h d", d=129),
                       r=[P(5 + hp)], w=[("ABs", p)])

            def epi_a(c_):
                tt, p = c_ % 4, c_ % 2
                A = ABsb[p]
                sm_ = smb[p]
                hs_ = hsb[p]
                den = sm_[:, 0:4]
                kd, kmv, krs = ("den", p), ("mv", p), ("rstd4", p)
                STT(den, A[:, :, 128], -1.0, A[:, :, 128], ALU.mult, ALU.max, r=[("ABs", p)], w=[kd])
                TT("dve", den, den, Wt[:, 2, c_ * 4:c_ * 4 + 4], ALU.max, r=[kd, "Wt"], w=[kd])
                RCP(den, den, r=[kd], w=[kd])
                mvv = sm_[:, 48:56].rearrange("p (h t) -> p h t", t=2)
                for h in range(4):
                    STT(hs_[:, h, :], A[:, h, 0:128], sm_[:, h:h + 1], sgo[:, tt, h * 128:(h + 1) * 128], ALU.mult, ALU.mult,
                        r=[("ABs", p), kd, ("sgo", tt)], w=[("hs", p, h)])
                    st = sm_[:, 16 + 8 * h:16 + 8 * h + 6]
                    S.op("dve", (lambda o, i: (lambda e: e.bn_stats(out=o, in_=i)))(st, hs_[:, h, :]), r=[("hs", p, h)], w=[("st", p, h)])
                    S.op("dve", (lambda o, i: (lambda e: e.bn_aggr(out=o, in_=i)))(mvv[:, h, :], st), r=[("st", p, h)], w=[kmv])
                rstd4 = sm_[:, 56:60]
                nmr4 = sm_[:, 60:64]
                TS("pool", rstd4, mvv[:, :, 1], EPS, None, ALU.add, None, r=[kmv], w=[krs])
                TT("pool", rstd4, rstd4, mhalf[:], ALU.pow, r=[krs, "mhalf"], w=[krs])
                TS("pool", nmr4, mvv[:, :, 0], -1.0, None, ALU.mult, None, r=[kmv], w=[("nmr4", p)])
                TT("pool", nmr4, nmr4, rstd4, ALU.mult, r=[("nmr4", p), krs], w=[("nmr4", p)])

            def epi_b(c_):
                p = c_ % 2
                sm_ = smb[p]
                for h in range(4):
                    ACT(lnbb[p][:, h, :], hsb[p][:, h, :], AF.Identity, r=[("hs", p, h), ("rstd4", p), ("nmr4", p)], w=[("lnb", p)],
                        scale=sm_[:, 56 + h:57 + h], bias=sm_[:, 60 + h:61 + h])

            def outp(c_):
                tt, p = c_ % 4, c_ % 2
                blk_ = c_ // 4
                tc_ = slice(tt * 128, (tt + 1) * 128)
                b = nextbank()
                for h in range(4):
                    TR(ps[b][:, h * 128:(h + 1) * 128], lnbb[p][:, h, :], r=[("lnb", p)], w=[P(b)])
                for h in range(4):
                    STT(ymt[:, h, :], ps[b][:, h * 128:(h + 1) * 128], part[:, MG + h:MG + h + 1], xs[:, h, tc_],
                        ALU.mult, ALU.add, r=[P(b), "par", "xs"], w=[("ymt", h)])
                    TT("pool", ymT[:, h, blk_ * 512 + tt * 128:blk_ * 512 + (tt + 1) * 128], ymt[:, h, :], zs[:, h, tc_],
                       ALU.mult, r=[("ymt", h), "zs"], w=["ymT"])

            def oproj_mm(blk, tt):
                for c in range(8):
                    MM(ps[3][:], hT[:, c, blk * 512 + tt * 128:blk * 512 + (tt + 1) * 128], wm_bf[:, c, 512:1024],
                       c == 0, c == 7, r=[("wm", c), ("hT", blk)], w=[P(3)])

            def oproj_ev(tt):
                ACT(sgo[:, tt, :], ps[3][:], AF.Sigmoid, r=[P(3)], w=[("sgo", tt)])

            for blk in range(NB if stop_after > 1 else 1):
                if blk == 0:
                    nxt = frontA(0)
                xm, kxm = nxt
                frontBC(blk, False)
                cols = slice(blk * 512, (blk + 1) * 512)
                for h in range(4):
                    b = nextbank()
                    for c in range(8):
                        MM(ps[b][:], wm_bf[:, c, 1024 + h * 128:1024 + (h + 1) * 128], hT[:, c, cols], c == 0, c == 7,
                           r=[("wm", c), ("hT", blk)], w=[P(b)])
                    ACT(zs[:, h, :], ps[b][:], AF.Silu, r=[P(b)], w=["zs"])
                    ACT(xs[:, h, :], xcv[:, h, :], AF.Copy, scale=part[:, SK + h:SK + h + 1],
                       r=[("xcv", h), "par"], w=["xs"])
                if blk == 0:
                    for tt in range(4):
                        oproj_mm(0, tt)
                        oproj_ev(tt)
                c0 = blk * 4
                pre(c0, xm, kxm)
                for tt in range(4):
                    c_ = c0 + tt
                    if tt < 3:
                        pre(c_ + 1, xm, kxm)
                    more = blk + 1 < (NB if stop_after > 1 else 1)
                    rec(c_, (lambda b_=blk + 1, t_=tt: oproj_mm(b_, t_)) if more else None)
                    if tt == 1 and more:
                        nxt = frontA(blk + 1)
                    epi_a(c_)
                    if more:
                        oproj_ev(tt)
                    if tt > 0:
                        epi_b(c_ - 1)
                        outp(c_ - 1)
                epi_b(c0 + 3)
                outp(c0 + 3)
            if dbg and dbg >= 3:
                dump(es, "ymT", ymT[:, :, 0:512], [128, 4, 512], BF16)
                dump(es, "hs", hsb[0][:], [128, 4, 128])
                dump(es, "Cn", Cn[:], [128, 4, 129])
            finish(blk1)
        wm_cm.__exit__(None, None, None)
        if stop_after <= 1:
            return nc

        rot2 = [0]

        def nb2():
            rot2[0] ^= 1
            return rot2[0]

        GROUPS = [(3 * g, 3 * g + 3) for g in range(10)] + [(30, 32)]
        DG = 44

        def proj_fm(wt, j, dst, blk, func, scale, kd):
            b = nb2()
            cols = slice(blk * 512, (blk + 1) * 512)
            for c in range(8):
                MM(ps[b][:], wt[:, j, c, :], hT[:, c, cols], c == 0, c == 7, r=[kd + "_w", ("hT", blk)], w=[P(b)])
            if func is None:
                CP("dve", dst[:, cols], ps[b][:], r=[P(b)], w=[kd])
            else:
                ACT(dst[:, cols], ps[b][:], func, r=[P(b)], w=[kd], scale=scale)

        ydT = sb(top, "ydT", [128, 4, S_LEN], BF16)
        with nc.Block() as blk2, ExitStack() as es:
            lamt = sb(es, "lamt", [128, 8], F32)
            j64 = sb(es, "j64", [128, 64], F32)
            STT(j64[:], part[:, 64:128], 1.0, part[:, 128:192], ALU.mult, ALU.mult, r=["par"], w=["j64", "lam"], accum=lamt[:, 0:1])
            STT(j64[:], part[:, 192:256], 1.0, part[:, 256:320], ALU.mult, ALU.mult, r=["par", "j64"], w=["j64", "lam"], accum=lamt[:, 1:2])
            ACT(lamt[:, 2:4], lamt[:, 0:2], AF.Exp, r=["lam"], w=["lam"])
            TT("dve", lamt[:, 4:5], lamt[:, 2:3], lamt[:, 3:4], ALU.subtract, r=["lam"], w=["lam"])
            TS("dve", lamt[:, 5:6], lamt[:, 4:5], 0.2, -1.0, ALU.add, ALU.mult, r=["lam"], w=["lam"])
            wdb = [sb(es, "wd%d" % i, [128, 4, 8, 128], BF16) for i in range(2)]
            qdT = sb(es, "qdT", [128, S_LEN], BF16)
            kpd = [sb(es, "kpd%d" % i, [128, S_LEN], BF16) for i in range(2)]
            MSET("pool", kpd[0][64:128, :], 0.0, w=["kd"])
            MSET("pool", kpd[1][0:64, :], 0.0, w=["kd"])
            zds = sb(es, "zds", [128, S_LEN], BF16)
            V1d = sb(es, "V1d", [128, 32, 128], BF16)
            ptb = [sb(es, "pt%d" % i, [128, 512], BF16) for i in range(4)]
            o0s = sb(es, "o0s", [128, 512], F32)
            o1s = sb(es, "o1s", [128, 512], F32)
            l0s = sb(es, "l0s", [128, 512], F32)
            l1s = sb(es, "l1s", [128, 512], F32)
            sqb = sb(es, "sqb", [128, 512], BF16)
            rsd = l0s
            dg08 = sb(es, "dg08", [128, 4], F32)
            TS("dve", dg08[:], part[:, DG:DG + 4], 0.8, None, ALU.mult, None, r=["par"], w=["dg08"])
            GROUPS4 = [(4 * g, 4 * g + 4) for g in range(8)]

            def proj_k(wt, blk):
                b = nb2()
                cols = slice(blk * 512, (blk + 1) * 512)
                for c in range(8):
                    MM(ps[b][:], wt[:, 1, c, :], hT[:, c, cols], c == 0, c == 7, r=["kd_w", ("hT", blk)], w=[P(b)])
                CP("dve", kpd[0][0:64, cols], ps[b][0:64, :], r=[P(b)], w=["kd"])
                CP("dve", kpd[1][64:128, cols], ps[b][64:128, :], r=[P(b)], w=["kd"])
            for h in range(4):
                wd = wdb[h % 2]
                for j, off, kk in ((0, 1536, "qd_w"), (1, 2048, "kd_w"), (3, 3072, "zd_w"), (2, 2560, "vd_w")):
                    DMA("pool", wd[:, j, :, :], win_v[:, :, off + h * 128:off + (h + 1) * 128], r=[], w=[kk])
                for blk in range(NB):
                    proj_fm(wd, 0, qdT, blk, AF.Identity, 0.125, "qd")
                    proj_k(wd, blk)
                    proj_fm(wd, 3, zds, blk, AF.Silu, None, "zd")
                    for tt in range(4):
                        tl = blk * 4 + tt
                        for c in range(8):
                            MM(ps[3][:, tt * 128:(tt + 1) * 128], hT[:, c, tl * 128:(tl + 1) * 128], wd[:, 2, c, :], c == 0, c == 7,
                               r=["vd_w", ("hT", blk)], w=[P(3)])
                    CP("dve", V1d[:, blk * 4:(blk + 1) * 4, :], ps[3][:].rearrange("p (t d) -> p t d", d=128), r=[P(3)], w=["V1d"])
                steps = [(g, qs, qe, m, kb) for g, (qs, qe) in enumerate(GROUPS4) for m in range(2) for kb in range(qe)]
                LA = 3
                sctr = [0]

                def emit_qk(i):
                    g, qs, qe, m, kb = steps[i]
                    nsub = qe - qs
                    pr = slice(m * 64, (m + 1) * 64)
                    sv = max(0, kb - qs)
                    sbk = sctr[0] % 4
                    sctr[0] += 1
                    fc = slice(sv * 128, nsub * 128)
                    MM(ps[sbk][:, fc], kpd[m][:, kb * 128:(kb + 1) * 128], qdT[:, (qs + sv) * 128:qe * 128], True, True,
                       r=["kd", "qd"], w=[P(sbk)])
                    pt = ptb[i % 4]
                    kpt = ("pt", i % 4)
                    ACT(pt[:, fc], ps[sbk][:, fc], AF.Exp, r=[P(sbk)], w=[kpt])
                    if kb >= qs:
                        dc = slice((kb - qs) * 128, (kb - qs + 1) * 128)
                        TT("dve", pt[:, dc], pt[:, dc], mask_bf[:], ALU.mult, r=[kpt, "mask_bf"], w=[kpt])

                def emit_pv(i):
                    g, qs, qe, m, kb = steps[i]
                    nsub = qe - qs
                    sv = max(0, kb - qs)
                    fc = slice(sv * 128, nsub * 128)
                    pt = ptb[i % 4]
                    kpt = ("pt", i % 4)
                    MM(ps[4 + m][:, fc], V1d[:, kb, :], pt[:, fc], kb == 0, kb == qe - 1, r=[kpt, "V1d"], w=[P(4 + m)])
                    MM(ps[6 + m][:, fc], ones_bf[:], pt[:, fc], kb == 0, kb == qe - 1, r=[kpt, "ones_bf"], w=[P(6 + m)])
                    if m == 1 and kb == qe - 1:
                        epilogue1(qs, qe)
                        pending.append((i + 12, qs, qe))

                pending = []

                def epilogue1(qs, qe):
                    CP("act", o0s[:], ps[4][:], r=[P(4)], w=["o0s"])
                    CP("dve", l0s[:], ps[6][:], r=[P(6)], w=["l0s"])
                    CP("act", o1s[:], ps[5][:], r=[P(5)], w=["o1s"])
                    CP("dve", l1s[:], ps[7][:], r=[P(7)], w=["l1s"])
                    RCP(l0s[:], l0s[:], r=["l0s"], w=["l0s"])
                    RCP(l1s[:], l1s[:], r=["l1s"], w=["l1s"])
                    TT("dve", o0s[:], o0s[:], l0s[:], ALU.mult, r=["o0s", "l0s"], w=["o0s"])
                    TT("dve", o1s[:], o1s[:], l1s[:], ALU.mult, r=["o1s", "l1s"], w=["o1s"])
                    STT(o0s[:], o1s[:], lamt[:, 5:6], o0s[:], ALU.mult, ALU.add, r=["o0s", "o1s", "lam"], w=["o0s"])

                def epilogue2(qs, qe):
                    cols = slice(qs * 128, qe * 128)
                    ACT(sqb[:], o0s[:], AF.Square, r=["o0s"], w=["sqb"])
                    eb = sctr[0] % 4
                    sctr[0] += 1
                    MM(ps[eb][:], ones_bf[:], sqb[:], True, True, r=["sqb", "ones_bf"], w=[P(eb)])
                    ACT(rsd[:], ps[eb][:], AF.Sqrt, r=[P(eb)], w=["l0s"], bias=EPS, scale=1.0 / 128)
                    RCP(rsd[:], rsd[:], r=["l0s"], w=["l0s"])
                    STT(o0s[:], o0s[:], dg08[:, h:h + 1], rsd[:], ALU.mult, ALU.mult, r=["o0s", "dg08", "l0s"], w=["o0s"])
                    TT("dve", ydT[:, h, cols], o0s[:], zds[:, cols], ALU.mult, r=["o0s", "zd"], w=["ydT"])

                for i in range(len(steps) + LA):
                    if i < len(steps):
                        emit_qk(i)
                    if i >= LA:
                        emit_pv(i - LA)
                    while pending and pending[0][0] <= i - LA:
                        _, pqs, pqe = pending.pop(0)
                        epilogue2(pqs, pqe)
                while pending:
                    _, pqs, pqe = pending.pop(0)
                    epilogue2(pqs, pqe)
            if dbg and dbg >= 4:
                dump(es, "ydT", ydT[:, :, 0:1024], [128, 4, 1024], BF16)
                dump(es, "lamt", lamt[:], [128, 8])
            finish(blk2)
        if stop_after <= 2:
            return nc

        ycT = sb(top, "ycT", [128, 4, S_LEN], BF16)
        with nc.Block() as blk3, ExitStack() as es:
            wcb = [sb(es, "wc%d" % i, [128, 2, 8, 128], BF16) for i in range(2)]
            qcT = sb(es, "qcT", [128, S_LEN], BF16)
            zcs = sb(es, "zcs", [128, S_LEN], BF16)
            ptb = [sb(es, "ptc%d" % i, [128, 512], BF16) for i in range(4)]
            rlb = [sb(es, "rl%d" % i, [128, 512], F32) for i in range(2)]
            onb = [sb(es, "on%d" % i, [128, 512], F32) for i in range(2)]
            for h in range(4):
                wc = wcb[h % 2]
                for j, off, kk in ((0, 3584, "qc_w"), (1, 4096, "zc_w")):
                    DMA("pool", wc[:, j, :, :], win_v[:, :, off + h * 128:off + (h + 1) * 128], r=[], w=[kk])
                for blk in range(NB):
                    proj_fm(wc, 0, qcT, blk, AF.Copy, None, "qc")
                    proj_fm(wc, 1, zcs, blk, AF.Silu, None, "zc")

                def c_qk(j):
                    cols = slice(j * 512, (j + 1) * 512)
                    for mc in range(2):
                        sbk = 2 * (j % 2) + mc
                        MM(ps[sbk][:], kmT[:, h, mc * 128:(mc + 1) * 128], qcT[:, cols], True, True, r=["kmT", "qc"], w=[P(sbk)])
                        ACT(ptb[sbk][:], ps[sbk][:], AF.Exp, r=[P(sbk)], w=[("ptc", sbk)])

                def c_pv(j):
                    cols = slice(j * 512, (j + 1) * 512)
                    p = j % 2
                    for mc in range(2):
                        sbk = 2 * p + mc
                        MM(ps[4 + p][:], vm1[:, mc, h, 0:128], ptb[sbk][:], mc == 0, mc == 1, r=[("ptc", sbk), "vm1"], w=[P(4 + p)])
                    for mc in range(2):
                        sbk = 2 * p + mc
                        MM(ps[6 + p][:], ones_bf[:], ptb[sbk][:], mc == 0, mc == 1, r=[("ptc", sbk), "ones_bf"], w=[P(6 + p)])
                    RCP(rlb[p][:], ps[6 + p][:], r=[P(6 + p)], w=[("rl", p)])
                    TT("dve", onb[p][:], ps[4 + p][:], rlb[p][:], ALU.mult, r=[P(4 + p), ("rl", p)], w=[("on", p)])
                    TT("pool", ycT[:, h, cols], onb[p][:], zcs[:, cols], ALU.mult, r=[("on", p), "zc"], w=["ycT"])

                for j in range(NB + 1):
                    if j < NB:
                        c_qk(j)
                    if j >= 1:
                        c_pv(j - 1)
            if dbg and dbg >= 5:
                dump(es, "ycT", ycT[:, :, 0:1024], [128, 4, 1024], BF16)
            finish(blk3)
        if stop_after <= 3:
            return nc

        with nc.Block() as blk4, ExitStack() as es:
            hflat = hT[:].rearrange("p c t -> p (c t)")
            wout_bf = hflat[:, 0:12 * D].rearrange("p (c f) -> p c f", f=D)
            wout_v = wout_d.rearrange("(c p) f -> p c f", p=128)
            for c in range(12):
                DMA("pool", wout_bf[:, c, :], wout_v[:, c, :], r=[], w=[("wout", c)])
            hf32 = hflat[:, 12 * D:].bitcast(F32)
            NXB = 6
            xtb = [hf32[:, i * D:(i + 1) * D] for i in range(NXB)]
            fgt = hf32[:, NXB * D:(NXB + 1) * D]
            DMA("sp", fgt, fg_d[:, :], r=[], w=["fg"])
            junk = sb(es, "junk4", [128, D], BF16)
            ss4 = sb(es, "ss4", [128, 2 * NXB], F32)
            ysrc = [(ymT, hh) for hh in range(4)] + [(ydT, hh) for hh in range(4)] + [(ycT, hh) for hh in range(4)]

            def load_x(i):
                DMA("sp", xtb[i % NXB], x_d[i * 128:(i + 1) * 128, :], r=[], w=[("xo", i % NXB)])

            PF = 4
            for i in range(PF):
                load_x(i)
            for i in range(NT):
                p2 = i % NXB
                xt = xtb[p2]
                kx = ("xo", p2)
                tcs = slice(i * 128, (i + 1) * 128)
                if i + PF < NT:
                    load_x(i + PF)
                for half in range(2):
                    b = 2 * (i % 4) + half
                    hc = slice(half * 512, (half + 1) * 512)
                    for ch, (src, hh) in enumerate(ysrc):
                        MM(ps[b][:], src[:, hh, tcs], wout_bf[:, ch, hc], ch == 0, ch == 11, r=[("wout", ch), "ycat"], w=[P(b)])
                    TT("dve", xt[:, hc], ps[b][:], xt[:, hc], ALU.add, r=[P(b), kx], w=[kx])
                ss = ss4[:, 2 * p2:2 * p2 + 1]
                rs = ss4[:, 2 * p2 + 1:2 * p2 + 2]
                ACT(junk[:], xt, AF.Square, r=[kx], w=["junk4", ("ss4", p2)], accum=ss)
                ACT(rs, ss, AF.Sqrt, r=[("ss4", p2)], w=[("ss4", p2)], bias=EPS, scale=1.0 / D)
                RCP(rs, rs, r=[("ss4", p2)], w=[("ss4", p2)])
                STT(xt, xt, rs, fgt, ALU.mult, ALU.mult, r=[kx, ("ss4", p2), "fg"], w=[kx])
                DMA("sp", out_d[i * 128:(i + 1) * 128, :], xt, r=[kx], w=[])
            finish(blk4)
    return nc


def make_consts():
    cst = np.zeros((128, 512), np.float32)
    cst[:, 0:128] = np.eye(128, dtype=np.float32)
    cst[:, 128:256] = np.triu(np.ones((128, 128), np.float32))
    p = np.arange(128)
    c, h = p // 4, p % 4
    cst[:, 256:384] = ((h[:, None] == h[None, :]) & (c[:, None] < c[None, :])).astype(np.float32)
    cst[:, 384:512] = 1.0
    return cst


def make_in_maps(inp):
    f = lambda a: np.ascontiguousarray(a, dtype=np.float32)
    cst = make_consts()
    par = np.zeros((128, 320), np.float32)
    par[:, 0:8] = inp["norm_g"][0].reshape(8, 128).T
    par[:, 8:16] = inp["mem_norm_g"][0].reshape(8, 128).T
    par[:, 16:32] = inp["conv_w"][0].reshape(4, 4, 128).transpose(2, 1, 0).reshape(128, 16)
    par[:, 32:36] = inp["conv_b"][0].reshape(4, 128).T
    par[:, 36:40] = inp["mnorm_g"][0].reshape(4, 128).T
    par[:, 40:44] = inp["skip_m"][0].reshape(4, 128).T
    par[:, 44:48] = inp["dnorm_g"][0].reshape(4, 128).T
    par[:, 48] = np.tile(inp["b_if"][0][0:4], 32)
    par[:, 49] = np.tile(inp["b_if"][0][4:8], 32)
    par[:, 64:128] = inp["lam_q1"][0][None, :]
    par[:, 128:192] = inp["lam_k1"][0][None, :]
    par[:, 192:256] = inp["lam_q2"][0][None, :]
    par[:, 256:320] = inp["lam_k2"][0][None, :]
    fg = np.ascontiguousarray(np.broadcast_to(inp["final_g"][None, :], (128, D)), dtype=np.float32)
    shared = {
        "w_in": f(inp["w_in"][0]), "w_kv": f(inp["w_mem_kv"][0]), "w_out": f(inp["w_out"][0]),
        "wq": f(inp["wq_m"][0]), "wk": f(inp["wk_m"][0]), "wv": f(inp["wv_m"][0]), "w_if": f(inp["w_if"][0]),
        "cst": cst, "par": par, "fg": fg,
    }
    maps = []
    for b in range(8):
        m = dict(shared)
        m["x"] = f(inp["x"][b])
        m["mem"] = f(inp["mem"][b])
        maps.append(m)
    return maps


_NC_CACHE = {}


def kernel(**inputs):
    if "nc" not in _NC_CACHE:
        _NC_CACHE["nc"] = build_nc()
    nc = _NC_CACHE["nc"]
    maps = make_in_maps(inputs)
    res = run_bass_kernel_spmd(nc, maps, core_ids=list(range(8)))
    return np.stack([np.asarray(r["out"], dtype=np.float32) for r in res.results], axis=0)
```

```python
import math
from contextlib import ExitStack

import numpy as np
import concourse.bass as bass
import concourse.mybir as mybir
from concourse.bass_utils import run_bass_kernel_spmd

F32 = mybir.dt.float32
BF16 = mybir.dt.bfloat16
AF = mybir.ActivationFunctionType
ALU = mybir.AluOpType

S_LEN = 4096
D = 1024
NT = 32
NB = 8
EPS = 1e-6
LN_SQRT_DH = 0.5 * math.log(128.0)


class Sched:
    ENG = ("pe", "act", "dve", "pool", "sp")

    def __init__(self, nc, sems, dsems):
        self.nc = nc
        self.sems = sems
        self.dsems = dsems
        self.q = {e: [] for e in self.ENG}
        self.cnt = {e: 0 for e in self.ENG}
        self.last_w = {}
        self.readers = {}
        self.seen = {e: {} for e in self.ENG}
        self.n_dma = len(dsems)
        self.dma_cnt = [0] * self.n_dma
        half = self.n_dma // 2
        self.dma_pool = {"sp": list(range(0, half)), "pool": list(range(half, self.n_dma))}
        self.dma_rr = {"sp": 0, "pool": 0}

    def _need(self, eng, tok, waits):
        src, val = tok
        if src == eng and eng in ("pe", "sp"):
            return
        if self.seen[eng].get(src, 0) >= val:
            return
        self.seen[eng][src] = val
        waits[src] = max(waits.get(src, 0), val)

    def op(self, eng, fn, r=(), w=(), dma=False):
        waits = {}
        for k in r:
            t = self.last_w.get(k)
            if t is not None:
                self._need(eng, t, waits)
        for k in w:
            t = self.last_w.get(k)
            if t is not None:
                self._need(eng, t, waits)
            for t in self.readers.get(k, ()):
                self._need(eng, t, waits)
        if dma:
            pool_ = self.dma_pool[eng]
            i = pool_[self.dma_rr[eng] % len(pool_)]
            self.dma_rr[eng] += 1
            if self.dma_cnt[i] > 0:
                self._need(eng, (("d", i), 16 * self.dma_cnt[i]), waits)
            self.dma_cnt[i] += 1
            tok = (("d", i), 16 * self.dma_cnt[i])
        else:
            self.cnt[eng] += 1
            tok = (eng, self.cnt[eng])
        self.q[eng].append((list(waits.items()), fn, tok))
        for k in w:
            self.last_w[k] = tok
            self.readers[k] = []
        for k in r:
            self.readers.setdefault(k, []).append(tok)
        return tok

    def barrier(self):
        toks = [(e, self.cnt[e]) for e in self.ENG if self.cnt[e] > 0]
        toks += [(("d", i), 16 * c) for i, c in enumerate(self.dma_cnt) if c > 0]
        for e in self.ENG:
            waits = {}
            for t in toks:
                if t[0] != e:
                    self._need(e, t, waits)
            if waits:
                self.q[e].append((list(waits.items()), None, None))

    def emit(self, block):
        sems, dsems = self.sems, self.dsems

        def run(e, engobj):
            for waits, fn, tok in self.q[e]:
                for src, val in waits:
                    s = dsems[src[1]] if isinstance(src, tuple) else sems[src]
                    engobj.wait_ge(s, val)
                if fn is None:
                    continue
                ins = fn(engobj)
                if isinstance(tok[0], tuple):
                    ins.then_inc(dsems[tok[0][1]], 16)
                else:
                    ins.then_inc(sems[tok[0]], 1)
            self.q[e] = []

        @block.tensor
        def _(eng):
            run("pe", eng)

        @block.scalar
        def _(eng):
            run("act", eng)

        @block.vector
        def _(eng):
            run("dve", eng)

        @block.gpsimd
        def _(eng):
            run("pool", eng)

        @block.sync
        def _(eng):
            run("sp", eng)


def build_nc(stop_after=99, dbg=None):
    nc = bass.Bass("TRN2", target_bir_lowering=False)
    din = lambda n, s: nc.dram_tensor(n, s, F32, kind="ExternalInput").ap()
    x_d = din("x", [S_LEN, D])
    mem_d = din("mem", [256, D])
    win_d = din("w_in", [D, 4608])
    wkv_d = din("w_kv", [D, 1024])
    wout_d = din("w_out", [1536, D])
    wq_d = din("wq", [4, 128, 128])
    wk_d = din("wk", [4, 128, 128])
    wv_d = din("wv", [4, 128, 128])
    wif_d = din("w_if", [1536, 8])
    cst_d = din("cst", [128, 512])
    par_d = din("par", [128, 320])
    fg_d = din("fg", [128, D])
    out_d = nc.dram_tensor("out", [S_LEN, D], F32, kind="ExternalOutput").ap()
    dbg_outs = []
    win_v = win_d.rearrange("(c p) f -> p c f", p=128)

    top = ExitStack()
    with top:
        sb = lambda es, name, shape, dt: es.enter_context(nc.sbuf_tensor(name, shape, dt))
        ps = [top.enter_context(nc.psum_tensor("ps%d" % i, [128, 512], F32)) for i in range(8)]
        sems = {e: top.enter_context(nc.semaphore("s_" + e)) for e in Sched.ENG}
        dsems = [top.enter_context(nc.semaphore("d%d" % i)) for i in range(32)]
        S = Sched(nc, sems, dsems)
        P = lambda i: ("ps", i)

        def MM(out, lhsT, rhs, start, stop, r, w):
            return S.op("pe", lambda e: e.matmul(out, lhsT=lhsT, rhs=rhs, start=start, stop=stop,
                                                  skip_group_check=True), r=r, w=w)

        def TR(out, in_, r, w):
            return S.op("pe", lambda e: e.transpose(out=out, in_=in_, identity=ident[:]), r=list(r) + ["cst"], w=w)

        def ACT(out, in_, func, r, w, bias=None, scale=None, accum=None, eng="act"):
            kw = {}
            if bias is not None:
                kw["bias"] = bias
            if scale is not None:
                kw["scale"] = scale
            if accum is not None:
                kw["accum_out"] = accum
            return S.op("act", lambda e: e.activation(out=out, in_=in_, func=func, **kw), r=r, w=w)

        def TS(eng, out, in0, s1, s2, op0, op1, r, w):
            if op1 is None:
                return S.op(eng, lambda e: e.tensor_scalar(out=out, in0=in0, scalar1=s1, scalar2=None, op0=op0), r=r, w=w)
            return S.op(eng, lambda e: e.tensor_scalar(out=out, in0=in0, scalar1=s1, scalar2=s2, op0=op0, op1=op1), r=r, w=w)

        def TT(eng, out, in0, in1, op, r, w):
            return S.op(eng, lambda e: e.tensor_tensor(out=out, in0=in0, in1=in1, op=op), r=r, w=w)

        def STT(out, in0, scalar, in1, op0, op1, r, w, accum=None):
            if accum is not None:
                return S.op("dve", lambda e: e.scalar_tensor_tensor(out=out, in0=in0, scalar=scalar, in1=in1, op0=op0, op1=op1, accum_out=accum), r=r, w=w)
            return S.op("dve", lambda e: e.scalar_tensor_tensor(out=out, in0=in0, scalar=scalar, in1=in1, op0=op0, op1=op1), r=r, w=w)

        def CP(eng, out, in_, r, w):
            if eng == "act":
                return S.op("act", lambda e: e.copy(out=out, in_=in_), r=r, w=w)
            return S.op(eng, lambda e: e.tensor_copy(out=out, in_=in_), r=r, w=w)

        def MSET(eng, ap, val, w):
            return S.op(eng, lambda e: e.memset(ap, val), w=w)

        def RCP(out, in_, r, w):
            return S.op("dve", lambda e: e.reciprocal(out=out, in_=in_), r=r, w=w)

        def SCAN(out, d0, init, op0, r, w):
            return S.op("dve", lambda e: e.tensor_tensor_scan(out=out, data0=d0, data1=d0, initial=init, op0=op0, op1=ALU.bypass), r=r, w=w)

        def DMA(eng, out, in_, r, w):
            return S.op(eng, lambda e: e.dma_start(out=out, in_=in_), r=r, w=w, dma=True)

        def dump(es, name, ap, shape, dt=F32):
            d = nc.dram_tensor("dbg_" + name, list(shape), dt, kind="ExternalOutput").ap()
            S.barrier()
            dbg_outs.append(DMA("sp", d, ap, r=[], w=[]))

        def finish(block):
            S.barrier()
            S.emit(block)

        cstt = sb(top, "cstt", [128, 512], F32)
        ident = cstt[:, 0:128]
        triu = cstt[:, 128:256]
        M1 = cstt[:, 256:384]
        ones_f = cstt[:, 384:512]
        part = sb(top, "part", [128, 320], F32)
        mask_bf = sb(top, "mask_bf", [128, 128], BF16)
        ones_bf = sb(top, "ones_bf", [128, 128], BF16)
        ident_bf = sb(top, "ident_bf", [128, 128], BF16)
        kmT = sb(top, "kmT", [128, 4, 256], BF16)
        vm1 = sb(top, "vm1", [128, 2, 4, 129], BF16)
        ymT = sb(top, "ymT", [128, 4, S_LEN], BF16)
        hT = sb(top, "hT", [128, 8, S_LEN], BF16)
        wm_cm = nc.sbuf_tensor("wm_bf", [128, 8, 1536], BF16)
        wm_bf = wm_cm.__enter__()

        def rms_tile_a(es_bufs, src_rows, i):
            xtb, junk, ssb, xnb = es_bufs
            xt = xtb[i % 4]
            ss = ssb[:, 2 * (i % 4):2 * (i % 4) + 1]
            rs = ssb[:, 2 * (i % 4) + 1:2 * (i % 4) + 2]
            kx, ks = ("xt", i % 4), ("ss", i % 4)
            DMA("sp", xt[:], src_rows, r=[], w=[kx])
            ACT(junk[:], xt[:], AF.Square, r=[kx], w=["junk", ks], accum=ss)
            ACT(rs, ss, AF.Sqrt, r=[ks], w=[ks], bias=EPS, scale=1.0 / D)
            RCP(rs, rs, r=[ks], w=[ks])

        def rms_tile_b(es_bufs, gcol, dstT, col0, i):
            xtb, junk, ssb, xnb = es_bufs
            xt = xtb[i % 4]
            xn = xnb[i % 3]
            rs = ssb[:, 2 * (i % 4) + 1:2 * (i % 4) + 2]
            kx, kn, ks = ("xt", i % 4), ("xn", i % 3), ("ss", i % 4)
            ACT(xn[:], xt[:], AF.Copy, r=[kx, ks], w=[kn], scale=rs)
            for half in range(2):
                b = 2 * (i % 4) + half
                psb = ps[b][:].bitcast(BF16)
                for j in range(4):
                    c = 4 * half + j
                    S.op("pe", (lambda o, a: (lambda e: e.transpose(out=o, in_=a, identity=ident_bf[:])))(
                        psb[:, j * 128:(j + 1) * 128], xn[:, c * 128:(c + 1) * 128]), r=[kn, "ident_bf"], w=[P(b)])
                TT("dve", dstT[:, 4 * half:4 * half + 4, col0:col0 + 128],
                   psb[:, 0:512].rearrange("p (j t) -> p j t", t=128),
                   part[:, gcol + 4 * half:gcol + 4 * half + 4].unsqueeze(2).to_broadcast([128, 4, 128]),
                   ALU.mult, r=[P(b), "par"], w=[("hT", col0 // 512) if dstT is hT else "memT"])

        with nc.Block() as blk0, ExitStack() as es:
            DMA("sp", cstt[:], cst_d[:, :], r=[], w=["cst"])
            DMA("sp", part[:], par_d[:, :], r=[], w=["par"])
            CP("dve", mask_bf[:], triu, r=["cst"], w=["mask_bf"])
            CP("dve", ones_bf[:], ones_f, r=["cst"], w=["ones_bf"])
            CP("dve", ident_bf[:], ident, r=["cst"], w=["ident_bf"])
            xtb = [sb(es, "xt%d" % i, [128, D], F32) for i in range(4)]
            xnb = [sb(es, "xn%d" % i, [128, D], BF16) for i in range(3)]
            junk = sb(es, "junk", [128, D], BF16)
            ssb = sb(es, "ssb", [128, 8], F32)
            bufs = (xtb, junk, ssb, xnb)
            memT = sb(es, "memT", [128, 8, 256], BF16)
            wkv_bf = sb(es, "wkv_bf", [128, 8, 1024], BF16)
            wkv_v = wkv_d.rearrange("(c p) f -> p c f", p=128)
            for c in range(8):
                DMA("pool", wkv_bf[:, c, :], wkv_v[:, c, :], r=[], w=["wkv"])
            for c in range(8):
                DMA("pool", wm_bf[:, c, :], win_v[:, c, 0:1536], r=[], w=[("wm", c)])
            seq = [(mem_d[i * 128:(i + 1) * 128, :], 8, memT, i * 128) for i in range(2)]
            seq += [(x_d[i * 128:(i + 1) * 128, :], 0, hT, i * 128) for i in range(NT)]
            rms_tile_a(bufs, seq[0][0], 0)
            for k, (src_rows, gcol, dstT, col0) in enumerate(seq):
                if k + 1 < len(seq):
                    rms_tile_a(bufs, seq[k + 1][0], k + 1)
                rms_tile_b(bufs, gcol, dstT, col0, k)
            for h in range(4):
                b = 4 + h % 2
                for c in range(8):
                    MM(ps[b][:, 0:256], wkv_bf[:, c, h * 128:(h + 1) * 128], memT[:, c, :], c == 0, c == 7,
                       r=["wkv", "memT"], w=[P(b)])
                ACT(kmT[:, h, :], ps[b][:, 0:256], AF.Identity, r=[P(b)], w=["kmT"], scale=128.0 ** -0.5)
            MSET("pool", vm1[:, :, :, 128:129], 1.0, w=["vm1"])
            for mt in range(2):
                b = 6 + mt
                for c in range(8):
                    MM(ps[b][:], memT[:, c, mt * 128:(mt + 1) * 128], wkv_bf[:, c, 512:1024], c == 0, c == 7,
                       r=["wkv", "memT"], w=[P(b)])
                CP("dve", vm1[:, mt, :, 0:128], ps[b][:].rearrange("p (h d) -> p h d", d=128), r=[P(b)], w=["vm1"])
            if dbg == 0:
                dump(es, "hT", hT[:, :, 0:512], [128, 8, 512], BF16)
                dump(es, "kmT", kmT[:], [128, 4, 256], BF16)
                dump(es, "vm1", vm1[:], [128, 2, 4, 129], BF16)
            finish(blk0)
        if stop_after == 0:
            return nc

        with nc.Block() as blk1, ExitStack() as es:
            wqkv = sb(es, "wqkv", [128, 3, 4, 128], BF16)
            for j, wd in enumerate((wq_d, wk_d, wv_d)):
                DMA("pool", wqkv[:, j, :, :], wd.rearrange("h d e -> d h e"), r=[], w=["wqkv"])
            wif_bf = sb(es, "wif_bf", [128, 12, 8], BF16)
            DMA("pool", wif_bf[:], wif_d.rearrange("(j p) g -> p j g", p=128), r=[], w=["wif"])
            ABf = sb(es, "ABf", [128, 2, 4, 8], BF16)
            wTs = [sb(es, "wTs%d" % i, [128, 128], BF16) for i in range(3)]
            psb4 = ps[4][:].bitcast(BF16)
            for h in range(4):
                for j in range(3):
                    S.op("pe", (lambda o, a: (lambda e: e.transpose(out=o, in_=a, identity=ident_bf[:])))(
                        psb4[:, j * 128:(j + 1) * 128], wqkv[:, j, h, :]), r=["wqkv", "ident_bf"], w=[P(4)])
                    CP("dve", wTs[j][:], psb4[:, j * 128:(j + 1) * 128], r=[P(4)], w=[("wTs", j)])
                MM(ps[5][:, h * 8:(h + 1) * 8], wTs[0][:], wif_bf[:, h, :], True, False, r=[("wTs", 0), "wif"], w=[P(5)])
                MM(ps[5][:, h * 8:(h + 1) * 8], wTs[1][:], wif_bf[:, 4 + h, :], False, True, r=[("wTs", 1), "wif"], w=[P(5)])
                MM(ps[5][:, 32 + h * 8:32 + (h + 1) * 8], wTs[2][:], wif_bf[:, 8 + h, :], True, True, r=[("wTs", 2), "wif"], w=[P(5)])
            CP("dve", ABf[:].rearrange("p a h g -> p (a h g)"), ps[5][:, 0:64], r=[P(5)], w=["ABf"])
            xmb = [sb(es, "xm%d" % i, [128, 4, 515], BF16) for i in range(2)]
            xcv = sb(es, "xcv", [128, 4, 512], BF16)
            xs = sb(es, "xs", [128, 4, 512], BF16)
            qT = sb(es, "qT", [128, 4, 512], BF16)
            kT = sb(es, "kT", [128, 4, 512], BF16)
            vT = None
            zs = sb(es, "zs", [128, 4, 512], BF16)
            sgo = sb(es, "sgo", [128, 4, 512], BF16)
            Dg = sb(es, "Dg", [128, 16, 128], BF16)
            for hj in range(16):
                TS("dve", Dg[:, hj, :], ident, part[:, 16 + hj:16 + hj + 1], None, ALU.mult, None, r=["cst", "par"], w=["Dg"])
            Gtok = sb(es, "Gtok", [128, 32, 8], F32)
            CONVW, CONVB, MG, SK = 16, 32, 36, 40
            rot = [0]

            def nextbank():
                rot[0] ^= 1
                return rot[0]

            def frontA(blk):
                xm = xmb[blk % 2]
                kxm = ("xm", blk % 2)
                cols = slice(blk * 512, (blk + 1) * 512)
                if blk == 0:
                    MSET("pool", xm[:, :, 0:3], 0.0, w=[kxm])
                else:
                    CP("pool", xm[:, :, 0:3], xmb[(blk - 1) % 2][:, :, 512:515],
                       r=[("xm", (blk - 1) % 2)] + [(("xm", (blk - 1) % 2), hh) for hh in range(4)], w=[kxm])
                for h in range(4):
                    b = nextbank()
                    for c in range(8):
                        MM(ps[b][:], wm_bf[:, c, h * 128:(h + 1) * 128], hT[:, c, cols], c == 0, c == 7,
                           r=[("wm", c), ("hT", blk)], w=[P(b)])
                    CP("act", xm[:, h, 3:515], ps[b][:], r=[P(b)], w=[(kxm, h)])
                return xm, kxm

            def frontBC(blk, need_v, do_qk=True):
                xm = xmb[blk % 2]
                kxm = ("xm", blk % 2)
                for h in range(4):
                    b = nextbank()
                    for j in range(4):
                        MM(ps[b][:], Dg[:, 4 * h + j, :], xm[:, h, j:j + 512], j == 0, j == 3,
                           r=["Dg", (kxm, h), kxm], w=[P(b)])
                    ACT(xcv[:, h, :], ps[b][:], AF.Silu, r=[P(b), "par"], w=[("xcv", h)],
                        bias=part[:, CONVB + h:CONVB + h + 1])
                for h in range(4 if do_qk else 0):
                    for j, dst, kd in ((0, qT, "qT"), (1, kT, "kT")) + (((2, vT, "vT"),) if need_v else ()):
                        b = nextbank()
                        src = xcv[:, h, :] if j < 2 else xm[:, h, 3:515]
                        MM(ps[b][:], wqkv[:, j, h, :], src, True, True, r=["wqkv", ("xcv", h) if j < 2 else (kxm, h)], w=[P(b)])
                        CP("act" if j == 0 else "dve", dst[:, h, :], ps[b][:], r=[P(b)], w=[kd])
                return xm, kxm

            for blk in range(NB):
                if blk == 0:
                    frontA(0)
                frontBC(blk, False, do_qk=False)
                if blk + 1 < NB:
                    frontA(blk + 1)
                xm_ = xmb[blk % 2]
                kxm_ = ("xm", blk % 2)
                for tt in range(4):
                    tc_ = slice(tt * 128, (tt + 1) * 128)
                    for h in range(4):
                        MM(ps[2][:, tt * 8:(tt + 1) * 8], xcv[:, h, tc_], ABf[:, 0, h, :], h == 0, False,
                           r=[("xcv", h), "ABf"], w=[P(2)])
                        MM(ps[2][:, tt * 8:(tt + 1) * 8], xm_[:, h, 3 + tt * 128:3 + (tt + 1) * 128], ABf[:, 1, h, :], False, h == 3,
                           r=[(kxm_, h), "ABf"], w=[P(2)])
                CP("dve", Gtok[:, blk * 4:(blk + 1) * 4, :], ps[2][:, 0:32].rearrange("p (t g) -> p t g", g=8),
                   r=[P(2)], w=["Gtok"])
            if dbg and dbg >= 1:
                dump(es, "Gtok", Gtok[:], [128, 32, 8])
                dump(es, "qT", qT[:], [128, 4, 512], BF16)
                dump(es, "xcv", xcv[:], [128, 4, 512], BF16)

            rows = sb(es, "rows", [128, 8, 128], F32)
            GIr, GFr, LF, Fp, U_, W_, W2_, TH_ = [rows[:, i, :] for i in range(8)]
            cols_ = sb(es, "colsb", [128, 16], F32)
            col = lambda i: cols_[:, i:i + 1]
            rowt = sb(es, "rowt", [1, 4, 128], F32)
            Ebc = sb(es, "Ebc", [128, 128], F32)
            Wt = sb(es, "Wt", [128, 3, 128], F32)
            BI, BFc = 48, 49
            Gv = Gtok[:].rearrange("p c g -> p (c g)")
            gsp = sb(es, "gsp", [128, 2, 128], F32)
            CP("dve", gsp[:, 0, :].rearrange("p (c h) -> p c h", h=4), Gtok[:, :, 0:4], r=["Gtok"], w=["gsp"])
            CP("dve", gsp[:, 1, :].rearrange("p (c h) -> p c h", h=4), Gtok[:, :, 4:8], r=["Gtok"], w=["gsp"])
            TR(ps[3][:, 0:128], gsp[:, 0, :], r=["gsp"], w=[P(3)])
            TR(ps[3][:, 128:256], gsp[:, 1, :], r=["gsp"], w=[P(3)])
            CP("dve", rows[:, 0:2, :], ps[3][:, 0:256].rearrange("p (a t) -> p a t", t=128), r=[P(3)], w=["rows"])
            TS("dve", col(0), part[:, BFc:BFc + 1], -1.0, None, ALU.mult, None, r=["par"], w=["cols"])
            ACT(LF, GFr, AF.Exp, r=["rows", "cols"], w=["rows"], bias=col(0), scale=-1.0)
            ACT(LF, LF, AF.Ln, r=["rows"], w=["rows"], bias=1.0)
            SCAN(Fp, LF, 0.0, ALU.add, r=["rows"], w=["rows"])
            MM(ps[3][:, 256:257], M1, rows[:, 3, 127:128], True, True, r=["cst", "rows"], w=[P(3)])
            CP("dve", col(1), ps[3][:, 256:257], r=[P(3)], w=["cols"])
            TS("dve", Fp, Fp, col(1), None, ALU.add, None, r=["rows", "cols"], w=["rows"])
            STT(GIr, GIr, part[:, BI:BI + 1], Fp, ALU.add, ALU.add, r=["rows", "par"], w=["rows"])
            SCAN(U_, GIr, 0.0, ALU.max, r=["rows"], w=["rows"])
            MM(ps[3][0:1, 384:512], rows[:, 4, 127:128], ident, True, True, r=["rows", "cst"], w=[P(3)])
            CP("dve", rowt[:, 0, :], ps[3][0:1, 384:512], r=[P(3)], w=["rowt"])
            for h in range(4):
                SCAN(rowt[:, 1, :].rearrange("p (c h) -> p h c", h=4)[:, h, :],
                     rowt[:, 0, :].rearrange("p (c h) -> p h c", h=4)[:, h, :], 0.0, ALU.max, r=["rowt"], w=["rowt"])
            MSET("dve", rowt[:, 2, 0:4], 0.0, w=["rowt"])
            CP("dve", rowt[:, 2, 4:128], rowt[:, 1, 0:124], r=["rowt"], w=["rowt"])
            TT("dve", rowt[:, 3, :], rowt[:, 2, :], rowt[:, 1, :], ALU.subtract, r=["rowt"], w=["rowt"])
            MM(ps[3][:, 257:258], rowt[:, 2, :], ones_f[0:1, 0:1], True, True, r=["rowt", "cst"], w=[P(3)])
            MM(ps[3][:, 258:259], rowt[:, 3, :], ones_f[0:1, 0:1], True, True, r=["rowt", "cst"], w=[P(3)])
            CP("dve", cols_[:, 2:4], ps[3][:, 257:259], r=[P(3)], w=["cols"])
            MM(ps[4][:, 0:128], ones_f[0:1, :], rowt[:, 3, :], True, True, r=["rowt", "cst"], w=[P(4)])
            ACT(Ebc[:], ps[4][:, 0:128], AF.Exp, r=[P(4)], w=["Ebc"])
            TS("dve", col(4), col(2), -1.0, -LN_SQRT_DH, ALU.mult, ALU.add, r=["cols"], w=["cols"])
            TS("dve", col(5), col(2), -1.0, None, ALU.mult, None, r=["cols"], w=["cols"])
            TT("dve", col(6), col(4), col(3), ALU.add, r=["cols"], w=["cols"])
            ACT(W_, GIr, AF.Exp, r=["rows", "cols"], w=["rows"], bias=col(4))
            ACT(W2_, GIr, AF.Exp, r=["rows", "cols"], w=["rows"], bias=col(6))
            ACT(TH_, Fp, AF.Exp, r=["rows", "cols"], w=["rows"], bias=col(5))
            for i in range(3):
                TR(ps[4][:, 128 * (i + 1):128 * (i + 2)], rows[:, 5 + i, :], r=["rows"], w=[P(4)])
            CP("dve", Wt[:], ps[4][:, 128:512].rearrange("p (a t) -> p a t", t=128), r=[P(4)], w=["Wt"])
            if dbg and dbg >= 2:
                dump(es, "rows", rows[:], [128, 8, 128])
                dump(es, "Wt", Wt[:], [128, 3, 128])
                dump(es, "Ebc", Ebc[:], [128, 128])
                dump(es, "rowt", rowt[:], [1, 4, 128])

            kwb = [sb(es, "kw%d" % i, [128, 4, 128], BF16) for i in range(2)]
            V1b = [sb(es, "V1_%d" % i, [128, 4, 129], BF16) for i in range(2)]
            sqkb = [sb(es, "sqk%d" % i, [128, 4, 128], BF16) for i in range(2)]
            ABsb = [sb(es, "ABs%d" % i, [128, 4, 129], F32) for i in range(2)]
            lnbb = [sb(es, "lnb%d" % i, [128, 4, 128], F32) for i in range(2)]
            Cn = sb(es, "Cn", [128, 4, 129], F32)
            Cnb = sb(es, "Cnb", [128, 4, 129], BF16)
            hsb = [sb(es, "hs%d" % i, [128, 4, 128], F32) for i in range(2)]
            smb = [sb(es, "sm%d" % i, [128, 64], F32) for i in range(2)]
            ymt = sb(es, "ymt", [128, 4, 128], F32)
            mhalf = sb(es, "mhalf", [128, 4], F32)
            MSET("pool", mhalf[:], -0.5, w=["mhalf"])
            for i in range(2):
                MSET("pool", V1b[i][:, :, 128:129], 1.0, w=[("V1", i)])
            MSET("pool", Cn[:], 0.0, w=["Cn"])
            MSET("pool", Cnb[:], 0.0, w=["Cnb"])

            def pre(c_, xm, kxm):
                tt, p = c_ % 4, c_ % 2
                tc_ = slice(tt * 128, (tt + 1) * 128)
                ch4 = slice(c_ * 4, c_ * 4 + 4)
                for h in range(4):
                    MM(ps[2][:, h * 128:(h + 1) * 128], xcv[:, h, tc_], wqkv[:, 1, h, :], True, True,
                       r=[("xcv", h), "wqkv"], w=[P(2)])
                    MM(ps[3][:, h * 128:(h + 1) * 128], xm[:, h, 3 + tt * 128:3 + (tt + 1) * 128], wqkv[:, 2, h, :],
                       True, True, r=[(kxm, h), "wqkv"], w=[P(3)])
                    MM(ps[4][:, h * 128:(h + 1) * 128], kT[:, h, tc_], qT[:, h, tc_], True, True,
                       r=["kT", "qT"], w=[P(4)])
                for h in range(4):
                    ACT(kwb[p][:, h, :], ps[2][:, h * 128:(h + 1) * 128], AF.Copy, r=[P(2), "Wt"], w=[("kw", p)],
                        scale=Wt[:, 1, c_ * 4 + h:c_ * 4 + h + 1])
                CP("act", V1b[p][:, :, 0:128], ps[3][:].rearrange("p (h d) -> p h d", d=128), r=[P(3)], w=[("V1", p)])
                for h in range(4):
                    STT(sqkb[p][:, h, :], ps[4][:, h * 128:(h + 1) * 128], Wt[:, 0, c_ * 4 + h:c_ * 4 + h + 1], triu,
                        ALU.mult, ALU.mult, r=[P(4), "Wt", "cst"], w=[("sqk", p)])

            def rec(c_):
                tt, p = c_ % 4, c_ % 2
                tc_ = slice(tt * 128, (tt + 1) * 128)
                kw, V1, sqk = kwb[p], V1b[p], sqkb[p]
                for hp in range(2):
                    ab = ps[5 + hp]
                    for hh in range(2):
                        h = 2 * hp + hh
                        MM(ab[:, hh * 129:(hh + 1) * 129], sqk[:, h, :], V1[:, h, :], hh == 0, False,
                           r=[("sqk", p), ("V1", p)], w=[P(5 + hp)])
                        MM(ab[:, hh * 129:(hh + 1) * 129], qT[:, h, tc_], Cnb[:, h, :], False, True,
                           r=["qT", "Cnb"], w=[P(5 + hp)])
                    for hh in range(2):
                        h = 2 * hp + hh
                        MM(ps[7][:, hh * 129:(hh + 1) * 129], kw[:, h, :], V1[:, h, :], hh == 0, True,
                           r=[("kw", p), ("V1", p)], w=[P(7)])
                    for hh in range(2):
                        h = 2 * hp + hh
                        STT(Cn[:, h, :], Cn[:, h, :], Ebc[:, c_ * 4 + h:c_ * 4 + h + 1], ps[7][:, hh * 129:(hh + 1) * 129],
                            ALU.mult, ALU.add, r=["Cn", "Ebc", P(7)], w=["Cn"])
                    CP("dve", Cnb[:, 2 * hp:2 * hp + 2, :], Cn[:, 2 * hp:2 * hp + 2, :], r=["Cn"], w=["Cnb"])
                    CP("act", ABsb[p][:, 2 * hp:2 * hp + 2, :], ab[:, 0:258].rearrange("p (h d) -> p h d", d=129),
                       r=[P(5 + hp)], w=[("ABs", p)])

            def epi_a(c_):
                tt, p = c_ % 4, c_ % 2
                A = ABsb[p]
                sm_ = smb[p]
                hs_ = hsb[p]
                den = sm_[:, 0:4]
                kd, kmv, krs = ("den", p), ("mv", p), ("rstd4", p)
                STT(den, A[:, :, 128], -1.0, A[:, :, 128], ALU.mult, ALU.max, r=[("ABs", p)], w=[kd])
                TT("dve", den, den, Wt[:, 2, c_ * 4:c_ * 4 + 4], ALU.max, r=[kd, "Wt"], w=[kd])
                RCP(den, den, r=[kd], w=[kd])
                mvv = sm_[:, 48:56].rearrange("p (h t) -> p h t", t=2)
                for h in range(4):
                    STT(hs_[:, h, :], A[:, h, 0:128], sm_[:, h:h + 1], sgo[:, tt, h * 128:(h + 1) * 128], ALU.mult, ALU.mult,
                        r=[("ABs", p), kd, ("sgo", tt)], w=[("hs", p, h)])
                    st = sm_[:, 16 + 8 * h:16 + 8 * h + 6]
                    S.op("dve", (lambda o, i: (lambda e: e.bn_stats(out=o, in_=i)))(st, hs_[:, h, :]), r=[("hs", p, h)], w=[("st", p, h)])
                    S.op("dve", (lambda o, i: (lambda e: e.bn_aggr(out=o, in_=i)))(mvv[:, h, :], st), r=[("st", p, h)], w=[kmv])
                rstd4 = sm_[:, 56:60]
                nmr4 = sm_[:, 60:64]
                TS("pool", rstd4, mvv[:, :, 1], EPS, None, ALU.add, None, r=[kmv], w=[krs])
                TT("pool", rstd4, rstd4, mhalf[:], ALU.pow, r=[krs, "mhalf"], w=[krs])
                TS("pool", nmr4, mvv[:, :, 0], -1.0, None, ALU.mult, None, r=[kmv], w=[("nmr4", p)])
                TT("pool", nmr4, nmr4, rstd4, ALU.mult, r=[("nmr4", p), krs], w=[("nmr4", p)])

            def epi_b(c_):
                p = c_ % 2
                sm_ = smb[p]
                for h in range(4):
                    ACT(lnbb[p][:, h, :], hsb[p][:, h, :], AF.Identity, r=[("hs", p, h), ("rstd4", p), ("nmr4", p)], w=[("lnb", p)],
                        scale=sm_[:, 56 + h:57 + h], bias=sm_[:, 60 + h:61 + h])

            def outp(c_):
                tt, p = c_ % 4, c_ % 2
                blk_ = c_ // 4
                tc_ = slice(tt * 128, (tt + 1) * 128)
                b = nextbank()
                for h in range(4):
                    TR(ps[b][:, h * 128:(h + 1) * 128], lnbb[p][:, h, :], r=[("lnb", p)], w=[P(b)])
                for h in range(4):
                    STT(ymt[:, h, :], ps[b][:, h * 128:(h + 1) * 128], part[:, MG + h:MG + h + 1], xs[:, h, tc_],
                        ALU.mult, ALU.add, r=[P(b), "par", "xs"], w=[("ymt", h)])
                    TT("pool", ymT[:, h, blk_ * 512 + tt * 128:blk_ * 512 + (tt + 1) * 128], ymt[:, h, :], zs[:, h, tc_],
                       ALU.mult, r=[("ymt", h), "zs"], w=["ymT"])

            for blk in range(NB if stop_after > 1 else 1):
                if blk == 0:
                    nxt = frontA(0)
                xm, kxm = nxt
                frontBC(blk, False)
                cols = slice(blk * 512, (blk + 1) * 512)
                for h in range(4):
                    b = nextbank()
                    for c in range(8):
                        MM(ps[b][:], wm_bf[:, c, 1024 + h * 128:1024 + (h + 1) * 128], hT[:, c, cols], c == 0, c == 7,
                           r=[("wm", c), ("hT", blk)], w=[P(b)])
                    ACT(zs[:, h, :], ps[b][:], AF.Silu, r=[P(b)], w=["zs"])
                    ACT(xs[:, h, :], xcv[:, h, :], AF.Copy, scale=part[:, SK + h:SK + h + 1],
                       r=[("xcv", h), "par"], w=["xs"])
                for tt in range(4):
                    b = nextbank()
                    for c in range(8):
                        MM(ps[b][:], hT[:, c, blk * 512 + tt * 128:blk * 512 + (tt + 1) * 128], wm_bf[:, c, 512:1024],
                           c == 0, c == 7, r=[("wm", c), ("hT", blk)], w=[P(b)])
                    ACT(sgo[:, tt, :], ps[b][:], AF.Sigmoid, r=[P(b)], w=[("sgo", tt)])
                c0 = blk * 4
                pre(c0, xm, kxm)
                for tt in range(4):
                    c_ = c0 + tt
                    if tt < 3:
                        pre(c_ + 1, xm, kxm)
                    rec(c_)
                    if tt == 1 and blk + 1 < (NB if stop_after > 1 else 1):
                        nxt = frontA(blk + 1)
                    epi_a(c_)
                    if tt > 0:
                        epi_b(c_ - 1)
                        outp(c_ - 1)
                epi_b(c0 + 3)
                outp(c0 + 3)
            if dbg and dbg >= 3:
                dump(es, "ymT", ymT[:, :, 0:512], [128, 4, 512], BF16)
                dump(es, "hs", hsb[0][:], [128, 4, 128])
                dump(es, "Cn", Cn[:], [128, 4, 129])
            finish(blk1)
        wm_cm.__exit__(None, None, None)
        if stop_after <= 1:
            return nc

        rot2 = [0]

        def nb2():
            rot2[0] ^= 1
            return rot2[0]

        GROUPS = [(3 * g, 3 * g + 3) for g in range(10)] + [(30, 32)]
        DG = 44

        def proj_fm(wt, j, dst, blk, func, scale, kd):
            b = nb2()
            cols = slice(blk * 512, (blk + 1) * 512)
            for c in range(8):
                MM(ps[b][:], wt[:, j, c, :], hT[:, c, cols], c == 0, c == 7, r=[kd + "_w", ("hT", blk)], w=[P(b)])
            if func is None:
                CP("dve", dst[:, cols], ps[b][:], r=[P(b)], w=[kd])
            else:
                ACT(dst[:, cols], ps[b][:], func, r=[P(b)], w=[kd], scale=scale)

        ydT = sb(top, "ydT", [128, 4, S_LEN], BF16)
        with nc.Block() as blk2, ExitStack() as es:
            lamt = sb(es, "lamt", [128, 8], F32)
            j64 = sb(es, "j64", [128, 64], F32)
            STT(j64[:], part[:, 64:128], 1.0, part[:, 128:192], ALU.mult, ALU.mult, r=["par"], w=["j64", "lam"], accum=lamt[:, 0:1])
            STT(j64[:], part[:, 192:256], 1.0, part[:, 256:320], ALU.mult, ALU.mult, r=["par", "j64"], w=["j64", "lam"], accum=lamt[:, 1:2])
            ACT(lamt[:, 2:4], lamt[:, 0:2], AF.Exp, r=["lam"], w=["lam"])
            TT("dve", lamt[:, 4:5], lamt[:, 2:3], lamt[:, 3:4], ALU.subtract, r=["lam"], w=["lam"])
            TS("dve", lamt[:, 5:6], lamt[:, 4:5], 0.2, -1.0, ALU.add, ALU.mult, r=["lam"], w=["lam"])
            wdb = [sb(es, "wd%d" % i, [128, 4, 8, 128], BF16) for i in range(2)]
            qdT = sb(es, "qdT", [128, S_LEN], BF16)
            kpd = [sb(es, "kpd%d" % i, [128, S_LEN], BF16) for i in range(2)]
            MSET("pool", kpd[0][64:128, :], 0.0, w=["kd"])
            MSET("pool", kpd[1][0:64, :], 0.0, w=["kd"])
            zds = sb(es, "zds", [128, S_LEN], BF16)
            V1d = sb(es, "V1d", [128, 32, 128], BF16)
            ptb = [sb(es, "pt%d" % i, [128, 512], BF16) for i in range(4)]
            o0s = sb(es, "o0s", [128, 512], F32)
            o1s = sb(es, "o1s", [128, 512], F32)
            l0s = sb(es, "l0s", [128, 512], F32)
            l1s = sb(es, "l1s", [128, 512], F32)
            sqb = sb(es, "sqb", [128, 512], BF16)
            rsd = l0s
            dg08 = sb(es, "dg08", [128, 4], F32)
            TS("dve", dg08[:], part[:, DG:DG + 4], 0.8, None, ALU.mult, None, r=["par"], w=["dg08"])
            GROUPS4 = [(4 * g, 4 * g + 4) for g in range(8)]

            def proj_k(wt, blk):
                b = nb2()
                cols = slice(blk * 512, (blk + 1) * 512)
                for c in range(8):
                    MM(ps[b][:], wt[:, 1, c, :], hT[:, c, cols], c == 0, c == 7, r=["kd_w", ("hT", blk)], w=[P(b)])
                CP("dve", kpd[0][0:64, cols], ps[b][0:64, :], r=[P(b)], w=["kd"])
                CP("dve", kpd[1][64:128, cols], ps[b][64:128, :], r=[P(b)], w=["kd"])
            for h in range(4):
                wd = wdb[h % 2]
                for j, off, kk in ((0, 1536, "qd_w"), (1, 2048, "kd_w"), (3, 3072, "zd_w"), (2, 2560, "vd_w")):
                    DMA("pool", wd[:, j, :, :], win_v[:, :, off + h * 128:off + (h + 1) * 128], r=[], w=[kk])
                for blk in range(NB):
                    proj_fm(wd, 0, qdT, blk, AF.Identity, 0.125, "qd")
                    proj_k(wd, blk)
                    proj_fm(wd, 3, zds, blk, AF.Silu, None, "zd")
                    for tt in range(4):
                        tl = blk * 4 + tt
                        for c in range(8):
                            MM(ps[3][:, tt * 128:(tt + 1) * 128], hT[:, c, tl * 128:(tl + 1) * 128], wd[:, 2, c, :], c == 0, c == 7,
                               r=["vd_w", ("hT", blk)], w=[P(3)])
                    CP("dve", V1d[:, blk * 4:(blk + 1) * 4, :], ps[3][:].rearrange("p (t d) -> p t d", d=128), r=[P(3)], w=["V1d"])
                steps = [(g, qs, qe, m, kb) for g, (qs, qe) in enumerate(GROUPS4) for m in range(2) for kb in range(qe)]
                LA = 3
                sctr = [0]

                def emit_qk(i):
                    g, qs, qe, m, kb = steps[i]
                    nsub = qe - qs
                    pr = slice(m * 64, (m + 1) * 64)
                    sv = max(0, kb - qs)
                    sbk = sctr[0] % 4
                    sctr[0] += 1
                    fc = slice(sv * 128, nsub * 128)
                    MM(ps[sbk][:, fc], kpd[m][:, kb * 128:(kb + 1) * 128], qdT[:, (qs + sv) * 128:qe * 128], True, True,
                       r=["kd", "qd"], w=[P(sbk)])
                    pt = ptb[i % 4]
                    kpt = ("pt", i % 4)
                    ACT(pt[:, fc], ps[sbk][:, fc], AF.Exp, r=[P(sbk)], w=[kpt])
                    if kb >= qs:
                        dc = slice((kb - qs) * 128, (kb - qs + 1) * 128)
                        TT("dve", pt[:, dc], pt[:, dc], mask_bf[:], ALU.mult, r=[kpt, "mask_bf"], w=[kpt])

                def emit_pv(i):
                    g, qs, qe, m, kb = steps[i]
                    nsub = qe - qs
                    sv = max(0, kb - qs)
                    fc = slice(sv * 128, nsub * 128)
                    pt = ptb[i % 4]
                    kpt = ("pt", i % 4)
                    MM(ps[4 + m][:, fc], V1d[:, kb, :], pt[:, fc], kb == 0, kb == qe - 1, r=[kpt, "V1d"], w=[P(4 + m)])
                    MM(ps[6 + m][:, fc], ones_bf[:], pt[:, fc], kb == 0, kb == qe - 1, r=[kpt, "ones_bf"], w=[P(6 + m)])
                    if m == 1 and kb == qe - 1:
                        epilogue1(qs, qe)
                        pending.append((i + 12, qs, qe))

                pending = []

                def epilogue1(qs, qe):
                    CP("act", o0s[:], ps[4][:], r=[P(4)], w=["o0s"])
                    CP("dve", l0s[:], ps[6][:], r=[P(6)], w=["l0s"])
                    CP("act", o1s[:], ps[5][:], r=[P(5)], w=["o1s"])
                    CP("dve", l1s[:], ps[7][:], r=[P(7)], w=["l1s"])
                    RCP(l0s[:], l0s[:], r=["l0s"], w=["l0s"])
                    RCP(l1s[:], l1s[:], r=["l1s"], w=["l1s"])
                    TT("dve", o0s[:], o0s[:], l0s[:], ALU.mult, r=["o0s", "l0s"], w=["o0s"])
                    TT("dve", o1s[:], o1s[:], l1s[:], ALU.mult, r=["o1s", "l1s"], w=["o1s"])
                    STT(o0s[:], o1s[:], lamt[:, 5:6], o0s[:], ALU.mult, ALU.add, r=["o0s", "o1s", "lam"], w=["o0s"])

                def epilogue2(qs, qe):
                    cols = slice(qs * 128, qe * 128)
                    ACT(sqb[:], o0s[:], AF.Square, r=["o0s"], w=["sqb"])
                    eb = sctr[0] % 4
                    sctr[0] += 1
                    MM(ps[eb][:], ones_bf[:], sqb[:], True, True, r=["sqb", "ones_bf"], w=[P(eb)])
                    ACT(rsd[:], ps[eb][:], AF.Sqrt, r=[P(eb)], w=["l0s"], bias=EPS, scale=1.0 / 128)
                    RCP(rsd[:], rsd[:], r=["l0s"], w=["l0s"])
                    STT(o0s[:], o0s[:], dg08[:, h:h + 1], rsd[:], ALU.mult, ALU.mult, r=["o0s", "dg08", "l0s"], w=["o0s"])
                    TT("dve", ydT[:, h, cols], o0s[:], zds[:, cols], ALU.mult, r=["o0s", "zd"], w=["ydT"])

                for i in range(len(steps) + LA):
                    if i < len(steps):
                        emit_qk(i)
                    if i >= LA:
                        emit_pv(i - LA)
                    while pending and pending[0][0] <= i - LA:
                        _, pqs, pqe = pending.pop(0)
                        epilogue2(pqs, pqe)
                while pending:
                    _, pqs, pqe = pending.pop(0)
                    epilogue2(pqs, pqe)
            if dbg and dbg >= 4:
                dump(es, "ydT", ydT[:, :, 0:1024], [128, 4, 1024], BF16)
                dump(es, "lamt", lamt[:], [128, 8])
            finish(blk2)
        if stop_after <= 2:
            return nc

        ycT = sb(top, "ycT", [128, 4, S_LEN], BF16)
        with nc.Block() as blk3, ExitStack() as es:
            wcb = [sb(es, "wc%d" % i, [128, 2, 8, 128], BF16) for i in range(2)]
            qcT = sb(es, "qcT", [128, S_LEN], BF16)
            zcs = sb(es, "zcs", [128, S_LEN], BF16)
            ptb = [sb(es, "ptc%d" % i, [128, 512], BF16) for i in range(4)]
            rlb = [sb(es, "rl%d" % i, [128, 512], F32) for i in range(2)]
            onb = [sb(es, "on%d" % i, [128, 512], F32) for i in range(2)]
            for h in range(4):
                wc = wcb[h % 2]
                for j, off, kk in ((0, 3584, "qc_w"), (1, 4096, "zc_w")):
                    DMA("pool", wc[:, j, :, :], win_v[:, :, off + h * 128:off + (h + 1) * 128], r=[], w=[kk])
                for blk in range(NB):
                    proj_fm(wc, 0, qcT, blk, AF.Copy, None, "qc")
                    proj_fm(wc, 1, zcs, blk, AF.Silu, None, "zc")

                def c_qk(j):
                    cols = slice(j * 512, (j + 1) * 512)
                    for mc in range(2):
                        sbk = 2 * (j % 2) + mc
                        MM(ps[sbk][:], kmT[:, h, mc * 128:(mc + 1) * 128], qcT[:, cols], True, True, r=["kmT", "qc"], w=[P(sbk)])
                        ACT(ptb[sbk][:], ps[sbk][:], AF.Exp, r=[P(sbk)], w=[("ptc", sbk)])

                def c_pv(j):
                    cols = slice(j * 512, (j + 1) * 512)
                    p = j % 2
                    for mc in range(2):
                        sbk = 2 * p + mc
                        MM(ps[4 + p][:], vm1[:, mc, h, 0:128], ptb[sbk][:], mc == 0, mc == 1, r=[("ptc", sbk), "vm1"], w=[P(4 + p)])
                    for mc in range(2):
                        sbk = 2 * p + mc
                        MM(ps[6 + p][:], ones_bf[:], ptb[sbk][:], mc == 0, mc == 1, r=[("ptc", sbk), "ones_bf"], w=[P(6 + p)])
                    RCP(rlb[p][:], ps[6 + p][:], r=[P(6 + p)], w=[("rl", p)])
                    TT("dve", onb[p][:], ps[4 + p][:], rlb[p][:], ALU.mult, r=[P(4 + p), ("rl", p)], w=[("on", p)])
                    TT("pool", ycT[:, h, cols], onb[p][:], zcs[:, cols], ALU.mult, r=[("on", p), "zc"], w=["ycT"])

                for j in range(NB + 1):
                    if j < NB:
                        c_qk(j)
                    if j >= 1:
                        c_pv(j - 1)
            if dbg and dbg >= 5:
                dump(es, "ycT", ycT[:, :, 0:1024], [128, 4, 1024], BF16)
            finish(blk3)
        if stop_after <= 3:
            return nc

        with nc.Block() as blk4, ExitStack() as es:
            hflat = hT[:].rearrange("p c t -> p (c t)")
            wout_bf = hflat[:, 0:12 * D].rearrange("p (c f) -> p c f", f=D)
            wout_v = wout_d.rearrange("(c p) f -> p c f", p=128)
            for c in range(12):
                DMA("pool", wout_bf[:, c, :], wout_v[:, c, :], r=[], w=[("wout", c)])
            hf32 = hflat[:, 12 * D:].bitcast(F32)
            NXB = 6
            xtb = [hf32[:, i * D:(i + 1) * D] for i in range(NXB)]
            fgt = hf32[:, NXB * D:(NXB + 1) * D]
            DMA("sp", fgt, fg_d[:, :], r=[], w=["fg"])
            junk = sb(es, "junk4", [128, D], BF16)
            ss4 = sb(es, "ss4", [128, 2 * NXB], F32)
            ysrc = [(ymT, hh) for hh in range(4)] + [(ydT, hh) for hh in range(4)] + [(ycT, hh) for hh in range(4)]

            def load_x(i):
                DMA("sp", xtb[i % NXB], x_d[i * 128:(i + 1) * 128, :], r=[], w=[("xo", i % NXB)])

            PF = 4
            for i in range(PF):
                load_x(i)
            for i in range(NT):
                p2 = i % NXB
                xt = xtb[p2]
                kx = ("xo", p2)
                tcs = slice(i * 128, (i + 1) * 128)
                if i + PF < NT:
                    load_x(i + PF)
                for half in range(2):
                    b = 2 * (i % 4) + half
                    hc = slice(half * 512, (half + 1) * 512)
                    for ch, (src, hh) in enumerate(ysrc):
                        MM(ps[b][:], src[:, hh, tcs], wout_bf[:, ch, hc], ch == 0, ch == 11, r=[("wout", ch), "ycat"], w=[P(b)])
                    TT("dve", xt[:, hc], ps[b][:], xt[:, hc], ALU.add, r=[P(b), kx], w=[kx])
                ss = ss4[:, 2 * p2:2 * p2 + 1]
                rs = ss4[:, 2 * p2 + 1:2 * p2 + 2]
                ACT(junk[:], xt, AF.Square, r=[kx], w=["junk4", ("ss4", p2)], accum=ss)
                ACT(rs, ss, AF.Sqrt, r=[("ss4", p2)], w=[("ss4", p2)], bias=EPS, scale=1.0 / D)
                RCP(rs, rs, r=[("ss4", p2)], w=[("ss4", p2)])
                STT(xt, xt, rs, fgt, ALU.mult, ALU.mult, r=[kx, ("ss4", p2), "fg"], w=[kx])
                DMA("sp", out_d[i * 128:(i + 1) * 128, :], xt, r=[kx], w=[])
            finish(blk4)
    return nc


def make_consts():
    cst = np.zeros((128, 512), np.float32)
    cst[:, 0:128] = np.eye(128, dtype=np.float32)
    cst[:, 128:256] = np.triu(np.ones((128, 128), np.float32))
    p = np.arange(128)
    c, h = p // 4, p % 4
    cst[:, 256:384] = ((h[:, None] == h[None, :]) & (c[:, None] < c[None, :])).astype(np.float32)
    cst[:, 384:512] = 1.0
    return cst


def make_in_maps(inp):
    f = lambda a: np.ascontiguousarray(a, dtype=np.float32)
    cst = make_consts()
    par = np.zeros((128, 320), np.float32)
    par[:, 0:8] = inp["norm_g"][0].reshape(8, 128).T
    par[:, 8:16] = inp["mem_norm_g"][0].reshape(8, 128).T
    par[:, 16:32] = inp["conv_w"][0].reshape(4, 4, 128).transpose(2, 1, 0).reshape(128, 16)
    par[:, 32:36] = inp["conv_b"][0].reshape(4, 128).T
    par[:, 36:40] = inp["mnorm_g"][0].reshape(4, 128).T
    par[:, 40:44] = inp["skip_m"][0].reshape(4, 128).T
    par[:, 44:48] = inp["dnorm_g"][0].reshape(4, 128).T
    par[:, 48] = np.tile(inp["b_if"][0][0:4], 32)
    par[:, 49] = np.tile(inp["b_if"][0][4:8], 32)
    par[:, 64:128] = inp["lam_q1"][0][None, :]
    par[:, 128:192] = inp["lam_k1"][0][None, :]
    par[:, 192:256] = inp["lam_q2"][0][None, :]
    par[:, 256:320] = inp["lam_k2"][0][None, :]
    fg = np.ascontiguousarray(np.broadcast_to(inp["final_g"][None, :], (128, D)), dtype=np.float32)
    shared = {
        "w_in": f(inp["w_in"][0]), "w_kv": f(inp["w_mem_kv"][0]), "w_out": f(inp["w_out"][0]),
        "wq": f(inp["wq_m"][0]), "wk": f(inp["wk_m"][0]), "wv": f(inp["wv_m"][0]), "w_if": f(inp["w_if"][0]),
        "cst": cst, "par": par, "fg": fg,
    }
    maps = []
    for b in range(8):
        m = dict(shared)
        m["x"] = f(inp["x"][b])
        m["mem"] = f(inp["mem"][b])
        maps.append(m)
    return maps


_NC_CACHE = {}


def kernel(**inputs):
    if "nc" not in _NC_CACHE:
        _NC_CACHE["nc"] = build_nc()
    nc = _NC_CACHE["nc"]
    maps = make_in_maps(inputs)
    res = run_bass_kernel_spmd(nc, maps, core_ids=list(range(8)))
    return np.stack([np.asarray(r["out"], dtype=np.float32) for r in res.results], axis=0)
```

```python
import math
from contextlib import ExitStack

import numpy as np
import concourse.bass as bass
import concourse.mybir as mybir
from concourse.bass_utils import run_bass_kernel_spmd

F32 = mybir.dt.float32
BF16 = mybir.dt.bfloat16
AF = mybir.ActivationFunctionType
ALU = mybir.AluOpType

S_LEN = 4096
D = 1024
NT = 32
NB = 8
EPS = 1e-6
LN_SQRT_DH = 0.5 * math.log(128.0)


class Sched:
    ENG = ("pe", "act", "dve", "pool", "sp")

    def __init__(self, nc, sems, dsems):
        self.nc = nc
        self.sems = sems
        self.dsems = dsems
        self.q = {e: [] for e in self.ENG}
        self.cnt = {e: 0 for e in self.ENG}
        self.last_w = {}
        self.readers = {}
        self.seen = {e: {} for e in self.ENG}
        self.n_dma = len(dsems)
        self.dma_cnt = [0] * self.n_dma
        half = self.n_dma // 2
        self.dma_pool = {"sp": list(range(0, half)), "pool": list(range(half, self.n_dma))}
        self.dma_rr = {"sp": 0, "pool": 0}

    def _need(self, eng, tok, waits):
        src, val = tok
        if src == eng and eng in ("pe", "sp"):
            return
        if self.seen[eng].get(src, 0) >= val:
            return
        self.seen[eng][src] = val
        waits[src] = max(waits.get(src, 0), val)

    def op(self, eng, fn, r=(), w=(), dma=False):
        waits = {}
        for k in r:
            t = self.last_w.get(k)
            if t is not None:
                self._need(eng, t, waits)
        for k in w:
            t = self.last_w.get(k)
            if t is not None:
                self._need(eng, t, waits)
            for t in self.readers.get(k, ()):
                self._need(eng, t, waits)
        if dma:
            pool_ = self.dma_pool[eng]
            i = pool_[self.dma_rr[eng] % len(pool_)]
            self.dma_rr[eng] += 1
            if self.dma_cnt[i] > 0:
                self._need(eng, (("d", i), 16 * self.dma_cnt[i]), waits)
            self.dma_cnt[i] += 1
            tok = (("d", i), 16 * self.dma_cnt[i])
        else:
            self.cnt[eng] += 1
            tok = (eng, self.cnt[eng])
        self.q[eng].append((list(waits.items()), fn, tok))
        for k in w:
            self.last_w[k] = tok
            self.readers[k] = []
        for k in r:
            self.readers.setdefault(k, []).append(tok)
        return tok

    def barrier(self):
        toks = [(e, self.cnt[e]) for e in self.ENG if self.cnt[e] > 0]
        toks += [(("d", i), 16 * c) for i, c in enumerate(self.dma_cnt) if c > 0]
        for e in self.ENG:
            waits = {}
            for t in toks:
                if t[0] != e:
                    self._need(e, t, waits)
            if waits:
                self.q[e].append((list(waits.items()), None, None))

    def emit(self, block):
        sems, dsems = self.sems, self.dsems

        def run(e, engobj):
            for waits, fn, tok in self.q[e]:
                for src, val in waits:
                    s = dsems[src[1]] if isinstance(src, tuple) else sems[src]
                    engobj.wait_ge(s, val)
                if fn is None:
                    continue
                ins = fn(engobj)
                if isinstance(tok[0], tuple):
                    ins.then_inc(dsems[tok[0][1]], 16)
                else:
                    ins.then_inc(sems[tok[0]], 1)
            self.q[e] = []

        @block.tensor
        def _(eng):
            run("pe", eng)

        @block.scalar
        def _(eng):
            run("act", eng)

        @block.vector
        def _(eng):
            run("dve", eng)

        @block.gpsimd
        def _(eng):
            run("pool", eng)

        @block.sync
        def _(eng):
            run("sp", eng)


def build_nc(stop_after=99, dbg=None):
    nc = bass.Bass("TRN2", target_bir_lowering=False)
    din = lambda n, s: nc.dram_tensor(n, s, F32, kind="ExternalInput").ap()
    x_d = din("x", [S_LEN, D])
    mem_d = din("mem", [256, D])
    win_d = din("w_in", [D, 4608])
    wkv_d = din("w_kv", [D, 1024])
    wout_d = din("w_out", [1536, D])
    wq_d = din("wq", [4, 128, 128])
    wk_d = din("wk", [4, 128, 128])
    wv_d = din("wv", [4, 128, 128])
    wif_d = din("w_if", [1536, 8])
    cst_d = din("cst", [128, 512])
    par_d = din("par", [128, 320])
    fg_d = din("fg", [128, D])
    out_d = nc.dram_tensor("out", [S_LEN, D], F32, kind="ExternalOutput").ap()
    dbg_outs = []
    win_v = win_d.rearrange("(c p) f -> p c f", p=128)

    top = ExitStack()
    with top:
        sb = lambda es, name, shape, dt: es.enter_context(nc.sbuf_tensor(name, shape, dt))
        ps = [top.enter_context(nc.psum_tensor("ps%d" % i, [128, 512], F32)) for i in range(8)]
        sems = {e: top.enter_context(nc.semaphore("s_" + e)) for e in Sched.ENG}
        dsems = [top.enter_context(nc.semaphore("d%d" % i)) for i in range(32)]
        S = Sched(nc, sems, dsems)
        P = lambda i: ("ps", i)

        def MM(out, lhsT, rhs, start, stop, r, w):
            return S.op("pe", lambda e: e.matmul(out, lhsT=lhsT, rhs=rhs, start=start, stop=stop,
                                                  skip_group_check=True), r=r, w=w)

        def TR(out, in_, r, w):
            return S.op("pe", lambda e: e.transpose(out=out, in_=in_, identity=ident[:]), r=list(r) + ["cst"], w=w)

        def ACT(out, in_, func, r, w, bias=None, scale=None, accum=None, eng="act"):
            kw = {}
            if bias is not None:
                kw["bias"] = bias
            if scale is not None:
                kw["scale"] = scale
            if accum is not None:
                kw["accum_out"] = accum
            return S.op("act", lambda e: e.activation(out=out, in_=in_, func=func, **kw), r=r, w=w)

        def TS(eng, out, in0, s1, s2, op0, op1, r, w):
            if op1 is None:
                return S.op(eng, lambda e: e.tensor_scalar(out=out, in0=in0, scalar1=s1, scalar2=None, op0=op0), r=r, w=w)
            return S.op(eng, lambda e: e.tensor_scalar(out=out, in0=in0, scalar1=s1, scalar2=s2, op0=op0, op1=op1), r=r, w=w)

        def TT(eng, out, in0, in1, op, r, w):
            return S.op(eng, lambda e: e.tensor_tensor(out=out, in0=in0, in1=in1, op=op), r=r, w=w)

        def STT(out, in0, scalar, in1, op0, op1, r, w, accum=None):
            if accum is not None:
                return S.op("dve", lambda e: e.scalar_tensor_tensor(out=out, in0=in0, scalar=scalar, in1=in1, op0=op0, op1=op1, accum_out=accum), r=r, w=w)
            return S.op("dve", lambda e: e.scalar_tensor_tensor(out=out, in0=in0, scalar=scalar, in1=in1, op0=op0, op1=op1), r=r, w=w)

        def CP(eng, out, in_, r, w):
            if eng == "act":
                return S.op("act", lambda e: e.copy(out=out, in_=in_), r=r, w=w)
            return S.op(eng, lambda e: e.tensor_copy(out=out, in_=in_), r=r, w=w)

        def MSET(eng, ap, val, w):
            return S.op(eng, lambda e: e.memset(ap, val), w=w)

        def RCP(out, in_, r, w):
            return S.op("dve", lambda e: e.reciprocal(out=out, in_=in_), r=r, w=w)

        def SCAN(out, d0, init, op0, r, w):
            return S.op("dve", lambda e: e.tensor_tensor_scan(out=out, data0=d0, data1=d0, initial=init, op0=op0, op1=ALU.bypass), r=r, w=w)

        def DMA(eng, out, in_, r, w):
            return S.op(eng, lambda e: e.dma_start(out=out, in_=in_), r=r, w=w, dma=True)

        def dump(es, name, ap, shape, dt=F32):
            d = nc.dram_tensor("dbg_" + name, list(shape), dt, kind="ExternalOutput").ap()
            S.barrier()
            dbg_outs.append(DMA("sp", d, ap, r=[], w=[]))

        def finish(block):
            S.barrier()
            S.emit(block)

        cstt = sb(top, "cstt", [128, 512], F32)
        ident = cstt[:, 0:128]
        triu = cstt[:, 128:256]
        M1 = cstt[:, 256:384]
        ones_f = cstt[:, 384:512]
        part = sb(top, "part", [128, 320], F32)
        mask_bf = sb(top, "mask_bf", [128, 128], BF16)
        ones_bf = sb(top, "ones_bf", [128, 128], BF16)
        ident_bf = sb(top, "ident_bf", [128, 128], BF16)
        kmT = sb(top, "kmT", [128, 4, 256], BF16)
        vm1 = sb(top, "vm1", [128, 2, 4, 129], BF16)
        ymT = sb(top, "ymT", [128, 4, S_LEN], BF16)
        hT = sb(top, "hT", [128, 8, S_LEN], BF16)
        wm_cm = nc.sbuf_tensor("wm_bf", [128, 8, 1536], BF16)
        wm_bf = wm_cm.__enter__()

        def rms_tile_a(es_bufs, src_rows, i):
            xtb, junk, ssb, xnb = es_bufs
            xt = xtb[i % 4]
            ss = ssb[:, 2 * (i % 4):2 * (i % 4) + 1]
            rs = ssb[:, 2 * (i % 4) + 1:2 * (i % 4) + 2]
            kx, ks = ("xt", i % 4), ("ss", i % 4)
            DMA("sp", xt[:], src_rows, r=[], w=[kx])
            ACT(junk[:], xt[:], AF.Square, r=[kx], w=["junk", ks], accum=ss)
            ACT(rs, ss, AF.Sqrt, r=[ks], w=[ks], bias=EPS, scale=1.0 / D)
            RCP(rs, rs, r=[ks], w=[ks])

        def rms_tile_b(es_bufs, gcol, dstT, col0, i):
            xtb, junk, ssb, xnb = es_bufs
            xt = xtb[i % 4]
            xn = xnb[i % 3]
            rs = ssb[:, 2 * (i % 4) + 1:2 * (i % 4) + 2]
            kx, kn, ks = ("xt", i % 4), ("xn", i % 3), ("ss", i % 4)
            ACT(xn[:], xt[:], AF.Copy, r=[kx, ks], w=[kn], scale=rs)
            for half in range(2):
                b = 2 * (i % 4) + half
                psb = ps[b][:].bitcast(BF16)
                for j in range(4):
                    c = 4 * half + j
                    S.op("pe", (lambda o, a: (lambda e: e.transpose(out=o, in_=a, identity=ident_bf[:])))(
                        psb[:, j * 128:(j + 1) * 128], xn[:, c * 128:(c + 1) * 128]), r=[kn, "ident_bf"], w=[P(b)])
                TT("dve", dstT[:, 4 * half:4 * half + 4, col0:col0 + 128],
                   psb[:, 0:512].rearrange("p (j t) -> p j t", t=128),
                   part[:, gcol + 4 * half:gcol + 4 * half + 4].unsqueeze(2).to_broadcast([128, 4, 128]),
                   ALU.mult, r=[P(b), "par"], w=[("hT", col0 // 512) if dstT is hT else "memT"])

        with nc.Block() as blk0, ExitStack() as es:
            DMA("sp", cstt[:], cst_d[:, :], r=[], w=["cst"])
            DMA("sp", part[:], par_d[:, :], r=[], w=["par"])
            CP("dve", mask_bf[:], triu, r=["cst"], w=["mask_bf"])
            CP("dve", ones_bf[:], ones_f, r=["cst"], w=["ones_bf"])
            CP("dve", ident_bf[:], ident, r=["cst"], w=["ident_bf"])
            xtb = [sb(es, "xt%d" % i, [128, D], F32) for i in range(4)]
            xnb = [sb(es, "xn%d" % i, [128, D], BF16) for i in range(3)]
            junk = sb(es, "junk", [128, D], BF16)
            ssb = sb(es, "ssb", [128, 8], F32)
            bufs = (xtb, junk, ssb, xnb)
            memT = sb(es, "memT", [128, 8, 256], BF16)
            wkv_bf = sb(es, "wkv_bf", [128, 8, 1024], BF16)
            wkv_v = wkv_d.rearrange("(c p) f -> p c f", p=128)
            for c in range(8):
                DMA("pool", wkv_bf[:, c, :], wkv_v[:, c, :], r=[], w=["wkv"])
            for c in range(8):
                DMA("pool", wm_bf[:, c, :], win_v[:, c, 0:1536], r=[], w=[("wm", c)])
            seq = [(mem_d[i * 128:(i + 1) * 128, :], 8, memT, i * 128) for i in range(2)]
            seq += [(x_d[i * 128:(i + 1) * 128, :], 0, hT, i * 128) for i in range(NT)]
            rms_tile_a(bufs, seq[0][0], 0)
            for k, (src_rows, gcol, dstT, col0) in enumerate(seq):
                if k + 1 < len(seq):
                    rms_tile_a(bufs, seq[k + 1][0], k + 1)
                rms_tile_b(bufs, gcol, dstT, col0, k)
            for h in range(4):
                b = 4 + h % 2
                for c in range(8):
                    MM(ps[b][:, 0:256], wkv_bf[:, c, h * 128:(h + 1) * 128], memT[:, c, :], c == 0, c == 7,
                       r=["wkv", "memT"], w=[P(b)])
                ACT(kmT[:, h, :], ps[b][:, 0:256], AF.Identity, r=[P(b)], w=["kmT"], scale=128.0 ** -0.5)
            MSET("pool", vm1[:, :, :, 128:129], 1.0, w=["vm1"])
            for mt in range(2):
                b = 6 + mt
                for c in range(8):
                    MM(ps[b][:], memT[:, c, mt * 128:(mt + 1) * 128], wkv_bf[:, c, 512:1024], c == 0, c == 7,
                       r=["wkv", "memT"], w=[P(b)])
                CP("dve", vm1[:, mt, :, 0:128], ps[b][:].rearrange("p (h d) -> p h d", d=128), r=[P(b)], w=["vm1"])
            if dbg == 0:
                dump(es, "hT", hT[:, :, 0:512], [128, 8, 512], BF16)
                dump(es, "kmT", kmT[:], [128, 4, 256], BF16)
                dump(es, "vm1", vm1[:], [128, 2, 4, 129], BF16)
            finish(blk0)
        if stop_after == 0:
            return nc

        with nc.Block() as blk1, ExitStack() as es:
            wqkv = sb(es, "wqkv", [128, 3, 4, 128], BF16)
            for j, wd in enumerate((wq_d, wk_d, wv_d)):
                DMA("pool", wqkv[:, j, :, :], wd.rearrange("h d e -> d h e"), r=[], w=["wqkv"])
            wif_bf = sb(es, "wif_bf", [128, 12, 8], BF16)
            DMA("pool", wif_bf[:], wif_d.rearrange("(j p) g -> p j g", p=128), r=[], w=["wif"])
            ABf = sb(es, "ABf", [128, 2, 4, 8], BF16)
            wTs = [sb(es, "wTs%d" % i, [128, 128], BF16) for i in range(3)]
            psb4 = ps[4][:].bitcast(BF16)
            for h in range(4):
                for j in range(3):
                    S.op("pe", (lambda o, a: (lambda e: e.transpose(out=o, in_=a, identity=ident_bf[:])))(
                        psb4[:, j * 128:(j + 1) * 128], wqkv[:, j, h, :]), r=["wqkv", "ident_bf"], w=[P(4)])
                    CP("dve", wTs[j][:], psb4[:, j * 128:(j + 1) * 128], r=[P(4)], w=[("wTs", j)])
                MM(ps[5][:, h * 8:(h + 1) * 8], wTs[0][:], wif_bf[:, h, :], True, False, r=[("wTs", 0), "wif"], w=[P(5)])
                MM(ps[5][:, h * 8:(h + 1) * 8], wTs[1][:], wif_bf[:, 4 + h, :], False, True, r=[("wTs", 1), "wif"], w=[P(5)])
                MM(ps[5][:, 32 + h * 8:32 + (h + 1) * 8], wTs[2][:], wif_bf[:, 8 + h, :], True, True, r=[("wTs", 2), "wif"], w=[P(5)])
            CP("dve", ABf[:].rearrange("p a h g -> p (a h g)"), ps[5][:, 0:64], r=[P(5)], w=["ABf"])
            xmb = [sb(es, "xm%d" % i, [128, 4, 515], BF16) for i in range(2)]
            xcv = sb(es, "xcv", [128, 4, 512], BF16)
            xs = sb(es, "xs", [128, 4, 512], BF16)
            qT = sb(es, "qT", [128, 4, 512], BF16)
            kT = sb(es, "kT", [128, 4, 512], BF16)
            vT = None
            zs = sb(es, "zs", [128, 4, 512], BF16)
            sgo = sb(es, "sgo", [128, 4, 512], BF16)
            Dg = sb(es, "Dg", [128, 16, 128], BF16)
            for hj in range(16):
                TS("dve", Dg[:, hj, :], ident, part[:, 16 + hj:16 + hj + 1], None, ALU.mult, None, r=["cst", "par"], w=["Dg"])
            Gtok = sb(es, "Gtok", [128, 32, 8], F32)
            CONVW, CONVB, MG, SK = 16, 32, 36, 40
            rot = [0]

            def nextbank():
                rot[0] ^= 1
                return rot[0]

            def frontA(blk):
                xm = xmb[blk % 2]
                kxm = ("xm", blk % 2)
                cols = slice(blk * 512, (blk + 1) * 512)
                if blk == 0:
                    MSET("pool", xm[:, :, 0:3], 0.0, w=[kxm])
                else:
                    CP("pool", xm[:, :, 0:3], xmb[(blk - 1) % 2][:, :, 512:515],
                       r=[("xm", (blk - 1) % 2)] + [(("xm", (blk - 1) % 2), hh) for hh in range(4)], w=[kxm])
                for h in range(4):
                    b = nextbank()
                    for c in range(8):
                        MM(ps[b][:], wm_bf[:, c, h * 128:(h + 1) * 128], hT[:, c, cols], c == 0, c == 7,
                           r=[("wm", c), ("hT", blk)], w=[P(b)])
                    CP("act", xm[:, h, 3:515], ps[b][:], r=[P(b)], w=[(kxm, h)])
                return xm, kxm

            def frontBC(blk, need_v, do_qk=True):
                xm = xmb[blk % 2]
                kxm = ("xm", blk % 2)
                for h in range(4):
                    b = nextbank()
                    for j in range(4):
                        MM(ps[b][:], Dg[:, 4 * h + j, :], xm[:, h, j:j + 512], j == 0, j == 3,
                           r=["Dg", (kxm, h), kxm], w=[P(b)])
                    ACT(xcv[:, h, :], ps[b][:], AF.Silu, r=[P(b), "par"], w=[("xcv", h)],
                        bias=part[:, CONVB + h:CONVB + h + 1])
                for h in range(4 if do_qk else 0):
                    for j, dst, kd in ((0, qT, "qT"), (1, kT, "kT")) + (((2, vT, "vT"),) if need_v else ()):
                        b = nextbank()
                        src = xcv[:, h, :] if j < 2 else xm[:, h, 3:515]
                        MM(ps[b][:], wqkv[:, j, h, :], src, True, True, r=["wqkv", ("xcv", h) if j < 2 else (kxm, h)], w=[P(b)])
                        CP("act" if j == 0 else "dve", dst[:, h, :], ps[b][:], r=[P(b)], w=[kd])
                return xm, kxm

            for blk in range(NB):
                if blk == 0:
                    frontA(0)
                frontBC(blk, False, do_qk=False)
                if blk + 1 < NB:
                    frontA(blk + 1)
                xm_ = xmb[blk % 2]
                kxm_ = ("xm", blk % 2)
                for tt in range(4):
                    tc_ = slice(tt * 128, (tt + 1) * 128)
                    for h in range(4):
                        MM(ps[2][:, tt * 8:(tt + 1) * 8], xcv[:, h, tc_], ABf[:, 0, h, :], h == 0, False,
                           r=[("xcv", h), "ABf"], w=[P(2)])
                        MM(ps[2][:, tt * 8:(tt + 1) * 8], xm_[:, h, 3 + tt * 128:3 + (tt + 1) * 128], ABf[:, 1, h, :], False, h == 3,
                           r=[(kxm_, h), "ABf"], w=[P(2)])
                CP("dve", Gtok[:, blk * 4:(blk + 1) * 4, :], ps[2][:, 0:32].rearrange("p (t g) -> p t g", g=8),
                   r=[P(2)], w=["Gtok"])
            if dbg and dbg >= 1:
                dump(es, "Gtok", Gtok[:], [128, 32, 8])
                dump(es, "qT", qT[:], [128, 4, 512], BF16)
                dump(es, "xcv", xcv[:], [128, 4, 512], BF16)

            rows = sb(es, "rows", [128, 8, 128], F32)
            GIr, GFr, LF, Fp, U_, W_, W2_, TH_ = [rows[:, i, :] for i in range(8)]
            cols_ = sb(es, "colsb", [128, 16], F32)
            col = lambda i: cols_[:, i:i + 1]
            rowt = sb(es, "rowt", [1, 4, 128], F32)
            Ebc = sb(es, "Ebc", [128, 128], F32)
            Wt = sb(es, "Wt", [128, 3, 128], F32)
            BI, BFc = 48, 49
            Gv = Gtok[:].rearrange("p c g -> p (c g)")
            gsp = sb(es, "gsp", [128, 2, 128], F32)
            CP("dve", gsp[:, 0, :].rearrange("p (c h) -> p c h", h=4), Gtok[:, :, 0:4], r=["Gtok"], w=["gsp"])
            CP("dve", gsp[:, 1, :].rearrange("p (c h) -> p c h", h=4), Gtok[:, :, 4:8], r=["Gtok"], w=["gsp"])
            TR(ps[3][:, 0:128], gsp[:, 0, :], r=["gsp"], w=[P(3)])
            TR(ps[3][:, 128:256], gsp[:, 1, :], r=["gsp"], w=[P(3)])
            CP("dve", rows[:, 0:2, :], ps[3][:, 0:256].rearrange("p (a t) -> p a t", t=128), r=[P(3)], w=["rows"])
            TS("dve", col(0), part[:, BFc:BFc + 1], -1.0, None, ALU.mult, None, r=["par"], w=["cols"])
            ACT(LF, GFr, AF.Exp, r=["rows", "cols"], w=["rows"], bias=col(0), scale=-1.0)
            ACT(LF, LF, AF.Ln, r=["rows"], w=["rows"], bias=1.0)
            SCAN(Fp, LF, 0.0, ALU.add, r=["rows"], w=["rows"])
            MM(ps[3][:, 256:257], M1, rows[:, 3, 127:128], True, True, r=["cst", "rows"], w=[P(3)])
            CP("dve", col(1), ps[3][:, 256:257], r=[P(3)], w=["cols"])
            TS("dve", Fp, Fp, col(1), None, ALU.add, None, r=["rows", "cols"], w=["rows"])
            STT(GIr, GIr, part[:, BI:BI + 1], Fp, ALU.add, ALU.add, r=["rows", "par"], w=["rows"])
            SCAN(U_, GIr, 0.0, ALU.max, r=["rows"], w=["rows"])
            MM(ps[3][0:1, 384:512], rows[:, 4, 127:128], ident, True, True, r=["rows", "cst"], w=[P(3)])
            CP("dve", rowt[:, 0, :], ps[3][0:1, 384:512], r=[P(3)], w=["rowt"])
            for h in range(4):
                SCAN(rowt[:, 1, :].rearrange("p (c h) -> p h c", h=4)[:, h, :],
                     rowt[:, 0, :].rearrange("p (c h) -> p h c", h=4)[:, h, :], 0.0, ALU.max, r=["rowt"], w=["rowt"])
            MSET("dve", rowt[:, 2, 0:4], 0.0, w=["rowt"])
            CP("dve", rowt[:, 2, 4:128], rowt[:, 1, 0:124], r=["rowt"], w=["rowt"])
            TT("dve", rowt[:, 3, :], rowt[:, 2, :], rowt[:, 1, :], ALU.subtract, r=["rowt"], w=["rowt"])
            MM(ps[3][:, 257:258], rowt[:, 2, :], ones_f[0:1, 0:1], True, True, r=["rowt", "cst"], w=[P(3)])
            MM(ps[3][:, 258:259], rowt[:, 3, :], ones_f[0:1, 0:1], True, True, r=["rowt", "cst"], w=[P(3)])
            CP("dve", cols_[:, 2:4], ps[3][:, 257:259], r=[P(3)], w=["cols"])
            MM(ps[4][:, 0:128], ones_f[0:1, :], rowt[:, 3, :], True, True, r=["rowt", "cst"], w=[P(4)])
            ACT(Ebc[:], ps[4][:, 0:128], AF.Exp, r=[P(4)], w=["Ebc"])
            TS("dve", col(4), col(2), -1.0, -LN_SQRT_DH, ALU.mult, ALU.add, r=["cols"], w=["cols"])
            TS("dve", col(5), col(2), -1.0, None, ALU.mult, None, r=["cols"], w=["cols"])
            TT("dve", col(6), col(4), col(3), ALU.add, r=["cols"], w=["cols"])
            ACT(W_, GIr, AF.Exp, r=["rows", "cols"], w=["rows"], bias=col(4))
            ACT(W2_, GIr, AF.Exp, r=["rows", "cols"], w=["rows"], bias=col(6))
            ACT(TH_, Fp, AF.Exp, r=["rows", "cols"], w=["rows"], bias=col(5))
            for i in range(3):
                TR(ps[4][:, 128 * (i + 1):128 * (i + 2)], rows[:, 5 + i, :], r=["rows"], w=[P(4)])
            CP("dve", Wt[:], ps[4][:, 128:512].rearrange("p (a t) -> p a t", t=128), r=[P(4)], w=["Wt"])
            if dbg and dbg >= 2:
                dump(es, "rows", rows[:], [128, 8, 128])
                dump(es, "Wt", Wt[:], [128, 3, 128])
                dump(es, "Ebc", Ebc[:], [128, 128])
                dump(es, "rowt", rowt[:], [1, 4, 128])

            kwb = [sb(es, "kw%d" % i, [128, 4, 128], BF16) for i in range(2)]
            V1b = [sb(es, "V1_%d" % i, [128, 4, 129], BF16) for i in range(2)]
            sqkb = [sb(es, "sqk%d" % i, [128, 4, 128], BF16) for i in range(2)]
            ABsb = [sb(es, "ABs%d" % i, [128, 4, 129], F32) for i in range(2)]
            lnbb = [sb(es, "lnb%d" % i, [128, 4, 128], F32) for i in range(2)]
            Cn = sb(es, "Cn", [128, 4, 129], F32)
            Cnb = sb(es, "Cnb", [128, 4, 129], BF16)
            hsb = [sb(es, "hs%d" % i, [128, 4, 128], F32) for i in range(2)]
            smb = [sb(es, "sm%d" % i, [128, 64], F32) for i in range(2)]
            ymt = sb(es, "ymt", [128, 4, 128], F32)
            mhalf = sb(es, "mhalf", [128, 4], F32)
            MSET("pool", mhalf[:], -0.5, w=["mhalf"])
            for i in range(2):
                MSET("pool", V1b[i][:, :, 128:129], 1.0, w=[("V1", i)])
            MSET("pool", Cn[:], 0.0, w=["Cn"])
            MSET("pool", Cnb[:], 0.0, w=["Cnb"])

            def pre(c_, xm, kxm):
                tt, p = c_ % 4, c_ % 2
                tc_ = slice(tt * 128, (tt + 1) * 128)
                ch4 = slice(c_ * 4, c_ * 4 + 4)
                for h in range(4):
                    MM(ps[2][:, h * 128:(h + 1) * 128], xcv[:, h, tc_], wqkv[:, 1, h, :], True, True,
                       r=[("xcv", h), "wqkv"], w=[P(2)])
                    MM(ps[3][:, h * 128:(h + 1) * 128], xm[:, h, 3 + tt * 128:3 + (tt + 1) * 128], wqkv[:, 2, h, :],
                       True, True, r=[(kxm, h), "wqkv"], w=[P(3)])
                    MM(ps[4][:, h * 128:(h + 1) * 128], kT[:, h, tc_], qT[:, h, tc_], True, True,
                       r=["kT", "qT"], w=[P(4)])
                for h in range(4):
                    ACT(kwb[p][:, h, :], ps[2][:, h * 128:(h + 1) * 128], AF.Copy, r=[P(2), "Wt"], w=[("kw", p)],
                        scale=Wt[:, 1, c_ * 4 + h:c_ * 4 + h + 1])
                CP("act", V1b[p][:, :, 0:128], ps[3][:].rearrange("p (h d) -> p h d", d=128), r=[P(3)], w=[("V1", p)])
                for h in range(4):
                    STT(sqkb[p][:, h, :], ps[4][:, h * 128:(h + 1) * 128], Wt[:, 0, c_ * 4 + h:c_ * 4 + h + 1], triu,
                        ALU.mult, ALU.mult, r=[P(4), "Wt", "cst"], w=[("sqk", p)])

            def rec(c_):
                tt, p = c_ % 4, c_ % 2
                tc_ = slice(tt * 128, (tt + 1) * 128)
                kw, V1, sqk = kwb[p], V1b[p], sqkb[p]
                for hp in range(2):
                    ab = ps[5 + hp]
                    for hh in range(2):
                        h = 2 * hp + hh
                        MM(ab[:, hh * 129:(hh + 1) * 129], sqk[:, h, :], V1[:, h, :], hh == 0, False,
                           r=[("sqk", p), ("V1", p)], w=[P(5 + hp)])
                        MM(ab[:, hh * 129:(hh + 1) * 129], qT[:, h, tc_], Cnb[:, h, :], False, True,
                           r=["qT", "Cnb"], w=[P(5 + hp)])
                    for hh in range(2):
                        h = 2 * hp + hh
                        MM(ps[7][:, hh * 129:(hh + 1) * 129], kw[:, h, :], V1[:, h, :], hh == 0, True,
                           r=[("kw", p), ("V1", p)], w=[P(7)])
                    for hh in range(2):
                        h = 2 * hp + hh
                        STT(Cn[:, h, :], Cn[:, h, :], Ebc[:, c_ * 4 + h:c_ * 4 + h + 1], ps[7][:, hh * 129:(hh + 1) * 129],
                            ALU.mult, ALU.add, r=["Cn", "Ebc", P(7)], w=["Cn"])
                    CP("dve", Cnb[:, 2 * hp:2 * hp + 2, :], Cn[:, 2 * hp:2 * hp + 2, :], r=["Cn"], w=["Cnb"])
                    CP("act", ABsb[p][:, 2 * hp:2 * hp + 2, :], ab[:, 0:258].rearrange("p (h d) -> p h d", d=129),
                       r=[P(5 + hp)], w=[("ABs", p)])

            def epi_a(c_):
                tt, p = c_ % 4, c_ % 2
                A = ABsb[p]
                sm_ = smb[p]
                hs_ = hsb[p]
                den = sm_[:, 0:4]
                kd, kmv, krs = ("den", p), ("mv", p), ("rstd4", p)
                STT(den, A[:, :, 128], -1.0, A[:, :, 128], ALU.mult, ALU.max, r=[("ABs", p)], w=[kd])
                TT("dve", den, den, Wt[:, 2, c_ * 4:c_ * 4 + 4], ALU.max, r=[kd, "Wt"], w=[kd])
                RCP(den, den, r=[kd], w=[kd])
                mvv = sm_[:, 48:56].rearrange("p (h t) -> p h t", t=2)
                for h in range(4):
                    STT(hs_[:, h, :], A[:, h, 0:128], sm_[:, h:h + 1], sgo[:, tt, h * 128:(h + 1) * 128], ALU.mult, ALU.mult,
                        r=[("ABs", p), kd, ("sgo", tt)], w=[("hs", p, h)])
                    st = sm_[:, 16 + 8 * h:16 + 8 * h + 6]
                    S.op("dve", (lambda o, i: (lambda e: e.bn_stats(out=o, in_=i)))(st, hs_[:, h, :]), r=[("hs", p, h)], w=[("st", p, h)])
                    S.op("dve", (lambda o, i: (lambda e: e.bn_aggr(out=o, in_=i)))(mvv[:, h, :], st), r=[("st", p, h)], w=[kmv])
                rstd4 = sm_[:, 56:60]
                nmr4 = sm_[:, 60:64]
                TS("pool", rstd4, mvv[:, :, 1], EPS, None, ALU.add, None, r=[kmv], w=[krs])
                TT("pool", rstd4, rstd4, mhalf[:], ALU.pow, r=[krs, "mhalf"], w=[krs])
                TS("pool", nmr4, mvv[:, :, 0], -1.0, None, ALU.mult, None, r=[kmv], w=[("nmr4", p)])
                TT("pool", nmr4, nmr4, rstd4, ALU.mult, r=[("nmr4", p), krs], w=[("nmr4", p)])

            def epi_b(c_):
                p = c_ % 2
                sm_ = smb[p]
                for h in range(4):
                    ACT(lnbb[p][:, h, :], hsb[p][:, h, :], AF.Identity, r=[("hs", p, h), ("rstd4", p), ("nmr4", p)], w=[("lnb", p)],
                        scale=sm_[:, 56 + h:57 + h], bias=sm_[:, 60 + h:61 + h])

            def outp(c_):
                tt, p = c_ % 4, c_ % 2
                blk_ = c_ // 4
                tc_ = slice(tt * 128, (tt + 1) * 128)
                b = nextbank()
                for h in range(4):
                    TR(ps[b][:, h * 128:(h + 1) * 128], lnbb[p][:, h, :], r=[("lnb", p)], w=[P(b)])
                for h in range(4):
                    STT(ymt[:, h, :], ps[b][:, h * 128:(h + 1) * 128], part[:, MG + h:MG + h + 1], xs[:, h, tc_],
                        ALU.mult, ALU.add, r=[P(b), "par", "xs"], w=[("ymt", h)])
                    TT("pool", ymT[:, h, blk_ * 512 + tt * 128:blk_ * 512 + (tt + 1) * 128], ymt[:, h, :], zs[:, h, tc_],
                       ALU.mult, r=[("ymt", h), "zs"], w=["ymT"])

            for blk in range(NB if stop_after > 1 else 1):
                if blk == 0:
                    nxt = frontA(0)
                xm, kxm = nxt
                frontBC(blk, False)
                cols = slice(blk * 512, (blk + 1) * 512)
                for h in range(4):
                    b = nextbank()
                    for c in range(8):
                        MM(ps[b][:], wm_bf[:, c, 1024 + h * 128:1024 + (h + 1) * 128], hT[:, c, cols], c == 0, c == 7,
                           r=[("wm", c), ("hT", blk)], w=[P(b)])
                    ACT(zs[:, h, :], ps[b][:], AF.Silu, r=[P(b)], w=["zs"])
                    ACT(xs[:, h, :], xcv[:, h, :], AF.Copy, scale=part[:, SK + h:SK + h + 1],
                       r=[("xcv", h), "par"], w=["xs"])
                for tt in range(4):
                    b = nextbank()
                    for c in range(8):
                        MM(ps[b][:], hT[:, c, blk * 512 + tt * 128:blk * 512 + (tt + 1) * 128], wm_bf[:, c, 512:1024],
                           c == 0, c == 7, r=[("wm", c), ("hT", blk)], w=[P(b)])
                    ACT(sgo[:, tt, :], ps[b][:], AF.Sigmoid, r=[P(b)], w=[("sgo", tt)])
                c0 = blk * 4
                pre(c0, xm, kxm)
                for tt in range(4):
                    c_ = c0 + tt
                    if tt < 3:
                        pre(c_ + 1, xm, kxm)
                    rec(c_)
                    if tt == 1 and blk + 1 < (NB if stop_after > 1 else 1):
                        nxt = frontA(blk + 1)
                    epi_a(c_)
                    if tt > 0:
                        epi_b(c_ - 1)
                        outp(c_ - 1)
                epi_b(c0 + 3)
                outp(c0 + 3)
            if dbg and dbg >= 3:
                dump(es, "ymT", ymT[:, :, 0:512], [128, 4, 512], BF16)
                dump(es, "hs", hsb[0][:], [128, 4, 128])
                dump(es, "Cn", Cn[:], [128, 4, 129])
            finish(blk1)
        wm_cm.__exit__(None, None, None)
        if stop_after <= 1:
            return nc

        rot2 = [0]

        def nb2():
            rot2[0] ^= 1
            return rot2[0]

        GROUPS = [(3 * g, 3 * g + 3) for g in range(10)] + [(30, 32)]
        DG = 44

        def proj_fm(wt, j, dst, blk, func, scale, kd, perblk=False):
            b = nb2()
            cols = slice(blk * 512, (blk + 1) * 512)
            wk_ = (kd, blk) if perblk else kd
            for c in range(8):
                MM(ps[b][:], wt[:, j, c, :], hT[:, c, cols], c == 0, c == 7, r=[kd + "_w", ("hT", blk)], w=[P(b)])
            if func is None:
                CP("dve", dst[:, cols], ps[b][:], r=[P(b)], w=[wk_])
            else:
                ACT(dst[:, cols], ps[b][:], func, r=[P(b)], w=[wk_], scale=scale)

        ydT = sb(top, "ydT", [128, 4, S_LEN], BF16)
        with nc.Block() as blk2, ExitStack() as es:
            lamt = sb(es, "lamt", [128, 8], F32)
            j64 = sb(es, "j64", [128, 64], F32)
            STT(j64[:], part[:, 64:128], 1.0, part[:, 128:192], ALU.mult, ALU.mult, r=["par"], w=["j64", "lam"], accum=lamt[:, 0:1])
            STT(j64[:], part[:, 192:256], 1.0, part[:, 256:320], ALU.mult, ALU.mult, r=["par", "j64"], w=["j64", "lam"], accum=lamt[:, 1:2])
            ACT(lamt[:, 2:4], lamt[:, 0:2], AF.Exp, r=["lam"], w=["lam"])
            TT("dve", lamt[:, 4:5], lamt[:, 2:3], lamt[:, 3:4], ALU.subtract, r=["lam"], w=["lam"])
            TS("dve", lamt[:, 5:6], lamt[:, 4:5], 0.2, -1.0, ALU.add, ALU.mult, r=["lam"], w=["lam"])
            wdb = [sb(es, "wd%d" % i, [128, 4, 8, 128], BF16) for i in range(2)]
            qdT = sb(es, "qdT", [128, S_LEN], BF16)
            kpd = [sb(es, "kpd%d" % i, [128, S_LEN], BF16) for i in range(2)]
            MSET("pool", kpd[0][64:128, :], 0.0, w=["kd"])
            MSET("pool", kpd[1][0:64, :], 0.0, w=["kd"])
            zds = sb(es, "zds", [128, S_LEN], BF16)
            V1d = sb(es, "V1d", [128, 32, 128], BF16)
            ptb = [sb(es, "pt%d" % i, [128, 512], BF16) for i in range(4)]
            o0s = sb(es, "o0s", [128, 512], F32)
            o1s = sb(es, "o1s", [128, 512], F32)
            l0s = sb(es, "l0s", [128, 512], F32)
            l1s = sb(es, "l1s", [128, 512], F32)
            sqb = sb(es, "sqb", [128, 512], BF16)
            rsd = l0s
            dg08 = sb(es, "dg08", [128, 4], F32)
            TS("dve", dg08[:], part[:, DG:DG + 4], 0.8, None, ALU.mult, None, r=["par"], w=["dg08"])
            GROUPS4 = [(4 * g, 4 * g + 4) for g in range(8)]

            def proj_k(wt, blk):
                b = nb2()
                cols = slice(blk * 512, (blk + 1) * 512)
                for c in range(8):
                    MM(ps[b][:], wt[:, 1, c, :], hT[:, c, cols], c == 0, c == 7, r=["kd_w", ("hT", blk)], w=[P(b)])
                CP("dve", kpd[0][0:64, cols], ps[b][0:64, :], r=[P(b)], w=["kd"])
                CP("dve", kpd[1][64:128, cols], ps[b][64:128, :], r=[P(b)], w=["kd"])
            for h in range(4):
                wd = wdb[h % 2]
                for j, off, kk in ((0, 1536, "qd_w"), (1, 2048, "kd_w"), (3, 3072, "zd_w"), (2, 2560, "vd_w")):
                    DMA("pool", wd[:, j, :, :], win_v[:, :, off + h * 128:off + (h + 1) * 128], r=[], w=[kk])
                for blk in range(NB):
                    proj_fm(wd, 0, qdT, blk, AF.Identity, 0.125, "qd")
                    proj_k(wd, blk)
                    proj_fm(wd, 3, zds, blk, AF.Silu, None, "zd")
                    for tt in range(4):
                        tl = blk * 4 + tt
                        for c in range(8):
                            MM(ps[3][:, tt * 128:(tt + 1) * 128], hT[:, c, tl * 128:(tl + 1) * 128], wd[:, 2, c, :], c == 0, c == 7,
                               r=["vd_w", ("hT", blk)], w=[P(3)])
                    CP("dve", V1d[:, blk * 4:(blk + 1) * 4, :], ps[3][:].rearrange("p (t d) -> p t d", d=128), r=[P(3)], w=["V1d"])
                steps = [(g, qs, qe, m, kb) for g, (qs, qe) in enumerate(GROUPS4) for m in range(2) for kb in range(qe)]
                LA = 3
                sctr = [0]

                def emit_qk(i):
                    g, qs, qe, m, kb = steps[i]
                    nsub = qe - qs
                    pr = slice(m * 64, (m + 1) * 64)
                    sv = max(0, kb - qs)
                    sbk = sctr[0] % 4
                    sctr[0] += 1
                    fc = slice(sv * 128, nsub * 128)
                    MM(ps[sbk][:, fc], kpd[m][:, kb * 128:(kb + 1) * 128], qdT[:, (qs + sv) * 128:qe * 128], True, True,
                       r=["kd", "qd"], w=[P(sbk)])
                    pt = ptb[i % 4]
                    kpt = ("pt", i % 4)
                    ACT(pt[:, fc], ps[sbk][:, fc], AF.Exp, r=[P(sbk)], w=[kpt])
                    if kb >= qs:
                        dc = slice((kb - qs) * 128, (kb - qs + 1) * 128)
                        TT("dve", pt[:, dc], pt[:, dc], mask_bf[:], ALU.mult, r=[kpt, "mask_bf"], w=[kpt])

                def emit_pv(i):
                    g, qs, qe, m, kb = steps[i]
                    nsub = qe - qs
                    sv = max(0, kb - qs)
                    fc = slice(sv * 128, nsub * 128)
                    pt = ptb[i % 4]
                    kpt = ("pt", i % 4)
                    MM(ps[4 + m][:, fc], V1d[:, kb, :], pt[:, fc], kb == 0, kb == qe - 1, r=[kpt, "V1d"], w=[P(4 + m)])
                    MM(ps[6 + m][:, fc], ones_bf[:], pt[:, fc], kb == 0, kb == qe - 1, r=[kpt, "ones_bf"], w=[P(6 + m)])
                    if m == 1 and kb == qe - 1:
                        epilogue1(qs, qe)
                        pending.append((i + 12, qs, qe))

                pending = []

                def epilogue1(qs, qe):
                    CP("act", o0s[:], ps[4][:], r=[P(4)], w=["o0s"])
                    CP("dve", l0s[:], ps[6][:], r=[P(6)], w=["l0s"])
                    CP("act", o1s[:], ps[5][:], r=[P(5)], w=["o1s"])
                    CP("dve", l1s[:], ps[7][:], r=[P(7)], w=["l1s"])
                    RCP(l0s[:], l0s[:], r=["l0s"], w=["l0s"])
                    RCP(l1s[:], l1s[:], r=["l1s"], w=["l1s"])
                    TT("dve", o0s[:], o0s[:], l0s[:], ALU.mult, r=["o0s", "l0s"], w=["o0s"])
                    TT("dve", o1s[:], o1s[:], l1s[:], ALU.mult, r=["o1s", "l1s"], w=["o1s"])
                    STT(o0s[:], o1s[:], lamt[:, 5:6], o0s[:], ALU.mult, ALU.add, r=["o0s", "o1s", "lam"], w=["o0s"])

                def epilogue2(qs, qe):
                    cols = slice(qs * 128, qe * 128)
                    ACT(sqb[:], o0s[:], AF.Square, r=["o0s"], w=["sqb"])
                    eb = sctr[0] % 4
                    sctr[0] += 1
                    MM(ps[eb][:], ones_bf[:], sqb[:], True, True, r=["sqb", "ones_bf"], w=[P(eb)])
                    ACT(rsd[:], ps[eb][:], AF.Sqrt, r=[P(eb)], w=["l0s"], bias=EPS, scale=1.0 / 128)
                    RCP(rsd[:], rsd[:], r=["l0s"], w=["l0s"])
                    STT(o0s[:], o0s[:], dg08[:, h:h + 1], rsd[:], ALU.mult, ALU.mult, r=["o0s", "dg08", "l0s"], w=["o0s"])
                    TT("dve", ydT[:, h, cols], o0s[:], zds[:, cols], ALU.mult, r=["o0s", "zd"], w=["ydT"])

                for i in range(len(steps) + LA):
                    if i < len(steps):
                        emit_qk(i)
                    if i >= LA:
                        emit_pv(i - LA)
                    while pending and pending[0][0] <= i - LA:
                        _, pqs, pqe = pending.pop(0)
                        epilogue2(pqs, pqe)
                while pending:
                    _, pqs, pqe = pending.pop(0)
                    epilogue2(pqs, pqe)
            if dbg and dbg >= 4:
                dump(es, "ydT", ydT[:, :, 0:1024], [128, 4, 1024], BF16)
                dump(es, "lamt", lamt[:], [128, 8])
            finish(blk2)
        if stop_after <= 2:
            return nc

        ycT = sb(top, "ycT", [128, 4, S_LEN], BF16)
        with nc.Block() as blk3, ExitStack() as es:
            wcb = [sb(es, "wc%d" % i, [128, 2, 8, 128], BF16) for i in range(2)]
            qcT = sb(es, "qcT", [128, S_LEN], BF16)
            zcs = sb(es, "zcs", [128, S_LEN], BF16)
            ptb = [sb(es, "ptc%d" % i, [128, 512], BF16) for i in range(4)]
            rlb = [sb(es, "rl%d" % i, [128, 512], F32) for i in range(2)]
            onb = [sb(es, "on%d" % i, [128, 512], F32) for i in range(2)]
            for h in range(4):
                wc = wcb[h % 2]
                for j, off, kk in ((0, 3584, "qc_w"), (1, 4096, "zc_w")):
                    DMA("pool", wc[:, j, :, :], win_v[:, :, off + h * 128:off + (h + 1) * 128], r=[], w=[kk])

                def c_qk(j):
                    cols = slice(j * 512, (j + 1) * 512)
                    for mc in range(2):
                        sbk = 2 * (j % 2) + mc
                        MM(ps[sbk][:], kmT[:, h, mc * 128:(mc + 1) * 128], qcT[:, cols], True, True, r=["kmT", ("qc", j)], w=[P(sbk)])
                        ACT(ptb[sbk][:], ps[sbk][:], AF.Exp, r=[P(sbk)], w=[("ptc", sbk)])

                def c_pv(j):
                    cols = slice(j * 512, (j + 1) * 512)
                    p = j % 2
                    for mc in range(2):
                        sbk = 2 * p + mc
                        MM(ps[4 + p][:], vm1[:, mc, h, 0:128], ptb[sbk][:], mc == 0, mc == 1, r=[("ptc", sbk), "vm1"], w=[P(4 + p)])
                    for mc in range(2):
                        sbk = 2 * p + mc
                        MM(ps[6 + p][:], ones_bf[:], ptb[sbk][:], mc == 0, mc == 1, r=[("ptc", sbk), "ones_bf"], w=[P(6 + p)])
                    RCP(rlb[p][:], ps[6 + p][:], r=[P(6 + p)], w=[("rl", p)])
                    TT("dve", onb[p][:], ps[4 + p][:], rlb[p][:], ALU.mult, r=[P(4 + p), ("rl", p)], w=[("on", p)])
                    TT("pool", ycT[:, h, cols], onb[p][:], zcs[:, cols], ALU.mult, r=[("on", p), ("zc", j)], w=["ycT"])

                for j in range(NB + 1):
                    if j < NB:
                        proj_fm(wc, 0, qcT, j, AF.Copy, None, "qc", perblk=True)
                        proj_fm(wc, 1, zcs, j, AF.Silu, None, "zc", perblk=True)
                        c_qk(j)
                    if j >= 1:
                        c_pv(j - 1)
            if dbg and dbg >= 5:
                dump(es, "ycT", ycT[:, :, 0:1024], [128, 4, 1024], BF16)
            finish(blk3)
        if stop_after <= 3:
            return nc

        with nc.Block() as blk4, ExitStack() as es:
            hflat = hT[:].rearrange("p c t -> p (c t)")
            wout_bf = hflat[:, 0:12 * D].rearrange("p (c f) -> p c f", f=D)
            wout_v = wout_d.rearrange("(c p) f -> p c f", p=128)
            for c in range(12):
                DMA("pool", wout_bf[:, c, :], wout_v[:, c, :], r=[], w=[("wout", c)])
            hf32 = hflat[:, 12 * D:].bitcast(F32)
            NXB = 6
            xtb = [hf32[:, i * D:(i + 1) * D] for i in range(NXB)]
            fgt = hf32[:, NXB * D:(NXB + 1) * D]
            DMA("sp", fgt, fg_d[:, :], r=[], w=["fg"])
            junk = sb(es, "junk4", [128, D], BF16)
            ss4 = sb(es, "ss4", [128, 2 * NXB], F32)
            ysrc = [(ymT, hh) for hh in range(4)] + [(ydT, hh) for hh in range(4)] + [(ycT, hh) for hh in range(4)]

            def load_x(i):
                DMA("sp", xtb[i % NXB], x_d[i * 128:(i + 1) * 128, :], r=[], w=[("xo", i % NXB)])

            PF = 4
            for i in range(PF):
                load_x(i)
            for i in range(NT):
                p2 = i % NXB
                xt = xtb[p2]
                kx = ("xo", p2)
                tcs = slice(i * 128, (i + 1) * 128)
                if i + PF < NT:
                    load_x(i + PF)
                for half in range(2):
                    b = 2 * (i % 4) + half
                    hc = slice(half * 512, (half + 1) * 512)
                    for ch, (src, hh) in enumerate(ysrc):
                        MM(ps[b][:], src[:, hh, tcs], wout_bf[:, ch, hc], ch == 0, ch == 11, r=[("wout", ch), "ycat"], w=[P(b)])
                    TT("dve", xt[:, hc], ps[b][:], xt[:, hc], ALU.add, r=[P(b), kx], w=[kx])
                ss = ss4[:, 2 * p2:2 * p2 + 1]
                rs = ss4[:, 2 * p2 + 1:2 * p2 + 2]
                ACT(junk[:], xt, AF.Square, r=[kx], w=["junk4", ("ss4", p2)], accum=ss)
                ACT(rs, ss, AF.Sqrt, r=[("ss4", p2)], w=[("ss4", p2)], bias=EPS, scale=1.0 / D)
                RCP(rs, rs, r=[("ss4", p2)], w=[("ss4", p2)])
                STT(xt, xt, rs, fgt, ALU.mult, ALU.mult, r=[kx, ("ss4", p2), "fg"], w=[kx])
                DMA("sp", out_d[i * 128:(i + 1) * 128, :], xt, r=[kx], w=[])
            finish(blk4)
    return nc


def make_consts():
    cst = np.zeros((128, 512), np.float32)
    cst[:, 0:128] = np.eye(128, dtype=np.float32)
    cst[:, 128:256] = np.triu(np.ones((128, 128), np.float32))
    p = np.arange(128)
    c, h = p // 4, p % 4
    cst[:, 256:384] = ((h[:, None] == h[None, :]) & (c[:, None] < c[None, :])).astype(np.float32)
    cst[:, 384:512] = 1.0
    return cst


def make_in_maps(inp):
    f = lambda a: np.ascontiguousarray(a, dtype=np.float32)
    cst = make_consts()
    par = np.zeros((128, 320), np.float32)
    par[:, 0:8] = inp["norm_g"][0].reshape(8, 128).T
    par[:, 8:16] = inp["mem_norm_g"][0].reshape(8, 128).T
    par[:, 16:32] = inp["conv_w"][0].reshape(4, 4, 128).transpose(2, 1, 0).reshape(128, 16)
    par[:, 32:36] = inp["conv_b"][0].reshape(4, 128).T
    par[:, 36:40] = inp["mnorm_g"][0].reshape(4, 128).T
    par[:, 40:44] = inp["skip_m"][0].reshape(4, 128).T
    par[:, 44:48] = inp["dnorm_g"][0].reshape(4, 128).T
    par[:, 48] = np.tile(inp["b_if"][0][0:4], 32)
    par[:, 49] = np.tile(inp["b_if"][0][4:8], 32)
    par[:, 64:128] = inp["lam_q1"][0][None, :]
    par[:, 128:192] = inp["lam_k1"][0][None, :]
    par[:, 192:256] = inp["lam_q2"][0][None, :]
    par[:, 256:320] = inp["lam_k2"][0][None, :]
    fg = np.ascontiguousarray(np.broadcast_to(inp["final_g"][None, :], (128, D)), dtype=np.float32)
    shared = {
        "w_in": f(inp["w_in"][0]), "w_kv": f(inp["w_mem_kv"][0]), "w_out": f(inp["w_out"][0]),
        "wq": f(inp["wq_m"][0]), "wk": f(inp["wk_m"][0]), "wv": f(inp["wv_m"][0]), "w_if": f(inp["w_if"][0]),
        "cst": cst, "par": par, "fg": fg,
    }
    maps = []
    for b in range(8):
        m = dict(shared)
        m["x"] = f(inp["x"][b])
        m["mem"] = f(inp["mem"][b])
        maps.append(m)
    return maps


_NC_CACHE = {}


def kernel(**inputs):
    if "nc" not in _NC_CACHE:
        _NC_CACHE["nc"] = build_nc()
    nc = _NC_CACHE["nc"]
    maps = make_in_maps(inputs)
    res = run_bass_kernel_spmd(nc, maps, core_ids=list(range(8)))
    return np.stack([np.asarray(r["out"], dtype=np.float32) for r in res.results], axis=0)
```

```python
import math
from contextlib import ExitStack

import numpy as np
import concourse.bass as bass
import concourse.mybir as mybir
from concourse.bass_utils import run_bass_kernel_spmd

F32 = mybir.dt.float32
BF16 = mybir.dt.bfloat16
AF = mybir.ActivationFunctionType
ALU = mybir.AluOpType

S_LEN = 4096
D = 1024
NT = 32
NB = 8
EPS = 1e-6
LN_SQRT_DH = 0.5 * math.log(128.0)


class Sched:
    ENG = ("pe", "act", "dve", "pool", "sp")

    def __init__(self, nc, sems, dsems):
        self.nc = nc
        self.sems = sems
        self.dsems = dsems
        self.q = {e: [] for e in self.ENG}
        self.cnt = {e: 0 for e in self.ENG}
        self.last_w = {}
        self.readers = {}
        self.seen = {e: {} for e in self.ENG}
        self.n_dma = len(dsems)
        self.dma_cnt = [0] * self.n_dma
        half = self.n_dma // 2
        self.dma_pool = {"sp": list(range(0, half)), "pool": list(range(half, self.n_dma))}
        self.dma_rr = {"sp": 0, "pool": 0}

    def _need(self, eng, tok, waits):
        src, val = tok
        if src == eng and eng in ("pe", "sp"):
            return
        if self.seen[eng].get(src, 0) >= val:
            return
        self.seen[eng][src] = val
        waits[src] = max(waits.get(src, 0), val)

    def op(self, eng, fn, r=(), w=(), dma=False):
        waits = {}
        for k in r:
            t = self.last_w.get(k)
            if t is not None:
                self._need(eng, t, waits)
        for k in w:
            t = self.last_w.get(k)
            if t is not None:
                self._need(eng, t, waits)
            for t in self.readers.get(k, ()):
                self._need(eng, t, waits)
        if dma:
            pool_ = self.dma_pool[eng]
            i = pool_[self.dma_rr[eng] % len(pool_)]
            self.dma_rr[eng] += 1
            if self.dma_cnt[i] > 0:
                self._need(eng, (("d", i), 16 * self.dma_cnt[i]), waits)
            self.dma_cnt[i] += 1
            tok = (("d", i), 16 * self.dma_cnt[i])
        else:
            self.cnt[eng] += 1
            tok = (eng, self.cnt[eng])
        self.q[eng].append((list(waits.items()), fn, tok))
        for k in w:
            self.last_w[k] = tok
            self.readers[k] = []
        for k in r:
            self.readers.setdefault(k, []).append(tok)
        return tok

    def barrier(self):
        toks = [(e, self.cnt[e]) for e in self.ENG if self.cnt[e] > 0]
        toks += [(("d", i), 16 * c) for i, c in enumerate(self.dma_cnt) if c > 0]
        for e in self.ENG:
            waits = {}
            for t in toks:
                if t[0] != e:
                    self._need(e, t, waits)
            if waits:
                self.q[e].append((list(waits.items()), None, None))

    def emit(self, block):
        sems, dsems = self.sems, self.dsems

        def run(e, engobj):
            for waits, fn, tok in self.q[e]:
                for src, val in waits:
                    s = dsems[src[1]] if isinstance(src, tuple) else sems[src]
                    engobj.wait_ge(s, val)
                if fn is None:
                    continue
                ins = fn(engobj)
                if isinstance(tok[0], tuple):
                    ins.then_inc(dsems[tok[0][1]], 16)
                else:
                    ins.then_inc(sems[tok[0]], 1)
            self.q[e] = []

        @block.tensor
        def _(eng):
            run("pe", eng)

        @block.scalar
        def _(eng):
            run("act", eng)

        @block.vector
        def _(eng):
            run("dve", eng)

        @block.gpsimd
        def _(eng):
            run("pool", eng)

        @block.sync
        def _(eng):
            run("sp", eng)


def build_nc(stop_after=99, dbg=None):
    nc = bass.Bass("TRN2", target_bir_lowering=False)
    din = lambda n, s: nc.dram_tensor(n, s, F32, kind="ExternalInput").ap()
    x_d = din("x", [S_LEN, D])
    mem_d = din("mem", [256, D])
    win_d = din("w_in", [D, 4608])
    wkv_d = din("w_kv", [D, 1024])
    wout_d = din("w_out", [1536, D])
    wq_d = din("wq", [4, 128, 128])
    wk_d = din("wk", [4, 128, 128])
    wv_d = din("wv", [4, 128, 128])
    wif_d = din("w_if", [1536, 8])
    cst_d = din("cst", [128, 512])
    par_d = din("par", [128, 320])
    fg_d = din("fg", [128, D])
    out_d = nc.dram_tensor("out", [S_LEN, D], F32, kind="ExternalOutput").ap()
    dbg_outs = []
    win_v = win_d.rearrange("(c p) f -> p c f", p=128)

    top = ExitStack()
    with top:
        sb = lambda es, name, shape, dt: es.enter_context(nc.sbuf_tensor(name, shape, dt))
        ps = [top.enter_context(nc.psum_tensor("ps%d" % i, [128, 512], F32)) for i in range(8)]
        sems = {e: top.enter_context(nc.semaphore("s_" + e)) for e in Sched.ENG}
        dsems = [top.enter_context(nc.semaphore("d%d" % i)) for i in range(32)]
        S = Sched(nc, sems, dsems)
        P = lambda i: ("ps", i)

        def MM(out, lhsT, rhs, start, stop, r, w):
            return S.op("pe", lambda e: e.matmul(out, lhsT=lhsT, rhs=rhs, start=start, stop=stop,
                                                  skip_group_check=True), r=r, w=w)

        def TR(out, in_, r, w):
            return S.op("pe", lambda e: e.transpose(out=out, in_=in_, identity=ident[:]), r=list(r) + ["cst"], w=w)

        def ACT(out, in_, func, r, w, bias=None, scale=None, accum=None, eng="act"):
            kw = {}
            if bias is not None:
                kw["bias"] = bias
            if scale is not None:
                kw["scale"] = scale
            if accum is not None:
                kw["accum_out"] = accum
            return S.op("act", lambda e: e.activation(out=out, in_=in_, func=func, **kw), r=r, w=w)

        def TS(eng, out, in0, s1, s2, op0, op1, r, w):
            if op1 is None:
                return S.op(eng, lambda e: e.tensor_scalar(out=out, in0=in0, scalar1=s1, scalar2=None, op0=op0), r=r, w=w)
            return S.op(eng, lambda e: e.tensor_scalar(out=out, in0=in0, scalar1=s1, scalar2=s2, op0=op0, op1=op1), r=r, w=w)

        def TT(eng, out, in0, in1, op, r, w):
            return S.op(eng, lambda e: e.tensor_tensor(out=out, in0=in0, in1=in1, op=op), r=r, w=w)

        def STT(out, in0, scalar, in1, op0, op1, r, w, accum=None):
            if accum is not None:
                return S.op("dve", lambda e: e.scalar_tensor_tensor(out=out, in0=in0, scalar=scalar, in1=in1, op0=op0, op1=op1, accum_out=accum), r=r, w=w)
            return S.op("dve", lambda e: e.scalar_tensor_tensor(out=out, in0=in0, scalar=scalar, in1=in1, op0=op0, op1=op1), r=r, w=w)

        def CP(eng, out, in_, r, w):
            if eng == "act":
                return S.op("act", lambda e: e.copy(out=out, in_=in_), r=r, w=w)
            return S.op(eng, lambda e: e.tensor_copy(out=out, in_=in_), r=r, w=w)

        def MSET(eng, ap, val, w):
            return S.op(eng, lambda e: e.memset(ap, val), w=w)

        def RCP(out, in_, r, w):
            return S.op("dve", lambda e: e.reciprocal(out=out, in_=in_), r=r, w=w)

        def SCAN(out, d0, init, op0, r, w):
            return S.op("dve", lambda e: e.tensor_tensor_scan(out=out, data0=d0, data1=d0, initial=init, op0=op0, op1=ALU.bypass), r=r, w=w)

        def DMA(eng, out, in_, r, w):
            return S.op(eng, lambda e: e.dma_start(out=out, in_=in_), r=r, w=w, dma=True)

        def dump(es, name, ap, shape, dt=F32):
            d = nc.dram_tensor("dbg_" + name, list(shape), dt, kind="ExternalOutput").ap()
            S.barrier()
            dbg_outs.append(DMA("sp", d, ap, r=[], w=[]))

        def finish(block):
            S.barrier()
            S.emit(block)

        cstt = sb(top, "cstt", [128, 512], F32)
        ident = cstt[:, 0:128]
        triu = cstt[:, 128:256]
        M1 = cstt[:, 256:384]
        ones_f = cstt[:, 384:512]
        part = sb(top, "part", [128, 320], F32)
        mask_bf = sb(top, "mask_bf", [128, 128], BF16)
        ones_bf = sb(top, "ones_bf", [128, 128], BF16)
        ident_bf = sb(top, "ident_bf", [128, 128], BF16)
        kmT = sb(top, "kmT", [128, 4, 256], BF16)
        vm1 = sb(top, "vm1", [128, 2, 4, 129], BF16)
        ymT = sb(top, "ymT", [128, 4, S_LEN], BF16)
        hT = sb(top, "hT", [128, 8, S_LEN], BF16)
        wm_cm = nc.sbuf_tensor("wm_bf", [128, 8, 1536], BF16)
        wm_bf = wm_cm.__enter__()

        def rms_tile_a(es_bufs, src_rows, i):
            xtb, junk, ssb, xnb = es_bufs
            xt = xtb[i % 4]
            ss = ssb[:, 2 * (i % 4):2 * (i % 4) + 1]
            rs = ssb[:, 2 * (i % 4) + 1:2 * (i % 4) + 2]
            kx, ks = ("xt", i % 4), ("ss", i % 4)
            DMA("sp", xt[:], src_rows, r=[], w=[kx])
            ACT(junk[:], xt[:], AF.Square, r=[kx], w=["junk", ks], accum=ss)
            ACT(rs, ss, AF.Sqrt, r=[ks], w=[ks], bias=EPS, scale=1.0 / D)
            RCP(rs, rs, r=[ks], w=[ks])

        def rms_tile_b(es_bufs, gcol, dstT, col0, i):
            xtb, junk, ssb, xnb = es_bufs
            xt = xtb[i % 4]
            xn = xnb[i % 3]
            rs = ssb[:, 2 * (i % 4) + 1:2 * (i % 4) + 2]
            kx, kn, ks = ("xt", i % 4), ("xn", i % 3), ("ss", i % 4)
            ACT(xn[:], xt[:], AF.Copy, r=[kx, ks], w=[kn], scale=rs)
            for half in range(2):
                b = 2 * (i % 4) + half
                psb = ps[b][:].bitcast(BF16)
                for j in range(4):
                    c = 4 * half + j
                    S.op("pe", (lambda o, a: (lambda e: e.transpose(out=o, in_=a, identity=ident_bf[:])))(
                        psb[:, j * 128:(j + 1) * 128], xn[:, c * 128:(c + 1) * 128]), r=[kn, "ident_bf"], w=[P(b)])
                TT("dve", dstT[:, 4 * half:4 * half + 4, col0:col0 + 128],
                   psb[:, 0:512].rearrange("p (j t) -> p j t", t=128),
                   part[:, gcol + 4 * half:gcol + 4 * half + 4].unsqueeze(2).to_broadcast([128, 4, 128]),
                   ALU.mult, r=[P(b), "par"], w=[("hT", col0 // 512) if dstT is hT else "memT"])

        with nc.Block() as blk0, ExitStack() as es:
            DMA("sp", cstt[:], cst_d[:, :], r=[], w=["cst"])
            DMA("sp", part[:], par_d[:, :], r=[], w=["par"])
            CP("dve", mask_bf[:], triu, r=["cst"], w=["mask_bf"])
            CP("dve", ones_bf[:], ones_f, r=["cst"], w=["ones_bf"])
            CP("dve", ident_bf[:], ident, r=["cst"], w=["ident_bf"])
            xtb = [sb(es, "xt%d" % i, [128, D], F32) for i in range(4)]
            xnb = [sb(es, "xn%d" % i, [128, D], BF16) for i in range(3)]
            junk = sb(es, "junk", [128, D], BF16)
            ssb = sb(es, "ssb", [128, 8], F32)
            bufs = (xtb, junk, ssb, xnb)
            memT = sb(es, "memT", [128, 8, 256], BF16)
            wkv_bf = sb(es, "wkv_bf", [128, 8, 1024], BF16)
            wkv_v = wkv_d.rearrange("(c p) f -> p c f", p=128)
            for c in range(8):
                DMA("pool", wkv_bf[:, c, :], wkv_v[:, c, :], r=[], w=["wkv"])
            for c in range(8):
                DMA("pool", wm_bf[:, c, :], win_v[:, c, 0:1536], r=[], w=[("wm", c)])
            seq = [(mem_d[i * 128:(i + 1) * 128, :], 8, memT, i * 128) for i in range(2)]
            seq += [(x_d[i * 128:(i + 1) * 128, :], 0, hT, i * 128) for i in range(NT)]
            rms_tile_a(bufs, seq[0][0], 0)
            for k, (src_rows, gcol, dstT, col0) in enumerate(seq):
                if k + 1 < len(seq):
                    rms_tile_a(bufs, seq[k + 1][0], k + 1)
                rms_tile_b(bufs, gcol, dstT, col0, k)
            for h in range(4):
                b = 4 + h % 2
                for c in range(8):
                    MM(ps[b][:, 0:256], wkv_bf[:, c, h * 128:(h + 1) * 128], memT[:, c, :], c == 0, c == 7,
                       r=["wkv", "memT"], w=[P(b)])
                ACT(kmT[:, h, :], ps[b][:, 0:256], AF.Identity, r=[P(b)], w=["kmT"], scale=128.0 ** -0.5)
            MSET("pool", vm1[:, :, :, 128:129], 1.0, w=["vm1"])
            for mt in range(2):
                b = 6 + mt
                for c in range(8):
                    MM(ps[b][:], memT[:, c, mt * 128:(mt + 1) * 128], wkv_bf[:, c, 512:1024], c == 0, c == 7,
                       r=["wkv", "memT"], w=[P(b)])
                CP("dve", vm1[:, mt, :, 0:128], ps[b][:].rearrange("p (h d) -> p h d", d=128), r=[P(b)], w=["vm1"])
            if dbg == 0:
                dump(es, "hT", hT[:, :, 0:512], [128, 8, 512], BF16)
                dump(es, "kmT", kmT[:], [128, 4, 256], BF16)
                dump(es, "vm1", vm1[:], [128, 2, 4, 129], BF16)
            finish(blk0)
        if stop_after == 0:
            return nc

        with nc.Block() as blk1, ExitStack() as es:
            wqkv = sb(es, "wqkv", [128, 3, 4, 128], BF16)
            for j, wd in enumerate((wq_d, wk_d, wv_d)):
                DMA("pool", wqkv[:, j, :, :], wd.rearrange("h d e -> d h e"), r=[], w=["wqkv"])
            wif_bf = sb(es, "wif_bf", [128, 12, 8], BF16)
            DMA("pool", wif_bf[:], wif_d.rearrange("(j p) g -> p j g", p=128), r=[], w=["wif"])
            ABf = sb(es, "ABf", [128, 2, 4, 8], BF16)
            wTs = [sb(es, "wTs%d" % i, [128, 128], BF16) for i in range(3)]
            psb4 = ps[4][:].bitcast(BF16)
            for h in range(4):
                for j in range(3):
                    S.op("pe", (lambda o, a: (lambda e: e.transpose(out=o, in_=a, identity=ident_bf[:])))(
                        psb4[:, j * 128:(j + 1) * 128], wqkv[:, j, h, :]), r=["wqkv", "ident_bf"], w=[P(4)])
                    CP("dve", wTs[j][:], psb4[:, j * 128:(j + 1) * 128], r=[P(4)], w=[("wTs", j)])
                MM(ps[5][:, h * 8:(h + 1) * 8], wTs[0][:], wif_bf[:, h, :], True, False, r=[("wTs", 0), "wif"], w=[P(5)])
                MM(ps[5][:, h * 8:(h + 1) * 8], wTs[1][:], wif_bf[:, 4 + h, :], False, True, r=[("wTs", 1), "wif"], w=[P(5)])
                MM(ps[5][:, 32 + h * 8:32 + (h + 1) * 8], wTs[2][:], wif_bf[:, 8 + h, :], True, True, r=[("wTs", 2), "wif"], w=[P(5)])
            CP("dve", ABf[:].rearrange("p a h g -> p (a h g)"), ps[5][:, 0:64], r=[P(5)], w=["ABf"])
            xmb = [sb(es, "xm%d" % i, [128, 4, 515], BF16) for i in range(2)]
            xcv = sb(es, "xcv", [128, 4, 512], BF16)
            xs = sb(es, "xs", [128, 4, 512], BF16)
            qT = sb(es, "qT", [128, 4, 512], BF16)
            kT = sb(es, "kT", [128, 4, 512], BF16)
            vT = None
            zs = sb(es, "zs", [128, 4, 512], BF16)
            sgo = sb(es, "sgo", [128, 4, 512], BF16)
            Dg = sb(es, "Dg", [128, 16, 128], BF16)
            for hj in range(16):
                TS("dve", Dg[:, hj, :], ident, part[:, 16 + hj:16 + hj + 1], None, ALU.mult, None, r=["cst", "par"], w=["Dg"])
            Gtok = sb(es, "Gtok", [128, 32, 8], F32)
            CONVW, CONVB, MG, SK = 16, 32, 36, 40
            rot = [0]

            def nextbank():
                rot[0] ^= 1
                return rot[0]

            def frontA(blk):
                xm = xmb[blk % 2]
                kxm = ("xm", blk % 2)
                cols = slice(blk * 512, (blk + 1) * 512)
                if blk == 0:
                    MSET("pool", xm[:, :, 0:3], 0.0, w=[kxm])
                else:
                    CP("pool", xm[:, :, 0:3], xmb[(blk - 1) % 2][:, :, 512:515],
                       r=[("xm", (blk - 1) % 2)] + [(("xm", (blk - 1) % 2), hh) for hh in range(4)], w=[kxm])
                for h in range(4):
                    b = nextbank()
                    for c in range(8):
                        MM(ps[b][:], wm_bf[:, c, h * 128:(h + 1) * 128], hT[:, c, cols], c == 0, c == 7,
                           r=[("wm", c), ("hT", blk)], w=[P(b)])
                    CP("act", xm[:, h, 3:515], ps[b][:], r=[P(b)], w=[(kxm, h)])
                return xm, kxm

            def frontBC(blk, need_v, do_qk=True):
                xm = xmb[blk % 2]
                kxm = ("xm", blk % 2)
                for h in range(4):
                    b = nextbank()
                    for j in range(4):
                        MM(ps[b][:], Dg[:, 4 * h + j, :], xm[:, h, j:j + 512], j == 0, j == 3,
                           r=["Dg", (kxm, h), kxm], w=[P(b)])
                    ACT(xcv[:, h, :], ps[b][:], AF.Silu, r=[P(b), "par"], w=[("xcv", h)],
                        bias=part[:, CONVB + h:CONVB + h + 1])
                for h in range(4 if do_qk else 0):
                    for j, dst, kd in ((0, qT, "qT"), (1, kT, "kT")) + (((2, vT, "vT"),) if need_v else ()):
                        b = nextbank()
                        src = xcv[:, h, :] if j < 2 else xm[:, h, 3:515]
                        MM(ps[b][:], wqkv[:, j, h, :], src, True, True, r=["wqkv", ("xcv", h) if j < 2 else (kxm, h)], w=[P(b)])
                        CP("act" if j == 0 else "dve", dst[:, h, :], ps[b][:], r=[P(b)], w=[kd])
                return xm, kxm

            for blk in range(NB):
                if blk == 0:
                    frontA(0)
                frontBC(blk, False, do_qk=False)
                if blk + 1 < NB:
                    frontA(blk + 1)
                xm_ = xmb[blk % 2]
                kxm_ = ("xm", blk % 2)
                for tt in range(4):
                    tc_ = slice(tt * 128, (tt + 1) * 128)
                    for h in range(4):
                        MM(ps[2][:, tt * 8:(tt + 1) * 8], xcv[:, h, tc_], ABf[:, 0, h, :], h == 0, False,
                           r=[("xcv", h), "ABf"], w=[P(2)])
                        MM(ps[2][:, tt * 8:(tt + 1) * 8], xm_[:, h, 3 + tt * 128:3 + (tt + 1) * 128], ABf[:, 1, h, :], False, h == 3,
                           r=[(kxm_, h), "ABf"], w=[P(2)])
                CP("dve", Gtok[:, blk * 4:(blk + 1) * 4, :], ps[2][:, 0:32].rearrange("p (t g) -> p t g", g=8),
                   r=[P(2)], w=["Gtok"])
            if dbg and dbg >= 1:
                dump(es, "Gtok", Gtok[:], [128, 32, 8])
                dump(es, "qT", qT[:], [128, 4, 512], BF16)
                dump(es, "xcv", xcv[:], [128, 4, 512], BF16)

            rows = sb(es, "rows", [128, 8, 128], F32)
            GIr, GFr, LF, Fp, U_, W_, W2_, TH_ = [rows[:, i, :] for i in range(8)]
            cols_ = sb(es, "colsb", [128, 16], F32)
            col = lambda i: cols_[:, i:i + 1]
            rowt = sb(es, "rowt", [1, 4, 128], F32)
            Ebc = sb(es, "Ebc", [128, 128], F32)
            Wt = sb(es, "Wt", [128, 3, 128], F32)
            BI, BFc = 48, 49
            Gv = Gtok[:].rearrange("p c g -> p (c g)")
            gsp = sb(es, "gsp", [128, 2, 128], F32)
            CP("dve", gsp[:, 0, :].rearrange("p (c h) -> p c h", h=4), Gtok[:, :, 0:4], r=["Gtok"], w=["gsp"])
            CP("dve", gsp[:, 1, :].rearrange("p (c h) -> p c h", h=4), Gtok[:, :, 4:8], r=["Gtok"], w=["gsp"])
            TR(ps[3][:, 0:128], gsp[:, 0, :], r=["gsp"], w=[P(3)])
            TR(ps[3][:, 128:256], gsp[:, 1, :], r=["gsp"], w=[P(3)])
            CP("dve", rows[:, 0:2, :], ps[3][:, 0:256].rearrange("p (a t) -> p a t", t=128), r=[P(3)], w=["rows"])
            TS("dve", col(0), part[:, BFc:BFc + 1], -1.0, None, ALU.mult, None, r=["par"], w=["cols"])
            ACT(LF, GFr, AF.Exp, r=["rows", "cols"], w=["rows"], bias=col(0), scale=-1.0)
            ACT(LF, LF, AF.Ln, r=["rows"], w=["rows"], bias=1.0)
            SCAN(Fp, LF, 0.0, ALU.add, r=["rows"], w=["rows"])
            MM(ps[3][:, 256:257], M1, rows[:, 3, 127:128], True, True, r=["cst", "rows"], w=[P(3)])
            CP("dve", col(1), ps[3][:, 256:257], r=[P(3)], w=["cols"])
            TS("dve", Fp, Fp, col(1), None, ALU.add, None, r=["rows", "cols"], w=["rows"])
            STT(GIr, GIr, part[:, BI:BI + 1], Fp, ALU.add, ALU.add, r=["rows", "par"], w=["rows"])
            SCAN(U_, GIr, 0.0, ALU.max, r=["rows"], w=["rows"])
            MM(ps[3][0:1, 384:512], rows[:, 4, 127:128], ident, True, True, r=["rows", "cst"], w=[P(3)])
            CP("dve", rowt[:, 0, :], ps[3][0:1, 384:512], r=[P(3)], w=["rowt"])
            for h in range(4):
                SCAN(rowt[:, 1, :].rearrange("p (c h) -> p h c", h=4)[:, h, :],
                     rowt[:, 0, :].rearrange("p (c h) -> p h c", h=4)[:, h, :], 0.0, ALU.max, r=["rowt"], w=["rowt"])
            MSET("dve", rowt[:, 2, 0:4], 0.0, w=["rowt"])
            CP("dve", rowt[:, 2, 4:128], rowt[:, 1, 0:124], r=["rowt"], w=["rowt"])
            TT("dve", rowt[:, 3, :], rowt[:, 2, :], rowt[:, 1, :], ALU.subtract, r=["rowt"], w=["rowt"])
            MM(ps[3][:, 257:258], rowt[:, 2, :], ones_f[0:1, 0:1], True, True, r=["rowt", "cst"], w=[P(3)])
            MM(ps[3][:, 258:259], rowt[:, 3, :], ones_f[0:1, 0:1], True, True, r=["rowt", "cst"], w=[P(3)])
            CP("dve", cols_[:, 2:4], ps[3][:, 257:259], r=[P(3)], w=["cols"])
            MM(ps[4][:, 0:128], ones_f[0:1, :], rowt[:, 3, :], True, True, r=["rowt", "cst"], w=[P(4)])
            ACT(Ebc[:], ps[4][:, 0:128], AF.Exp, r=[P(4)], w=["Ebc"])
            TS("dve", col(4), col(2), -1.0, -LN_SQRT_DH, ALU.mult, ALU.add, r=["cols"], w=["cols"])
            TS("dve", col(5), col(2), -1.0, None, ALU.mult, None, r=["cols"], w=["cols"])
            TT("dve", col(6), col(4), col(3), ALU.add, r=["cols"], w=["cols"])
            ACT(W_, GIr, AF.Exp, r=["rows", "cols"], w=["rows"], bias=col(4))
            ACT(W2_, GIr, AF.Exp, r=["rows", "cols"], w=["rows"], bias=col(6))
            ACT(TH_, Fp, AF.Exp, r=["rows", "cols"], w=["rows"], bias=col(5))
            for i in range(3):
                TR(ps[4][:, 128 * (i + 1):128 * (i + 2)], rows[:, 5 + i, :], r=["rows"], w=[P(4)])
            CP("dve", Wt[:], ps[4][:, 128:512].rearrange("p (a t) -> p a t", t=128), r=[P(4)], w=["Wt"])
            if dbg and dbg >= 2:
                dump(es, "rows", rows[:], [128, 8, 128])
                dump(es, "Wt", Wt[:], [128, 3, 128])
                dump(es, "Ebc", Ebc[:], [128, 128])
                dump(es, "rowt", rowt[:], [1, 4, 128])

            kwb = [sb(es, "kw%d" % i, [128, 4, 128], BF16) for i in range(2)]
            V1b = [sb(es, "V1_%d" % i, [128, 4, 129], BF16) for i in range(2)]
            sqkb = [sb(es, "sqk%d" % i, [128, 4, 128], BF16) for i in range(2)]
            ABsb = [sb(es, "ABs%d" % i, [128, 4, 129], F32) for i in range(2)]
            lnbb = [sb(es, "lnb%d" % i, [128, 4, 128], F32) for i in range(2)]
            Cn = sb(es, "Cn", [128, 4, 129], F32)
            Cnb = sb(es, "Cnb", [128, 4, 129], BF16)
            hsb = [sb(es, "hs%d" % i, [128, 4, 128], F32) for i in range(2)]
            smb = [sb(es, "sm%d" % i, [128, 64], F32) for i in range(2)]
            ymt = sb(es, "ymt", [128, 4, 128], F32)
            mhalf = sb(es, "mhalf", [128, 4], F32)
            MSET("pool", mhalf[:], -0.5, w=["mhalf"])
            for i in range(2):
                MSET("pool", V1b[i][:, :, 128:129], 1.0, w=[("V1", i)])
            MSET("pool", Cn[:], 0.0, w=["Cn"])
            MSET("pool", Cnb[:], 0.0, w=["Cnb"])

            def pre(c_, xm, kxm):
                tt, p = c_ % 4, c_ % 2
                tc_ = slice(tt * 128, (tt + 1) * 128)
                ch4 = slice(c_ * 4, c_ * 4 + 4)
                for h in range(4):
                    MM(ps[2][:, h * 128:(h + 1) * 128], xcv[:, h, tc_], wqkv[:, 1, h, :], True, True,
                       r=[("xcv", h), "wqkv"], w=[P(2)])
                    MM(ps[3][:, h * 128:(h + 1) * 128], xm[:, h, 3 + tt * 128:3 + (tt + 1) * 128], wqkv[:, 2, h, :],
                       True, True, r=[(kxm, h), "wqkv"], w=[P(3)])
                    MM(ps[4][:, h * 128:(h + 1) * 128], kT[:, h, tc_], qT[:, h, tc_], True, True,
                       r=["kT", "qT"], w=[P(4)])
                for h in range(4):
                    ACT(kwb[p][:, h, :], ps[2][:, h * 128:(h + 1) * 128], AF.Copy, r=[P(2), "Wt"], w=[("kw", p)],
                        scale=Wt[:, 1, c_ * 4 + h:c_ * 4 + h + 1])
                CP("act", V1b[p][:, :, 0:128], ps[3][:].rearrange("p (h d) -> p h d", d=128), r=[P(3)], w=[("V1", p)])
                for h in range(4):
                    STT(sqkb[p][:, h, :], ps[4][:, h * 128:(h + 1) * 128], Wt[:, 0, c_ * 4 + h:c_ * 4 + h + 1], triu,
                        ALU.mult, ALU.mult, r=[P(4), "Wt", "cst"], w=[("sqk", p)])

            def rec(c_):
                tt, p = c_ % 4, c_ % 2
                tc_ = slice(tt * 128, (tt + 1) * 128)
                kw, V1, sqk = kwb[p], V1b[p], sqkb[p]
                for hp in range(2):
                    ab = ps[5 + hp]
                    for hh in range(2):
                        h = 2 * hp + hh
                        MM(ab[:, hh * 129:(hh + 1) * 129], sqk[:, h, :], V1[:, h, :], hh == 0, False,
                           r=[("sqk", p), ("V1", p)], w=[P(5 + hp)])
                        MM(ab[:, hh * 129:(hh + 1) * 129], qT[:, h, tc_], Cnb[:, h, :], False, True,
                           r=["qT", "Cnb"], w=[P(5 + hp)])
                    for hh in range(2):
                        h = 2 * hp + hh
                        MM(ps[7][:, hh * 129:(hh + 1) * 129], kw[:, h, :], V1[:, h, :], hh == 0, True,
                           r=[("kw", p), ("V1", p)], w=[P(7)])
                    for hh in range(2):
                        h = 2 * hp + hh
                        STT(Cn[:, h, :], Cn[:, h, :], Ebc[:, c_ * 4 + h:c_ * 4 + h + 1], ps[7][:, hh * 129:(hh + 1) * 129],
                            ALU.mult, ALU.add, r=["Cn", "Ebc", P(7)], w=["Cn"])
                    CP("dve", Cnb[:, 2 * hp:2 * hp + 2, :], Cn[:, 2 * hp:2 * hp + 2, :], r=["Cn"], w=["Cnb"])
                    CP("act", ABsb[p][:, 2 * hp:2 * hp + 2, :], ab[:, 0:258].rearrange("p (h d) -> p h d", d=129),
                       r=[P(5 + hp)], w=[("ABs", p)])

            def epi_a(c_):
                tt, p = c_ % 4, c_ % 2
                A = ABsb[p]
                sm_ = smb[p]
                hs_ = hsb[p]
                den = sm_[:, 0:4]
                kd, kmv, krs = ("den", p), ("mv", p), ("rstd4", p)
                STT(den, A[:, :, 128], -1.0, A[:, :, 128], ALU.mult, ALU.max, r=[("ABs", p)], w=[kd])
                TT("dve", den, den, Wt[:, 2, c_ * 4:c_ * 4 + 4], ALU.max, r=[kd, "Wt"], w=[kd])
                RCP(den, den, r=[kd], w=[kd])
                mvv = sm_[:, 48:56].rearrange("p (h t) -> p h t", t=2)
                for h in range(4):
                    STT(hs_[:, h, :], A[:, h, 0:128], sm_[:, h:h + 1], sgo[:, tt, h * 128:(h + 1) * 128], ALU.mult, ALU.mult,
                        r=[("ABs", p), kd, ("sgo", tt)], w=[("hs", p, h)])
                    st = sm_[:, 16 + 8 * h:16 + 8 * h + 6]
                    S.op("dve", (lambda o, i: (lambda e: e.bn_stats(out=o, in_=i)))(st, hs_[:, h, :]), r=[("hs", p, h)], w=[("st", p, h)])
                    S.op("dve", (lambda o, i: (lambda e: e.bn_aggr(out=o, in_=i)))(mvv[:, h, :], st), r=[("st", p, h)], w=[kmv])
                rstd4 = sm_[:, 56:60]
                nmr4 = sm_[:, 60:64]
                TS("pool", rstd4, mvv[:, :, 1], EPS, None, ALU.add, None, r=[kmv], w=[krs])
                TT("pool", rstd4, rstd4, mhalf[:], ALU.pow, r=[krs, "mhalf"], w=[krs])
                TS("pool", nmr4, mvv[:, :, 0], -1.0, None, ALU.mult, None, r=[kmv], w=[("nmr4", p)])
                TT("pool", nmr4, nmr4, rstd4, ALU.mult, r=[("nmr4", p), krs], w=[("nmr4", p)])

            def epi_b(c_):
                p = c_ % 2
                sm_ = smb[p]
                for h in range(4):
                    ACT(lnbb[p][:, h, :], hsb[p][:, h, :], AF.Identity, r=[("hs", p, h), ("rstd4", p), ("nmr4", p)], w=[("lnb", p)],
                        scale=sm_[:, 56 + h:57 + h], bias=sm_[:, 60 + h:61 + h])

            def outp(c_):
                tt, p = c_ % 4, c_ % 2
                blk_ = c_ // 4
                tc_ = slice(tt * 128, (tt + 1) * 128)
                b = nextbank()
                for h in range(4):
                    TR(ps[b][:, h * 128:(h + 1) * 128], lnbb[p][:, h, :], r=[("lnb", p)], w=[P(b)])
                for h in range(4):
                    STT(ymt[:, h, :], ps[b][:, h * 128:(h + 1) * 128], part[:, MG + h:MG + h + 1], xs[:, h, tc_],
                        ALU.mult, ALU.add, r=[P(b), "par", "xs"], w=[("ymt", h)])
                    TT("pool", ymT[:, h, blk_ * 512 + tt * 128:blk_ * 512 + (tt + 1) * 128], ymt[:, h, :], zs[:, h, tc_],
                       ALU.mult, r=[("ymt", h), "zs"], w=["ymT"])

            for blk in range(NB if stop_after > 1 else 1):
                if blk == 0:
                    nxt = frontA(0)
                xm, kxm = nxt
                frontBC(blk, False)
                cols = slice(blk * 512, (blk + 1) * 512)
                for h in range(4):
                    b = nextbank()
                    for c in range(8):
                        MM(ps[b][:], wm_bf[:, c, 1024 + h * 128:1024 + (h + 1) * 128], hT[:, c, cols], c == 0, c == 7,
                           r=[("wm", c), ("hT", blk)], w=[P(b)])
                    ACT(zs[:, h, :], ps[b][:], AF.Silu, r=[P(b)], w=["zs"])
                    ACT(xs[:, h, :], xcv[:, h, :], AF.Copy, scale=part[:, SK + h:SK + h + 1],
                       r=[("xcv", h), "par"], w=["xs"])
                for tt in range(4):
                    b = nextbank()
                    for c in range(8):
                        MM(ps[b][:], hT[:, c, blk * 512 + tt * 128:blk * 512 + (tt + 1) * 128], wm_bf[:, c, 512:1024],
                           c == 0, c == 7, r=[("wm", c), ("hT", blk)], w=[P(b)])
                    ACT(sgo[:, tt, :], ps[b][:], AF.Sigmoid, r=[P(b)], w=[("sgo", tt)])
                c0 = blk * 4
                pre(c0, xm, kxm)
                for tt in range(4):
                    c_ = c0 + tt
                    if tt < 3:
                        pre(c_ + 1, xm, kxm)
                    rec(c_)
                    if tt == 1 and blk + 1 < (NB if stop_after > 1 else 1):
                        nxt = frontA(blk + 1)
                    epi_a(c_)
                    if tt > 0:
                        epi_b(c_ - 1)
                        outp(c_ - 1)
                epi_b(c0 + 3)
                outp(c0 + 3)
            if dbg and dbg >= 3:
                dump(es, "ymT", ymT[:, :, 0:512], [128, 4, 512], BF16)
                dump(es, "hs", hsb[0][:], [128, 4, 128])
                dump(es, "Cn", Cn[:], [128, 4, 129])
            finish(blk1)
        wm_cm.__exit__(None, None, None)
        if stop_after <= 1:
            return nc

        rot2 = [0]

        def nb2():
            rot2[0] ^= 1
            return rot2[0]

        GROUPS = [(3 * g, 3 * g + 3) for g in range(10)] + [(30, 32)]
        DG = 44

        def proj_fm(wt, j, dst, blk, func, scale, kd, perblk=False):
            b = nb2()
            cols = slice(blk * 512, (blk + 1) * 512)
            wk_ = (kd, blk) if perblk else kd
            for c in range(8):
                MM(ps[b][:], wt[:, j, c, :], hT[:, c, cols], c == 0, c == 7, r=[kd + "_w", ("hT", blk)], w=[P(b)])
            if func is None:
                CP("dve", dst[:, cols], ps[b][:], r=[P(b)], w=[wk_])
            else:
                ACT(dst[:, cols], ps[b][:], func, r=[P(b)], w=[wk_], scale=scale)

        ydT = sb(top, "ydT", [128, 4, S_LEN], BF16)
        with nc.Block() as blk2, ExitStack() as es:
            lamt = sb(es, "lamt", [128, 8], F32)
            j64 = sb(es, "j64", [128, 64], F32)
            STT(j64[:], part[:, 64:128], 1.0, part[:, 128:192], ALU.mult, ALU.mult, r=["par"], w=["j64", "lam"], accum=lamt[:, 0:1])
            STT(j64[:], part[:, 192:256], 1.0, part[:, 256:320], ALU.mult, ALU.mult, r=["par", "j64"], w=["j64", "lam"], accum=lamt[:, 1:2])
            ACT(lamt[:, 2:4], lamt[:, 0:2], AF.Exp, r=["lam"], w=["lam"])
            TT("dve", lamt[:, 4:5], lamt[:, 2:3], lamt[:, 3:4], ALU.subtract, r=["lam"], w=["lam"])
            TS("dve", lamt[:, 5:6], lamt[:, 4:5], 0.2, -1.0, ALU.add, ALU.mult, r=["lam"], w=["lam"])
            wdb = [sb(es, "wd%d" % i, [128, 4, 8, 128], BF16) for i in range(2)]
            qdT = sb(es, "qdT", [128, S_LEN], BF16)
            kpd = [sb(es, "kpd%d" % i, [128, S_LEN], BF16) for i in range(2)]
            MSET("pool", kpd[0][64:128, :], 0.0, w=["kd"])
            MSET("pool", kpd[1][0:64, :], 0.0, w=["kd"])
            zds = sb(es, "zds", [128, S_LEN], BF16)
            V1d = sb(es, "V1d", [128, 32, 128], BF16)
            ptb = [sb(es, "pt%d" % i, [128, 512], BF16) for i in range(4)]
            o0s = sb(es, "o0s", [128, 512], F32)
            o1s = sb(es, "o1s", [128, 512], F32)
            l0s = sb(es, "l0s", [128, 512], F32)
            l1s = sb(es, "l1s", [128, 512], F32)
            sqb = sb(es, "sqb", [128, 512], BF16)
            rsd = l0s
            dg08 = sb(es, "dg08", [128, 4], F32)
            TS("dve", dg08[:], part[:, DG:DG + 4], 0.8, None, ALU.mult, None, r=["par"], w=["dg08"])
            GROUPS4 = [(4 * g, 4 * g + 4) for g in range(8)]

            def proj_k(wt, blk):
                b = nb2()
                cols = slice(blk * 512, (blk + 1) * 512)
                for c in range(8):
                    MM(ps[b][:], wt[:, 1, c, :], hT[:, c, cols], c == 0, c == 7, r=["kd_w", ("hT", blk)], w=[P(b)])
                CP("dve", kpd[0][0:64, cols], ps[b][0:64, :], r=[P(b)], w=["kd"])
                CP("dve", kpd[1][64:128, cols], ps[b][64:128, :], r=[P(b)], w=["kd"])
            for h in range(4):
                wd = wdb[h % 2]
                for j, off, kk in ((0, 1536, "qd_w"), (1, 2048, "kd_w"), (3, 3072, "zd_w"), (2, 2560, "vd_w")):
                    DMA("pool", wd[:, j, :, :], win_v[:, :, off + h * 128:off + (h + 1) * 128], r=[], w=[kk])
                for blk in range(NB):
                    proj_fm(wd, 0, qdT, blk, AF.Identity, 0.125, "qd")
                    proj_k(wd, blk)
                    proj_fm(wd, 3, zds, blk, AF.Silu, None, "zd")
                    for tt in range(4):
                        tl = blk * 4 + tt
                        for c in range(8):
                            MM(ps[3][:, tt * 128:(tt + 1) * 128], hT[:, c, tl * 128:(tl + 1) * 128], wd[:, 2, c, :], c == 0, c == 7,
                               r=["vd_w", ("hT", blk)], w=[P(3)])
                    CP("dve", V1d[:, blk * 4:(blk + 1) * 4, :], ps[3][:].rearrange("p (t d) -> p t d", d=128), r=[P(3)], w=["V1d"])
                steps = [(g, qs, qe, m, kb) for g, (qs, qe) in enumerate(GROUPS4) for m in range(2) for kb in range(qe)]
                LA = 3
                sctr = [0]

                def emit_qk(i):
                    g, qs, qe, m, kb = steps[i]
                    nsub = qe - qs
                    pr = slice(m * 64, (m + 1) * 64)
                    sv = max(0, kb - qs)
                    sbk = sctr[0] % 4
                    sctr[0] += 1
                    fc = slice(sv * 128, nsub * 128)
                    MM(ps[sbk][:, fc], kpd[m][:, kb * 128:(kb + 1) * 128], qdT[:, (qs + sv) * 128:qe * 128], True, True,
                       r=["kd", "qd"], w=[P(sbk)])
                    pt = ptb[i % 4]
                    kpt = ("pt", i % 4)
                    ACT(pt[:, fc], ps[sbk][:, fc], AF.Exp, r=[P(sbk)], w=[kpt])
                    if kb >= qs:
                        dc = slice((kb - qs) * 128, (kb - qs + 1) * 128)
                        TT("dve", pt[:, dc], pt[:, dc], mask_bf[:], ALU.mult, r=[kpt, "mask_bf"], w=[kpt])

                def emit_pv(i):
                    g, qs, qe, m, kb = steps[i]
                    nsub = qe - qs
                    sv = max(0, kb - qs)
                    fc = slice(sv * 128, nsub * 128)
                    pt = ptb[i % 4]
                    kpt = ("pt", i % 4)
                    MM(ps[4 + m][:, fc], V1d[:, kb, :], pt[:, fc], kb == 0, kb == qe - 1, r=[kpt, "V1d"], w=[P(4 + m)])
                    MM(ps[6 + m][:, fc], ones_bf[:], pt[:, fc], kb == 0, kb == qe - 1, r=[kpt, "ones_bf"], w=[P(6 + m)])
                    if m == 1 and kb == qe - 1:
                        epilogue1(qs, qe)
                        pending.append((i + 12, qs, qe))

                pending = []

                def epilogue1(qs, qe):
                    CP("act", o0s[:], ps[4][:], r=[P(4)], w=["o0s"])
                    CP("dve", l0s[:], ps[6][:], r=[P(6)], w=["l0s"])
                    CP("act", o1s[:], ps[5][:], r=[P(5)], w=["o1s"])
                    CP("dve", l1s[:], ps[7][:], r=[P(7)], w=["l1s"])
                    RCP(l0s[:], l0s[:], r=["l0s"], w=["l0s"])
                    RCP(l1s[:], l1s[:], r=["l1s"], w=["l1s"])
                    TT("dve", o0s[:], o0s[:], l0s[:], ALU.mult, r=["o0s", "l0s"], w=["o0s"])
                    TT("dve", o1s[:], o1s[:], l1s[:], ALU.mult, r=["o1s", "l1s"], w=["o1s"])
                    STT(o0s[:], o1s[:], lamt[:, 5:6], o0s[:], ALU.mult, ALU.add, r=["o0s", "o1s", "lam"], w=["o0s"])

                def epilogue2(qs, qe):
                    cols = slice(qs * 128, qe * 128)
                    ACT(sqb[:], o0s[:], AF.Square, r=["o0s"], w=["sqb"])
                    eb = sctr[0] % 4
                    sctr[0] += 1
                    MM(ps[eb][:], ones_bf[:], sqb[:], True, True, r=["sqb", "ones_bf"], w=[P(eb)])
                    ACT(rsd[:], ps[eb][:], AF.Sqrt, r=[P(eb)], w=["l0s"], bias=EPS, scale=1.0 / 128)
                    RCP(rsd[:], rsd[:], r=["l0s"], w=["l0s"])
                    STT(o0s[:], o0s[:], dg08[:, h:h + 1], rsd[:], ALU.mult, ALU.mult, r=["o0s", "dg08", "l0s"], w=["o0s"])
                    TT("dve", ydT[:, h, cols], o0s[:], zds[:, cols], ALU.mult, r=["o0s", "zd"], w=["ydT"])

                for i in range(len(steps) + LA):
                    if i < len(steps):
                        emit_qk(i)
                    if i >= LA:
                        emit_pv(i - LA)
                    while pending and pending[0][0] <= i - LA:
                        _, pqs, pqe = pending.pop(0)
                        epilogue2(pqs, pqe)
                while pending:
                    _, pqs, pqe = pending.pop(0)
                    epilogue2(pqs, pqe)
            if dbg and dbg >= 4:
                dump(es, "ydT", ydT[:, :, 0:1024], [128, 4, 1024], BF16)
                dump(es, "lamt", lamt[:], [128, 8])
            finish(blk2)
        if stop_after <= 2:
            return nc

        ycT = sb(top, "ycT", [128, 4, S_LEN], BF16)
        with nc.Block() as blk3, ExitStack() as es:
            wcb = [sb(es, "wc%d" % i, [128, 2, 8, 128], BF16) for i in range(2)]
            qcT = sb(es, "qcT", [128, S_LEN], BF16)
            zcs = sb(es, "zcs", [128, S_LEN], BF16)
            ptb = [sb(es, "ptc%d" % i, [128, 512], BF16) for i in range(4)]
            rlb = [sb(es, "rl%d" % i, [128, 512], F32) for i in range(2)]
            onb = [sb(es, "on%d" % i, [128, 512], F32) for i in range(2)]
            for h in range(4):
                wc = wcb[h % 2]
                for j, off, kk in ((0, 3584, "qc_w"), (1, 4096, "zc_w")):
                    DMA("pool", wc[:, j, :, :], win_v[:, :, off + h * 128:off + (h + 1) * 128], r=[], w=[kk])

                def c_qk(j):
                    cols = slice(j * 512, (j + 1) * 512)
                    for mc in range(2):
                        sbk = 2 * (j % 2) + mc
                        sb_ = 2 + mc
                        MM(ps[sb_][:], kmT[:, h, mc * 128:(mc + 1) * 128], qcT[:, cols], True, True, r=["kmT", ("qc", j)], w=[P(sb_)])
                        ACT(ptb[sbk][:], ps[sb_][:], AF.Exp, r=[P(sb_)], w=[("ptc", sbk)])

                def c_pv(j):
                    cols = slice(j * 512, (j + 1) * 512)
                    p = j % 2
                    for mc in range(2):
                        sbk = 2 * p + mc
                        MM(ps[4 + p][:], vm1[:, mc, h, 0:128], ptb[sbk][:], mc == 0, mc == 1, r=[("ptc", sbk), "vm1"], w=[P(4 + p)])
                    for mc in range(2):
                        sbk = 2 * p + mc
                        MM(ps[6 + p][:], ones_bf[:], ptb[sbk][:], mc == 0, mc == 1, r=[("ptc", sbk), "ones_bf"], w=[P(6 + p)])
                    RCP(rlb[p][:], ps[6 + p][:], r=[P(6 + p)], w=[("rl", p)])
                    TT("dve", onb[p][:], ps[4 + p][:], rlb[p][:], ALU.mult, r=[P(4 + p), ("rl", p)], w=[("on", p)])
                    TT("pool", ycT[:, h, cols], onb[p][:], zcs[:, cols], ALU.mult, r=[("on", p), ("zc", j)], w=["ycT"])

                for j in range(NB + 1):
                    if j < NB:
                        proj_fm(wc, 0, qcT, j, AF.Copy, None, "qc", perblk=True)
                        proj_fm(wc, 1, zcs, j, AF.Silu, None, "zc", perblk=True)
                        c_qk(j)
                    if j >= 1:
                        c_pv(j - 1)
            if dbg and dbg >= 5:
                dump(es, "ycT", ycT[:, :, 0:1024], [128, 4, 1024], BF16)
            finish(blk3)
        if stop_after <= 3:
            return nc

        with nc.Block() as blk4, ExitStack() as es:
            hflat = hT[:].rearrange("p c t -> p (c t)")
            wout_bf = hflat[:, 0:12 * D].rearrange("p (c f) -> p c f", f=D)
            wout_v = wout_d.rearrange("(c p) f -> p c f", p=128)
            for c in range(12):
                DMA("pool", wout_bf[:, c, :], wout_v[:, c, :], r=[], w=[("wout", c)])
            hf32 = hflat[:, 12 * D:].bitcast(F32)
            NXB = 6
            xtb = [hf32[:, i * D:(i + 1) * D] for i in range(NXB)]
            fgt = hf32[:, NXB * D:(NXB + 1) * D]
            DMA("sp", fgt, fg_d[:, :], r=[], w=["fg"])
            junk = sb(es, "junk4", [128, D], BF16)
            ss4 = sb(es, "ss4", [128, 2 * NXB], F32)
            ysrc = [(ymT, hh) for hh in range(4)] + [(ydT, hh) for hh in range(4)] + [(ycT, hh) for hh in range(4)]

            def load_x(i):
                DMA("sp", xtb[i % NXB], x_d[i * 128:(i + 1) * 128, :], r=[], w=[("xo", i % NXB)])

            PF = 4
            for i in range(PF):
                load_x(i)
            for i in range(NT):
                p2 = i % NXB
                xt = xtb[p2]
                kx = ("xo", p2)
                tcs = slice(i * 128, (i + 1) * 128)
                if i + PF < NT:
                    load_x(i + PF)
                for half in range(2):
                    b = 2 * (i % 4) + half
                    hc = slice(half * 512, (half + 1) * 512)
                    for ch, (src, hh) in enumerate(ysrc):
                        MM(ps[b][:], src[:, hh, tcs], wout_bf[:, ch, hc], ch == 0, ch == 11, r=[("wout", ch), "ycat"], w=[P(b)])
                    TT("dve", xt[:, hc], ps[b][:], xt[:, hc], ALU.add, r=[P(b), kx], w=[kx])
                ss = ss4[:, 2 * p2:2 * p2 + 1]
                rs = ss4[:, 2 * p2 + 1:2 * p2 + 2]
                ACT(junk[:], xt, AF.Square, r=[kx], w=["junk4", ("ss4", p2)], accum=ss)
                ACT(rs, ss, AF.Sqrt, r=[("ss4", p2)], w=[("ss4", p2)], bias=EPS, scale=1.0 / D)
                RCP(rs, rs, r=[("ss4", p2)], w=[("ss4", p2)])
                STT(xt, xt, rs, fgt, ALU.mult, ALU.mult, r=[kx, ("ss4", p2), "fg"], w=[kx])
                DMA("sp", out_d[i * 128:(i + 1) * 128, :], xt, r=[kx], w=[])
            finish(blk4)
    return nc


def make_consts():
    cst = np.zeros((128, 512), np.float32)
    cst[:, 0:128] = np.eye(128, dtype=np.float32)
    cst[:, 128:256] = np.triu(np.ones((128, 128), np.float32))
    p = np.arange(128)
    c, h = p // 4, p % 4
    cst[:, 256:384] = ((h[:, None] == h[None, :]) & (c[:, None] < c[None, :])).astype(np.float32)
    cst[:, 384:512] = 1.0
    return cst


def make_in_maps(inp):
    f = lambda a: np.ascontiguousarray(a, dtype=np.float32)
    cst = make_consts()
    par = np.zeros((128, 320), np.float32)
    par[:, 0:8] = inp["norm_g"][0].reshape(8, 128).T
    par[:, 8:16] = inp["mem_norm_g"][0].reshape(8, 128).T
    par[:, 16:32] = inp["conv_w"][0].reshape(4, 4, 128).transpose(2, 1, 0).reshape(128, 16)
    par[:, 32:36] = inp["conv_b"][0].reshape(4, 128).T
    par[:, 36:40] = inp["mnorm_g"][0].reshape(4, 128).T
    par[:, 40:44] = inp["skip_m"][0].reshape(4, 128).T
    par[:, 44:48] = inp["dnorm_g"][0].reshape(4, 128).T
    par[:, 48] = np.tile(inp["b_if"][0][0:4], 32)
    par[:, 49] = np.tile(inp["b_if"][0][4:8], 32)
    par[:, 64:128] = inp["lam_q1"][0][None, :]
    par[:, 128:192] = inp["lam_k1"][0][None, :]
    par[:, 192:256] = inp["lam_q2"][0][None, :]
    par[:, 256:320] = inp["lam_k2"][0][None, :]
    fg = np.ascontiguousarray(np.broadcast_to(inp["final_g"][None, :], (128, D)), dtype=np.float32)
    shared = {
        "w_in": f(inp["w_in"][0]), "w_kv": f(inp["w_mem_kv"][0]), "w_out": f(inp["w_out"][0]),
        "wq": f(inp["wq_m"][0]), "wk": f(inp["wk_m"][0]), "wv": f(inp["wv_m"][0]), "w_if": f(inp["w_if"][0]),
        "cst": cst, "par": par, "fg": fg,
    }
    maps = []
    for b in range(8):
        m = dict(shared)
        m["x"] = f(inp["x"][b])
        m["mem"] = f(inp["mem"][b])
        maps.append(m)
    return maps


_NC_CACHE = {}


def kernel(**inputs):
    if "nc" not in _NC_CACHE:
        _NC_CACHE["nc"] = build_nc()
    nc = _NC_CACHE["nc"]
    maps = make_in_maps(inputs)
    res = run_bass_kernel_spmd(nc, maps, core_ids=list(range(8)))
    return np.stack([np.asarray(r["out"], dtype=np.float32) for r in res.results], axis=0)
```

```python
import math
from contextlib import ExitStack

import numpy as np
import concourse.bass as bass
import concourse.mybir as mybir
from concourse.bass_utils import run_bass_kernel_spmd

F32 = mybir.dt.float32
BF16 = mybir.dt.bfloat16
AF = mybir.ActivationFunctionType
ALU = mybir.AluOpType

S_LEN = 4096
D = 1024
NT = 32
NB = 8
EPS = 1e-6
LN_SQRT_DH = 0.5 * math.log(128.0)


class Sched:
    ENG = ("pe", "act", "dve", "pool", "sp")

    def __init__(self, nc, sems, dsems):
        self.nc = nc
        self.sems = sems
        self.dsems = dsems
        self.q = {e: [] for e in self.ENG}
        self.cnt = {e: 0 for e in self.ENG}
        self.last_w = {}
        self.readers = {}
        self.seen = {e: {} for e in self.ENG}
        self.n_dma = len(dsems)
        self.dma_cnt = [0] * self.n_dma
        half = self.n_dma // 2
        self.dma_pool = {"sp": list(range(0, half)), "pool": list(range(half, self.n_dma))}
        self.dma_rr = {"sp": 0, "pool": 0}

    def _need(self, eng, tok, waits):
        src, val = tok
        if src == eng and eng in ("pe", "sp"):
            return
        if self.seen[eng].get(src, 0) >= val:
            return
        self.seen[eng][src] = val
        waits[src] = max(waits.get(src, 0), val)

    def op(self, eng, fn, r=(), w=(), dma=False):
        waits = {}
        for k in r:
            t = self.last_w.get(k)
            if t is not None:
                self._need(eng, t, waits)
        for k in w:
            t = self.last_w.get(k)
            if t is not None:
                self._need(eng, t, waits)
            for t in self.readers.get(k, ()):
                self._need(eng, t, waits)
        if dma:
            pool_ = self.dma_pool[eng]
            i = pool_[self.dma_rr[eng] % len(pool_)]
            self.dma_rr[eng] += 1
            if self.dma_cnt[i] > 0:
                self._need(eng, (("d", i), 16 * self.dma_cnt[i]), waits)
            self.dma_cnt[i] += 1
            tok = (("d", i), 16 * self.dma_cnt[i])
        else:
            self.cnt[eng] += 1
            tok = (eng, self.cnt[eng])
        self.q[eng].append((list(waits.items()), fn, tok))
        for k in w:
            self.last_w[k] = tok
            self.readers[k] = []
        for k in r:
            self.readers.setdefault(k, []).append(tok)
        return tok

    def barrier(self):
        toks = [(e, self.cnt[e]) for e in self.ENG if self.cnt[e] > 0]
        toks += [(("d", i), 16 * c) for i, c in enumerate(self.dma_cnt) if c > 0]
        for e in self.ENG:
            waits = {}
            for t in toks:
                if t[0] != e:
                    self._need(e, t, waits)
            if waits:
                self.q[e].append((list(waits.items()), None, None))

    def emit(self, block):
        sems, dsems = self.sems, self.dsems

        def run(e, engobj):
            for waits, fn, tok in self.q[e]:
                for src, val in waits:
                    s = dsems[src[1]] if isinstance(src, tuple) else sems[src]
                    engobj.wait_ge(s, val)
                if fn is None:
                    continue
                ins = fn(engobj)
                if isinstance(tok[0], tuple):
                    ins.then_inc(dsems[tok[0][1]], 16)
                else:
                    ins.then_inc(sems[tok[0]], 1)
            self.q[e] = []

        @block.tensor
        def _(eng):
            run("pe", eng)

        @block.scalar
        def _(eng):
            run("act", eng)

        @block.vector
        def _(eng):
            run("dve", eng)

        @block.gpsimd
        def _(eng):
            run("pool", eng)

        @block.sync
        def _(eng):
            run("sp", eng)


def build_nc(stop_after=99, dbg=None):
    nc = bass.Bass("TRN2", target_bir_lowering=False)
    din = lambda n, s: nc.dram_tensor(n, s, F32, kind="ExternalInput").ap()
    x_d = din("x", [S_LEN, D])
    mem_d = din("mem", [256, D])
    win_d = din("w_in", [D, 4608])
    wkv_d = din("w_kv", [D, 1024])
    wout_d = din("w_out", [1536, D])
    wq_d = din("wq", [4, 128, 128])
    wk_d = din("wk", [4, 128, 128])
    wv_d = din("wv", [4, 128, 128])
    wif_d = din("w_if", [1536, 8])
    cst_d = din("cst", [128, 512])
    par_d = din("par", [128, 320])
    fg_d = din("fg", [128, D])
    out_d = nc.dram_tensor("out", [S_LEN, D], F32, kind="ExternalOutput").ap()
    dbg_outs = []
    win_v = win_d.rearrange("(c p) f -> p c f", p=128)

    top = ExitStack()
    with top:
        sb = lambda es, name, shape, dt: es.enter_context(nc.sbuf_tensor(name, shape, dt))
        ps = [top.enter_context(nc.psum_tensor("ps%d" % i, [128, 512], F32)) for i in range(8)]
        sems = {e: top.enter_context(nc.semaphore("s_" + e)) for e in Sched.ENG}
        dsems = [top.enter_context(nc.semaphore("d%d" % i)) for i in range(32)]
        S = Sched(nc, sems, dsems)
        P = lambda i: ("ps", i)

        def MM(out, lhsT, rhs, start, stop, r, w):
            return S.op("pe", lambda e: e.matmul(out, lhsT=lhsT, rhs=rhs, start=start, stop=stop,
                                                  skip_group_check=True), r=r, w=w)

        def TR(out, in_, r, w):
            return S.op("pe", lambda e: e.transpose(out=out, in_=in_, identity=ident[:]), r=list(r) + ["cst"], w=w)

        def ACT(out, in_, func, r, w, bias=None, scale=None, accum=None, eng="act"):
            kw = {}
            if bias is not None:
                kw["bias"] = bias
            if scale is not None:
                kw["scale"] = scale
            if accum is not None:
                kw["accum_out"] = accum
            return S.op("act", lambda e: e.activation(out=out, in_=in_, func=func, **kw), r=r, w=w)

        def TS(eng, out, in0, s1, s2, op0, op1, r, w):
            if op1 is None:
                return S.op(eng, lambda e: e.tensor_scalar(out=out, in0=in0, scalar1=s1, scalar2=None, op0=op0), r=r, w=w)
            return S.op(eng, lambda e: e.tensor_scalar(out=out, in0=in0, scalar1=s1, scalar2=s2, op0=op0, op1=op1), r=r, w=w)

        def TT(eng, out, in0, in1, op, r, w):
            return S.op(eng, lambda e: e.tensor_tensor(out=out, in0=in0, in1=in1, op=op), r=r, w=w)

        def STT(out, in0, scalar, in1, op0, op1, r, w, accum=None):
            if accum is not None:
                return S.op("dve", lambda e: e.scalar_tensor_tensor(out=out, in0=in0, scalar=scalar, in1=in1, op0=op0, op1=op1, accum_out=accum), r=r, w=w)
            return S.op("dve", lambda e: e.scalar_tensor_tensor(out=out, in0=in0, scalar=scalar, in1=in1, op0=op0, op1=op1), r=r, w=w)

        def CP(eng, out, in_, r, w):
            if eng == "act":
                return S.op("act", lambda e: e.copy(out=out, in_=in_), r=r, w=w)
            return S.op(eng, lambda e: e.tensor_copy(out=out, in_=in_), r=r, w=w)

        def MSET(eng, ap, val, w):
            return S.op(eng, lambda e: e.memset(ap, val), w=w)

        def RCP(out, in_, r, w):
            return S.op("dve", lambda e: e.reciprocal(out=out, in_=in_), r=r, w=w)

        def SCAN(out, d0, init, op0, r, w):
            return S.op("dve", lambda e: e.tensor_tensor_scan(out=out, data0=d0, data1=d0, initial=init, op0=op0, op1=ALU.bypass), r=r, w=w)

        def DMA(eng, out, in_, r, w):
            return S.op(eng, lambda e: e.dma_start(out=out, in_=in_), r=r, w=w, dma=True)

        def dump(es, name, ap, shape, dt=F32):
            d = nc.dram_tensor("dbg_" + name, list(shape), dt, kind="ExternalOutput").ap()
            S.barrier()
            dbg_outs.append(DMA("sp", d, ap, r=[], w=[]))

        def finish(block):
            S.barrier()
            S.emit(block)

        cstt = sb(top, "cstt", [128, 512], F32)
        ident = cstt[:, 0:128]
        triu = cstt[:, 128:256]
        M1 = cstt[:, 256:384]
        ones_f = cstt[:, 384:512]
        part = sb(top, "part", [128, 320], F32)
        mask_bf = sb(top, "mask_bf", [128, 128], BF16)
        ones_bf = sb(top, "ones_bf", [128, 128], BF16)
        ident_bf = sb(top, "ident_bf", [128, 128], BF16)
        kmT = sb(top, "kmT", [128, 4, 256], BF16)
        vm1 = sb(top, "vm1", [128, 2, 4, 129], BF16)
        ymT = sb(top, "ymT", [128, 4, S_LEN], BF16)
        hT = sb(top, "hT", [128, 8, S_LEN], BF16)
        wm_cm = nc.sbuf_tensor("wm_bf", [128, 8, 1536], BF16)
        wm_bf = wm_cm.__enter__()

        def rms_tile_a(es_bufs, src_rows, i):
            xtb, junk, ssb, xnb = es_bufs
            xt = xtb[i % 4]
            ss = ssb[:, 2 * (i % 4):2 * (i % 4) + 1]
            rs = ssb[:, 2 * (i % 4) + 1:2 * (i % 4) + 2]
            kx, ks = ("xt", i % 4), ("ss", i % 4)
            DMA("sp", xt[:], src_rows, r=[], w=[kx])
            ACT(junk[:], xt[:], AF.Square, r=[kx], w=["junk", ks], accum=ss)
            ACT(rs, ss, AF.Sqrt, r=[ks], w=[ks], bias=EPS, scale=1.0 / D)
            RCP(rs, rs, r=[ks], w=[ks])

        def rms_tile_b(es_bufs, gcol, dstT, col0, i):
            xtb, junk, ssb, xnb = es_bufs
            xt = xtb[i % 4]
            xn = xnb[i % 3]
            rs = ssb[:, 2 * (i % 4) + 1:2 * (i % 4) + 2]
            kx, kn, ks = ("xt", i % 4), ("xn", i % 3), ("ss", i % 4)
            ACT(xn[:], xt[:], AF.Copy, r=[kx, ks], w=[kn], scale=rs)
            for half in range(2):
                b = 2 * (i % 4) + half
                psb = ps[b][:].bitcast(BF16)
                for j in range(4):
                    c = 4 * half + j
                    S.op("pe", (lambda o, a: (lambda e: e.transpose(out=o, in_=a, identity=ident_bf[:])))(
                        psb[:, j * 128:(j + 1) * 128], xn[:, c * 128:(c + 1) * 128]), r=[kn, "ident_bf"], w=[P(b)])
                TT("dve", dstT[:, 4 * half:4 * half + 4, col0:col0 + 128],
                   psb[:, 0:512].rearrange("p (j t) -> p j t", t=128),
                   part[:, gcol + 4 * half:gcol + 4 * half + 4].unsqueeze(2).to_broadcast([128, 4, 128]),
                   ALU.mult, r=[P(b), "par"], w=[("hT", col0 // 512) if dstT is hT else "memT"])

        with nc.Block() as blk0, ExitStack() as es:
            DMA("sp", cstt[:], cst_d[:, :], r=[], w=["cst"])
            DMA("sp", part[:], par_d[:, :], r=[], w=["par"])
            CP("dve", mask_bf[:], triu, r=["cst"], w=["mask_bf"])
            CP("dve", ones_bf[:], ones_f, r=["cst"], w=["ones_bf"])
            CP("dve", ident_bf[:], ident, r=["cst"], w=["ident_bf"])
            xtb = [sb(es, "xt%d" % i, [128, D], F32) for i in range(4)]
            xnb = [sb(es, "xn%d" % i, [128, D], BF16) for i in range(3)]
            junk = sb(es, "junk", [128, D], BF16)
            ssb = sb(es, "ssb", [128, 8], F32)
            bufs = (xtb, junk, ssb, xnb)
            memT = sb(es, "memT", [128, 8, 256], BF16)
            wkv_bf = sb(es, "wkv_bf", [128, 8, 1024], BF16)
            wkv_v = wkv_d.rearrange("(c p) f -> p c f", p=128)
            for c in range(8):
                DMA("pool", wkv_bf[:, c, :], wkv_v[:, c, :], r=[], w=["wkv"])
            for c in range(8):
                DMA("pool", wm_bf[:, c, :], win_v[:, c, 0:1536], r=[], w=[("wm", c)])
            seq = [(mem_d[i * 128:(i + 1) * 128, :], 8, memT, i * 128) for i in range(2)]
            seq += [(x_d[i * 128:(i + 1) * 128, :], 0, hT, i * 128) for i in range(NT)]
            rms_tile_a(bufs, seq[0][0], 0)
            for k, (src_rows, gcol, dstT, col0) in enumerate(seq):
                if k + 1 < len(seq):
                    rms_tile_a(bufs, seq[k + 1][0], k + 1)
                rms_tile_b(bufs, gcol, dstT, col0, k)
            for h in range(4):
                b = 4 + h % 2
                for c in range(8):
                    MM(ps[b][:, 0:256], wkv_bf[:, c, h * 128:(h + 1) * 128], memT[:, c, :], c == 0, c == 7,
                       r=["wkv", "memT"], w=[P(b)])
                ACT(kmT[:, h, :], ps[b][:, 0:256], AF.Identity, r=[P(b)], w=["kmT"], scale=128.0 ** -0.5)
            MSET("pool", vm1[:, :, :, 128:129], 1.0, w=["vm1"])
            for mt in range(2):
                b = 6 + mt
                for c in range(8):
                    MM(ps[b][:], memT[:, c, mt * 128:(mt + 1) * 128], wkv_bf[:, c, 512:1024], c == 0, c == 7,
                       r=["wkv", "memT"], w=[P(b)])
                CP("dve", vm1[:, mt, :, 0:128], ps[b][:].rearrange("p (h d) -> p h d", d=128), r=[P(b)], w=["vm1"])
            if dbg == 0:
                dump(es, "hT", hT[:, :, 0:512], [128, 8, 512], BF16)
                dump(es, "kmT", kmT[:], [128, 4, 256], BF16)
                dump(es, "vm1", vm1[:], [128, 2, 4, 129], BF16)
            finish(blk0)
        if stop_after == 0:
            return nc

        with nc.Block() as blk1, ExitStack() as es:
            wqkv = sb(es, "wqkv", [128, 3, 4, 128], BF16)
            for j, wd in enumerate((wq_d, wk_d, wv_d)):
                DMA("pool", wqkv[:, j, :, :], wd.rearrange("h d e -> d h e"), r=[], w=["wqkv"])
            wif_bf = sb(es, "wif_bf", [128, 12, 8], BF16)
            DMA("pool", wif_bf[:], wif_d.rearrange("(j p) g -> p j g", p=128), r=[], w=["wif"])
            ABf = sb(es, "ABf", [128, 2, 4, 8], BF16)
            wTs = [sb(es, "wTs%d" % i, [128, 128], BF16) for i in range(3)]
            psb4 = ps[4][:].bitcast(BF16)
            for h in range(4):
                for j in range(3):
                    S.op("pe", (lambda o, a: (lambda e: e.transpose(out=o, in_=a, identity=ident_bf[:])))(
                        psb4[:, j * 128:(j + 1) * 128], wqkv[:, j, h, :]), r=["wqkv", "ident_bf"], w=[P(4)])
                    CP("dve", wTs[j][:], psb4[:, j * 128:(j + 1) * 128], r=[P(4)], w=[("wTs", j)])
                MM(ps[5][:, h * 8:(h + 1) * 8], wTs[0][:], wif_bf[:, h, :], True, False, r=[("wTs", 0), "wif"], w=[P(5)])
                MM(ps[5][:, h * 8:(h + 1) * 8], wTs[1][:], wif_bf[:, 4 + h, :], False, True, r=[("wTs", 1), "wif"], w=[P(5)])
                MM(ps[5][:, 32 + h * 8:32 + (h + 1) * 8], wTs[2][:], wif_bf[:, 8 + h, :], True, True, r=[("wTs", 2), "wif"], w=[P(5)])
            CP("dve", ABf[:].rearrange("p a h g -> p (a h g)"), ps[5][:, 0:64], r=[P(5)], w=["ABf"])
            xmb = [sb(es, "xm%d" % i, [128, 4, 515], BF16) for i in range(2)]
            xcv = sb(es, "xcv", [128, 4, 512], BF16)
            xs = sb(es, "xs", [128, 4, 512], BF16)
            qT = sb(es, "qT", [128, 4, 512], BF16)
            kT = sb(es, "kT", [128, 4, 512], BF16)
            vT = None
            zs = sb(es, "zs", [128, 4, 512], BF16)
            sgo = sb(es, "sgo", [128, 4, 512], BF16)
            Dg = sb(es, "Dg", [128, 16, 128], BF16)
            for hj in range(16):
                TS("dve", Dg[:, hj, :], ident, part[:, 16 + hj:16 + hj + 1], None, ALU.mult, None, r=["cst", "par"], w=["Dg"])
            Gtok = sb(es, "Gtok", [128, 32, 8], F32)
            CONVW, CONVB, MG, SK = 16, 32, 36, 40
            rot = [0]

            def nextbank():
                rot[0] ^= 1
                return rot[0]

            def frontA(blk):
                xm = xmb[blk % 2]
                kxm = ("xm", blk % 2)
                cols = slice(blk * 512, (blk + 1) * 512)
                if blk == 0:
                    MSET("pool", xm[:, :, 0:3], 0.0, w=[kxm])
                else:
                    CP("pool", xm[:, :, 0:3], xmb[(blk - 1) % 2][:, :, 512:515],
                       r=[("xm", (blk - 1) % 2)] + [(("xm", (blk - 1) % 2), hh) for hh in range(4)], w=[kxm])
                for h in range(4):
                    b = nextbank()
                    for c in range(8):
                        MM(ps[b][:], wm_bf[:, c, h * 128:(h + 1) * 128], hT[:, c, cols], c == 0, c == 7,
                           r=[("wm", c), ("hT", blk)], w=[P(b)])
                    CP("act", xm[:, h, 3:515], ps[b][:], r=[P(b)], w=[(kxm, h)])
                return xm, kxm

            def frontBC(blk, need_v, do_qk=True):
                xm = xmb[blk % 2]
                kxm = ("xm", blk % 2)
                for h in range(4):
                    b = nextbank()
                    for j in range(4):
                        MM(ps[b][:], Dg[:, 4 * h + j, :], xm[:, h, j:j + 512], j == 0, j == 3,
                           r=["Dg", (kxm, h), kxm], w=[P(b)])
                    ACT(xcv[:, h, :], ps[b][:], AF.Silu, r=[P(b), "par"], w=[("xcv", h)],
                        bias=part[:, CONVB + h:CONVB + h + 1])
                for h in range(4 if do_qk else 0):
                    for j, dst, kd in ((0, qT, "qT"), (1, kT, "kT")) + (((2, vT, "vT"),) if need_v else ()):
                        b = nextbank()
                        src = xcv[:, h, :] if j < 2 else xm[:, h, 3:515]
                        MM(ps[b][:], wqkv[:, j, h, :], src, True, True, r=["wqkv", ("xcv", h) if j < 2 else (kxm, h)], w=[P(b)])
                        CP("act" if j == 0 else "dve", dst[:, h, :], ps[b][:], r=[P(b)], w=[kd])
                return xm, kxm

            for blk in range(NB):
                if blk == 0:
                    frontA(0)
                frontBC(blk, False, do_qk=False)
                if blk + 1 < NB:
                    frontA(blk + 1)
                xm_ = xmb[blk % 2]
                kxm_ = ("xm", blk % 2)
                for tt in range(4):
                    tc_ = slice(tt * 128, (tt + 1) * 128)
                    for h in range(4):
                        MM(ps[2][:, tt * 8:(tt + 1) * 8], xcv[:, h, tc_], ABf[:, 0, h, :], h == 0, False,
                           r=[("xcv", h), "ABf"], w=[P(2)])
                        MM(ps[2][:, tt * 8:(tt + 1) * 8], xm_[:, h, 3 + tt * 128:3 + (tt + 1) * 128], ABf[:, 1, h, :], False, h == 3,
                           r=[(kxm_, h), "ABf"], w=[P(2)])
                CP("dve", Gtok[:, blk * 4:(blk + 1) * 4, :], ps[2][:, 0:32].rearrange("p (t g) -> p t g", g=8),
                   r=[P(2)], w=["Gtok"])
            if dbg and dbg >= 1:
                dump(es, "Gtok", Gtok[:], [128, 32, 8])
                dump(es, "qT", qT[:], [128, 4, 512], BF16)
                dump(es, "xcv", xcv[:], [128, 4, 512], BF16)

            rows = sb(es, "rows", [128, 8, 128], F32)
            GIr, GFr, LF, Fp, U_, W_, W2_, TH_ = [rows[:, i, :] for i in range(8)]
            cols_ = sb(es, "colsb", [128, 16], F32)
            col = lambda i: cols_[:, i:i + 1]
            rowt = sb(es, "rowt", [1, 4, 128], F32)
            Ebc = sb(es, "Ebc", [128, 128], F32)
            Wt = sb(es, "Wt", [128, 3, 128], F32)
            BI, BFc = 48, 49
            Gv = Gtok[:].rearrange("p c g -> p (c g)")
            gsp = sb(es, "gsp", [128, 2, 128], F32)
            CP("dve", gsp[:, 0, :].rearrange("p (c h) -> p c h", h=4), Gtok[:, :, 0:4], r=["Gtok"], w=["gsp"])
            CP("dve", gsp[:, 1, :].rearrange("p (c h) -> p c h", h=4), Gtok[:, :, 4:8], r=["Gtok"], w=["gsp"])
            TR(ps[3][:, 0:128], gsp[:, 0, :], r=["gsp"], w=[P(3)])
            TR(ps[3][:, 128:256], gsp[:, 1, :], r=["gsp"], w=[P(3)])
            CP("dve", rows[:, 0:2, :], ps[3][:, 0:256].rearrange("p (a t) -> p a t", t=128), r=[P(3)], w=["rows"])
            TS("dve", col(0), part[:, BFc:BFc + 1], -1.0, None, ALU.mult, None, r=["par"], w=["cols"])
            ACT(LF, GFr, AF.Exp, r=["rows", "cols"], w=["rows"], bias=col(0), scale=-1.0)
            ACT(LF, LF, AF.Ln, r=["rows"], w=["rows"], bias=1.0)
            SCAN(Fp, LF, 0.0, ALU.add, r=["rows"], w=["rows"])
            MM(ps[3][:, 256:257], M1, rows[:, 3, 127:128], True, True, r=["cst", "rows"], w=[P(3)])
            CP("dve", col(1), ps[3][:, 256:257], r=[P(3)], w=["cols"])
            TS("dve", Fp, Fp, col(1), None, ALU.add, None, r=["rows", "cols"], w=["rows"])
            STT(GIr, GIr, part[:, BI:BI + 1], Fp, ALU.add, ALU.add, r=["rows", "par"], w=["rows"])
            SCAN(U_, GIr, 0.0, ALU.max, r=["rows"], w=["rows"])
            MM(ps[3][0:1, 384:512], rows[:, 4, 127:128], ident, True, True, r=["rows", "cst"], w=[P(3)])
            CP("dve", rowt[:, 0, :], ps[3][0:1, 384:512], r=[P(3)], w=["rowt"])
            for h in range(4):
                SCAN(rowt[:, 1, :].rearrange("p (c h) -> p h c", h=4)[:, h, :],
                     rowt[:, 0, :].rearrange("p (c h) -> p h c", h=4)[:, h, :], 0.0, ALU.max, r=["rowt"], w=["rowt"])
            MSET("dve", rowt[:, 2, 0:4], 0.0, w=["rowt"])
            CP("dve", rowt[:, 2, 4:128], rowt[:, 1, 0:124], r=["rowt"], w=["rowt"])
            TT("dve", rowt[:, 3, :], rowt[:, 2, :], rowt[:, 1, :], ALU.subtract, r=["rowt"], w=["rowt"])
            MM(ps[3][:, 257:258], rowt[:, 2, :], ones_f[0:1, 0:1], True, True, r=["rowt", "cst"], w=[P(3)])
            MM(ps[3][:, 258:259], rowt[:, 3, :], ones_f[0:1, 0:1], True, True, r=["rowt", "cst"], w=[P(3)])
            CP("dve", cols_[:, 2:4], ps[3][:, 257:259], r=[P(3)], w=["cols"])
            MM(ps[4][:, 0:128], ones_f[0:1, :], rowt[:, 3, :], True, True, r=["rowt", "cst"], w=[P(4)])
            ACT(Ebc[:], ps[4][:, 0:128], AF.Exp, r=[P(4)], w=["Ebc"])
            TS("dve", col(4), col(2), -1.0, -LN_SQRT_DH, ALU.mult, ALU.add, r=["cols"], w=["cols"])
            TS("dve", col(5), col(2), -1.0, None, ALU.mult, None, r=["cols"], w=["cols"])
            TT("dve", col(6), col(4), col(3), ALU.add, r=["cols"], w=["cols"])
            ACT(W_, GIr, AF.Exp, r=["rows", "cols"], w=["rows"], bias=col(4))
            ACT(W2_, GIr, AF.Exp, r=["rows", "cols"], w=["rows"], bias=col(6))
            ACT(TH_, Fp, AF.Exp, r=["rows", "cols"], w=["rows"], bias=col(5))
            for i in range(3):
                TR(ps[4][:, 128 * (i + 1):128 * (i + 2)], rows[:, 5 + i, :], r=["rows"], w=[P(4)])
            CP("dve", Wt[:], ps[4][:, 128:512].rearrange("p (a t) -> p a t", t=128), r=[P(4)], w=["Wt"])
            if dbg and dbg >= 2:
                dump(es, "rows", rows[:], [128, 8, 128])
                dump(es, "Wt", Wt[:], [128, 3, 128])
                dump(es, "Ebc", Ebc[:], [128, 128])
                dump(es, "rowt", rowt[:], [1, 4, 128])

            kwb = [sb(es, "kw%d" % i, [128, 4, 128], BF16) for i in range(2)]
            V1b = [sb(es, "V1_%d" % i, [128, 4, 129], BF16) for i in range(2)]
            sqkb = [sb(es, "sqk%d" % i, [128, 4, 128], BF16) for i in range(2)]
            ABsb = [sb(es, "ABs%d" % i, [128, 4, 129], F32) for i in range(2)]
            lnbb = [sb(es, "lnb%d" % i, [128, 4, 128], F32) for i in range(2)]
            Cn = sb(es, "Cn", [128, 4, 129], F32)
            Cnb = sb(es, "Cnb", [128, 4, 129], BF16)
            hsb = [sb(es, "hs%d" % i, [128, 4, 128], F32) for i in range(2)]
            smb = [sb(es, "sm%d" % i, [128, 64], F32) for i in range(2)]
            ymt = sb(es, "ymt", [128, 4, 128], F32)
            mhalf = sb(es, "mhalf", [128, 4], F32)
            MSET("pool", mhalf[:], -0.5, w=["mhalf"])
            for i in range(2):
                MSET("pool", V1b[i][:, :, 128:129], 1.0, w=[("V1", i)])
            MSET("pool", Cn[:], 0.0, w=["Cn"])
            MSET("pool", Cnb[:], 0.0, w=["Cnb"])

            def pre(c_, xm, kxm):
                tt, p = c_ % 4, c_ % 2
                tc_ = slice(tt * 128, (tt + 1) * 128)
                ch4 = slice(c_ * 4, c_ * 4 + 4)
                for h in range(4):
                    MM(ps[2][:, h * 128:(h + 1) * 128], xcv[:, h, tc_], wqkv[:, 1, h, :], True, True,
                       r=[("xcv", h), "wqkv"], w=[P(2)])
                    MM(ps[3][:, h * 128:(h + 1) * 128], xm[:, h, 3 + tt * 128:3 + (tt + 1) * 128], wqkv[:, 2, h, :],
                       True, True, r=[(kxm, h), "wqkv"], w=[P(3)])
                    MM(ps[4][:, h * 128:(h + 1) * 128], kT[:, h, tc_], qT[:, h, tc_], True, True,
                       r=["kT", "qT"], w=[P(4)])
                for h in range(4):
                    ACT(kwb[p][:, h, :], ps[2][:, h * 128:(h + 1) * 128], AF.Copy, r=[P(2), "Wt"], w=[("kw", p)],
                        scale=Wt[:, 1, c_ * 4 + h:c_ * 4 + h + 1])
                CP("act", V1b[p][:, :, 0:128], ps[3][:].rearrange("p (h d) -> p h d", d=128), r=[P(3)], w=[("V1", p)])
                for h in range(4):
                    STT(sqkb[p][:, h, :], ps[4][:, h * 128:(h + 1) * 128], Wt[:, 0, c_ * 4 + h:c_ * 4 + h + 1], triu,
                        ALU.mult, ALU.mult, r=[P(4), "Wt", "cst"], w=[("sqk", p)])

            def rec(c_):
                tt, p = c_ % 4, c_ % 2
                tc_ = slice(tt * 128, (tt + 1) * 128)
                kw, V1, sqk = kwb[p], V1b[p], sqkb[p]
                for hp in range(2):
                    ab = ps[5 + hp]
                    for hh in range(2):
                        h = 2 * hp + hh
                        MM(ab[:, hh * 129:(hh + 1) * 129], sqk[:, h, :], V1[:, h, :], hh == 0, False,
                           r=[("sqk", p), ("V1", p)], w=[P(5 + hp)])
                        MM(ab[:, hh * 129:(hh + 1) * 129], qT[:, h, tc_], Cnb[:, h, :], False, True,
                           r=["qT", "Cnb"], w=[P(5 + hp)])
                    for hh in range(2):
                        h = 2 * hp + hh
                        MM(ps[7][:, hh * 129:(hh + 1) * 129], kw[:, h, :], V1[:, h, :], hh == 0, True,
                           r=[("kw", p), ("V1", p)], w=[P(7)])
                    for hh in range(2):
                        h = 2 * hp + hh
                        STT(Cn[:, h, :], Cn[:, h, :], Ebc[:, c_ * 4 + h:c_ * 4 + h + 1], ps[7][:, hh * 129:(hh + 1) * 129],
                            ALU.mult, ALU.add, r=["Cn", "Ebc", P(7)], w=["Cn"])
                    CP("dve", Cnb[:, 2 * hp:2 * hp + 2, :], Cn[:, 2 * hp:2 * hp + 2, :], r=["Cn"], w=["Cnb"])
                    CP("act", ABsb[p][:, 2 * hp:2 * hp + 2, :], ab[:, 0:258].rearrange("p (h d) -> p h d", d=129),
                       r=[P(5 + hp)], w=[("ABs", p)])

            def epi_a(c_):
                tt, p = c_ % 4, c_ % 2
                A = ABsb[p]
                sm_ = smb[p]
                hs_ = hsb[p]
                den = sm_[:, 0:4]
                kd, kmv, krs = ("den", p), ("mv", p), ("rstd4", p)
                STT(den, A[:, :, 128], -1.0, A[:, :, 128], ALU.mult, ALU.max, r=[("ABs", p)], w=[kd])
                TT("dve", den, den, Wt[:, 2, c_ * 4:c_ * 4 + 4], ALU.max, r=[kd, "Wt"], w=[kd])
                RCP(den, den, r=[kd], w=[kd])
                mvv = sm_[:, 48:56].rearrange("p (h t) -> p h t", t=2)
                for h in range(4):
                    STT(hs_[:, h, :], A[:, h, 0:128], sm_[:, h:h + 1], sgo[:, tt, h * 128:(h + 1) * 128], ALU.mult, ALU.mult,
                        r=[("ABs", p), kd, ("sgo", tt)], w=[("hs", p, h)])
                    st = sm_[:, 16 + 8 * h:16 + 8 * h + 6]
                    S.op("dve", (lambda o, i: (lambda e: e.bn_stats(out=o, in_=i)))(st, hs_[:, h, :]), r=[("hs", p, h)], w=[("st", p, h)])
                    S.op("dve", (lambda o, i: (lambda e: e.bn_aggr(out=o, in_=i)))(mvv[:, h, :], st), r=[("st", p, h)], w=[kmv])
                rstd4 = sm_[:, 56:60]
                nmr4 = sm_[:, 60:64]
                TS("pool", rstd4, mvv[:, :, 1], EPS, None, ALU.add, None, r=[kmv], w=[krs])
                TT("pool", rstd4, rstd4, mhalf[:], ALU.pow, r=[krs, "mhalf"], w=[krs])
                TS("pool", nmr4, mvv[:, :, 0], -1.0, None, ALU.mult, None, r=[kmv], w=[("nmr4", p)])
                TT("pool", nmr4, nmr4, rstd4, ALU.mult, r=[("nmr4", p), krs], w=[("nmr4", p)])

            def epi_b(c_):
                p = c_ % 2
                sm_ = smb[p]
                for h in range(4):
                    ACT(lnbb[p][:, h, :], hsb[p][:, h, :], AF.Identity, r=[("hs", p, h), ("rstd4", p), ("nmr4", p)], w=[("lnb", p)],
                        scale=sm_[:, 56 + h:57 + h], bias=sm_[:, 60 + h:61 + h])

            def outp(c_):
                tt, p = c_ % 4, c_ % 2
                blk_ = c_ // 4
                tc_ = slice(tt * 128, (tt + 1) * 128)
                b = nextbank()
                for h in range(4):
                    TR(ps[b][:, h * 128:(h + 1) * 128], lnbb[p][:, h, :], r=[("lnb", p)], w=[P(b)])
                for h in range(4):
                    STT(ymt[:, h, :], ps[b][:, h * 128:(h + 1) * 128], part[:, MG + h:MG + h + 1], xs[:, h, tc_],
                        ALU.mult, ALU.add, r=[P(b), "par", "xs"], w=[("ymt", h)])
                    TT("pool", ymT[:, h, blk_ * 512 + tt * 128:blk_ * 512 + (tt + 1) * 128], ymt[:, h, :], zs[:, h, tc_],
                       ALU.mult, r=[("ymt", h), "zs"], w=["ymT"])

            for blk in range(NB if stop_after > 1 else 1):
                if blk == 0:
                    nxt = frontA(0)
                xm, kxm = nxt
                frontBC(blk, False)
                cols = slice(blk * 512, (blk + 1) * 512)
                for h in range(4):
                    b = nextbank()
                    for c in range(8):
                        MM(ps[b][:], wm_bf[:, c, 1024 + h * 128:1024 + (h + 1) * 128], hT[:, c, cols], c == 0, c == 7,
                           r=[("wm", c), ("hT", blk)], w=[P(b)])
                    ACT(zs[:, h, :], ps[b][:], AF.Silu, r=[P(b)], w=["zs"])
                    ACT(xs[:, h, :], xcv[:, h, :], AF.Copy, scale=part[:, SK + h:SK + h + 1],
                       r=[("xcv", h), "par"], w=["xs"])
                for tt in range(4):
                    b = nextbank()
                    for c in range(8):
                        MM(ps[b][:], hT[:, c, blk * 512 + tt * 128:blk * 512 + (tt + 1) * 128], wm_bf[:, c, 512:1024],
                           c == 0, c == 7, r=[("wm", c), ("hT", blk)], w=[P(b)])
                    ACT(sgo[:, tt, :], ps[b][:], AF.Sigmoid, r=[P(b)], w=[("sgo", tt)])
                c0 = blk * 4
                pre(c0, xm, kxm)
                for tt in range(4):
                    c_ = c0 + tt
                    if tt < 3:
                        pre(c_ + 1, xm, kxm)
                    rec(c_)
                    if tt == 1 and blk + 1 < (NB if stop_after > 1 else 1):
                        nxt = frontA(blk + 1)
                    epi_a(c_)
                    if tt > 0:
                        epi_b(c_ - 1)
                        outp(c_ - 1)
                epi_b(c0 + 3)
                outp(c0 + 3)
            if dbg and dbg >= 3:
                dump(es, "ymT", ymT[:, :, 0:512], [128, 4, 512], BF16)
                dump(es, "hs", hsb[0][:], [128, 4, 128])
                dump(es, "Cn", Cn[:], [128, 4, 129])
            finish(blk1)
        wm_cm.__exit__(None, None, None)
        if stop_after <= 1:
            return nc

        rot2 = [0]

        def nb2():
            rot2[0] ^= 1
            return rot2[0]

        GROUPS = [(3 * g, 3 * g + 3) for g in range(10)] + [(30, 32)]
        DG = 44

        def proj_fm(wt, j, dst, blk, func, scale, kd, perblk=False):
            b = nb2()
            cols = slice(blk * 512, (blk + 1) * 512)
            wk_ = (kd, blk) if perblk else kd
            for c in range(8):
                MM(ps[b][:], wt[:, j, c, :], hT[:, c, cols], c == 0, c == 7, r=[kd + "_w", ("hT", blk)], w=[P(b)])
            if func is None:
                CP("dve", dst[:, cols], ps[b][:], r=[P(b)], w=[wk_])
            else:
                ACT(dst[:, cols], ps[b][:], func, r=[P(b)], w=[wk_], scale=scale)

        ydT = sb(top, "ydT", [128, 4, S_LEN], BF16)
        with nc.Block() as blk2, ExitStack() as es:
            lamt = sb(es, "lamt", [128, 8], F32)
            j64 = sb(es, "j64", [128, 64], F32)
            STT(j64[:], part[:, 64:128], 1.0, part[:, 128:192], ALU.mult, ALU.mult, r=["par"], w=["j64", "lam"], accum=lamt[:, 0:1])
            STT(j64[:], part[:, 192:256], 1.0, part[:, 256:320], ALU.mult, ALU.mult, r=["par", "j64"], w=["j64", "lam"], accum=lamt[:, 1:2])
            ACT(lamt[:, 2:4], lamt[:, 0:2], AF.Exp, r=["lam"], w=["lam"])
            TT("dve", lamt[:, 4:5], lamt[:, 2:3], lamt[:, 3:4], ALU.subtract, r=["lam"], w=["lam"])
            TS("dve", lamt[:, 5:6], lamt[:, 4:5], 0.2, -1.0, ALU.add, ALU.mult, r=["lam"], w=["lam"])
            wdb = [sb(es, "wd%d" % i, [128, 4, 8, 128], BF16) for i in range(2)]
            qdT = sb(es, "qdT", [128, S_LEN], BF16)
            kpd = [sb(es, "kpd%d" % i, [128, S_LEN], BF16) for i in range(2)]
            MSET("pool", kpd[0][64:128, :], 0.0, w=["kd"])
            MSET("pool", kpd[1][0:64, :], 0.0, w=["kd"])
            zds = sb(es, "zds", [128, S_LEN], BF16)
            V1d = sb(es, "V1d", [128, 32, 128], BF16)
            ptb = [sb(es, "pt%d" % i, [128, 512], BF16) for i in range(4)]
            o0s = sb(es, "o0s", [128, 512], F32)
            o1s = sb(es, "o1s", [128, 512], F32)
            l0s = sb(es, "l0s", [128, 512], F32)
            l1s = sb(es, "l1s", [128, 512], F32)
            sqb = sb(es, "sqb", [128, 512], BF16)
            rsd = l0s
            dg08 = sb(es, "dg08", [128, 4], F32)
            TS("dve", dg08[:], part[:, DG:DG + 4], 0.8, None, ALU.mult, None, r=["par"], w=["dg08"])
            GROUPS4 = [(4 * g, 4 * g + 4) for g in range(8)]

            def proj_k(wt, blk):
                b = nb2()
                cols = slice(blk * 512, (blk + 1) * 512)
                for c in range(8):
                    MM(ps[b][:], wt[:, 1, c, :], hT[:, c, cols], c == 0, c == 7, r=["kd_w", ("hT", blk)], w=[P(b)])
                CP("dve", kpd[0][0:64, cols], ps[b][0:64, :], r=[P(b)], w=["kd"])
                CP("dve", kpd[1][64:128, cols], ps[b][64:128, :], r=[P(b)], w=["kd"])
            for h in range(4):
                wd = wdb[h % 2]
                for j, off, kk in ((0, 1536, "qd_w"), (1, 2048, "kd_w"), (3, 3072, "zd_w"), (2, 2560, "vd_w")):
                    DMA("pool", wd[:, j, :, :], win_v[:, :, off + h * 128:off + (h + 1) * 128], r=[], w=[kk])
                for blk in range(NB):
                    proj_fm(wd, 0, qdT, blk, AF.Identity, 0.125, "qd")
                    proj_k(wd, blk)
                    proj_fm(wd, 3, zds, blk, AF.Silu, None, "zd")
                    for tt in range(4):
                        tl = blk * 4 + tt
                        for c in range(8):
                            MM(ps[3][:, tt * 128:(tt + 1) * 128], hT[:, c, tl * 128:(tl + 1) * 128], wd[:, 2, c, :], c == 0, c == 7,
                               r=["vd_w", ("hT", blk)], w=[P(3)])
                    CP("dve", V1d[:, blk * 4:(blk + 1) * 4, :], ps[3][:].rearrange("p (t d) -> p t d", d=128), r=[P(3)], w=["V1d"])
                steps = [(g, qs, qe, m, kb) for g, (qs, qe) in enumerate(GROUPS4) for m in range(2) for kb in range(qe)]
                LA = 3
                sctr = [0]

                def emit_qk(i):
                    g, qs, qe, m, kb = steps[i]
                    nsub = qe - qs
                    pr = slice(m * 64, (m + 1) * 64)
                    sv = max(0, kb - qs)
                    sbk = sctr[0] % 4
                    sctr[0] += 1
                    fc = slice(sv * 128, nsub * 128)
                    MM(ps[sbk][:, fc], kpd[m][:, kb * 128:(kb + 1) * 128], qdT[:, (qs + sv) * 128:qe * 128], True, True,
                       r=["kd", "qd"], w=[P(sbk)])
                    pt = ptb[i % 4]
                    kpt = ("pt", i % 4)
                    ACT(pt[:, fc], ps[sbk][:, fc], AF.Exp, r=[P(sbk)], w=[kpt])
                    if kb >= qs:
                        dc = slice((kb - qs) * 128, (kb - qs + 1) * 128)
                        TT("dve", pt[:, dc], pt[:, dc], mask_bf[:], ALU.mult, r=[kpt, "mask_bf"], w=[kpt])

                def emit_pv(i):
                    g, qs, qe, m, kb = steps[i]
                    nsub = qe - qs
                    sv = max(0, kb - qs)
                    fc = slice(sv * 128, nsub * 128)
                    pt = ptb[i % 4]
                    kpt = ("pt", i % 4)
                    MM(ps[4 + m][:, fc], V1d[:, kb, :], pt[:, fc], kb == 0, kb == qe - 1, r=[kpt, "V1d"], w=[P(4 + m)])
                    MM(ps[6 + m][:, fc], ones_bf[:], pt[:, fc], kb == 0, kb == qe - 1, r=[kpt, "ones_bf"], w=[P(6 + m)])
                    if m == 1 and kb == qe - 1:
                        epilogue1(qs, qe)
                        pending.append((i + 12, qs, qe))

                pending = []

                def epilogue1(qs, qe):
                    CP("act", o0s[:], ps[4][:], r=[P(4)], w=["o0s"])
                    CP("dve", l0s[:], ps[6][:], r=[P(6)], w=["l0s"])
                    CP("act", o1s[:], ps[5][:], r=[P(5)], w=["o1s"])
                    CP("dve", l1s[:], ps[7][:], r=[P(7)], w=["l1s"])
                    RCP(l0s[:], l0s[:], r=["l0s"], w=["l0s"])
                    RCP(l1s[:], l1s[:], r=["l1s"], w=["l1s"])
                    TT("dve", o0s[:], o0s[:], l0s[:], ALU.mult, r=["o0s", "l0s"], w=["o0s"])
                    TT("dve", o1s[:], o1s[:], l1s[:], ALU.mult, r=["o1s", "l1s"], w=["o1s"])
                    STT(o0s[:], o1s[:], lamt[:, 5:6], o0s[:], ALU.mult, ALU.add, r=["o0s", "o1s", "lam"], w=["o0s"])

                def epilogue2(qs, qe):
                    cols = slice(qs * 128, qe * 128)
                    ACT(sqb[:], o0s[:], AF.Square, r=["o0s"], w=["sqb"])
                    eb = sctr[0] % 4
                    sctr[0] += 1
                    MM(ps[eb][:], ones_bf[:], sqb[:], True, True, r=["sqb", "ones_bf"], w=[P(eb)])
                    ACT(rsd[:], ps[eb][:], AF.Sqrt, r=[P(eb)], w=["l0s"], bias=EPS, scale=1.0 / 128)
                    RCP(rsd[:], rsd[:], r=["l0s"], w=["l0s"])
                    STT(o0s[:], o0s[:], dg08[:, h:h + 1], rsd[:], ALU.mult, ALU.mult, r=["o0s", "dg08", "l0s"], w=["o0s"])
                    TT("dve", ydT[:, h, cols], o0s[:], zds[:, cols], ALU.mult, r=["o0s", "zd"], w=["ydT"])

                for i in range(len(steps) + LA):
                    if i < len(steps):
                        emit_qk(i)
                    if i >= LA:
                        emit_pv(i - LA)
                    while pending and pending[0][0] <= i - LA:
                        _, pqs, pqe = pending.pop(0)
                        epilogue2(pqs, pqe)
                while pending:
                    _, pqs, pqe = pending.pop(0)
                    epilogue2(pqs, pqe)
            if dbg and dbg >= 4:
                dump(es, "ydT", ydT[:, :, 0:1024], [128, 4, 1024], BF16)
                dump(es, "lamt", lamt[:], [128, 8])
            finish(blk2)
        if stop_after <= 2:
            return nc

        ycT = sb(top, "ycT", [128, 4, S_LEN], BF16)
        with nc.Block() as blk3, ExitStack() as es:
            wcb = [sb(es, "wc%d" % i, [128, 2, 8, 128], BF16) for i in range(2)]
            qcT = sb(es, "qcT", [128, S_LEN], BF16)
            zcs = sb(es, "zcs", [128, S_LEN], BF16)
            ptb = [sb(es, "ptc%d" % i, [128, 512], BF16) for i in range(4)]
            rlb = [sb(es, "rl%d" % i, [128, 512], F32) for i in range(2)]
            onb = [sb(es, "on%d" % i, [128, 512], F32) for i in range(2)]
            for h in range(4):
                wc = wcb[h % 2]
                for j, off, kk in ((0, 3584, "qc_w"), (1, 4096, "zc_w")):
                    DMA("pool", wc[:, j, :, :], win_v[:, :, off + h * 128:off + (h + 1) * 128], r=[], w=[kk])

                def c_qk(j):
                    cols = slice(j * 512, (j + 1) * 512)
                    for mc in range(2):
                        sbk = 2 * (j % 2) + mc
                        sb_ = 2 + mc
                        MM(ps[sb_][:], kmT[:, h, mc * 128:(mc + 1) * 128], qcT[:, cols], True, True, r=["kmT", ("qc", j)], w=[P(sb_)])
                        ACT(ptb[sbk][:], ps[sb_][:], AF.Exp, r=[P(sb_)], w=[("ptc", sbk)])

                def c_pv(j):
                    cols = slice(j * 512, (j + 1) * 512)
                    p = j % 2
                    for mc in range(2):
                        sbk = 2 * p + mc
                        MM(ps[4 + p][:], vm1[:, mc, h, 0:128], ptb[sbk][:], mc == 0, mc == 1, r=[("ptc", sbk), "vm1"], w=[P(4 + p)])
                    for mc in range(2):
                        sbk = 2 * p + mc
                        MM(ps[6 + p][:], ones_bf[:], ptb[sbk][:], mc == 0, mc == 1, r=[("ptc", sbk), "ones_bf"], w=[P(6 + p)])
                    RCP(rlb[p][:], ps[6 + p][:], r=[P(6 + p)], w=[("rl", p)])
                    TT("dve", onb[p][:], ps[4 + p][:], rlb[p][:], ALU.mult, r=[P(4 + p), ("rl", p)], w=[("on", p)])
                    TT("pool", ycT[:, h, cols], onb[p][:], zcs[:, cols], ALU.mult, r=[("on", p), ("zc", j)], w=["ycT"])

                for j in range(NB + 1):
                    if j < NB:
                        proj_fm(wc, 0, qcT, j, AF.Copy, None, "qc", perblk=True)
                        proj_fm(wc, 1, zcs, j, AF.Silu, None, "zc", perblk=True)
                        c_qk(j)
                    if j >= 1:
                        c_pv(j - 1)
            if dbg and dbg >= 5:
                dump(es, "ycT", ycT[:, :, 0:1024], [128, 4, 1024], BF16)
            finish(blk3)
        if stop_after <= 3:
            return nc

        with nc.Block() as blk4, ExitStack() as es:
            hflat = hT[:].rearrange("p c t -> p (c t)")
            wout_bf = hflat[:, 0:12 * D].rearrange("p (c f) -> p c f", f=D)
            wout_v = wout_d.rearrange("(c p) f -> p c f", p=128)
            for c in range(12):
                DMA("pool", wout_bf[:, c, :], wout_v[:, c, :], r=[], w=[("wout", c)])
            hf32 = hflat[:, 12 * D:].bitcast(F32)
            NXB = 8
            xtb = [hf32[:, i * D:(i + 1) * D] for i in range(NXB)]
            fgt = hf32[:, NXB * D:(NXB + 1) * D]
            DMA("sp", fgt, fg_d[:, :], r=[], w=["fg"])
            junk = sb(es, "junk4", [128, D], BF16)
            ss4 = sb(es, "ss4", [128, 2 * NXB], F32)
            ysrc = [(ymT, hh) for hh in range(4)] + [(ydT, hh) for hh in range(4)] + [(ycT, hh) for hh in range(4)]

            def load_x(i):
                DMA("sp", xtb[i % NXB], x_d[i * 128:(i + 1) * 128, :], r=[], w=[("xo", i % NXB)])

            PF = 6
            for i in range(PF):
                load_x(i)
            for i in range(NT):
                p2 = i % NXB
                xt = xtb[p2]
                kx = ("xo", p2)
                tcs = slice(i * 128, (i + 1) * 128)
                if i + PF < NT:
                    load_x(i + PF)
                for half in range(2):
                    b = 2 * (i % 4) + half
                    hc = slice(half * 512, (half + 1) * 512)
                    for ch, (src, hh) in enumerate(ysrc):
                        MM(ps[b][:], src[:, hh, tcs], wout_bf[:, ch, hc], ch == 0, ch == 11, r=[("wout", ch), "ycat"], w=[P(b)])
                    TT("dve", xt[:, hc], ps[b][:], xt[:, hc], ALU.add, r=[P(b), kx], w=[kx])
                ss = ss4[:, 2 * p2:2 * p2 + 1]
                rs = ss4[:, 2 * p2 + 1:2 * p2 + 2]
                ACT(junk[:], xt, AF.Square, r=[kx], w=["junk4", ("ss4", p2)], accum=ss)
                ACT(rs, ss, AF.Sqrt, r=[("ss4", p2)], w=[("ss4", p2)], bias=EPS, scale=1.0 / D)
                RCP(rs, rs, r=[("ss4", p2)], w=[("ss4", p2)])
                STT(xt, xt, rs, fgt, ALU.mult, ALU.mult, r=[kx, ("ss4", p2), "fg"], w=[kx])
                DMA("sp", out_d[i * 128:(i + 1) * 128, :], xt, r=[kx], w=[])
            finish(blk4)
    return nc


def make_consts():
    cst = np.zeros((128, 512), np.float32)
    cst[:, 0:128] = np.eye(128, dtype=np.float32)
    cst[:, 128:256] = np.triu(np.ones((128, 128), np.float32))
    p = np.arange(128)
    c, h = p // 4, p % 4
    cst[:, 256:384] = ((h[:, None] == h[None, :]) & (c[:, None] < c[None, :])).astype(np.float32)
    cst[:, 384:512] = 1.0
    return cst


def make_in_maps(inp):
    f = lambda a: np.ascontiguousarray(a, dtype=np.float32)
    cst = make_consts()
    par = np.zeros((128, 320), np.float32)
    par[:, 0:8] = inp["norm_g"][0].reshape(8, 128).T
    par[:, 8:16] = inp["mem_norm_g"][0].reshape(8, 128).T
    par[:, 16:32] = inp["conv_w"][0].reshape(4, 4, 128).transpose(2, 1, 0).reshape(128, 16)
    par[:, 32:36] = inp["conv_b"][0].reshape(4, 128).T
    par[:, 36:40] = inp["mnorm_g"][0].reshape(4, 128).T
    par[:, 40:44] = inp["skip_m"][0].reshape(4, 128).T
    par[:, 44:48] = inp["dnorm_g"][0].reshape(4, 128).T
    par[:, 48] = np.tile(inp["b_if"][0][0:4], 32)
    par[:, 49] = np.tile(inp["b_if"][0][4:8], 32)
    par[:, 64:128] = inp["lam_q1"][0][None, :]
    par[:, 128:192] = inp["lam_k1"][0][None, :]
    par[:, 192:256] = inp["lam_q2"][0][None, :]
    par[:, 256:320] = inp["lam_k2"][0][None, :]
    fg = np.ascontiguousarray(np.broadcast_to(inp["final_g"][None, :], (128, D)), dtype=np.float32)
    shared = {
        "w_in": f(inp["w_in"][0]), "w_kv": f(inp["w_mem_kv"][0]), "w_out": f(inp["w_out"][0]),
        "wq": f(inp["wq_m"][0]), "wk": f(inp["wk_m"][0]), "wv": f(inp["wv_m"][0]), "w_if": f(inp["w_if"][0]),
        "cst": cst, "par": par, "fg": fg,
    }
    maps = []
    for b in range(8):
        m = dict(shared)
        m["x"] = f(inp["x"][b])
        m["mem"] = f(inp["mem"][b])
        maps.append(m)
    return maps


_NC_CACHE = {}


def kernel(**inputs):
    if "nc" not in _NC_CACHE:
        _NC_CACHE["nc"] = build_nc()
    nc = _NC_CACHE["nc"]
    maps = make_in_maps(inputs)
    res = run_bass_kernel_spmd(nc, maps, core_ids=list(range(8)))
    return np.stack([np.asarray(r["out"], dtype=np.float32) for r in res.results], axis=0)
```

```python
import math
from contextlib import ExitStack

import numpy as np
import concourse.bass as bass
import concourse.mybir as mybir
from concourse.bass_utils import run_bass_kernel_spmd

F32 = mybir.dt.float32
BF16 = mybir.dt.bfloat16
AF = mybir.ActivationFunctionType
ALU = mybir.AluOpType

S_LEN = 4096
D = 1024
NT = 32
NB = 8
EPS = 1e-6
LN_SQRT_DH = 0.5 * math.log(128.0)


class Sched:
    ENG = ("pe", "act", "dve", "pool", "sp")

    def __init__(self, nc, sems, dsems):
        self.nc = nc
        self.sems = sems
        self.dsems = dsems
        self.q = {e: [] for e in self.ENG}
        self.cnt = {e: 0 for e in self.ENG}
        self.last_w = {}
        self.readers = {}
        self.seen = {e: {} for e in self.ENG}
        self.n_dma = len(dsems)
        self.dma_cnt = [0] * self.n_dma
        half = self.n_dma // 2
        self.dma_pool = {"sp": list(range(0, half)), "pool": list(range(half, self.n_dma))}
        self.dma_rr = {"sp": 0, "pool": 0}

    def _need(self, eng, tok, waits):
        src, val = tok
        if src == eng and eng in ("pe", "sp"):
            return
        if self.seen[eng].get(src, 0) >= val:
            return
        self.seen[eng][src] = val
        waits[src] = max(waits.get(src, 0), val)

    def op(self, eng, fn, r=(), w=(), dma=False):
        waits = {}
        for k in r:
            t = self.last_w.get(k)
            if t is not None:
                self._need(eng, t, waits)
        for k in w:
            t = self.last_w.get(k)
            if t is not None:
                self._need(eng, t, waits)
            for t in self.readers.get(k, ()):
                self._need(eng, t, waits)
        if dma:
            pool_ = self.dma_pool[eng]
            i = pool_[self.dma_rr[eng] % len(pool_)]
            self.dma_rr[eng] += 1
            if self.dma_cnt[i] > 0:
                self._need(eng, (("d", i), 16 * self.dma_cnt[i]), waits)
            self.dma_cnt[i] += 1
            tok = (("d", i), 16 * self.dma_cnt[i])
        else:
            self.cnt[eng] += 1
            tok = (eng, self.cnt[eng])
        self.q[eng].append((list(waits.items()), fn, tok))
        for k in w:
            self.last_w[k] = tok
            self.readers[k] = []
        for k in r:
            self.readers.setdefault(k, []).append(tok)
        return tok

    def barrier(self):
        toks = [(e, self.cnt[e]) for e in self.ENG if self.cnt[e] > 0]
        toks += [(("d", i), 16 * c) for i, c in enumerate(self.dma_cnt) if c > 0]
        for e in self.ENG:
            waits = {}
            for t in toks:
                if t[0] != e:
                    self._need(e, t, waits)
            if waits:
                self.q[e].append((list(waits.items()), None, None))

    def emit(self, block):
        sems, dsems = self.sems, self.dsems

        def run(e, engobj):
            for waits, fn, tok in self.q[e]:
                for src, val in waits:
                    s = dsems[src[1]] if isinstance(src, tuple) else sems[src]
                    engobj.wait_ge(s, val)
                if fn is None:
                    continue
                ins = fn(engobj)
                if isinstance(tok[0], tuple):
                    ins.then_inc(dsems[tok[0][1]], 16)
                else:
                    ins.then_inc(sems[tok[0]], 1)
            self.q[e] = []

        @block.tensor
        def _(eng):
            run("pe", eng)

        @block.scalar
        def _(eng):
            run("act", eng)

        @block.vector
        def _(eng):
            run("dve", eng)

        @block.gpsimd
        def _(eng):
            run("pool", eng)

        @block.sync
        def _(eng):
            run("sp", eng)


def build_nc(stop_after=99, dbg=None):
    nc = bass.Bass("TRN2", target_bir_lowering=False)
    din = lambda n, s: nc.dram_tensor(n, s, F32, kind="ExternalInput").ap()
    x_d = din("x", [S_LEN, D])
    mem_d = din("mem", [256, D])
    win_d = din("w_in", [D, 4608])
    wkv_d = din("w_kv", [D, 1024])
    wout_d = din("w_out", [1536, D])
    wq_d = din("wq", [4, 128, 128])
    wk_d = din("wk", [4, 128, 128])
    wv_d = din("wv", [4, 128, 128])
    wif_d = din("w_if", [1536, 8])
    cst_d = din("cst", [128, 512])
    par_d = din("par", [128, 320])
    fg_d = din("fg", [128, D])
    out_d = nc.dram_tensor("out", [S_LEN, D], F32, kind="ExternalOutput").ap()
    dbg_outs = []
    win_v = win_d.rearrange("(c p) f -> p c f", p=128)

    top = ExitStack()
    with top:
        sb = lambda es, name, shape, dt: es.enter_context(nc.sbuf_tensor(name, shape, dt))
        ps = [top.enter_context(nc.psum_tensor("ps%d" % i, [128, 512], F32)) for i in range(8)]
        sems = {e: top.enter_context(nc.semaphore("s_" + e)) for e in Sched.ENG}
        dsems = [top.enter_context(nc.semaphore("d%d" % i)) for i in range(32)]
        S = Sched(nc, sems, dsems)
        P = lambda i: ("ps", i)

        def MM(out, lhsT, rhs, start, stop, r, w):
            return S.op("pe", lambda e: e.matmul(out, lhsT=lhsT, rhs=rhs, start=start, stop=stop,
                                                  skip_group_check=True), r=r, w=w)

        def TR(out, in_, r, w):
            return S.op("pe", lambda e: e.transpose(out=out, in_=in_, identity=ident[:]), r=list(r) + ["cst"], w=w)

        def ACT(out, in_, func, r, w, bias=None, scale=None, accum=None, eng="act"):
            kw = {}
            if bias is not None:
                kw["bias"] = bias
            if scale is not None:
                kw["scale"] = scale
            if accum is not None:
                kw["accum_out"] = accum
            return S.op("act", lambda e: e.activation(out=out, in_=in_, func=func, **kw), r=r, w=w)

        def TS(eng, out, in0, s1, s2, op0, op1, r, w):
            if op1 is None:
                return S.op(eng, lambda e: e.tensor_scalar(out=out, in0=in0, scalar1=s1, scalar2=None, op0=op0), r=r, w=w)
            return S.op(eng, lambda e: e.tensor_scalar(out=out, in0=in0, scalar1=s1, scalar2=s2, op0=op0, op1=op1), r=r, w=w)

        def TT(eng, out, in0, in1, op, r, w):
            return S.op(eng, lambda e: e.tensor_tensor(out=out, in0=in0, in1=in1, op=op), r=r, w=w)

        def STT(out, in0, scalar, in1, op0, op1, r, w, accum=None):
            if accum is not None:
                return S.op("dve", lambda e: e.scalar_tensor_tensor(out=out, in0=in0, scalar=scalar, in1=in1, op0=op0, op1=op1, accum_out=accum), r=r, w=w)
            return S.op("dve", lambda e: e.scalar_tensor_tensor(out=out, in0=in0, scalar=scalar, in1=in1, op0=op0, op1=op1), r=r, w=w)

        def CP(eng, out, in_, r, w):
            if eng == "act":
                return S.op("act", lambda e: e.copy(out=out, in_=in_), r=r, w=w)
            return S.op(eng, lambda e: e.tensor_copy(out=out, in_=in_), r=r, w=w)

        def MSET(eng, ap, val, w):
            return S.op(eng, lambda e: e.memset(ap, val), w=w)

        def RCP(out, in_, r, w):
            return S.op("dve", lambda e: e.reciprocal(out=out, in_=in_), r=r, w=w)

        def SCAN(out, d0, init, op0, r, w):
            return S.op("dve", lambda e: e.tensor_tensor_scan(out=out, data0=d0, data1=d0, initial=init, op0=op0, op1=ALU.bypass), r=r, w=w)

        def DMA(eng, out, in_, r, w):
            return S.op(eng, lambda e: e.dma_start(out=out, in_=in_), r=r, w=w, dma=True)

        def dump(es, name, ap, shape, dt=F32):
            d = nc.dram_tensor("dbg_" + name, list(shape), dt, kind="ExternalOutput").ap()
            S.barrier()
            dbg_outs.append(DMA("sp", d, ap, r=[], w=[]))

        def finish(block):
            S.barrier()
            S.emit(block)

        cstt = sb(top, "cstt", [128, 512], F32)
        ident = cstt[:, 0:128]
        triu = cstt[:, 128:256]
        M1 = cstt[:, 256:384]
        ones_f = cstt[:, 384:512]
        part = sb(top, "part", [128, 320], F32)
        mask_bf = sb(top, "mask_bf", [128, 128], BF16)
        ones_bf = sb(top, "ones_bf", [128, 128], BF16)
        ident_bf = sb(top, "ident_bf", [128, 128], BF16)
        kmT = sb(top, "kmT", [128, 4, 256], BF16)
        vm1 = sb(top, "vm1", [128, 2, 4, 129], BF16)
        ymT = sb(top, "ymT", [128, 4, S_LEN], BF16)
        hT = sb(top, "hT", [128, 8, S_LEN], BF16)
        wm_cm = nc.sbuf_tensor("wm_bf", [128, 8, 1536], BF16)
        wm_bf = wm_cm.__enter__()

        def rms_tile_a(es_bufs, src_rows, i):
            xtb, junk, ssb, xnb = es_bufs
            xt = xtb[i % 6]
            ss = ssb[:, 2 * (i % 6):2 * (i % 6) + 1]
            rs = ssb[:, 2 * (i % 6) + 1:2 * (i % 6) + 2]
            kx, ks = ("xt", i % 6), ("ss", i % 6)
            DMA("sp", xt[:], src_rows, r=[], w=[kx])
            ACT(junk[:], xt[:], AF.Square, r=[kx], w=["junk", ks], accum=ss)
            ACT(rs, ss, AF.Sqrt, r=[ks], w=[ks], bias=EPS, scale=1.0 / D)
            RCP(rs, rs, r=[ks], w=[ks])

        def rms_tile_b(es_bufs, gcol, dstT, col0, i):
            xtb, junk, ssb, xnb = es_bufs
            xt = xtb[i % 6]
            xn = xnb[i % 3]
            rs = ssb[:, 2 * (i % 6) + 1:2 * (i % 6) + 2]
            kx, kn, ks = ("xt", i % 6), ("xn", i % 3), ("ss", i % 6)
            ACT(xn[:], xt[:], AF.Copy, r=[kx, ks], w=[kn], scale=rs)
            for half in range(2):
                b = 2 * (i % 4) + half
                psb = ps[b][:].bitcast(BF16)
                for j in range(4):
                    c = 4 * half + j
                    S.op("pe", (lambda o, a: (lambda e: e.transpose(out=o, in_=a, identity=ident_bf[:])))(
                        psb[:, j * 128:(j + 1) * 128], xn[:, c * 128:(c + 1) * 128]), r=[kn, "ident_bf"], w=[P(b)])
                TT("dve", dstT[:, 4 * half:4 * half + 4, col0:col0 + 128],
                   psb[:, 0:512].rearrange("p (j t) -> p j t", t=128),
                   part[:, gcol + 4 * half:gcol + 4 * half + 4].unsqueeze(2).to_broadcast([128, 4, 128]),
                   ALU.mult, r=[P(b), "par"], w=[("hT", col0 // 512) if dstT is hT else "memT"])

        with nc.Block() as blk0, ExitStack() as es:
            DMA("sp", cstt[:], cst_d[:, :], r=[], w=["cst"])
            DMA("sp", part[:], par_d[:, :], r=[], w=["par"])
            CP("dve", mask_bf[:], triu, r=["cst"], w=["mask_bf"])
            CP("dve", ones_bf[:], ones_f, r=["cst"], w=["ones_bf"])
            CP("dve", ident_bf[:], ident, r=["cst"], w=["ident_bf"])
            xtb = [sb(es, "xt%d" % i, [128, D], F32) for i in range(6)]
            xnb = [sb(es, "xn%d" % i, [128, D], BF16) for i in range(3)]
            junk = sb(es, "junk", [128, D], BF16)
            ssb = sb(es, "ssb", [128, 12], F32)
            bufs = (xtb, junk, ssb, xnb)
            memT = sb(es, "memT", [128, 8, 256], BF16)
            wkv_bf = sb(es, "wkv_bf", [128, 8, 1024], BF16)
            wkv_v = wkv_d.rearrange("(c p) f -> p c f", p=128)
            for c in range(8):
                DMA("pool", wkv_bf[:, c, :], wkv_v[:, c, :], r=[], w=["wkv"])
            for c in range(8):
                DMA("pool", wm_bf[:, c, :], win_v[:, c, 0:1536], r=[], w=[("wm", c)])
            seq = [(mem_d[i * 128:(i + 1) * 128, :], 8, memT, i * 128) for i in range(2)]
            seq += [(x_d[i * 128:(i + 1) * 128, :], 0, hT, i * 128) for i in range(NT)]
            rms_tile_a(bufs, seq[0][0], 0)
            for k, (src_rows, gcol, dstT, col0) in enumerate(seq):
                if k + 1 < len(seq):
                    rms_tile_a(bufs, seq[k + 1][0], k + 1)
                rms_tile_b(bufs, gcol, dstT, col0, k)
            for h in range(4):
                b = 4 + h % 2
                for c in range(8):
                    MM(ps[b][:, 0:256], wkv_bf[:, c, h * 128:(h + 1) * 128], memT[:, c, :], c == 0, c == 7,
                       r=["wkv", "memT"], w=[P(b)])
                ACT(kmT[:, h, :], ps[b][:, 0:256], AF.Identity, r=[P(b)], w=["kmT"], scale=128.0 ** -0.5)
            MSET("pool", vm1[:, :, :, 128:129], 1.0, w=["vm1"])
            for mt in range(2):
                b = 6 + mt
                for c in range(8):
                    MM(ps[b][:], memT[:, c, mt * 128:(mt + 1) * 128], wkv_bf[:, c, 512:1024], c == 0, c == 7,
                       r=["wkv", "memT"], w=[P(b)])
                CP("dve", vm1[:, mt, :, 0:128], ps[b][:].rearrange("p (h d) -> p h d", d=128), r=[P(b)], w=["vm1"])
            if dbg == 0:
                dump(es, "hT", hT[:, :, 0:512], [128, 8, 512], BF16)
                dump(es, "kmT", kmT[:], [128, 4, 256], BF16)
                dump(es, "vm1", vm1[:], [128, 2, 4, 129], BF16)
            finish(blk0)
        if stop_after == 0:
            return nc

        with nc.Block() as blk1, ExitStack() as es:
            wqkv = sb(es, "wqkv", [128, 3, 4, 128], BF16)
            for j, wd in enumerate((wq_d, wk_d, wv_d)):
                DMA("pool", wqkv[:, j, :, :], wd.rearrange("h d e -> d h e"), r=[], w=["wqkv"])
            wif_bf = sb(es, "wif_bf", [128, 12, 8], BF16)
            DMA("pool", wif_bf[:], wif_d.rearrange("(j p) g -> p j g", p=128), r=[], w=["wif"])
            ABf = sb(es, "ABf", [128, 2, 4, 8], BF16)
            wTs = [sb(es, "wTs%d" % i, [128, 128], BF16) for i in range(3)]
            psb4 = ps[4][:].bitcast(BF16)
            for h in range(4):
                for j in range(3):
                    S.op("pe", (lambda o, a: (lambda e: e.transpose(out=o, in_=a, identity=ident_bf[:])))(
                        psb4[:, j * 128:(j + 1) * 128], wqkv[:, j, h, :]), r=["wqkv", "ident_bf"], w=[P(4)])
                    CP("dve", wTs[j][:], psb4[:, j * 128:(j + 1) * 128], r=[P(4)], w=[("wTs", j)])
                MM(ps[5][:, h * 8:(h + 1) * 8], wTs[0][:], wif_bf[:, h, :], True, False, r=[("wTs", 0), "wif"], w=[P(5)])
                MM(ps[5][:, h * 8:(h + 1) * 8], wTs[1][:], wif_bf[:, 4 + h, :], False, True, r=[("wTs", 1), "wif"], w=[P(5)])
                MM(ps[5][:, 32 + h * 8:32 + (h + 1) * 8], wTs[2][:], wif_bf[:, 8 + h, :], True, True, r=[("wTs", 2), "wif"], w=[P(5)])
            CP("dve", ABf[:].rearrange("p a h g -> p (a h g)"), ps[5][:, 0:64], r=[P(5)], w=["ABf"])
            xmb = [sb(es, "xm%d" % i, [128, 4, 515], BF16) for i in range(2)]
            xcv = sb(es, "xcv", [128, 4, 512], BF16)
            xs = sb(es, "xs", [128, 4, 512], BF16)
            qT = sb(es, "qT", [128, 4, 512], BF16)
            kT = sb(es, "kT", [128, 4, 512], BF16)
            vT = None
            zs = sb(es, "zs", [128, 4, 512], BF16)
            sgo = sb(es, "sgo", [128, 4, 512], BF16)
            Dg = sb(es, "Dg", [128, 16, 128], BF16)
            for hj in range(16):
                TS("dve", Dg[:, hj, :], ident, part[:, 16 + hj:16 + hj + 1], None, ALU.mult, None, r=["cst", "par"], w=["Dg"])
            Gtok = sb(es, "Gtok", [128, 32, 8], F32)
            CONVW, CONVB, MG, SK = 16, 32, 36, 40
            rot = [0]

            def nextbank():
                rot[0] ^= 1
                return rot[0]

            def frontA(blk):
                xm = xmb[blk % 2]
                kxm = ("xm", blk % 2)
                cols = slice(blk * 512, (blk + 1) * 512)
                if blk == 0:
                    MSET("pool", xm[:, :, 0:3], 0.0, w=[kxm])
                else:
                    CP("pool", xm[:, :, 0:3], xmb[(blk - 1) % 2][:, :, 512:515],
                       r=[("xm", (blk - 1) % 2)] + [(("xm", (blk - 1) % 2), hh) for hh in range(4)], w=[kxm])
                for h in range(4):
                    b = nextbank()
                    for c in range(8):
                        MM(ps[b][:], wm_bf[:, c, h * 128:(h + 1) * 128], hT[:, c, cols], c == 0, c == 7,
                           r=[("wm", c), ("hT", blk)], w=[P(b)])
                    CP("act", xm[:, h, 3:515], ps[b][:], r=[P(b)], w=[(kxm, h)])
                return xm, kxm

            def frontBC(blk, need_v, do_qk=True):
                xm = xmb[blk % 2]
                kxm = ("xm", blk % 2)
                for h in range(4):
                    b = nextbank()
                    for j in range(4):
                        MM(ps[b][:], Dg[:, 4 * h + j, :], xm[:, h, j:j + 512], j == 0, j == 3,
                           r=["Dg", (kxm, h), kxm], w=[P(b)])
                    ACT(xcv[:, h, :], ps[b][:], AF.Silu, r=[P(b), "par"], w=[("xcv", h)],
                        bias=part[:, CONVB + h:CONVB + h + 1])
                for h in range(4 if do_qk else 0):
                    for j, dst, kd in ((0, qT, "qT"), (1, kT, "kT")) + (((2, vT, "vT"),) if need_v else ()):
                        b = nextbank()
                        src = xcv[:, h, :] if j < 2 else xm[:, h, 3:515]
                        MM(ps[b][:], wqkv[:, j, h, :], src, True, True, r=["wqkv", ("xcv", h) if j < 2 else (kxm, h)], w=[P(b)])
                        CP("act" if j == 0 else "dve", dst[:, h, :], ps[b][:], r=[P(b)], w=[kd])
                return xm, kxm

            for blk in range(NB):
                if blk == 0:
                    frontA(0)
                frontBC(blk, False, do_qk=False)
                if blk + 1 < NB:
                    frontA(blk + 1)
                xm_ = xmb[blk % 2]
                kxm_ = ("xm", blk % 2)
                for tt in range(4):
                    tc_ = slice(tt * 128, (tt + 1) * 128)
                    for h in range(4):
                        MM(ps[2][:, tt * 8:(tt + 1) * 8], xcv[:, h, tc_], ABf[:, 0, h, :], h == 0, False,
                           r=[("xcv", h), "ABf"], w=[P(2)])
                        MM(ps[2][:, tt * 8:(tt + 1) * 8], xm_[:, h, 3 + tt * 128:3 + (tt + 1) * 128], ABf[:, 1, h, :], False, h == 3,
                           r=[(kxm_, h), "ABf"], w=[P(2)])
                CP("dve", Gtok[:, blk * 4:(blk + 1) * 4, :], ps[2][:, 0:32].rearrange("p (t g) -> p t g", g=8),
                   r=[P(2)], w=["Gtok"])
            if dbg and dbg >= 1:
                dump(es, "Gtok", Gtok[:], [128, 32, 8])
                dump(es, "qT", qT[:], [128, 4, 512], BF16)
                dump(es, "xcv", xcv[:], [128, 4, 512], BF16)

            rows = sb(es, "rows", [128, 8, 128], F32)
            GIr, GFr, LF, Fp, U_, W_, W2_, TH_ = [rows[:, i, :] for i in range(8)]
            cols_ = sb(es, "colsb", [128, 16], F32)
            col = lambda i: cols_[:, i:i + 1]
            rowt = sb(es, "rowt", [1, 4, 128], F32)
            Ebc = sb(es, "Ebc", [128, 128], F32)
            Wt = sb(es, "Wt", [128, 3, 128], F32)
            BI, BFc = 48, 49
            Gv = Gtok[:].rearrange("p c g -> p (c g)")
            gsp = sb(es, "gsp", [128, 2, 128], F32)
            CP("dve", gsp[:, 0, :].rearrange("p (c h) -> p c h", h=4), Gtok[:, :, 0:4], r=["Gtok"], w=["gsp"])
            CP("dve", gsp[:, 1, :].rearrange("p (c h) -> p c h", h=4), Gtok[:, :, 4:8], r=["Gtok"], w=["gsp"])
            TR(ps[3][:, 0:128], gsp[:, 0, :], r=["gsp"], w=[P(3)])
            TR(ps[3][:, 128:256], gsp[:, 1, :], r=["gsp"], w=[P(3)])
            CP("dve", rows[:, 0:2, :], ps[3][:, 0:256].rearrange("p (a t) -> p a t", t=128), r=[P(3)], w=["rows"])
            TS("dve", col(0), part[:, BFc:BFc + 1], -1.0, None, ALU.mult, None, r=["par"], w=["cols"])
            ACT(LF, GFr, AF.Exp, r=["rows", "cols"], w=["rows"], bias=col(0), scale=-1.0)
            ACT(LF, LF, AF.Ln, r=["rows"], w=["rows"], bias=1.0)
            SCAN(Fp, LF, 0.0, ALU.add, r=["rows"], w=["rows"])
            MM(ps[3][:, 256:257], M1, rows[:, 3, 127:128], True, True, r=["cst", "rows"], w=[P(3)])
            CP("dve", col(1), ps[3][:, 256:257], r=[P(3)], w=["cols"])
            TS("dve", Fp, Fp, col(1), None, ALU.add, None, r=["rows", "cols"], w=["rows"])
            STT(GIr, GIr, part[:, BI:BI + 1], Fp, ALU.add, ALU.add, r=["rows", "par"], w=["rows"])
            SCAN(U_, GIr, 0.0, ALU.max, r=["rows"], w=["rows"])
            MM(ps[3][0:1, 384:512], rows[:, 4, 127:128], ident, True, True, r=["rows", "cst"], w=[P(3)])
            CP("dve", rowt[:, 0, :], ps[3][0:1, 384:512], r=[P(3)], w=["rowt"])
            for h in range(4):
                SCAN(rowt[:, 1, :].rearrange("p (c h) -> p h c", h=4)[:, h, :],
                     rowt[:, 0, :].rearrange("p (c h) -> p h c", h=4)[:, h, :], 0.0, ALU.max, r=["rowt"], w=["rowt"])
            MSET("dve", rowt[:, 2, 0:4], 0.0, w=["rowt"])
            CP("dve", rowt[:, 2, 4:128], rowt[:, 1, 0:124], r=["rowt"], w=["rowt"])
            TT("dve", rowt[:, 3, :], rowt[:, 2, :], rowt[:, 1, :], ALU.subtract, r=["rowt"], w=["rowt"])
            MM(ps[3][:, 257:258], rowt[:, 2, :], ones_f[0:1, 0:1], True, True, r=["rowt", "cst"], w=[P(3)])
            MM(ps[3][:, 258:259], rowt[:, 3, :], ones_f[0:1, 0:1], True, True, r=["rowt", "cst"], w=[P(3)])
            CP("dve", cols_[:, 2:4], ps[3][:, 257:259], r=[P(3)], w=["cols"])
            MM(ps[4][:, 0:128], ones_f[0:1, :], rowt[:, 3, :], True, True, r=["rowt", "cst"], w=[P(4)])
            ACT(Ebc[:], ps[4][:, 0:128], AF.Exp, r=[P(4)], w=["Ebc"])
            TS("dve", col(4), col(2), -1.0, -LN_SQRT_DH, ALU.mult, ALU.add, r=["cols"], w=["cols"])
            TS("dve", col(5), col(2), -1.0, None, ALU.mult, None, r=["cols"], w=["cols"])
            TT("dve", col(6), col(4), col(3), ALU.add, r=["cols"], w=["cols"])
            ACT(W_, GIr, AF.Exp, r=["rows", "cols"], w=["rows"], bias=col(4))
            ACT(W2_, GIr, AF.Exp, r=["rows", "cols"], w=["rows"], bias=col(6))
            ACT(TH_, Fp, AF.Exp, r=["rows", "cols"], w=["rows"], bias=col(5))
            for i in range(3):
                TR(ps[4][:, 128 * (i + 1):128 * (i + 2)], rows[:, 5 + i, :], r=["rows"], w=[P(4)])
            CP("dve", Wt[:], ps[4][:, 128:512].rearrange("p (a t) -> p a t", t=128), r=[P(4)], w=["Wt"])
            if dbg and dbg >= 2:
                dump(es, "rows", rows[:], [128, 8, 128])
                dump(es, "Wt", Wt[:], [128, 3, 128])
                dump(es, "Ebc", Ebc[:], [128, 128])
                dump(es, "rowt", rowt[:], [1, 4, 128])

            kwb = [sb(es, "kw%d" % i, [128, 4, 128], BF16) for i in range(2)]
            V1b = [sb(es, "V1_%d" % i, [128, 4, 129], BF16) for i in range(2)]
            sqkb = [sb(es, "sqk%d" % i, [128, 4, 128], BF16) for i in range(2)]
            ABsb = [sb(es, "ABs%d" % i, [128, 4, 129], F32) for i in range(2)]
            lnbb = [sb(es, "lnb%d" % i, [128, 4, 128], F32) for i in range(2)]
            Cn = sb(es, "Cn", [128, 4, 129], F32)
            Cnb = sb(es, "Cnb", [128, 4, 129], BF16)
            hsb = [sb(es, "hs%d" % i, [128, 4, 128], F32) for i in range(2)]
            smb = [sb(es, "sm%d" % i, [128, 64], F32) for i in range(2)]
            ymt = sb(es, "ymt", [128, 4, 128], F32)
            mhalf = sb(es, "mhalf", [128, 4], F32)
            MSET("pool", mhalf[:], -0.5, w=["mhalf"])
            for i in range(2):
                MSET("pool", V1b[i][:, :, 128:129], 1.0, w=[("V1", i)])
            MSET("pool", Cn[:], 0.0, w=["Cn"])
            MSET("pool", Cnb[:], 0.0, w=["Cnb"])

            def pre(c_, xm, kxm):
                tt, p = c_ % 4, c_ % 2
                tc_ = slice(tt * 128, (tt + 1) * 128)
                ch4 = slice(c_ * 4, c_ * 4 + 4)
                for h in range(4):
                    MM(ps[2][:, h * 128:(h + 1) * 128], xcv[:, h, tc_], wqkv[:, 1, h, :], True, True,
                       r=[("xcv", h), "wqkv"], w=[P(2)])
                    MM(ps[3][:, h * 128:(h + 1) * 128], xm[:, h, 3 + tt * 128:3 + (tt + 1) * 128], wqkv[:, 2, h, :],
                       True, True, r=[(kxm, h), "wqkv"], w=[P(3)])
                    MM(ps[4][:, h * 128:(h + 1) * 128], kT[:, h, tc_], qT[:, h, tc_], True, True,
                       r=["kT", "qT"], w=[P(4)])
                for h in range(4):
                    ACT(kwb[p][:, h, :], ps[2][:, h * 128:(h + 1) * 128], AF.Copy, r=[P(2), "Wt"], w=[("kw", p)],
                        scale=Wt[:, 1, c_ * 4 + h:c_ * 4 + h + 1])
                CP("act", V1b[p][:, :, 0:128], ps[3][:].rearrange("p (h d) -> p h d", d=128), r=[P(3)], w=[("V1", p)])
                for h in range(4):
                    STT(sqkb[p][:, h, :], ps[4][:, h * 128:(h + 1) * 128], Wt[:, 0, c_ * 4 + h:c_ * 4 + h + 1], triu,
                        ALU.mult, ALU.mult, r=[P(4), "Wt", "cst"], w=[("sqk", p)])

            def rec(c_):
                tt, p = c_ % 4, c_ % 2
                tc_ = slice(tt * 128, (tt + 1) * 128)
                kw, V1, sqk = kwb[p], V1b[p], sqkb[p]
                for hp in range(2):
                    ab = ps[5 + hp]
                    for hh in range(2):
                        h = 2 * hp + hh
                        MM(ab[:, hh * 129:(hh + 1) * 129], sqk[:, h, :], V1[:, h, :], hh == 0, False,
                           r=[("sqk", p), ("V1", p)], w=[P(5 + hp)])
                        MM(ab[:, hh * 129:(hh + 1) * 129], qT[:, h, tc_], Cnb[:, h, :], False, True,
                           r=["qT", "Cnb"], w=[P(5 + hp)])
                    for hh in range(2):
                        h = 2 * hp + hh
                        MM(ps[7][:, hh * 129:(hh + 1) * 129], kw[:, h, :], V1[:, h, :], hh == 0, True,
                           r=[("kw", p), ("V1", p)], w=[P(7)])
                    for hh in range(2):
                        h = 2 * hp + hh
                        STT(Cn[:, h, :], Cn[:, h, :], Ebc[:, c_ * 4 + h:c_ * 4 + h + 1], ps[7][:, hh * 129:(hh + 1) * 129],
                            ALU.mult, ALU.add, r=["Cn", "Ebc", P(7)], w=["Cn"])
                    CP("dve", Cnb[:, 2 * hp:2 * hp + 2, :], Cn[:, 2 * hp:2 * hp + 2, :], r=["Cn"], w=["Cnb"])
                    CP("act", ABsb[p][:, 2 * hp:2 * hp + 2, :], ab[:, 0:258].rearrange("p (h d) -> p h d", d=129),
                       r=[P(5 + hp)], w=[("ABs", p)])

            def epi_a(c_):
                tt, p = c_ % 4, c_ % 2
                A = ABsb[p]
                sm_ = smb[p]
                hs_ = hsb[p]
                den = sm_[:, 0:4]
                kd, kmv, krs = ("den", p), ("mv", p), ("rstd4", p)
                STT(den, A[:, :, 128], -1.0, A[:, :, 128], ALU.mult, ALU.max, r=[("ABs", p)], w=[kd])
                TT("dve", den, den, Wt[:, 2, c_ * 4:c_ * 4 + 4], ALU.max, r=[kd, "Wt"], w=[kd])
                RCP(den, den, r=[kd], w=[kd])
                mvv = sm_[:, 48:56].rearrange("p (h t) -> p h t", t=2)
                for h in range(4):
                    STT(hs_[:, h, :], A[:, h, 0:128], sm_[:, h:h + 1], sgo[:, tt, h * 128:(h + 1) * 128], ALU.mult, ALU.mult,
                        r=[("ABs", p), kd, ("sgo", tt)], w=[("hs", p, h)])
                    st = sm_[:, 16 + 8 * h:16 + 8 * h + 6]
                    S.op("dve", (lambda o, i: (lambda e: e.bn_stats(out=o, in_=i)))(st, hs_[:, h, :]), r=[("hs", p, h)], w=[("st", p, h)])
                    S.op("dve", (lambda o, i: (lambda e: e.bn_aggr(out=o, in_=i)))(mvv[:, h, :], st), r=[("st", p, h)], w=[kmv])
                rstd4 = sm_[:, 56:60]
                nmr4 = sm_[:, 60:64]
                TS("pool", rstd4, mvv[:, :, 1], EPS, None, ALU.add, None, r=[kmv], w=[krs])
                TT("pool", rstd4, rstd4, mhalf[:], ALU.pow, r=[krs, "mhalf"], w=[krs])
                TS("pool", nmr4, mvv[:, :, 0], -1.0, None, ALU.mult, None, r=[kmv], w=[("nmr4", p)])
                TT("pool", nmr4, nmr4, rstd4, ALU.mult, r=[("nmr4", p), krs], w=[("nmr4", p)])

            def epi_b(c_):
                p = c_ % 2
                sm_ = smb[p]
                for h in range(4):
                    ACT(lnbb[p][:, h, :], hsb[p][:, h, :], AF.Identity, r=[("hs", p, h), ("rstd4", p), ("nmr4", p)], w=[("lnb", p)],
                        scale=sm_[:, 56 + h:57 + h], bias=sm_[:, 60 + h:61 + h])

            def outp(c_):
                tt, p = c_ % 4, c_ % 2
                blk_ = c_ // 4
                tc_ = slice(tt * 128, (tt + 1) * 128)
                b = nextbank()
                for h in range(4):
                    TR(ps[b][:, h * 128:(h + 1) * 128], lnbb[p][:, h, :], r=[("lnb", p)], w=[P(b)])
                for h in range(4):
                    STT(ymt[:, h, :], ps[b][:, h * 128:(h + 1) * 128], part[:, MG + h:MG + h + 1], xs[:, h, tc_],
                        ALU.mult, ALU.add, r=[P(b), "par", "xs"], w=[("ymt", h)])
                    TT("pool", ymT[:, h, blk_ * 512 + tt * 128:blk_ * 512 + (tt + 1) * 128], ymt[:, h, :], zs[:, h, tc_],
                       ALU.mult, r=[("ymt", h), "zs"], w=["ymT"])

            for blk in range(NB if stop_after > 1 else 1):
                if blk == 0:
                    nxt = frontA(0)
                xm, kxm = nxt
                frontBC(blk, False)
                cols = slice(blk * 512, (blk + 1) * 512)
                for h in range(4):
                    b = nextbank()
                    for c in range(8):
                        MM(ps[b][:], wm_bf[:, c, 1024 + h * 128:1024 + (h + 1) * 128], hT[:, c, cols], c == 0, c == 7,
                           r=[("wm", c), ("hT", blk)], w=[P(b)])
                    ACT(zs[:, h, :], ps[b][:], AF.Silu, r=[P(b)], w=["zs"])
                    ACT(xs[:, h, :], xcv[:, h, :], AF.Copy, scale=part[:, SK + h:SK + h + 1],
                       r=[("xcv", h), "par"], w=["xs"])
                for tt in range(4):
                    b = nextbank()
                    for c in range(8):
                        MM(ps[b][:], hT[:, c, blk * 512 + tt * 128:blk * 512 + (tt + 1) * 128], wm_bf[:, c, 512:1024],
                           c == 0, c == 7, r=[("wm", c), ("hT", blk)], w=[P(b)])
                    ACT(sgo[:, tt, :], ps[b][:], AF.Sigmoid, r=[P(b)], w=[("sgo", tt)])
                c0 = blk * 4
                pre(c0, xm, kxm)
                for tt in range(4):
                    c_ = c0 + tt
                    if tt < 3:
                        pre(c_ + 1, xm, kxm)
                    rec(c_)
                    if tt == 1 and blk + 1 < (NB if stop_after > 1 else 1):
                        nxt = frontA(blk + 1)
                    epi_a(c_)
                    if tt > 0:
                        epi_b(c_ - 1)
                        outp(c_ - 1)
                epi_b(c0 + 3)
                outp(c0 + 3)
            if dbg and dbg >= 3:
                dump(es, "ymT", ymT[:, :, 0:512], [128, 4, 512], BF16)
                dump(es, "hs", hsb[0][:], [128, 4, 128])
                dump(es, "Cn", Cn[:], [128, 4, 129])
            finish(blk1)
        wm_cm.__exit__(None, None, None)
        if stop_after <= 1:
            return nc

        rot2 = [0]

        def nb2():
            rot2[0] ^= 1
            return rot2[0]

        GROUPS = [(3 * g, 3 * g + 3) for g in range(10)] + [(30, 32)]
        DG = 44

        def proj_fm(wt, j, dst, blk, func, scale, kd, perblk=False):
            b = nb2()
            cols = slice(blk * 512, (blk + 1) * 512)
            wk_ = (kd, blk) if perblk else kd
            for c in range(8):
                MM(ps[b][:], wt[:, j, c, :], hT[:, c, cols], c == 0, c == 7, r=[kd + "_w", ("hT", blk)], w=[P(b)])
            if func is None:
                CP("dve", dst[:, cols], ps[b][:], r=[P(b)], w=[wk_])
            else:
                ACT(dst[:, cols], ps[b][:], func, r=[P(b)], w=[wk_], scale=scale)

        ydT = sb(top, "ydT", [128, 4, S_LEN], BF16)
        with nc.Block() as blk2, ExitStack() as es:
            lamt = sb(es, "lamt", [128, 8], F32)
            j64 = sb(es, "j64", [128, 64], F32)
            STT(j64[:], part[:, 64:128], 1.0, part[:, 128:192], ALU.mult, ALU.mult, r=["par"], w=["j64", "lam"], accum=lamt[:, 0:1])
            STT(j64[:], part[:, 192:256], 1.0, part[:, 256:320], ALU.mult, ALU.mult, r=["par", "j64"], w=["j64", "lam"], accum=lamt[:, 1:2])
            ACT(lamt[:, 2:4], lamt[:, 0:2], AF.Exp, r=["lam"], w=["lam"])
            TT("dve", lamt[:, 4:5], lamt[:, 2:3], lamt[:, 3:4], ALU.subtract, r=["lam"], w=["lam"])
            TS("dve", lamt[:, 5:6], lamt[:, 4:5], 0.2, -1.0, ALU.add, ALU.mult, r=["lam"], w=["lam"])
            wdb = [sb(es, "wd%d" % i, [128, 4, 8, 128], BF16) for i in range(2)]
            qdT = sb(es, "qdT", [128, S_LEN], BF16)
            kpd = [sb(es, "kpd%d" % i, [128, S_LEN], BF16) for i in range(2)]
            MSET("pool", kpd[0][64:128, :], 0.0, w=["kd"])
            MSET("pool", kpd[1][0:64, :], 0.0, w=["kd"])
            zds = sb(es, "zds", [128, S_LEN], BF16)
            V1d = sb(es, "V1d", [128, 32, 128], BF16)
            ptb = [sb(es, "pt%d" % i, [128, 512], BF16) for i in range(4)]
            o0s = sb(es, "o0s", [128, 512], F32)
            o1s = sb(es, "o1s", [128, 512], F32)
            l0s = sb(es, "l0s", [128, 512], F32)
            l1s = sb(es, "l1s", [128, 512], F32)
            sqb = sb(es, "sqb", [128, 512], BF16)
            rsd = l0s
            dg08 = sb(es, "dg08", [128, 4], F32)
            TS("dve", dg08[:], part[:, DG:DG + 4], 0.8, None, ALU.mult, None, r=["par"], w=["dg08"])
            GROUPS4 = [(4 * g, 4 * g + 4) for g in range(8)]

            def proj_k(wt, blk):
                b = nb2()
                cols = slice(blk * 512, (blk + 1) * 512)
                for c in range(8):
                    MM(ps[b][:], wt[:, 1, c, :], hT[:, c, cols], c == 0, c == 7, r=["kd_w", ("hT", blk)], w=[P(b)])
                CP("dve", kpd[0][0:64, cols], ps[b][0:64, :], r=[P(b)], w=["kd"])
                CP("dve", kpd[1][64:128, cols], ps[b][64:128, :], r=[P(b)], w=["kd"])
            for h in range(4):
                wd = wdb[h % 2]
                for j, off, kk in ((0, 1536, "qd_w"), (1, 2048, "kd_w"), (3, 3072, "zd_w"), (2, 2560, "vd_w")):
                    DMA("pool", wd[:, j, :, :], win_v[:, :, off + h * 128:off + (h + 1) * 128], r=[], w=[kk])
                for blk in range(NB):
                    proj_fm(wd, 0, qdT, blk, AF.Identity, 0.125, "qd")
                    proj_k(wd, blk)
                    proj_fm(wd, 3, zds, blk, AF.Silu, None, "zd")
                    for tt in range(4):
                        tl = blk * 4 + tt
                        for c in range(8):
                            MM(ps[3][:, tt * 128:(tt + 1) * 128], hT[:, c, tl * 128:(tl + 1) * 128], wd[:, 2, c, :], c == 0, c == 7,
                               r=["vd_w", ("hT", blk)], w=[P(3)])
                    CP("dve", V1d[:, blk * 4:(blk + 1) * 4, :], ps[3][:].rearrange("p (t d) -> p t d", d=128), r=[P(3)], w=["V1d"])
                steps = [(g, qs, qe, m, kb) for g, (qs, qe) in enumerate(GROUPS4) for m in range(2) for kb in range(qe)]
                LA = 3
                sctr = [0]

                def emit_qk(i):
                    g, qs, qe, m, kb = steps[i]
                    nsub = qe - qs
                    pr = slice(m * 64, (m + 1) * 64)
                    sv = max(0, kb - qs)
                    sbk = sctr[0] % 4
                    sctr[0] += 1
                    fc = slice(sv * 128, nsub * 128)
                    MM(ps[sbk][:, fc], kpd[m][:, kb * 128:(kb + 1) * 128], qdT[:, (qs + sv) * 128:qe * 128], True, True,
                       r=["kd", "qd"], w=[P(sbk)])
                    pt = ptb[i % 4]
                    kpt = ("pt", i % 4)
                    ACT(pt[:, fc], ps[sbk][:, fc], AF.Exp, r=[P(sbk)], w=[kpt])
                    if kb >= qs:
                        dc = slice((kb - qs) * 128, (kb - qs + 1) * 128)
                        TT("dve", pt[:, dc], pt[:, dc], mask_bf[:], ALU.mult, r=[kpt, "mask_bf"], w=[kpt])

                def emit_pv(i):
                    g, qs, qe, m, kb = steps[i]
                    nsub = qe - qs
                    sv = max(0, kb - qs)
                    fc = slice(sv * 128, nsub * 128)
                    pt = ptb[i % 4]
                    kpt = ("pt", i % 4)
                    MM(ps[4 + m][:, fc], V1d[:, kb, :], pt[:, fc], kb == 0, kb == qe - 1, r=[kpt, "V1d"], w=[P(4 + m)])
                    MM(ps[6 + m][:, fc], ones_bf[:], pt[:, fc], kb == 0, kb == qe - 1, r=[kpt, "ones_bf"], w=[P(6 + m)])
                    if m == 1 and kb == qe - 1:
                        epilogue1(qs, qe)
                        pending.append((i + 12, qs, qe))

                pending = []

                def epilogue1(qs, qe):
                    CP("act", o0s[:], ps[4][:], r=[P(4)], w=["o0s"])
                    CP("dve", l0s[:], ps[6][:], r=[P(6)], w=["l0s"])
                    CP("act", o1s[:], ps[5][:], r=[P(5)], w=["o1s"])
                    CP("dve", l1s[:], ps[7][:], r=[P(7)], w=["l1s"])
                    RCP(l0s[:], l0s[:], r=["l0s"], w=["l0s"])
                    RCP(l1s[:], l1s[:], r=["l1s"], w=["l1s"])
                    TT("dve", o0s[:], o0s[:], l0s[:], ALU.mult, r=["o0s", "l0s"], w=["o0s"])
                    TT("dve", o1s[:], o1s[:], l1s[:], ALU.mult, r=["o1s", "l1s"], w=["o1s"])
                    STT(o0s[:], o1s[:], lamt[:, 5:6], o0s[:], ALU.mult, ALU.add, r=["o0s", "o1s", "lam"], w=["o0s"])

                def epilogue2(qs, qe):
                    cols = slice(qs * 128, qe * 128)
                    ACT(sqb[:], o0s[:], AF.Square, r=["o0s"], w=["sqb"])
                    eb = sctr[0] % 4
                    sctr[0] += 1
                    MM(ps[eb][:], ones_bf[:], sqb[:], True, True, r=["sqb", "ones_bf"], w=[P(eb)])
                    ACT(rsd[:], ps[eb][:], AF.Sqrt, r=[P(eb)], w=["l0s"], bias=EPS, scale=1.0 / 128)
                    RCP(rsd[:], rsd[:], r=["l0s"], w=["l0s"])
                    STT(o0s[:], o0s[:], dg08[:, h:h + 1], rsd[:], ALU.mult, ALU.mult, r=["o0s", "dg08", "l0s"], w=["o0s"])
                    TT("dve", ydT[:, h, cols], o0s[:], zds[:, cols], ALU.mult, r=["o0s", "zd"], w=["ydT"])

                for i in range(len(steps) + LA):
                    if i < len(steps):
                        emit_qk(i)
                    if i >= LA:
                        emit_pv(i - LA)
                    while pending and pending[0][0] <= i - LA:
                        _, pqs, pqe = pending.pop(0)
                        epilogue2(pqs, pqe)
                while pending:
                    _, pqs, pqe = pending.pop(0)
                    epilogue2(pqs, pqe)
            if dbg and dbg >= 4:
                dump(es, "ydT", ydT[:, :, 0:1024], [128, 4, 1024], BF16)
                dump(es, "lamt", lamt[:], [128, 8])
            finish(blk2)
        if stop_after <= 2:
            return nc

        ycT = sb(top, "ycT", [128, 4, S_LEN], BF16)
        with nc.Block() as blk3, ExitStack() as es:
            wcb = [sb(es, "wc%d" % i, [128, 2, 8, 128], BF16) for i in range(2)]
            qcT = sb(es, "qcT", [128, S_LEN], BF16)
            zcs = sb(es, "zcs", [128, S_LEN], BF16)
            ptb = [sb(es, "ptc%d" % i, [128, 512], BF16) for i in range(4)]
            rlb = [sb(es, "rl%d" % i, [128, 512], F32) for i in range(2)]
            onb = [sb(es, "on%d" % i, [128, 512], F32) for i in range(2)]
            for h in range(4):
                wc = wcb[h % 2]
                for j, off, kk in ((0, 3584, "qc_w"), (1, 4096, "zc_w")):
                    DMA("pool", wc[:, j, :, :], win_v[:, :, off + h * 128:off + (h + 1) * 128], r=[], w=[kk])

                def c_qk(j):
                    cols = slice(j * 512, (j + 1) * 512)
                    for mc in range(2):
                        sbk = 2 * (j % 2) + mc
                        sb_ = 2 + mc
                        MM(ps[sb_][:], kmT[:, h, mc * 128:(mc + 1) * 128], qcT[:, cols], True, True, r=["kmT", ("qc", j)], w=[P(sb_)])
                        ACT(ptb[sbk][:], ps[sb_][:], AF.Exp, r=[P(sb_)], w=[("ptc", sbk)])

                def c_pv(j):
                    cols = slice(j * 512, (j + 1) * 512)
                    p = j % 2
                    for mc in range(2):
                        sbk = 2 * p + mc
                        MM(ps[4 + p][:], vm1[:, mc, h, 0:128], ptb[sbk][:], mc == 0, mc == 1, r=[("ptc", sbk), "vm1"], w=[P(4 + p)])
                    for mc in range(2):
                        sbk = 2 * p + mc
                        MM(ps[6 + p][:], ones_bf[:], ptb[sbk][:], mc == 0, mc == 1, r=[("ptc", sbk), "ones_bf"], w=[P(6 + p)])
                    RCP(rlb[p][:], ps[6 + p][:], r=[P(6 + p)], w=[("rl", p)])
                    TT("dve", onb[p][:], ps[4 + p][:], rlb[p][:], ALU.mult, r=[P(4 + p), ("rl", p)], w=[("on", p)])
                    TT("pool", ycT[:, h, cols], onb[p][:], zcs[:, cols], ALU.mult, r=[("on", p), ("zc", j)], w=["ycT"])

                for j in range(NB + 1):
                    if j < NB:
                        proj_fm(wc, 0, qcT, j, AF.Copy, None, "qc", perblk=True)
                        proj_fm(wc, 1, zcs, j, AF.Silu, None, "zc", perblk=True)
                        c_qk(j)
                    if j >= 1:
                        c_pv(j - 1)
            if dbg and dbg >= 5:
                dump(es, "ycT", ycT[:, :, 0:1024], [128, 4, 1024], BF16)
            finish(blk3)
        if stop_after <= 3:
            return nc

        with nc.Block() as blk4, ExitStack() as es:
            hflat = hT[:].rearrange("p c t -> p (c t)")
            wout_bf = hflat[:, 0:12 * D].rearrange("p (c f) -> p c f", f=D)
            wout_v = wout_d.rearrange("(c p) f -> p c f", p=128)
            for c in range(12):
                DMA("pool", wout_bf[:, c, :], wout_v[:, c, :], r=[], w=[("wout", c)])
            hf32 = hflat[:, 12 * D:].bitcast(F32)
            NXB = 8
            xtb = [hf32[:, i * D:(i + 1) * D] for i in range(NXB)]
            fgt = hf32[:, NXB * D:(NXB + 1) * D]
            DMA("sp", fgt, fg_d[:, :], r=[], w=["fg"])
            junk = sb(es, "junk4", [128, D], BF16)
            ss4 = sb(es, "ss4", [128, 2 * NXB], F32)
            ysrc = [(ymT, hh) for hh in range(4)] + [(ydT, hh) for hh in range(4)] + [(ycT, hh) for hh in range(4)]

            def load_x(i):
                DMA("sp", xtb[i % NXB], x_d[i * 128:(i + 1) * 128, :], r=[], w=[("xo", i % NXB)])

            PF = 6
            for i in range(PF):
                load_x(i)
            for i in range(NT):
                p2 = i % NXB
                xt = xtb[p2]
                kx = ("xo", p2)
                tcs = slice(i * 128, (i + 1) * 128)
                if i + PF < NT:
                    load_x(i + PF)
                for half in range(2):
                    b = 2 * (i % 4) + half
                    hc = slice(half * 512, (half + 1) * 512)
                    for ch, (src, hh) in enumerate(ysrc):
                        MM(ps[b][:], src[:, hh, tcs], wout_bf[:, ch, hc], ch == 0, ch == 11, r=[("wout", ch), "ycat"], w=[P(b)])
                    TT("dve", xt[:, hc], ps[b][:], xt[:, hc], ALU.add, r=[P(b), kx], w=[kx])
                ss = ss4[:, 2 * p2:2 * p2 + 1]
                rs = ss4[:, 2 * p2 + 1:2 * p2 + 2]
                ACT(junk[:], xt, AF.Square, r=[kx], w=["junk4", ("ss4", p2)], accum=ss)
                ACT(rs, ss, AF.Sqrt, r=[("ss4", p2)], w=[("ss4", p2)], bias=EPS, scale=1.0 / D)
                RCP(rs, rs, r=[("ss4", p2)], w=[("ss4", p2)])
                STT(xt, xt, rs, fgt, ALU.mult, ALU.mult, r=[kx, ("ss4", p2), "fg"], w=[kx])
                DMA("sp", out_d[i * 128:(i + 1) * 128, :], xt, r=[kx], w=[])
            finish(blk4)
    return nc


def make_consts():
    cst = np.zeros((128, 512), np.float32)
    cst[:, 0:128] = np.eye(128, dtype=np.float32)
    cst[:, 128:256] = np.triu(np.ones((128, 128), np.float32))
    p = np.arange(128)
    c, h = p // 4, p % 4
    cst[:, 256:384] = ((h[:, None] == h[None, :]) & (c[:, None] < c[None, :])).astype(np.float32)
    cst[:, 384:512] = 1.0
    return cst


def make_in_maps(inp):
    f = lambda a: np.ascontiguousarray(a, dtype=np.float32)
    cst = make_consts()
    par = np.zeros((128, 320), np.float32)
    par[:, 0:8] = inp["norm_g"][0].reshape(8, 128).T
    par[:, 8:16] = inp["mem_norm_g"][0].reshape(8, 128).T
    par[:, 16:32] = inp["conv_w"][0].reshape(4, 4, 128).transpose(2, 1, 0).reshape(128, 16)
    par[:, 32:36] = inp["conv_b"][0].reshape(4, 128).T
    par[:, 36:40] = inp["mnorm_g"][0].reshape(4, 128).T
    par[:, 40:44] = inp["skip_m"][0].reshape(4, 128).T
    par[:, 44:48] = inp["dnorm_g"][0].reshape(4, 128).T
    par[:, 48] = np.tile(inp["b_if"][0][0:4], 32)
    par[:, 49] = np.tile(inp["b_if"][0][4:8], 32)
    par[:, 64:128] = inp["lam_q1"][0][None, :]
    par[:, 128:192] = inp["lam_k1"][0][None, :]
    par[:, 192:256] = inp["lam_q2"][0][None, :]
    par[:, 256:320] = inp["lam_k2"][0][None, :]
    fg = np.ascontiguousarray(np.broadcast_to(inp["final_g"][None, :], (128, D)), dtype=np.float32)
    shared = {
        "w_in": f(inp["w_in"][0]), "w_kv": f(inp["w_mem_kv"][0]), "w_out": f(inp["w_out"][0]),
        "wq": f(inp["wq_m"][0]), "wk": f(inp["wk_m"][0]), "wv": f(inp["wv_m"][0]), "w_if": f(inp["w_if"][0]),
        "cst": cst, "par": par, "fg": fg,
    }
    maps = []
    for b in range(8):
        m = dict(shared)
        m["x"] = f(inp["x"][b])
        m["mem"] = f(inp["mem"][b])
        maps.append(m)
    return maps


_NC_CACHE = {}


def kernel(**inputs):
    if "nc" not in _NC_CACHE:
        _NC_CACHE["nc"] = build_nc()
    nc = _NC_CACHE["nc"]
    maps = make_in_maps(inputs)
    res = run_bass_kernel_spmd(nc, maps, core_ids=list(range(8)))
    return np.stack([np.asarray(r["out"], dtype=np.float32) for r in res.results], axis=0)
```

```python
import math
from contextlib import ExitStack

import numpy as np
import concourse.bass as bass
import concourse.mybir as mybir
from concourse.bass_utils import run_bass_kernel_spmd

F32 = mybir.dt.float32
BF16 = mybir.dt.bfloat16
AF = mybir.ActivationFunctionType
ALU = mybir.AluOpType

S_LEN = 4096
D = 1024
NT = 32
NB = 8
EPS = 1e-6
LN_SQRT_DH = 0.5 * math.log(128.0)


class Sched:
    ENG = ("pe", "act", "dve", "pool", "sp")

    def __init__(self, nc, sems, dsems):
        self.nc = nc
        self.sems = sems
        self.dsems = dsems
        self.q = {e: [] for e in self.ENG}
        self.cnt = {e: 0 for e in self.ENG}
        self.last_w = {}
        self.readers = {}
        self.seen = {e: {} for e in self.ENG}
        self.n_dma = len(dsems)
        self.dma_cnt = [0] * self.n_dma
        half = self.n_dma // 2
        self.dma_pool = {"sp": list(range(0, half)), "pool": list(range(half, self.n_dma))}
        self.dma_rr = {"sp": 0, "pool": 0}

    def _need(self, eng, tok, waits):
        src, val = tok
        if src == eng and eng in ("pe", "sp"):
            return
        if self.seen[eng].get(src, 0) >= val:
            return
        self.seen[eng][src] = val
        waits[src] = max(waits.get(src, 0), val)

    def op(self, eng, fn, r=(), w=(), dma=False):
        waits = {}
        for k in r:
            t = self.last_w.get(k)
            if t is not None:
                self._need(eng, t, waits)
        for k in w:
            t = self.last_w.get(k)
            if t is not None:
                self._need(eng, t, waits)
            for t in self.readers.get(k, ()):
                self._need(eng, t, waits)
        if dma:
            pool_ = self.dma_pool[eng]
            i = pool_[self.dma_rr[eng] % len(pool_)]
            self.dma_rr[eng] += 1
            if self.dma_cnt[i] > 0:
                self._need(eng, (("d", i), 16 * self.dma_cnt[i]), waits)
            self.dma_cnt[i] += 1
            tok = (("d", i), 16 * self.dma_cnt[i])
        else:
            self.cnt[eng] += 1
            tok = (eng, self.cnt[eng])
        self.q[eng].append((list(waits.items()), fn, tok))
        for k in w:
            self.last_w[k] = tok
            self.readers[k] = []
        for k in r:
            self.readers.setdefault(k, []).append(tok)
        return tok

    def barrier(self):
        toks = [(e, self.cnt[e]) for e in self.ENG if self.cnt[e] > 0]
        toks += [(("d", i), 16 * c) for i, c in enumerate(self.dma_cnt) if c > 0]
        for e in self.ENG:
            waits = {}
            for t in toks:
                if t[0] != e:
                    self._need(e, t, waits)
            if waits:
                self.q[e].append((list(waits.items()), None, None))

    def emit(self, block):
        sems, dsems = self.sems, self.dsems

        def run(e, engobj):
            for waits, fn, tok in self.q[e]:
                for src, val in waits:
                    s = dsems[src[1]] if isinstance(src, tuple) else sems[src]
                    engobj.wait_ge(s, val)
                if fn is None:
                    continue
                ins = fn(engobj)
                if isinstance(tok[0], tuple):
                    ins.then_inc(dsems[tok[0][1]], 16)
                else:
                    ins.then_inc(sems[tok[0]], 1)
            self.q[e] = []

        @block.tensor
        def _(eng):
            run("pe", eng)

        @block.scalar
        def _(eng):
            run("act", eng)

        @block.vector
        def _(eng):
            run("dve", eng)

        @block.gpsimd
        def _(eng):
            run("pool", eng)

        @block.sync
        def _(eng):
            run("sp", eng)


def build_nc(stop_after=99, dbg=None):
    nc = bass.Bass("TRN2", target_bir_lowering=False)
    din = lambda n, s: nc.dram_tensor(n, s, F32, kind="ExternalInput").ap()
    x_d = din("x", [S_LEN, D])
    mem_d = din("mem", [256, D])
    win_d = din("w_in", [D, 4608])
    wkv_d = din("w_kv", [D, 1024])
    wout_d = din("w_out", [1536, D])
    wq_d = din("wq", [4, 128, 128])
    wk_d = din("wk", [4, 128, 128])
    wv_d = din("wv", [4, 128, 128])
    wif_d = din("w_if", [1536, 8])
    cst_d = din("cst", [128, 512])
    par_d = din("par", [128, 320])
    fg_d = din("fg", [128, D])
    out_d = nc.dram_tensor("out", [S_LEN, D], F32, kind="ExternalOutput").ap()
    dbg_outs = []
    win_v = win_d.rearrange("(c p) f -> p c f", p=128)

    top = ExitStack()
    with top:
        sb = lambda es, name, shape, dt: es.enter_context(nc.sbuf_tensor(name, shape, dt))
        ps = [top.enter_context(nc.psum_tensor("ps%d" % i, [128, 512], F32)) for i in range(8)]
        sems = {e: top.enter_context(nc.semaphore("s_" + e)) for e in Sched.ENG}
        dsems = [top.enter_context(nc.semaphore("d%d" % i)) for i in range(32)]
        S = Sched(nc, sems, dsems)
        P = lambda i: ("ps", i)

        def MM(out, lhsT, rhs, start, stop, r, w):
            return S.op("pe", lambda e: e.matmul(out, lhsT=lhsT, rhs=rhs, start=start, stop=stop,
                                                  skip_group_check=True), r=r, w=w)

        def TR(out, in_, r, w):
            return S.op("pe", lambda e: e.transpose(out=out, in_=in_, identity=ident[:]), r=list(r) + ["cst"], w=w)

        def ACT(out, in_, func, r, w, bias=None, scale=None, accum=None, eng="act"):
            kw = {}
            if bias is not None:
                kw["bias"] = bias
            if scale is not None:
                kw["scale"] = scale
            if accum is not None:
                kw["accum_out"] = accum
            return S.op("act", lambda e: e.activation(out=out, in_=in_, func=func, **kw), r=r, w=w)

        def TS(eng, out, in0, s1, s2, op0, op1, r, w):
            if op1 is None:
                return S.op(eng, lambda e: e.tensor_scalar(out=out, in0=in0, scalar1=s1, scalar2=None, op0=op0), r=r, w=w)
            return S.op(eng, lambda e: e.tensor_scalar(out=out, in0=in0, scalar1=s1, scalar2=s2, op0=op0, op1=op1), r=r, w=w)

        def TT(eng, out, in0, in1, op, r, w):
            return S.op(eng, lambda e: e.tensor_tensor(out=out, in0=in0, in1=in1, op=op), r=r, w=w)

        def STT(out, in0, scalar, in1, op0, op1, r, w, accum=None):
            if accum is not None:
                return S.op("dve", lambda e: e.scalar_tensor_tensor(out=out, in0=in0, scalar=scalar, in1=in1, op0=op0, op1=op1, accum_out=accum), r=r, w=w)
            return S.op("dve", lambda e: e.scalar_tensor_tensor(out=out, in0=in0, scalar=scalar, in1=in1, op0=op0, op1=op1), r=r, w=w)

        def CP(eng, out, in_, r, w):
            if eng == "act":
                return S.op("act", lambda e: e.copy(out=out, in_=in_), r=r, w=w)
            return S.op(eng, lambda e: e.tensor_copy(out=out, in_=in_), r=r, w=w)

        def MSET(eng, ap, val, w):
            return S.op(eng, lambda e: e.memset(ap, val), w=w)

        def RCP(out, in_, r, w):
            return S.op("dve", lambda e: e.reciprocal(out=out, in_=in_), r=r, w=w)

        def SCAN(out, d0, init, op0, r, w):
            return S.op("dve", lambda e: e.tensor_tensor_scan(out=out, data0=d0, data1=d0, initial=init, op0=op0, op1=ALU.bypass), r=r, w=w)

        def DMA(eng, out, in_, r, w):
            return S.op(eng, lambda e: e.dma_start(out=out, in_=in_), r=r, w=w, dma=True)

        def dump(es, name, ap, shape, dt=F32):
            d = nc.dram_tensor("dbg_" + name, list(shape), dt, kind="ExternalOutput").ap()
            S.barrier()
            dbg_outs.append(DMA("sp", d, ap, r=[], w=[]))

        def finish(block):
            S.barrier()
            S.emit(block)

        cstt = sb(top, "cstt", [128, 512], F32)
        ident = cstt[:, 0:128]
        triu = cstt[:, 128:256]
        M1 = cstt[:, 256:384]
        ones_f = cstt[:, 384:512]
        part = sb(top, "part", [128, 320], F32)
        mask_bf = sb(top, "mask_bf", [128, 128], BF16)
        ones_bf = sb(top, "ones_bf", [128, 128], BF16)
        ident_bf = sb(top, "ident_bf", [128, 128], BF16)
        kmT = sb(top, "kmT", [128, 4, 256], BF16)
        vm1 = sb(top, "vm1", [128, 2, 4, 129], BF16)
        ymT = sb(top, "ymT", [128, 4, S_LEN], BF16)
        hT = sb(top, "hT", [128, 8, S_LEN], BF16)
        wm_cm = nc.sbuf_tensor("wm_bf", [128, 8, 1536], BF16)
        wm_bf = wm_cm.__enter__()

        def rms_tile_a(es_bufs, src_rows, i):
            xtb, junk, ssb, xnb = es_bufs
            xt = xtb[i % 6]
            ss = ssb[:, 2 * (i % 6):2 * (i % 6) + 1]
            rs = ssb[:, 2 * (i % 6) + 1:2 * (i % 6) + 2]
            kx, ks = ("xt", i % 6), ("ss", i % 6)
            DMA("sp", xt[:], src_rows, r=[], w=[kx])
            ACT(junk[:], xt[:], AF.Square, r=[kx], w=["junk", ks], accum=ss)
            ACT(rs, ss, AF.Sqrt, r=[ks], w=[ks], bias=EPS, scale=1.0 / D)
            RCP(rs, rs, r=[ks], w=[ks])

        def rms_tile_b(es_bufs, gcol, dstT, col0, i):
            xtb, junk, ssb, xnb = es_bufs
            xt = xtb[i % 6]
            xn = xnb[i % 3]
            rs = ssb[:, 2 * (i % 6) + 1:2 * (i % 6) + 2]
            kx, kn, ks = ("xt", i % 6), ("xn", i % 3), ("ss", i % 6)
            ACT(xn[:], xt[:], AF.Copy, r=[kx, ks], w=[kn], scale=rs)
            for half in range(2):
                b = 2 * (i % 4) + half
                psb = ps[b][:].bitcast(BF16)
                for j in range(4):
                    c = 4 * half + j
                    S.op("pe", (lambda o, a: (lambda e: e.transpose(out=o, in_=a, identity=ident_bf[:])))(
                        psb[:, j * 128:(j + 1) * 128], xn[:, c * 128:(c + 1) * 128]), r=[kn, "ident_bf"], w=[P(b)])
                TT("dve", dstT[:, 4 * half:4 * half + 4, col0:col0 + 128],
                   psb[:, 0:512].rearrange("p (j t) -> p j t", t=128),
                   part[:, gcol + 4 * half:gcol + 4 * half + 4].unsqueeze(2).to_broadcast([128, 4, 128]),
                   ALU.mult, r=[P(b), "par"], w=[("hT", col0 // 512) if dstT is hT else "memT"])

        with nc.Block() as blk0, ExitStack() as es:
            DMA("sp", cstt[:], cst_d[:, :], r=[], w=["cst"])
            DMA("sp", part[:], par_d[:, :], r=[], w=["par"])
            CP("dve", mask_bf[:], triu, r=["cst"], w=["mask_bf"])
            CP("dve", ones_bf[:], ones_f, r=["cst"], w=["ones_bf"])
            CP("dve", ident_bf[:], ident, r=["cst"], w=["ident_bf"])
            xtb = [sb(es, "xt%d" % i, [128, D], F32) for i in range(6)]
            xnb = [sb(es, "xn%d" % i, [128, D], BF16) for i in range(3)]
            junk = sb(es, "junk", [128, D], BF16)
            ssb = sb(es, "ssb", [128, 12], F32)
            bufs = (xtb, junk, ssb, xnb)
            memT = sb(es, "memT", [128, 8, 256], BF16)
            wkv_bf = sb(es, "wkv_bf", [128, 8, 1024], BF16)
            wkv_v = wkv_d.rearrange("(c p) f -> p c f", p=128)
            for c in range(8):
                DMA("pool", wkv_bf[:, c, :], wkv_v[:, c, :], r=[], w=["wkv"])
            for c in range(8):
                DMA("pool", wm_bf[:, c, :], win_v[:, c, 0:1536], r=[], w=[("wm", c)])
            seq = [(mem_d[i * 128:(i + 1) * 128, :], 8, memT, i * 128) for i in range(2)]
            seq += [(x_d[i * 128:(i + 1) * 128, :], 0, hT, i * 128) for i in range(NT)]
            rms_tile_a(bufs, seq[0][0], 0)
            for k, (src_rows, gcol, dstT, col0) in enumerate(seq):
                if k + 1 < len(seq):
                    rms_tile_a(bufs, seq[k + 1][0], k + 1)
                rms_tile_b(bufs, gcol, dstT, col0, k)
            for h in range(4):
                b = 4 + h % 2
                for c in range(8):
                    MM(ps[b][:, 0:256], wkv_bf[:, c, h * 128:(h + 1) * 128], memT[:, c, :], c == 0, c == 7,
                       r=["wkv", "memT"], w=[P(b)])
                ACT(kmT[:, h, :], ps[b][:, 0:256], AF.Identity, r=[P(b)], w=["kmT"], scale=128.0 ** -0.5)
            MSET("pool", vm1[:, :, :, 128:129], 1.0, w=["vm1"])
            for mt in range(2):
                b = 6 + mt
                for c in range(8):
                    MM(ps[b][:], memT[:, c, mt * 128:(mt + 1) * 128], wkv_bf[:, c, 512:1024], c == 0, c == 7,
                       r=["wkv", "memT"], w=[P(b)])
                CP("dve", vm1[:, mt, :, 0:128], ps[b][:].rearrange("p (h d) -> p h d", d=128), r=[P(b)], w=["vm1"])
            if dbg == 0:
                dump(es, "hT", hT[:, :, 0:512], [128, 8, 512], BF16)
                dump(es, "kmT", kmT[:], [128, 4, 256], BF16)
                dump(es, "vm1", vm1[:], [128, 2, 4, 129], BF16)
            finish(blk0)
        if stop_after == 0:
            return nc

        with nc.Block() as blk1, ExitStack() as es:
            wqkv = sb(es, "wqkv", [128, 3, 4, 128], BF16)
            for j, wd in enumerate((wq_d, wk_d, wv_d)):
                DMA("pool", wqkv[:, j, :, :], wd.rearrange("h d e -> d h e"), r=[], w=["wqkv"])
            wif_bf = sb(es, "wif_bf", [128, 12, 8], BF16)
            DMA("pool", wif_bf[:], wif_d.rearrange("(j p) g -> p j g", p=128), r=[], w=["wif"])
            ABf = sb(es, "ABf", [128, 2, 4, 8], BF16)
            wTs = [sb(es, "wTs%d" % i, [128, 128], BF16) for i in range(3)]
            psb4 = ps[4][:].bitcast(BF16)
            for h in range(4):
                for j in range(3):
                    S.op("pe", (lambda o, a: (lambda e: e.transpose(out=o, in_=a, identity=ident_bf[:])))(
                        psb4[:, j * 128:(j + 1) * 128], wqkv[:, j, h, :]), r=["wqkv", "ident_bf"], w=[P(4)])
                    CP("dve", wTs[j][:], psb4[:, j * 128:(j + 1) * 128], r=[P(4)], w=[("wTs", j)])
                MM(ps[5][:, h * 8:(h + 1) * 8], wTs[0][:], wif_bf[:, h, :], True, False, r=[("wTs", 0), "wif"], w=[P(5)])
                MM(ps[5][:, h * 8:(h + 1) * 8], wTs[1][:], wif_bf[:, 4 + h, :], False, True, r=[("wTs", 1), "wif"], w=[P(5)])
                MM(ps[5][:, 32 + h * 8:32 + (h + 1) * 8], wTs[2][:], wif_bf[:, 8 + h, :], True, True, r=[("wTs", 2), "wif"], w=[P(5)])
            CP("dve", ABf[:].rearrange("p a h g -> p (a h g)"), ps[5][:, 0:64], r=[P(5)], w=["ABf"])
            xmb = [sb(es, "xm%d" % i, [128, 4, 515], BF16) for i in range(2)]
            xcv = sb(es, "xcv", [128, 4, 512], BF16)
            xs = sb(es, "xs", [128, 4, 512], BF16)
            qT = sb(es, "qT", [128, 4, 512], BF16)
            kT = sb(es, "kT", [128, 4, 512], BF16)
            vT = None
            zs = sb(es, "zs", [128, 4, 512], BF16)
            sgo = sb(es, "sgo", [128, 4, 512], BF16)
            Dg = sb(es, "Dg", [128, 16, 128], BF16)
            for hj in range(16):
                TS("dve", Dg[:, hj, :], ident, part[:, 16 + hj:16 + hj + 1], None, ALU.mult, None, r=["cst", "par"], w=["Dg"])
            Gtok = sb(es, "Gtok", [128, 32, 8], F32)
            CONVW, CONVB, MG, SK = 16, 32, 36, 40
            rot = [0]

            def nextbank():
                rot[0] ^= 1
                return rot[0]

            def frontA(blk):
                xm = xmb[blk % 2]
                kxm = ("xm", blk % 2)
                cols = slice(blk * 512, (blk + 1) * 512)
                if blk == 0:
                    MSET("pool", xm[:, :, 0:3], 0.0, w=[kxm])
                else:
                    CP("pool", xm[:, :, 0:3], xmb[(blk - 1) % 2][:, :, 512:515],
                       r=[("xm", (blk - 1) % 2)] + [(("xm", (blk - 1) % 2), hh) for hh in range(4)], w=[kxm])
                for h in range(4):
                    b = nextbank()
                    for c in range(8):
                        MM(ps[b][:], wm_bf[:, c, h * 128:(h + 1) * 128], hT[:, c, cols], c == 0, c == 7,
                           r=[("wm", c), ("hT", blk)], w=[P(b)])
                    CP("act", xm[:, h, 3:515], ps[b][:], r=[P(b)], w=[(kxm, h)])
                return xm, kxm

            def frontBC(blk, need_v, do_qk=True):
                xm = xmb[blk % 2]
                kxm = ("xm", blk % 2)
                for h in range(4):
                    b = nextbank()
                    for j in range(4):
                        MM(ps[b][:], Dg[:, 4 * h + j, :], xm[:, h, j:j + 512], j == 0, j == 3,
                           r=["Dg", (kxm, h), kxm], w=[P(b)])
                    ACT(xcv[:, h, :], ps[b][:], AF.Silu, r=[P(b), "par"], w=[("xcv", h)],
                        bias=part[:, CONVB + h:CONVB + h + 1])
                for h in range(4 if do_qk else 0):
                    for j, dst, kd in ((0, qT, "qT"), (1, kT, "kT")) + (((2, vT, "vT"),) if need_v else ()):
                        b = nextbank()
                        src = xcv[:, h, :] if j < 2 else xm[:, h, 3:515]
                        MM(ps[b][:], wqkv[:, j, h, :], src, True, True, r=["wqkv", ("xcv", h) if j < 2 else (kxm, h)], w=[P(b)])
                        CP("act" if j == 0 else "dve", dst[:, h, :], ps[b][:], r=[P(b)], w=[kd])
                return xm, kxm

            for blk in range(NB):
                if blk == 0:
                    frontA(0)
                frontBC(blk, False, do_qk=False)
                if blk + 1 < NB:
                    frontA(blk + 1)
                xm_ = xmb[blk % 2]
                kxm_ = ("xm", blk % 2)
                for tt in range(4):
                    tc_ = slice(tt * 128, (tt + 1) * 128)
                    for h in range(4):
                        MM(ps[2][:, tt * 8:(tt + 1) * 8], xcv[:, h, tc_], ABf[:, 0, h, :], h == 0, False,
                           r=[("xcv", h), "ABf"], w=[P(2)])
                        MM(ps[2][:, tt * 8:(tt + 1) * 8], xm_[:, h, 3 + tt * 128:3 + (tt + 1) * 128], ABf[:, 1, h, :], False, h == 3,
                           r=[(kxm_, h), "ABf"], w=[P(2)])
                CP("dve", Gtok[:, blk * 4:(blk + 1) * 4, :], ps[2][:, 0:32].rearrange("p (t g) -> p t g", g=8),
                   r=[P(2)], w=["Gtok"])
            if dbg and dbg >= 1:
                dump(es, "Gtok", Gtok[:], [128, 32, 8])
                dump(es, "qT", qT[:], [128, 4, 512], BF16)
                dump(es, "xcv", xcv[:], [128, 4, 512], BF16)

            rows = sb(es, "rows", [128, 8, 128], F32)
            GIr, GFr, LF, Fp, U_, W_, W2_, TH_ = [rows[:, i, :] for i in range(8)]
            cols_ = sb(es, "colsb", [128, 16], F32)
            col = lambda i: cols_[:, i:i + 1]
            rowt = sb(es, "rowt", [1, 4, 128], F32)
            Ebc = sb(es, "Ebc", [128, 128], F32)
            Wt = sb(es, "Wt", [128, 3, 128], F32)
            BI, BFc = 48, 49
            Gv = Gtok[:].rearrange("p c g -> p (c g)")
            gsp = sb(es, "gsp", [128, 2, 128], F32)
            CP("dve", gsp[:, 0, :].rearrange("p (c h) -> p c h", h=4), Gtok[:, :, 0:4], r=["Gtok"], w=["gsp"])
            CP("dve", gsp[:, 1, :].rearrange("p (c h) -> p c h", h=4), Gtok[:, :, 4:8], r=["Gtok"], w=["gsp"])
            TR(ps[3][:, 0:128], gsp[:, 0, :], r=["gsp"], w=[P(3)])
            TR(ps[3][:, 128:256], gsp[:, 1, :], r=["gsp"], w=[P(3)])
            CP("dve", rows[:, 0:2, :], ps[3][:, 0:256].rearrange("p (a t) -> p a t", t=128), r=[P(3)], w=["rows"])
            TS("dve", col(0), part[:, BFc:BFc + 1], -1.0, None, ALU.mult, None, r=["par"], w=["cols"])
            ACT(LF, GFr, AF.Exp, r=["rows", "cols"], w=["rows"], bias=col(0), scale=-1.0)
            ACT(LF, LF, AF.Ln, r=["rows"], w=["rows"], bias=1.0)
            SCAN(Fp, LF, 0.0, ALU.add, r=["rows"], w=["rows"])
            MM(ps[3][:, 256:257], M1, rows[:, 3, 127:128], True, True, r=["cst", "rows"], w=[P(3)])
            CP("dve", col(1), ps[3][:, 256:257], r=[P(3)], w=["cols"])
            TS("dve", Fp, Fp, col(1), None, ALU.add, None, r=["rows", "cols"], w=["rows"])
            STT(GIr, GIr, part[:, BI:BI + 1], Fp, ALU.add, ALU.add, r=["rows", "par"], w=["rows"])
            SCAN(U_, GIr, 0.0, ALU.max, r=["rows"], w=["rows"])
            MM(ps[3][0:1, 384:512], rows[:, 4, 127:128], ident, True, True, r=["rows", "cst"], w=[P(3)])
            CP("dve", rowt[:, 0, :], ps[3][0:1, 384:512], r=[P(3)], w=["rowt"])
            for h in range(4):
                SCAN(rowt[:, 1, :].rearrange("p (c h) -> p h c", h=4)[:, h, :],
                     rowt[:, 0, :].rearrange("p (c h) -> p h c", h=4)[:, h, :], 0.0, ALU.max, r=["rowt"], w=["rowt"])
            MSET("dve", rowt[:, 2, 0:4], 0.0, w=["rowt"])
            CP("dve", rowt[:, 2, 4:128], rowt[:, 1, 0:124], r=["rowt"], w=["rowt"])
            TT("dve", rowt[:, 3, :], rowt[:, 2, :], rowt[:, 1, :], ALU.subtract, r=["rowt"], w=["rowt"])
            MM(ps[3][:, 257:258], rowt[:, 2, :], ones_f[0:1, 0:1], True, True, r=["rowt", "cst"], w=[P(3)])
            MM(ps[3][:, 258:259], rowt[:, 3, :], ones_f[0:1, 0:1], True, True, r=["rowt", "cst"], w=[P(3)])
            CP("dve", cols_[:, 2:4], ps[3][:, 257:259], r=[P(3)], w=["cols"])
            MM(ps[4][:, 0:128], ones_f[0:1, :], rowt[:, 3, :], True, True, r=["rowt", "cst"], w=[P(4)])
            ACT(Ebc[:], ps[4][:, 0:128], AF.Exp, r=[P(4)], w=["Ebc"])
            TS("dve", col(4), col(2), -1.0, -LN_SQRT_DH, ALU.mult, ALU.add, r=["cols"], w=["cols"])
            TS("dve", col(5), col(2), -1.0, None, ALU.mult, None, r=["cols"], w=["cols"])
            TT("dve", col(6), col(4), col(3), ALU.add, r=["cols"], w=["cols"])
            ACT(W_, GIr, AF.Exp, r=["rows", "cols"], w=["rows"], bias=col(4))
            ACT(W2_, GIr, AF.Exp, r=["rows", "cols"], w=["rows"], bias=col(6))
            ACT(TH_, Fp, AF.Exp, r=["rows", "cols"], w=["rows"], bias=col(5))
            for i in range(3):
                TR(ps[4][:, 128 * (i + 1):128 * (i + 2)], rows[:, 5 + i, :], r=["rows"], w=[P(4)])
            CP("dve", Wt[:], ps[4][:, 128:512].rearrange("p (a t) -> p a t", t=128), r=[P(4)], w=["Wt"])
            if dbg and dbg >= 2:
                dump(es, "rows", rows[:], [128, 8, 128])
                dump(es, "Wt", Wt[:], [128, 3, 128])
                dump(es, "Ebc", Ebc[:], [128, 128])
                dump(es, "rowt", rowt[:], [1, 4, 128])

            kwb = [sb(es, "kw%d" % i, [128, 4, 128], BF16) for i in range(2)]
            V1b = [sb(es, "V1_%d" % i, [128, 4, 129], BF16) for i in range(2)]
            sqkb = [sb(es, "sqk%d" % i, [128, 4, 128], BF16) for i in range(2)]
            ABsb = [sb(es, "ABs%d" % i, [128, 4, 129], F32) for i in range(2)]
            lnbb = [sb(es, "lnb%d" % i, [128, 4, 128], F32) for i in range(2)]
            Cn = sb(es, "Cn", [128, 4, 129], F32)
            Cnb = sb(es, "Cnb", [128, 4, 129], BF16)
            hsb = [sb(es, "hs%d" % i, [128, 4, 128], F32) for i in range(2)]
            smb = [sb(es, "sm%d" % i, [128, 64], F32) for i in range(2)]
            ymt = sb(es, "ymt", [128, 4, 128], F32)
            mhalf = sb(es, "mhalf", [128, 4], F32)
            MSET("pool", mhalf[:], -0.5, w=["mhalf"])
            for i in range(2):
                MSET("pool", V1b[i][:, :, 128:129], 1.0, w=[("V1", i)])
            MSET("pool", Cn[:], 0.0, w=["Cn"])
            MSET("pool", Cnb[:], 0.0, w=["Cnb"])

            def pre(c_, xm, kxm):
                tt, p = c_ % 4, c_ % 2
                tc_ = slice(tt * 128, (tt + 1) * 128)
                ch4 = slice(c_ * 4, c_ * 4 + 4)
                for h in range(4):
                    MM(ps[2][:, h * 128:(h + 1) * 128], xcv[:, h, tc_], wqkv[:, 1, h, :], True, True,
                       r=[("xcv", h), "wqkv"], w=[P(2)])
                    MM(ps[3][:, h * 128:(h + 1) * 128], xm[:, h, 3 + tt * 128:3 + (tt + 1) * 128], wqkv[:, 2, h, :],
                       True, True, r=[(kxm, h), "wqkv"], w=[P(3)])
                    MM(ps[4][:, h * 128:(h + 1) * 128], kT[:, h, tc_], qT[:, h, tc_], True, True,
                       r=["kT", "qT"], w=[P(4)])
                for h in range(4):
                    ACT(kwb[p][:, h, :], ps[2][:, h * 128:(h + 1) * 128], AF.Copy, r=[P(2), "Wt"], w=[("kw", p)],
                        scale=Wt[:, 1, c_ * 4 + h:c_ * 4 + h + 1])
                CP("act", V1b[p][:, :, 0:128], ps[3][:].rearrange("p (h d) -> p h d", d=128), r=[P(3)], w=[("V1", p)])
                for h in range(4):
                    STT(sqkb[p][:, h, :], ps[4][:, h * 128:(h + 1) * 128], Wt[:, 0, c_ * 4 + h:c_ * 4 + h + 1], triu,
                        ALU.mult, ALU.mult, r=[P(4), "Wt", "cst"], w=[("sqk", p)])

            def rec(c_):
                tt, p = c_ % 4, c_ % 2
                tc_ = slice(tt * 128, (tt + 1) * 128)
                kw, V1, sqk = kwb[p], V1b[p], sqkb[p]
                for hp in range(2):
                    ab = ps[5 + hp]
                    for hh in range(2):
                        h = 2 * hp + hh
                        MM(ab[:, hh * 129:(hh + 1) * 129], sqk[:, h, :], V1[:, h, :], hh == 0, False,
                           r=[("sqk", p), ("V1", p)], w=[P(5 + hp)])
                        MM(ab[:, hh * 129:(hh + 1) * 129], qT[:, h, tc_], Cnb[:, h, :], False, True,
                           r=["qT", "Cnb"], w=[P(5 + hp)])
                    for hh in range(2):
                        h = 2 * hp + hh
                        MM(ps[7][:, hh * 129:(hh + 1) * 129], kw[:, h, :], V1[:, h, :], hh == 0, True,
                           r=[("kw", p), ("V1", p)], w=[P(7)])
                    for hh in range(2):
                        h = 2 * hp + hh
                        STT(Cn[:, h, :], Cn[:, h, :], Ebc[:, c_ * 4 + h:c_ * 4 + h + 1], ps[7][:, hh * 129:(hh + 1) * 129],
                            ALU.mult, ALU.add, r=["Cn", "Ebc", P(7)], w=["Cn"])
                    CP("dve", Cnb[:, 2 * hp:2 * hp + 2, :], Cn[:, 2 * hp:2 * hp + 2, :], r=["Cn"], w=["Cnb"])
                    CP("act", ABsb[p][:, 2 * hp:2 * hp + 2, :], ab[:, 0:258].rearrange("p (h d) -> p h d", d=129),
                       r=[P(5 + hp)], w=[("ABs", p)])

            def epi_a(c_):
                tt, p = c_ % 4, c_ % 2
                A = ABsb[p]
                sm_ = smb[p]
                hs_ = hsb[p]
                den = sm_[:, 0:4]
                kd, kmv, krs = ("den", p), ("mv", p), ("rstd4", p)
                STT(den, A[:, :, 128], -1.0, A[:, :, 128], ALU.mult, ALU.max, r=[("ABs", p)], w=[kd])
                TT("dve", den, den, Wt[:, 2, c_ * 4:c_ * 4 + 4], ALU.max, r=[kd, "Wt"], w=[kd])
                RCP(den, den, r=[kd], w=[kd])
                mvv = sm_[:, 48:56].rearrange("p (h t) -> p h t", t=2)
                for h in range(4):
                    STT(hs_[:, h, :], A[:, h, 0:128], sm_[:, h:h + 1], sgo[:, tt, h * 128:(h + 1) * 128], ALU.mult, ALU.mult,
                        r=[("ABs", p), kd, ("sgo", tt)], w=[("hs", p, h)])
                    st = sm_[:, 16 + 8 * h:16 + 8 * h + 6]
                    S.op("dve", (lambda o, i: (lambda e: e.bn_stats(out=o, in_=i)))(st, hs_[:, h, :]), r=[("hs", p, h)], w=[("st", p, h)])
                    S.op("dve", (lambda o, i: (lambda e: e.bn_aggr(out=o, in_=i)))(mvv[:, h, :], st), r=[("st", p, h)], w=[kmv])
                rstd4 = sm_[:, 56:60]
                nmr4 = sm_[:, 60:64]
                TS("pool", rstd4, mvv[:, :, 1], EPS, None, ALU.add, None, r=[kmv], w=[krs])
                TT("pool", rstd4, rstd4, mhalf[:], ALU.pow, r=[krs, "mhalf"], w=[krs])
                TS("pool", nmr4, mvv[:, :, 0], -1.0, None, ALU.mult, None, r=[kmv], w=[("nmr4", p)])
                TT("pool", nmr4, nmr4, rstd4, ALU.mult, r=[("nmr4", p), krs], w=[("nmr4", p)])

            def epi_b(c_):
                p = c_ % 2
                sm_ = smb[p]
                for h in range(4):
                    ACT(lnbb[p][:, h, :], hsb[p][:, h, :], AF.Identity, r=[("hs", p, h), ("rstd4", p), ("nmr4", p)], w=[("lnb", p)],
                        scale=sm_[:, 56 + h:57 + h], bias=sm_[:, 60 + h:61 + h])

            def outp(c_):
                tt, p = c_ % 4, c_ % 2
                blk_ = c_ // 4
                tc_ = slice(tt * 128, (tt + 1) * 128)
                b = nextbank()
                for h in range(4):
                    TR(ps[b][:, h * 128:(h + 1) * 128], lnbb[p][:, h, :], r=[("lnb", p)], w=[P(b)])
                for h in range(4):
                    STT(ymt[:, h, :], ps[b][:, h * 128:(h + 1) * 128], part[:, MG + h:MG + h + 1], xs[:, h, tc_],
                        ALU.mult, ALU.add, r=[P(b), "par", "xs"], w=[("ymt", h)])
                    TT("pool", ymT[:, h, blk_ * 512 + tt * 128:blk_ * 512 + (tt + 1) * 128], ymt[:, h, :], zs[:, h, tc_],
                       ALU.mult, r=[("ymt", h), "zs"], w=["ymT"])

            for blk in range(NB if stop_after > 1 else 1):
                if blk == 0:
                    nxt = frontA(0)
                xm, kxm = nxt
                frontBC(blk, False)
                cols = slice(blk * 512, (blk + 1) * 512)
                for h in range(4):
                    b = nextbank()
                    for c in range(8):
                        MM(ps[b][:], wm_bf[:, c, 1024 + h * 128:1024 + (h + 1) * 128], hT[:, c, cols], c == 0, c == 7,
                           r=[("wm", c), ("hT", blk)], w=[P(b)])
                    ACT(zs[:, h, :], ps[b][:], AF.Silu, r=[P(b)], w=["zs"])
                    ACT(xs[:, h, :], xcv[:, h, :], AF.Copy, scale=part[:, SK + h:SK + h + 1],
                       r=[("xcv", h), "par"], w=["xs"])
                for tt in range(4):
                    b = nextbank()
                    for c in range(8):
                        MM(ps[b][:], hT[:, c, blk * 512 + tt * 128:blk * 512 + (tt + 1) * 128], wm_bf[:, c, 512:1024],
                           c == 0, c == 7, r=[("wm", c), ("hT", blk)], w=[P(b)])
                    ACT(sgo[:, tt, :], ps[b][:], AF.Sigmoid, r=[P(b)], w=[("sgo", tt)])
                c0 = blk * 4
                pre(c0, xm, kxm)
                for tt in range(4):
                    c_ = c0 + tt
                    if tt < 3:
                        pre(c_ + 1, xm, kxm)
                    rec(c_)
                    if tt == 1 and blk + 1 < (NB if stop_after > 1 else 1):
                        nxt = frontA(blk + 1)
                    epi_a(c_)
                    if tt > 0:
                        epi_b(c_ - 1)
                        outp(c_ - 1)
                epi_b(c0 + 3)
                outp(c0 + 3)
            if dbg and dbg >= 3:
                dump(es, "ymT", ymT[:, :, 0:512], [128, 4, 512], BF16)
                dump(es, "hs", hsb[0][:], [128, 4, 128])
                dump(es, "Cn", Cn[:], [128, 4, 129])
            finish(blk1)
        wm_cm.__exit__(None, None, None)
        if stop_after <= 1:
            return nc

        rot2 = [0]

        def nb2():
            rot2[0] ^= 1
            return rot2[0]

        GROUPS = [(3 * g, 3 * g + 3) for g in range(10)] + [(30, 32)]
        DG = 44

        def proj_fm(wt, j, dst, blk, func, scale, kd, perblk=False):
            b = nb2()
            cols = slice(blk * 512, (blk + 1) * 512)
            wk_ = (kd, blk) if perblk else kd
            for c in range(8):
                MM(ps[b][:], wt[:, j, c, :], hT[:, c, cols], c == 0, c == 7, r=[kd + "_w", ("hT", blk)], w=[P(b)])
            if func is None:
                CP("dve", dst[:, cols], ps[b][:], r=[P(b)], w=[wk_])
            else:
                ACT(dst[:, cols], ps[b][:], func, r=[P(b)], w=[wk_], scale=scale)

        ydT = sb(top, "ydT", [128, 4, S_LEN], BF16)
        with nc.Block() as blk2, ExitStack() as es:
            lamt = sb(es, "lamt", [128, 8], F32)
            j64 = sb(es, "j64", [128, 64], F32)
            STT(j64[:], part[:, 64:128], 1.0, part[:, 128:192], ALU.mult, ALU.mult, r=["par"], w=["j64", "lam"], accum=lamt[:, 0:1])
            STT(j64[:], part[:, 192:256], 1.0, part[:, 256:320], ALU.mult, ALU.mult, r=["par", "j64"], w=["j64", "lam"], accum=lamt[:, 1:2])
            ACT(lamt[:, 2:4], lamt[:, 0:2], AF.Exp, r=["lam"], w=["lam"])
            TT("dve", lamt[:, 4:5], lamt[:, 2:3], lamt[:, 3:4], ALU.subtract, r=["lam"], w=["lam"])
            TS("dve", lamt[:, 5:6], lamt[:, 4:5], 0.2, -1.0, ALU.add, ALU.mult, r=["lam"], w=["lam"])
            wdb = [sb(es, "wd%d" % i, [128, 4, 8, 128], BF16) for i in range(2)]
            qdT = sb(es, "qdT", [128, S_LEN], BF16)
            kpd = [sb(es, "kpd%d" % i, [128, S_LEN], BF16) for i in range(2)]
            MSET("pool", kpd[0][64:128, :], 0.0, w=["kd"])
            MSET("pool", kpd[1][0:64, :], 0.0, w=["kd"])
            zds = sb(es, "zds", [128, S_LEN], BF16)
            V1d = sb(es, "V1d", [128, 32, 128], BF16)
            ptb = [sb(es, "pt%d" % i, [128, 512], BF16) for i in range(4)]
            o0s = sb(es, "o0s", [128, 512], F32)
            o1s = sb(es, "o1s", [128, 512], F32)
            l0s = sb(es, "l0s", [128, 512], F32)
            l1s = sb(es, "l1s", [128, 512], F32)
            sqb = sb(es, "sqb", [128, 512], BF16)
            rsd = l0s
            dg08 = sb(es, "dg08", [128, 4], F32)
            TS("dve", dg08[:], part[:, DG:DG + 4], 0.8, None, ALU.mult, None, r=["par"], w=["dg08"])
            GROUPS4 = [(4 * g, 4 * g + 4) for g in range(8)]

            def proj_k(wt, blk):
                b = nb2()
                cols = slice(blk * 512, (blk + 1) * 512)
                for c in range(8):
                    MM(ps[b][:], wt[:, 1, c, :], hT[:, c, cols], c == 0, c == 7, r=["kd_w", ("hT", blk)], w=[P(b)])
                CP("dve", kpd[0][0:64, cols], ps[b][0:64, :], r=[P(b)], w=["kd"])
                CP("dve", kpd[1][64:128, cols], ps[b][64:128, :], r=[P(b)], w=["kd"])
            for h in range(4):
                wd = wdb[h % 2]
                for j, off, kk in ((0, 1536, "qd_w"), (1, 2048, "kd_w"), (3, 3072, "zd_w"), (2, 2560, "vd_w")):
                    DMA("pool", wd[:, j, :, :], win_v[:, :, off + h * 128:off + (h + 1) * 128], r=[], w=[kk])
                for blk in range(NB):
                    proj_fm(wd, 0, qdT, blk, AF.Identity, 0.125, "qd")
                    proj_k(wd, blk)
                    proj_fm(wd, 3, zds, blk, AF.Silu, None, "zd")
                    for tt in range(4):
                        tl = blk * 4 + tt
                        for c in range(8):
                            MM(ps[3][:, tt * 128:(tt + 1) * 128], hT[:, c, tl * 128:(tl + 1) * 128], wd[:, 2, c, :], c == 0, c == 7,
                               r=["vd_w", ("hT", blk)], w=[P(3)])
                    CP("dve", V1d[:, blk * 4:(blk + 1) * 4, :], ps[3][:].rearrange("p (t d) -> p t d", d=128), r=[P(3)], w=["V1d"])
                steps = [(g, qs, qe, m, kb) for g, (qs, qe) in enumerate(GROUPS4) for m in range(2) for kb in range(qe)]
                LA = 3
                sctr = [0]

                def emit_qk(i):
                    g, qs, qe, m, kb = steps[i]
                    nsub = qe - qs
                    pr = slice(m * 64, (m + 1) * 64)
                    sv = max(0, kb - qs)
                    sbk = sctr[0] % 4
                    sctr[0] += 1
                    fc = slice(sv * 128, nsub * 128)
                    MM(ps[sbk][:, fc], kpd[m][:, kb * 128:(kb + 1) * 128], qdT[:, (qs + sv) * 128:qe * 128], True, True,
                       r=["kd", "qd"], w=[P(sbk)])
                    pt = ptb[i % 4]
                    kpt = ("pt", i % 4)
                    ACT(pt[:, fc], ps[sbk][:, fc], AF.Exp, r=[P(sbk)], w=[kpt])
                    if kb >= qs:
                        dc = slice((kb - qs) * 128, (kb - qs + 1) * 128)
                        TT("dve", pt[:, dc], pt[:, dc], mask_bf[:], ALU.mult, r=[kpt, "mask_bf"], w=[kpt])

                def emit_pv(i):
                    g, qs, qe, m, kb = steps[i]
                    nsub = qe - qs
                    sv = max(0, kb - qs)
                    fc = slice(sv * 128, nsub * 128)
                    pt = ptb[i % 4]
                    kpt = ("pt", i % 4)
                    MM(ps[4 + m][:, fc], V1d[:, kb, :], pt[:, fc], kb == 0, kb == qe - 1, r=[kpt, "V1d"], w=[P(4 + m)])
                    MM(ps[6 + m][:, fc], ones_bf[:], pt[:, fc], kb == 0, kb == qe - 1, r=[kpt, "ones_bf"], w=[P(6 + m)])
                    if m == 1 and kb == qe - 1:
                        epilogue1(qs, qe)
                        pending.append((i + 12, qs, qe))

                pending = []

                def epilogue1(qs, qe):
                    CP("act", o0s[:], ps[4][:], r=[P(4)], w=["o0s"])
                    CP("dve", l0s[:], ps[6][:], r=[P(6)], w=["l0s"])
                    CP("act", o1s[:], ps[5][:], r=[P(5)], w=["o1s"])
                    CP("dve", l1s[:], ps[7][:], r=[P(7)], w=["l1s"])
                    RCP(l0s[:], l0s[:], r=["l0s"], w=["l0s"])
                    RCP(l1s[:], l1s[:], r=["l1s"], w=["l1s"])
                    TT("dve", o0s[:], o0s[:], l0s[:], ALU.mult, r=["o0s", "l0s"], w=["o0s"])
                    TT("dve", o1s[:], o1s[:], l1s[:], ALU.mult, r=["o1s", "l1s"], w=["o1s"])
                    STT(o0s[:], o1s[:], lamt[:, 5:6], o0s[:], ALU.mult, ALU.add, r=["o0s", "o1s", "lam"], w=["o0s"])

                def epilogue2(qs, qe):
                    cols = slice(qs * 128, qe * 128)
                    ACT(sqb[:], o0s[:], AF.Square, r=["o0s"], w=["sqb"])
                    eb = sctr[0] % 4
                    sctr[0] += 1
                    MM(ps[eb][:], ones_bf[:], sqb[:], True, True, r=["sqb", "ones_bf"], w=[P(eb)])
                    ACT(rsd[:], ps[eb][:], AF.Sqrt, r=[P(eb)], w=["l0s"], bias=EPS, scale=1.0 / 128)
                    RCP(rsd[:], rsd[:], r=["l0s"], w=["l0s"])
                    STT(o0s[:], o0s[:], dg08[:, h:h + 1], rsd[:], ALU.mult, ALU.mult, r=["o0s", "dg08", "l0s"], w=["o0s"])
                    TT("dve", ydT[:, h, cols], o0s[:], zds[:, cols], ALU.mult, r=["o0s", "zd"], w=["ydT"])

                for i in range(len(steps) + LA):
                    if i < len(steps):
                        emit_qk(i)
                    if i >= LA:
                        emit_pv(i - LA)
                    while pending and pending[0][0] <= i - LA:
                        _, pqs, pqe = pending.pop(0)
                        epilogue2(pqs, pqe)
                while pending:
                    _, pqs, pqe = pending.pop(0)
                    epilogue2(pqs, pqe)
            if dbg and dbg >= 4:
                dump(es, "ydT", ydT[:, :, 0:1024], [128, 4, 1024], BF16)
                dump(es, "lamt", lamt[:], [128, 8])
            finish(blk2)
        if stop_after <= 2:
            return nc

        ycT = sb(top, "ycT", [128, 4, S_LEN], BF16)
        with nc.Block() as blk3, ExitStack() as es:
            wcb = [sb(es, "wc%d" % i, [128, 2, 8, 128], BF16) for i in range(2)]
            qcT = sb(es, "qcT", [128, S_LEN], BF16)
            zcs = sb(es, "zcs", [128, S_LEN], BF16)
            ptb = [sb(es, "ptc%d" % i, [128, 512], BF16) for i in range(4)]
            rlb = [sb(es, "rl%d" % i, [128, 512], F32) for i in range(2)]
            onb = [sb(es, "on%d" % i, [128, 512], F32) for i in range(2)]
            for h in range(4):
                wc = wcb[h % 2]
                for j, off, kk in ((0, 3584, "qc_w"), (1, 4096, "zc_w")):
                    DMA("pool", wc[:, j, :, :], win_v[:, :, off + h * 128:off + (h + 1) * 128], r=[], w=[kk])

                def c_qk(j):
                    cols = slice(j * 512, (j + 1) * 512)
                    for mc in range(2):
                        sbk = 2 * (j % 2) + mc
                        sb_ = 2 + mc
                        MM(ps[sb_][:], kmT[:, h, mc * 128:(mc + 1) * 128], qcT[:, cols], True, True, r=["kmT", ("qc", j)], w=[P(sb_)])
                        ACT(ptb[sbk][:], ps[sb_][:], AF.Exp, r=[P(sb_)], w=[("ptc", sbk)])

                def c_pv(j):
                    cols = slice(j * 512, (j + 1) * 512)
                    p = j % 2
                    for mc in range(2):
                        sbk = 2 * p + mc
                        MM(ps[4 + p][:], vm1[:, mc, h, 0:128], ptb[sbk][:], mc == 0, mc == 1, r=[("ptc", sbk), "vm1"], w=[P(4 + p)])
                    for mc in range(2):
                        sbk = 2 * p + mc
                        MM(ps[6 + p][:], ones_bf[:], ptb[sbk][:], mc == 0, mc == 1, r=[("ptc", sbk), "ones_bf"], w=[P(6 + p)])
                    RCP(rlb[p][:], ps[6 + p][:], r=[P(6 + p)], w=[("rl", p)])
                    TT("dve", onb[p][:], ps[4 + p][:], rlb[p][:], ALU.mult, r=[P(4 + p), ("rl", p)], w=[("on", p)])
                    TT("pool", ycT[:, h, cols], onb[p][:], zcs[:, cols], ALU.mult, r=[("on", p), ("zc", j)], w=["ycT"])

                for j in range(NB + 1):
                    if j < NB:
                        proj_fm(wc, 0, qcT, j, AF.Copy, None, "qc", perblk=True)
                        proj_fm(wc, 1, zcs, j, AF.Silu, None, "zc", perblk=True)
                        c_qk(j)
                    if j >= 1:
                        c_pv(j - 1)
            if dbg and dbg >= 5:
                dump(es, "ycT", ycT[:, :, 0:1024], [128, 4, 1024], BF16)
            finish(blk3)
        if stop_after <= 3:
            return nc

        with nc.Block() as blk4, ExitStack() as es:
            hflat = hT[:].rearrange("p c t -> p (c t)")
            wout_bf = hflat[:, 0:12 * D].rearrange("p (c f) -> p c f", f=D)
            wout_v = wout_d.rearrange("(c p) f -> p c f", p=128)
            for c in range(12):
                DMA("pool", wout_bf[:, c, :], wout_v[:, c, :], r=[], w=[("wout", c)])
            hf32 = hflat[:, 12 * D:].bitcast(F32)
            NXB = 9
            xtb = [hf32[:, i * D:(i + 1) * D] for i in range(NXB)]
            fgt = hf32[:, NXB * D:(NXB + 1) * D]
            DMA("sp", fgt, fg_d[:, :], r=[], w=["fg"])
            junk = sb(es, "junk4", [128, D], BF16)
            ss4 = sb(es, "ss4", [128, 2 * NXB], F32)
            ysrc = [(ymT, hh) for hh in range(4)] + [(ydT, hh) for hh in range(4)] + [(ycT, hh) for hh in range(4)]

            def load_x(i):
                DMA("sp", xtb[i % NXB], x_d[i * 128:(i + 1) * 128, :], r=[], w=[("xo", i % NXB)])

            PF = 7
            for i in range(PF):
                load_x(i)
            for i in range(NT):
                p2 = i % NXB
                xt = xtb[p2]
                kx = ("xo", p2)
                tcs = slice(i * 128, (i + 1) * 128)
                if i + PF < NT:
                    load_x(i + PF)
                for half in range(2):
                    b = 2 * (i % 4) + half
                    hc = slice(half * 512, (half + 1) * 512)
                    for ch, (src, hh) in enumerate(ysrc):
                        MM(ps[b][:], src[:, hh, tcs], wout_bf[:, ch, hc], ch == 0, ch == 11, r=[("wout", ch), "ycat"], w=[P(b)])
                    TT("dve", xt[:, hc], ps[b][:], xt[:, hc], ALU.add, r=[P(b), kx], w=[kx])
                ss = ss4[:, 2 * p2:2 * p2 + 1]
                rs = ss4[:, 2 * p2 + 1:2 * p2 + 2]
                ACT(junk[:], xt, AF.Square, r=[kx], w=["junk4", ("ss4", p2)], accum=ss)
                ACT(rs, ss, AF.Sqrt, r=[("ss4", p2)], w=[("ss4", p2)], bias=EPS, scale=1.0 / D)
                RCP(rs, rs, r=[("ss4", p2)], w=[("ss4", p2)])
                STT(xt, xt, rs, fgt, ALU.mult, ALU.mult, r=[kx, ("ss4", p2), "fg"], w=[kx])
                DMA("sp", out_d[i * 128:(i + 1) * 128, :], xt, r=[kx], w=[])
            finish(blk4)
    return nc


def make_consts():
    cst = np.zeros((128, 512), np.float32)
    cst[:, 0:128] = np.eye(128, dtype=np.float32)
    cst[:, 128:256] = np.triu(np.ones((128, 128), np.float32))
    p = np.arange(128)
    c, h = p // 4, p % 4
    cst[:, 256:384] = ((h[:, None] == h[None, :]) & (c[:, None] < c[None, :])).astype(np.float32)
    cst[:, 384:512] = 1.0
    return cst


def make_in_maps(inp):
    f = lambda a: np.ascontiguousarray(a, dtype=np.float32)
    cst = make_consts()
    par = np.zeros((128, 320), np.float32)
    par[:, 0:8] = inp["norm_g"][0].reshape(8, 128).T
    par[:, 8:16] = inp["mem_norm_g"][0].reshape(8, 128).T
    par[:, 16:32] = inp["conv_w"][0].reshape(4, 4, 128).transpose(2, 1, 0).reshape(128, 16)
    par[:, 32:36] = inp["conv_b"][0].reshape(4, 128).T
    par[:, 36:40] = inp["mnorm_g"][0].reshape(4, 128).T
    par[:, 40:44] = inp["skip_m"][0].reshape(4, 128).T
    par[:, 44:48] = inp["dnorm_g"][0].reshape(4, 128).T
    par[:, 48] = np.tile(inp["b_if"][0][0:4], 32)
    par[:, 49] = np.tile(inp["b_if"][0][4:8], 32)
    par[:, 64:128] = inp["lam_q1"][0][None, :]
    par[:, 128:192] = inp["lam_k1"][0][None, :]
    par[:, 192:256] = inp["lam_q2"][0][None, :]
    par[:, 256:320] = inp["lam_k2"][0][None, :]
    fg = np.ascontiguousarray(np.broadcast_to(inp["final_g"][None, :], (128, D)), dtype=np.float32)
    shared = {
        "w_in": f(inp["w_in"][0]), "w_kv": f(inp["w_mem_kv"][0]), "w_out": f(inp["w_out"][0]),
        "wq": f(inp["wq_m"][0]), "wk": f(inp["wk_m"][0]), "wv": f(inp["wv_m"][0]), "w_if": f(inp["w_if"][0]),
        "cst": cst, "par": par, "fg": fg,
    }
    maps = []
    for b in range(8):
        m = dict(shared)
        m["x"] = f(inp["x"][b])
        m["mem"] = f(inp["mem"][b])
        maps.append(m)
    return maps


_NC_CACHE = {}


def kernel(**inputs):
    if "nc" not in _NC_CACHE:
        _NC_CACHE["nc"] = build_nc()
    nc = _NC_CACHE["nc"]
    maps = make_in_maps(inputs)
    res = run_bass_kernel_spmd(nc, maps, core_ids=list(range(8)))
    return np.stack([np.asarray(r["out"], dtype=np.float32) for r in res.results], axis=0)
```

```python
import math
from contextlib import ExitStack

import numpy as np
import concourse.bass as bass
import concourse.mybir as mybir
from concourse.bass_utils import run_bass_kernel_spmd

F32 = mybir.dt.float32
BF16 = mybir.dt.bfloat16
AF = mybir.ActivationFunctionType
ALU = mybir.AluOpType

S_LEN = 4096
D = 1024
NT = 32
NB = 8
EPS = 1e-6
LN_SQRT_DH = 0.5 * math.log(128.0)


class Sched:
    ENG = ("pe", "act", "dve", "pool", "sp")

    def __init__(self, nc, sems, dsems):
        self.nc = nc
        self.sems = sems
        self.dsems = dsems
        self.q = {e: [] for e in self.ENG}
        self.cnt = {e: 0 for e in self.ENG}
        self.last_w = {}
        self.readers = {}
        self.seen = {e: {} for e in self.ENG}
        self.n_dma = len(dsems)
        self.dma_cnt = [0] * self.n_dma
        half = self.n_dma // 2
        self.dma_pool = {"sp": list(range(0, half)), "pool": list(range(half, self.n_dma))}
        self.dma_rr = {"sp": 0, "pool": 0}

    def _need(self, eng, tok, waits):
        src, val = tok
        if src == eng and eng in ("pe", "sp"):
            return
        if self.seen[eng].get(src, 0) >= val:
            return
        self.seen[eng][src] = val
        waits[src] = max(waits.get(src, 0), val)

    def op(self, eng, fn, r=(), w=(), dma=False):
        waits = {}
        for k in r:
            t = self.last_w.get(k)
            if t is not None:
                self._need(eng, t, waits)
        for k in w:
            t = self.last_w.get(k)
            if t is not None:
                self._need(eng, t, waits)
            for t in self.readers.get(k, ()):
                self._need(eng, t, waits)
        if dma:
            pool_ = self.dma_pool[eng]
            i = pool_[self.dma_rr[eng] % len(pool_)]
            self.dma_rr[eng] += 1
            if self.dma_cnt[i] > 0:
                self._need(eng, (("d", i), 16 * self.dma_cnt[i]), waits)
            self.dma_cnt[i] += 1
            tok = (("d", i), 16 * self.dma_cnt[i])
        else:
            self.cnt[eng] += 1
            tok = (eng, self.cnt[eng])
        self.q[eng].append((list(waits.items()), fn, tok))
        for k in w:
            self.last_w[k] = tok
            self.readers[k] = []
        for k in r:
            self.readers.setdefault(k, []).append(tok)
        return tok

    def barrier(self):
        toks = [(e, self.cnt[e]) for e in self.ENG if self.cnt[e] > 0]
        toks += [(("d", i), 16 * c) for i, c in enumerate(self.dma_cnt) if c > 0]
        for e in self.ENG:
            waits = {}
            for t in toks:
                if t[0] != e:
                    self._need(e, t, waits)
            if waits:
                self.q[e].append((list(waits.items()), None, None))

    def emit(self, block):
        sems, dsems = self.sems, self.dsems

        def run(e, engobj):
            for waits, fn, tok in self.q[e]:
                for src, val in waits:
                    s = dsems[src[1]] if isinstance(src, tuple) else sems[src]
                    engobj.wait_ge(s, val)
                if fn is None:
                    continue
                ins = fn(engobj)
                if isinstance(tok[0], tuple):
                    ins.then_inc(dsems[tok[0][1]], 16)
                else:
                    ins.then_inc(sems[tok[0]], 1)
            self.q[e] = []

        @block.tensor
        def _(eng):
            run("pe", eng)

        @block.scalar
        def _(eng):
            run("act", eng)

        @block.vector
        def _(eng):
            run("dve", eng)

        @block.gpsimd
        def _(eng):
            run("pool", eng)

        @block.sync
        def _(eng):
            run("sp", eng)


def build_nc(stop_after=99, dbg=None):
    nc = bass.Bass("TRN2", target_bir_lowering=False)
    din = lambda n, s: nc.dram_tensor(n, s, F32, kind="ExternalInput").ap()
    x_d = din("x", [S_LEN, D])
    mem_d = din("mem", [256, D])
    win_d = din("w_in", [D, 4608])
    wkv_d = din("w_kv", [D, 1024])
    wout_d = din("w_out", [1536, D])
    wq_d = din("wq", [4, 128, 128])
    wk_d = din("wk", [4, 128, 128])
    wv_d = din("wv", [4, 128, 128])
    wif_d = din("w_if", [1536, 8])
    cst_d = din("cst", [128, 512])
    par_d = din("par", [128, 320])
    fg_d = din("fg", [128, D])
    out_d = nc.dram_tensor("out", [S_LEN, D], F32, kind="ExternalOutput").ap()
    dbg_outs = []
    win_v = win_d.rearrange("(c p) f -> p c f", p=128)

    top = ExitStack()
    with top:
        sb = lambda es, name, shape, dt: es.enter_context(nc.sbuf_tensor(name, shape, dt))
        ps = [top.enter_context(nc.psum_tensor("ps%d" % i, [128, 512], F32)) for i in range(8)]
        sems = {e: top.enter_context(nc.semaphore("s_" + e)) for e in Sched.ENG}
        dsems = [top.enter_context(nc.semaphore("d%d" % i)) for i in range(32)]
        S = Sched(nc, sems, dsems)
        P = lambda i: ("ps", i)

        def MM(out, lhsT, rhs, start, stop, r, w):
            return S.op("pe", lambda e: e.matmul(out, lhsT=lhsT, rhs=rhs, start=start, stop=stop,
                                                  skip_group_check=True), r=r, w=w)

        def TR(out, in_, r, w):
            return S.op("pe", lambda e: e.transpose(out=out, in_=in_, identity=ident[:]), r=list(r) + ["cst"], w=w)

        def ACT(out, in_, func, r, w, bias=None, scale=None, accum=None, eng="act"):
            kw = {}
            if bias is not None:
                kw["bias"] = bias
            if scale is not None:
                kw["scale"] = scale
            if accum is not None:
                kw["accum_out"] = accum
            return S.op("act", lambda e: e.activation(out=out, in_=in_, func=func, **kw), r=r, w=w)

        def TS(eng, out, in0, s1, s2, op0, op1, r, w):
            if op1 is None:
                return S.op(eng, lambda e: e.tensor_scalar(out=out, in0=in0, scalar1=s1, scalar2=None, op0=op0), r=r, w=w)
            return S.op(eng, lambda e: e.tensor_scalar(out=out, in0=in0, scalar1=s1, scalar2=s2, op0=op0, op1=op1), r=r, w=w)

        def TT(eng, out, in0, in1, op, r, w):
            return S.op(eng, lambda e: e.tensor_tensor(out=out, in0=in0, in1=in1, op=op), r=r, w=w)

        def STT(out, in0, scalar, in1, op0, op1, r, w, accum=None):
            if accum is not None:
                return S.op("dve", lambda e: e.scalar_tensor_tensor(out=out, in0=in0, scalar=scalar, in1=in1, op0=op0, op1=op1, accum_out=accum), r=r, w=w)
            return S.op("dve", lambda e: e.scalar_tensor_tensor(out=out, in0=in0, scalar=scalar, in1=in1, op0=op0, op1=op1), r=r, w=w)

        def CP(eng, out, in_, r, w):
            if eng == "act":
                return S.op("act", lambda e: e.copy(out=out, in_=in_), r=r, w=w)
            return S.op(eng, lambda e: e.tensor_copy(out=out, in_=in_), r=r, w=w)

        def MSET(eng, ap, val, w):
            return S.op(eng, lambda e: e.memset(ap, val), w=w)

        def RCP(out, in_, r, w):
            return S.op("dve", lambda e: e.reciprocal(out=out, in_=in_), r=r, w=w)

        def SCAN(out, d0, init, op0, r, w):
            return S.op("dve", lambda e: e.tensor_tensor_scan(out=out, data0=d0, data1=d0, initial=init, op0=op0, op1=ALU.bypass), r=r, w=w)

        def DMA(eng, out, in_, r, w):
            return S.op(eng, lambda e: e.dma_start(out=out, in_=in_), r=r, w=w, dma=True)

        def dump(es, name, ap, shape, dt=F32):
            d = nc.dram_tensor("dbg_" + name, list(shape), dt, kind="ExternalOutput").ap()
            S.barrier()
            dbg_outs.append(DMA("sp", d, ap, r=[], w=[]))

        def finish(block):
            S.barrier()
            S.emit(block)

        cstt = sb(top, "cstt", [128, 512], F32)
        ident = cstt[:, 0:128]
        triu = cstt[:, 128:256]
        M1 = cstt[:, 256:384]
        ones_f = cstt[:, 384:512]
        part = sb(top, "part", [128, 320], F32)
        mask_bf = sb(top, "mask_bf", [128, 128], BF16)
        ones_bf = sb(top, "ones_bf", [128, 128], BF16)
        ident_bf = sb(top, "ident_bf", [128, 128], BF16)
        kmT = sb(top, "kmT", [128, 4, 256], BF16)
        vm1 = sb(top, "vm1", [128, 2, 4, 129], BF16)
        ymT = sb(top, "ymT", [128, 4, S_LEN], BF16)
        hT = sb(top, "hT", [128, 8, S_LEN], BF16)
        wm_cm = nc.sbuf_tensor("wm_bf", [128, 8, 1536], BF16)
        wm_bf = wm_cm.__enter__()

        def rms_tile_a(es_bufs, src_rows, i):
            xtb, junk, ssb, xnb = es_bufs
            xt = xtb[i % 6]
            ss = ssb[:, 2 * (i % 6):2 * (i % 6) + 1]
            rs = ssb[:, 2 * (i % 6) + 1:2 * (i % 6) + 2]
            kx, ks = ("xt", i % 6), ("ss", i % 6)
            DMA("sp", xt[:], src_rows, r=[], w=[kx])
            ACT(junk[:], xt[:], AF.Square, r=[kx], w=["junk", ks], accum=ss)
            ACT(rs, ss, AF.Sqrt, r=[ks], w=[ks], bias=EPS, scale=1.0 / D)
            RCP(rs, rs, r=[ks], w=[ks])

        def rms_tile_b(es_bufs, gcol, dstT, col0, i):
            xtb, junk, ssb, xnb = es_bufs
            xt = xtb[i % 6]
            xn = xnb[i % 3]
            rs = ssb[:, 2 * (i % 6) + 1:2 * (i % 6) + 2]
            kx, kn, ks = ("xt", i % 6), ("xn", i % 3), ("ss", i % 6)
            ACT(xn[:], xt[:], AF.Copy, r=[kx, ks], w=[kn], scale=rs)
            for half in range(2):
                b = 2 * (i % 4) + half
                psb = ps[b][:].bitcast(BF16)
                for j in range(4):
                    c = 4 * half + j
                    S.op("pe", (lambda o, a: (lambda e: e.transpose(out=o, in_=a, identity=ident_bf[:])))(
                        psb[:, j * 128:(j + 1) * 128], xn[:, c * 128:(c + 1) * 128]), r=[kn, "ident_bf"], w=[P(b)])
                TT("dve", dstT[:, 4 * half:4 * half + 4, col0:col0 + 128],
                   psb[:, 0:512].rearrange("p (j t) -> p j t", t=128),
                   part[:, gcol + 4 * half:gcol + 4 * half + 4].unsqueeze(2).to_broadcast([128, 4, 128]),
                   ALU.mult, r=[P(b), "par"], w=[("hT", col0 // 512) if dstT is hT else "memT"])

        with nc.Block() as blk0, ExitStack() as es:
            DMA("sp", cstt[:], cst_d[:, :], r=[], w=["cst"])
            DMA("sp", part[:], par_d[:, :], r=[], w=["par"])
            CP("dve", mask_bf[:], triu, r=["cst"], w=["mask_bf"])
            CP("dve", ones_bf[:], ones_f, r=["cst"], w=["ones_bf"])
            CP("dve", ident_bf[:], ident, r=["cst"], w=["ident_bf"])
            xtb = [sb(es, "xt%d" % i, [128, D], F32) for i in range(6)]
            xnb = [sb(es, "xn%d" % i, [128, D], BF16) for i in range(3)]
            junk = sb(es, "junk", [128, D], BF16)
            ssb = sb(es, "ssb", [128, 12], F32)
            bufs = (xtb, junk, ssb, xnb)
            memT = sb(es, "memT", [128, 8, 256], BF16)
            wkv_bf = sb(es, "wkv_bf", [128, 8, 1024], BF16)
            wkv_v = wkv_d.rearrange("(c p) f -> p c f", p=128)
            for c in range(8):
                DMA("pool", wkv_bf[:, c, :], wkv_v[:, c, :], r=[], w=["wkv"])
            for c in range(8):
                DMA("pool", wm_bf[:, c, :], win_v[:, c, 0:1536], r=[], w=[("wm", c)])
            seq = [(mem_d[i * 128:(i + 1) * 128, :], 8, memT, i * 128) for i in range(2)]
            seq += [(x_d[i * 128:(i + 1) * 128, :], 0, hT, i * 128) for i in range(NT)]
            rms_tile_a(bufs, seq[0][0], 0)
            for k, (src_rows, gcol, dstT, col0) in enumerate(seq):
                if k + 1 < len(seq):
                    rms_tile_a(bufs, seq[k + 1][0], k + 1)
                rms_tile_b(bufs, gcol, dstT, col0, k)
            for h in range(4):
                b = 4 + h % 2
                for c in range(8):
                    MM(ps[b][:, 0:256], wkv_bf[:, c, h * 128:(h + 1) * 128], memT[:, c, :], c == 0, c == 7,
                       r=["wkv", "memT"], w=[P(b)])
                ACT(kmT[:, h, :], ps[b][:, 0:256], AF.Identity, r=[P(b)], w=["kmT"], scale=128.0 ** -0.5)
            MSET("pool", vm1[:, :, :, 128:129], 1.0, w=["vm1"])
            for mt in range(2):
                b = 6 + mt
                for c in range(8):
                    MM(ps[b][:], memT[:, c, mt * 128:(mt + 1) * 128], wkv_bf[:, c, 512:1024], c == 0, c == 7,
                       r=["wkv", "memT"], w=[P(b)])
                CP("dve", vm1[:, mt, :, 0:128], ps[b][:].rearrange("p (h d) -> p h d", d=128), r=[P(b)], w=["vm1"])
            if dbg == 0:
                dump(es, "hT", hT[:, :, 0:512], [128, 8, 512], BF16)
                dump(es, "kmT", kmT[:], [128, 4, 256], BF16)
                dump(es, "vm1", vm1[:], [128, 2, 4, 129], BF16)
            finish(blk0)
        if stop_after == 0:
            return nc

        with nc.Block() as blk1, ExitStack() as es:
            wqkv = sb(es, "wqkv", [128, 3, 4, 128], BF16)
            for j, wd in enumerate((wq_d, wk_d, wv_d)):
                DMA("pool", wqkv[:, j, :, :], wd.rearrange("h d e -> d h e"), r=[], w=["wqkv"])
            wif_bf = sb(es, "wif_bf", [128, 12, 8], BF16)
            DMA("pool", wif_bf[:], wif_d.rearrange("(j p) g -> p j g", p=128), r=[], w=["wif"])
            ABf = sb(es, "ABf", [128, 2, 4, 8], BF16)
            wTs = [sb(es, "wTs%d" % i, [128, 128], BF16) for i in range(3)]
            psb4 = ps[4][:].bitcast(BF16)
            for h in range(4):
                for j in range(3):
                    S.op("pe", (lambda o, a: (lambda e: e.transpose(out=o, in_=a, identity=ident_bf[:])))(
                        psb4[:, j * 128:(j + 1) * 128], wqkv[:, j, h, :]), r=["wqkv", "ident_bf"], w=[P(4)])
                    CP("dve", wTs[j][:], psb4[:, j * 128:(j + 1) * 128], r=[P(4)], w=[("wTs", j)])
                MM(ps[5][:, h * 8:(h + 1) * 8], wTs[0][:], wif_bf[:, h, :], True, False, r=[("wTs", 0), "wif"], w=[P(5)])
                MM(ps[5][:, h * 8:(h + 1) * 8], wTs[1][:], wif_bf[:, 4 + h, :], False, True, r=[("wTs", 1), "wif"], w=[P(5)])
                MM(ps[5][:, 32 + h * 8:32 + (h + 1) * 8], wTs[2][:], wif_bf[:, 8 + h, :], True, True, r=[("wTs", 2), "wif"], w=[P(5)])
            CP("dve", ABf[:].rearrange("p a h g -> p (a h g)"), ps[5][:, 0:64], r=[P(5)], w=["ABf"])
            xmb = [sb(es, "xm%d" % i, [128, 4, 515], BF16) for i in range(2)]
            xcv = sb(es, "xcv", [128, 4, 512], BF16)
            xs = sb(es, "xs", [128, 4, 512], BF16)
            qT = sb(es, "qT", [128, 4, 512], BF16)
            kT = sb(es, "kT", [128, 4, 512], BF16)
            vT = None
            zs = sb(es, "zs", [128, 4, 512], BF16)
            sgo = sb(es, "sgo", [128, 4, 512], BF16)
            Dg = sb(es, "Dg", [128, 16, 128], BF16)
            for hj in range(16):
                TS("dve", Dg[:, hj, :], ident, part[:, 16 + hj:16 + hj + 1], None, ALU.mult, None, r=["cst", "par"], w=["Dg"])
            Gtok = sb(es, "Gtok", [128, 32, 8], F32)
            CONVW, CONVB, MG, SK = 16, 32, 36, 40
            rot = [0]

            def nextbank():
                rot[0] ^= 1
                return rot[0]

            def frontA(blk):
                xm = xmb[blk % 2]
                kxm = ("xm", blk % 2)
                cols = slice(blk * 512, (blk + 1) * 512)
                if blk == 0:
                    MSET("pool", xm[:, :, 0:3], 0.0, w=[kxm])
                else:
                    CP("pool", xm[:, :, 0:3], xmb[(blk - 1) % 2][:, :, 512:515],
                       r=[("xm", (blk - 1) % 2)] + [(("xm", (blk - 1) % 2), hh) for hh in range(4)], w=[kxm])
                for h in range(4):
                    b = nextbank()
                    for c in range(8):
                        MM(ps[b][:], wm_bf[:, c, h * 128:(h + 1) * 128], hT[:, c, cols], c == 0, c == 7,
                           r=[("wm", c), ("hT", blk)], w=[P(b)])
                    CP("act", xm[:, h, 3:515], ps[b][:], r=[P(b)], w=[(kxm, h)])
                return xm, kxm

            def frontBC(blk, need_v, do_qk=True):
                xm = xmb[blk % 2]
                kxm = ("xm", blk % 2)
                for h in range(4):
                    b = nextbank()
                    for j in range(4):
                        MM(ps[b][:], Dg[:, 4 * h + j, :], xm[:, h, j:j + 512], j == 0, j == 3,
                           r=["Dg", (kxm, h), kxm], w=[P(b)])
                    ACT(xcv[:, h, :], ps[b][:], AF.Silu, r=[P(b), "par"], w=[("xcv", h)],
                        bias=part[:, CONVB + h:CONVB + h + 1])
                for h in range(4 if do_qk else 0):
                    for j, dst, kd in ((0, qT, "qT"), (1, kT, "kT")) + (((2, vT, "vT"),) if need_v else ()):
                        b = nextbank()
                        src = xcv[:, h, :] if j < 2 else xm[:, h, 3:515]
                        MM(ps[b][:], wqkv[:, j, h, :], src, True, True, r=["wqkv", ("xcv", h) if j < 2 else (kxm, h)], w=[P(b)])
                        CP("act" if j == 0 else "dve", dst[:, h, :], ps[b][:], r=[P(b)], w=[kd])
                return xm, kxm

            for blk in range(NB):
                if blk == 0:
                    frontA(0)
                frontBC(blk, False, do_qk=False)
                if blk + 1 < NB:
                    frontA(blk + 1)
                xm_ = xmb[blk % 2]
                kxm_ = ("xm", blk % 2)
                for tt in range(4):
                    tc_ = slice(tt * 128, (tt + 1) * 128)
                    for h in range(4):
                        MM(ps[2][:, tt * 8:(tt + 1) * 8], xcv[:, h, tc_], ABf[:, 0, h, :], h == 0, False,
                           r=[("xcv", h), "ABf"], w=[P(2)])
                        MM(ps[2][:, tt * 8:(tt + 1) * 8], xm_[:, h, 3 + tt * 128:3 + (tt + 1) * 128], ABf[:, 1, h, :], False, h == 3,
                           r=[(kxm_, h), "ABf"], w=[P(2)])
                CP("dve", Gtok[:, blk * 4:(blk + 1) * 4, :], ps[2][:, 0:32].rearrange("p (t g) -> p t g", g=8),
                   r=[P(2)], w=["Gtok"])
            if dbg and dbg >= 1:
                dump(es, "Gtok", Gtok[:], [128, 32, 8])
                dump(es, "qT", qT[:], [128, 4, 512], BF16)
                dump(es, "xcv", xcv[:], [128, 4, 512], BF16)

            rows = sb(es, "rows", [128, 8, 128], F32)
            GIr, GFr, LF, Fp, U_, W_, W2_, TH_ = [rows[:, i, :] for i in range(8)]
            cols_ = sb(es, "colsb", [128, 16], F32)
            col = lambda i: cols_[:, i:i + 1]
            rowt = sb(es, "rowt", [1, 4, 128], F32)
            Ebc = sb(es, "Ebc", [128, 128], F32)
            Wt = sb(es, "Wt", [128, 3, 128], F32)
            BI, BFc = 48, 49
            Gv = Gtok[:].rearrange("p c g -> p (c g)")
            gsp = sb(es, "gsp", [128, 2, 128], F32)
            CP("dve", gsp[:, 0, :].rearrange("p (c h) -> p c h", h=4), Gtok[:, :, 0:4], r=["Gtok"], w=["gsp"])
            CP("dve", gsp[:, 1, :].rearrange("p (c h) -> p c h", h=4), Gtok[:, :, 4:8], r=["Gtok"], w=["gsp"])
            TR(ps[3][:, 0:128], gsp[:, 0, :], r=["gsp"], w=[P(3)])
            TR(ps[3][:, 128:256], gsp[:, 1, :], r=["gsp"], w=[P(3)])
            CP("dve", rows[:, 0:2, :], ps[3][:, 0:256].rearrange("p (a t) -> p a t", t=128), r=[P(3)], w=["rows"])
            TS("dve", col(0), part[:, BFc:BFc + 1], -1.0, None, ALU.mult, None, r=["par"], w=["cols"])
            ACT(LF, GFr, AF.Exp, r=["rows", "cols"], w=["rows"], bias=col(0), scale=-1.0)
            ACT(LF, LF, AF.Ln, r=["rows"], w=["rows"], bias=1.0)
            SCAN(Fp, LF, 0.0, ALU.add, r=["rows"], w=["rows"])
            MM(ps[3][:, 256:257], M1, rows[:, 3, 127:128], True, True, r=["cst", "rows"], w=[P(3)])
            CP("dve", col(1), ps[3][:, 256:257], r=[P(3)], w=["cols"])
            TS("dve", Fp, Fp, col(1), None, ALU.add, None, r=["rows", "cols"], w=["rows"])
            STT(GIr, GIr, part[:, BI:BI + 1], Fp, ALU.add, ALU.add, r=["rows", "par"], w=["rows"])
            SCAN(U_, GIr, 0.0, ALU.max, r=["rows"], w=["rows"])
            MM(ps[3][0:1, 384:512], rows[:, 4, 127:128], ident, True, True, r=["rows", "cst"], w=[P(3)])
            CP("dve", rowt[:, 0, :], ps[3][0:1, 384:512], r=[P(3)], w=["rowt"])
            for h in range(4):
                SCAN(rowt[:, 1, :].rearrange("p (c h) -> p h c", h=4)[:, h, :],
                     rowt[:, 0, :].rearrange("p (c h) -> p h c", h=4)[:, h, :], 0.0, ALU.max, r=["rowt"], w=["rowt"])
            MSET("dve", rowt[:, 2, 0:4], 0.0, w=["rowt"])
            CP("dve", rowt[:, 2, 4:128], rowt[:, 1, 0:124], r=["rowt"], w=["rowt"])
            TT("dve", rowt[:, 3, :], rowt[:, 2, :], rowt[:, 1, :], ALU.subtract, r=["rowt"], w=["rowt"])
            MM(ps[3][:, 257:258], rowt[:, 2, :], ones_f[0:1, 0:1], True, True, r=["rowt", "cst"], w=[P(3)])
            MM(ps[3][:, 258:259], rowt[:, 3, :], ones_f[0:1, 0:1], True, True, r=["rowt", "cst"], w=[P(3)])
            CP("dve", cols_[:, 2:4], ps[3][:, 257:259], r=[P(3)], w=["cols"])
            MM(ps[4][:, 0:128], ones_f[0:1, :], rowt[:, 3, :], True, True, r=["rowt", "cst"], w=[P(4)])
            ACT(Ebc[:], ps[4][:, 0:128], AF.Exp, r=[P(4)], w=["Ebc"])
            TS("dve", col(4), col(2), -1.0, -LN_SQRT_DH, ALU.mult, ALU.add, r=["cols"], w=["cols"])
            TS("dve", col(5), col(2), -1.0, None, ALU.mult, None, r=["cols"], w=["cols"])
            TT("dve", col(6), col(4), col(3), ALU.add, r=["cols"], w=["cols"])
            ACT(W_, GIr, AF.Exp, r=["rows", "cols"], w=["rows"], bias=col(4))
            ACT(W2_, GIr, AF.Exp, r=["rows", "cols"], w=["rows"], bias=col(6))
            ACT(TH_, Fp, AF.Exp, r=["rows", "cols"], w=["rows"], bias=col(5))
            for i in range(3):
                TR(ps[4][:, 128 * (i + 1):128 * (i + 2)], rows[:, 5 + i, :], r=["rows"], w=[P(4)])
            CP("dve", Wt[:], ps[4][:, 128:512].rearrange("p (a t) -> p a t", t=128), r=[P(4)], w=["Wt"])
            if dbg and dbg >= 2:
                dump(es, "rows", rows[:], [128, 8, 128])
                dump(es, "Wt", Wt[:], [128, 3, 128])
                dump(es, "Ebc", Ebc[:], [128, 128])
                dump(es, "rowt", rowt[:], [1, 4, 128])

            kwb = [sb(es, "kw%d" % i, [128, 4, 128], BF16) for i in range(2)]
            V1b = [sb(es, "V1_%d" % i, [128, 4, 129], BF16) for i in range(2)]
            sqkb = [sb(es, "sqk%d" % i, [128, 4, 128], BF16) for i in range(2)]
            ABsb = [sb(es, "ABs%d" % i, [128, 4, 129], F32) for i in range(2)]
            lnbb = [sb(es, "lnb%d" % i, [128, 4, 128], F32) for i in range(2)]
            Cn = sb(es, "Cn", [128, 4, 129], F32)
            Cnb = sb(es, "Cnb", [128, 4, 129], BF16)
            hsb = [sb(es, "hs%d" % i, [128, 4, 128], F32) for i in range(2)]
            smb = [sb(es, "sm%d" % i, [128, 64], F32) for i in range(2)]
            ymt = sb(es, "ymt", [128, 4, 128], F32)
            mhalf = sb(es, "mhalf", [128, 4], F32)
            MSET("pool", mhalf[:], -0.5, w=["mhalf"])
            for i in range(2):
                MSET("pool", V1b[i][:, :, 128:129], 1.0, w=[("V1", i)])
            MSET("pool", Cn[:], 0.0, w=["Cn"])
            MSET("pool", Cnb[:], 0.0, w=["Cnb"])

            def pre(c_, xm, kxm):
                tt, p = c_ % 4, c_ % 2
                tc_ = slice(tt * 128, (tt + 1) * 128)
                ch4 = slice(c_ * 4, c_ * 4 + 4)
                for h in range(4):
                    MM(ps[2][:, h * 128:(h + 1) * 128], xcv[:, h, tc_], wqkv[:, 1, h, :], True, True,
                       r=[("xcv", h), "wqkv"], w=[P(2)])
                    MM(ps[3][:, h * 128:(h + 1) * 128], xm[:, h, 3 + tt * 128:3 + (tt + 1) * 128], wqkv[:, 2, h, :],
                       True, True, r=[(kxm, h), "wqkv"], w=[P(3)])
                    MM(ps[4][:, h * 128:(h + 1) * 128], kT[:, h, tc_], qT[:, h, tc_], True, True,
                       r=["kT", "qT"], w=[P(4)])
                for h in range(4):
                    ACT(kwb[p][:, h, :], ps[2][:, h * 128:(h + 1) * 128], AF.Copy, r=[P(2), "Wt"], w=[("kw", p)],
                        scale=Wt[:, 1, c_ * 4 + h:c_ * 4 + h + 1])
                CP("act", V1b[p][:, :, 0:128], ps[3][:].rearrange("p (h d) -> p h d", d=128), r=[P(3)], w=[("V1", p)])
                for h in range(4):
                    STT(sqkb[p][:, h, :], ps[4][:, h * 128:(h + 1) * 128], Wt[:, 0, c_ * 4 + h:c_ * 4 + h + 1], triu,
                        ALU.mult, ALU.mult, r=[P(4), "Wt", "cst"], w=[("sqk", p)])

            def rec(c_):
                tt, p = c_ % 4, c_ % 2
                tc_ = slice(tt * 128, (tt + 1) * 128)
                kw, V1, sqk = kwb[p], V1b[p], sqkb[p]
                for hp in range(2):
                    ab = ps[5 + hp]
                    for hh in range(2):
                        h = 2 * hp + hh
                        MM(ab[:, hh * 129:(hh + 1) * 129], sqk[:, h, :], V1[:, h, :], hh == 0, False,
                           r=[("sqk", p), ("V1", p)], w=[P(5 + hp)])
                        MM(ab[:, hh * 129:(hh + 1) * 129], qT[:, h, tc_], Cnb[:, h, :], False, True,
                           r=["qT", "Cnb"], w=[P(5 + hp)])
                    for hh in range(2):
                        h = 2 * hp + hh
                        MM(ps[7][:, hh * 129:(hh + 1) * 129], kw[:, h, :], V1[:, h, :], hh == 0, True,
                           r=[("kw", p), ("V1", p)], w=[P(7)])
                    for hh in range(2):
                        h = 2 * hp + hh
                        STT(Cn[:, h, :], Cn[:, h, :], Ebc[:, c_ * 4 + h:c_ * 4 + h + 1], ps[7][:, hh * 129:(hh + 1) * 129],
                            ALU.mult, ALU.add, r=["Cn", "Ebc", P(7)], w=["Cn"])
                    CP("dve", Cnb[:, 2 * hp:2 * hp + 2, :], Cn[:, 2 * hp:2 * hp + 2, :], r=["Cn"], w=["Cnb"])
                    CP("act", ABsb[p][:, 2 * hp:2 * hp + 2, :], ab[:, 0:258].rearrange("p (h d) -> p h d", d=129),
                       r=[P(5 + hp)], w=[("ABs", p)])

            def epi_a(c_):
                tt, p = c_ % 4, c_ % 2
                A = ABsb[p]
                sm_ = smb[p]
                hs_ = hsb[p]
                den = sm_[:, 0:4]
                kd, kmv, krs = ("den", p), ("mv", p), ("rstd4", p)
                STT(den, A[:, :, 128], -1.0, A[:, :, 128], ALU.mult, ALU.max, r=[("ABs", p)], w=[kd])
                TT("dve", den, den, Wt[:, 2, c_ * 4:c_ * 4 + 4], ALU.max, r=[kd, "Wt"], w=[kd])
                RCP(den, den, r=[kd], w=[kd])
                mvv = sm_[:, 48:56].rearrange("p (h t) -> p h t", t=2)
                for h in range(4):
                    STT(hs_[:, h, :], A[:, h, 0:128], sm_[:, h:h + 1], sgo[:, tt, h * 128:(h + 1) * 128], ALU.mult, ALU.mult,
                        r=[("ABs", p), kd, ("sgo", tt)], w=[("hs", p, h)])
                    st = sm_[:, 16 + 8 * h:16 + 8 * h + 6]
                    S.op("dve", (lambda o, i: (lambda e: e.bn_stats(out=o, in_=i)))(st, hs_[:, h, :]), r=[("hs", p, h)], w=[("st", p, h)])
                    S.op("dve", (lambda o, i: (lambda e: e.bn_aggr(out=o, in_=i)))(mvv[:, h, :], st), r=[("st", p, h)], w=[kmv])
                rstd4 = sm_[:, 56:60]
                nmr4 = sm_[:, 60:64]
                TS("pool", rstd4, mvv[:, :, 1], EPS, None, ALU.add, None, r=[kmv], w=[krs])
                TT("pool", rstd4, rstd4, mhalf[:], ALU.pow, r=[krs, "mhalf"], w=[krs])
                TS("pool", nmr4, mvv[:, :, 0], -1.0, None, ALU.mult, None, r=[kmv], w=[("nmr4", p)])
                TT("pool", nmr4, nmr4, rstd4, ALU.mult, r=[("nmr4", p), krs], w=[("nmr4", p)])

            def epi_b(c_):
                p = c_ % 2
                sm_ = smb[p]
                for h in range(4):
                    ACT(lnbb[p][:, h, :], hsb[p][:, h, :], AF.Identity, r=[("hs", p, h), ("rstd4", p), ("nmr4", p)], w=[("lnb", p)],
                        scale=sm_[:, 56 + h:57 + h], bias=sm_[:, 60 + h:61 + h])

            def outp(c_):
                tt, p = c_ % 4, c_ % 2
                blk_ = c_ // 4
                tc_ = slice(tt * 128, (tt + 1) * 128)
                b = nextbank()
                for h in range(4):
                    TR(ps[b][:, h * 128:(h + 1) * 128], lnbb[p][:, h, :], r=[("lnb", p)], w=[P(b)])
                for h in range(4):
                    STT(ymt[:, h, :], ps[b][:, h * 128:(h + 1) * 128], part[:, MG + h:MG + h + 1], xs[:, h, tc_],
                        ALU.mult, ALU.add, r=[P(b), "par", "xs"], w=[("ymt", h)])
                    TT("pool", ymT[:, h, blk_ * 512 + tt * 128:blk_ * 512 + (tt + 1) * 128], ymt[:, h, :], zs[:, h, tc_],
                       ALU.mult, r=[("ymt", h), "zs"], w=["ymT"])

            for blk in range(NB if stop_after > 1 else 1):
                if blk == 0:
                    nxt = frontA(0)
                xm, kxm = nxt
                frontBC(blk, False)
                cols = slice(blk * 512, (blk + 1) * 512)
                for h in range(4):
                    b = nextbank()
                    for c in range(8):
                        MM(ps[b][:], wm_bf[:, c, 1024 + h * 128:1024 + (h + 1) * 128], hT[:, c, cols], c == 0, c == 7,
                           r=[("wm", c), ("hT", blk)], w=[P(b)])
                    ACT(zs[:, h, :], ps[b][:], AF.Silu, r=[P(b)], w=["zs"])
                    ACT(xs[:, h, :], xcv[:, h, :], AF.Copy, scale=part[:, SK + h:SK + h + 1],
                       r=[("xcv", h), "par"], w=["xs"])
                for tt in range(4):
                    b = nextbank()
                    for c in range(8):
                        MM(ps[b][:], hT[:, c, blk * 512 + tt * 128:blk * 512 + (tt + 1) * 128], wm_bf[:, c, 512:1024],
                           c == 0, c == 7, r=[("wm", c), ("hT", blk)], w=[P(b)])
                    ACT(sgo[:, tt, :], ps[b][:], AF.Sigmoid, r=[P(b)], w=[("sgo", tt)])
                c0 = blk * 4
                pre(c0, xm, kxm)
                for tt in range(4):
                    c_ = c0 + tt
                    if tt < 3:
                        pre(c_ + 1, xm, kxm)
                    rec(c_)
                    if tt == 1 and blk + 1 < (NB if stop_after > 1 else 1):
                        nxt = frontA(blk + 1)
                    epi_a(c_)
                    if tt > 0:
                        epi_b(c_ - 1)
                        outp(c_ - 1)
                epi_b(c0 + 3)
                outp(c0 + 3)
            if dbg and dbg >= 3:
                dump(es, "ymT", ymT[:, :, 0:512], [128, 4, 512], BF16)
                dump(es, "hs", hsb[0][:], [128, 4, 128])
                dump(es, "Cn", Cn[:], [128, 4, 129])
            finish(blk1)
        wm_cm.__exit__(None, None, None)
        if stop_after <= 1:
            return nc

        rot2 = [0]

        def nb2():
            rot2[0] ^= 1
            return rot2[0]

        GROUPS = [(3 * g, 3 * g + 3) for g in range(10)] + [(30, 32)]
        DG = 44

        def proj_fm(wt, j, dst, blk, func, scale, kd, perblk=False):
            b = nb2()
            cols = slice(blk * 512, (blk + 1) * 512)
            wk_ = (kd, blk) if perblk else kd
            for c in range(8):
                MM(ps[b][:], wt[:, j, c, :], hT[:, c, cols], c == 0, c == 7, r=[kd + "_w", ("hT", blk)], w=[P(b)])
            if func is None:
                CP("dve", dst[:, cols], ps[b][:], r=[P(b)], w=[wk_])
            else:
                ACT(dst[:, cols], ps[b][:], func, r=[P(b)], w=[wk_], scale=scale)

        ydT = sb(top, "ydT", [128, 4, S_LEN], BF16)
        with nc.Block() as blk2, ExitStack() as es:
            lamt = sb(es, "lamt", [128, 8], F32)
            j64 = sb(es, "j64", [128, 64], F32)
            STT(j64[:], part[:, 64:128], 1.0, part[:, 128:192], ALU.mult, ALU.mult, r=["par"], w=["j64", "lam"], accum=lamt[:, 0:1])
            STT(j64[:], part[:, 192:256], 1.0, part[:, 256:320], ALU.mult, ALU.mult, r=["par", "j64"], w=["j64", "lam"], accum=lamt[:, 1:2])
            ACT(lamt[:, 2:4], lamt[:, 0:2], AF.Exp, r=["lam"], w=["lam"])
            TT("dve", lamt[:, 4:5], lamt[:, 2:3], lamt[:, 3:4], ALU.subtract, r=["lam"], w=["lam"])
            TS("dve", lamt[:, 5:6], lamt[:, 4:5], 0.2, -1.0, ALU.add, ALU.mult, r=["lam"], w=["lam"])
            wdb = [sb(es, "wd%d" % i, [128, 4, 8, 128], BF16) for i in range(2)]
            qdT = sb(es, "qdT", [128, S_LEN], BF16)
            kpd = [sb(es, "kpd%d" % i, [128, S_LEN], BF16) for i in range(2)]
            MSET("pool", kpd[0][64:128, :], 0.0, w=["kd"])
            MSET("pool", kpd[1][0:64, :], 0.0, w=["kd"])
            zds = sb(es, "zds", [128, S_LEN], BF16)
            V1d = sb(es, "V1d", [128, 32, 128], BF16)
            ptb = [sb(es, "pt%d" % i, [128, 512], BF16) for i in range(4)]
            o0s = sb(es, "o0s", [128, 512], F32)
            o1s = sb(es, "o1s", [128, 512], F32)
            l0s = sb(es, "l0s", [128, 512], F32)
            l1s = sb(es, "l1s", [128, 512], F32)
            sqb = sb(es, "sqb", [128, 512], BF16)
            rsd = l0s
            dg08 = sb(es, "dg08", [128, 4], F32)
            TS("dve", dg08[:], part[:, DG:DG + 4], 0.8, None, ALU.mult, None, r=["par"], w=["dg08"])
            GROUPS4 = [(4 * g, 4 * g + 4) for g in range(8)]

            def proj_k(wt, blk):
                b = nb2()
                cols = slice(blk * 512, (blk + 1) * 512)
                for c in range(8):
                    MM(ps[b][:], wt[:, 1, c, :], hT[:, c, cols], c == 0, c == 7, r=["kd_w", ("hT", blk)], w=[P(b)])
                CP("dve", kpd[0][0:64, cols], ps[b][0:64, :], r=[P(b)], w=["kd"])
                CP("dve", kpd[1][64:128, cols], ps[b][64:128, :], r=[P(b)], w=["kd"])
            for h in range(4):
                wd = wdb[h % 2]
                for j, off, kk in ((0, 1536, "qd_w"), (1, 2048, "kd_w"), (3, 3072, "zd_w"), (2, 2560, "vd_w")):
                    DMA("pool", wd[:, j, :, :], win_v[:, :, off + h * 128:off + (h + 1) * 128], r=[], w=[kk])
                for blk in range(NB):
                    proj_fm(wd, 0, qdT, blk, AF.Identity, 0.125, "qd")
                    proj_k(wd, blk)
                    proj_fm(wd, 3, zds, blk, AF.Silu, None, "zd")
                    for tt in range(4):
                        tl = blk * 4 + tt
                        for c in range(8):
                            MM(ps[3][:, tt * 128:(tt + 1) * 128], hT[:, c, tl * 128:(tl + 1) * 128], wd[:, 2, c, :], c == 0, c == 7,
                               r=["vd_w", ("hT", blk)], w=[P(3)])
                    CP("dve", V1d[:, blk * 4:(blk + 1) * 4, :], ps[3][:].rearrange("p (t d) -> p t d", d=128), r=[P(3)], w=["V1d"])
                steps = [(g, qs, qe, m, kb) for g, (qs, qe) in enumerate(GROUPS4) for m in range(2) for kb in range(qe)]
                LA = 3
                sctr = [0]

                def emit_qk(i):
                    g, qs, qe, m, kb = steps[i]
                    nsub = qe - qs
                    pr = slice(m * 64, (m + 1) * 64)
                    sv = max(0, kb - qs)
                    sbk = sctr[0] % 4
                    sctr[0] += 1
                    fc = slice(sv * 128, nsub * 128)
                    MM(ps[sbk][:, fc], kpd[m][:, kb * 128:(kb + 1) * 128], qdT[:, (qs + sv) * 128:qe * 128], True, True,
                       r=["kd", "qd"], w=[P(sbk)])
                    pt = ptb[i % 4]
                    kpt = ("pt", i % 4)
                    ACT(pt[:, fc], ps[sbk][:, fc], AF.Exp, r=[P(sbk)], w=[kpt])
                    if kb >= qs:
                        dc = slice((kb - qs) * 128, (kb - qs + 1) * 128)
                        TT("dve", pt[:, dc], pt[:, dc], mask_bf[:], ALU.mult, r=[kpt, "mask_bf"], w=[kpt])

                def emit_pv(i):
                    g, qs, qe, m, kb = steps[i]
                    nsub = qe - qs
                    sv = max(0, kb - qs)
                    fc = slice(sv * 128, nsub * 128)
                    pt = ptb[i % 4]
                    kpt = ("pt", i % 4)
                    MM(ps[4 + m][:, fc], V1d[:, kb, :], pt[:, fc], kb == 0, kb == qe - 1, r=[kpt, "V1d"], w=[P(4 + m)])
                    MM(ps[6 + m][:, fc], ones_bf[:], pt[:, fc], kb == 0, kb == qe - 1, r=[kpt, "ones_bf"], w=[P(6 + m)])
                    if m == 1 and kb == qe - 1:
                        epilogue1(qs, qe)
                        pending.append((i + 14, qs, qe))

                pending = []

                def epilogue1(qs, qe):
                    CP("act", o0s[:], ps[4][:], r=[P(4)], w=["o0s"])
                    CP("dve", l0s[:], ps[6][:], r=[P(6)], w=["l0s"])
                    CP("act", o1s[:], ps[5][:], r=[P(5)], w=["o1s"])
                    CP("dve", l1s[:], ps[7][:], r=[P(7)], w=["l1s"])
                    RCP(l0s[:], l0s[:], r=["l0s"], w=["l0s"])
                    RCP(l1s[:], l1s[:], r=["l1s"], w=["l1s"])
                    TT("dve", o0s[:], o0s[:], l0s[:], ALU.mult, r=["o0s", "l0s"], w=["o0s"])
                    TT("dve", o1s[:], o1s[:], l1s[:], ALU.mult, r=["o1s", "l1s"], w=["o1s"])
                    STT(o0s[:], o1s[:], lamt[:, 5:6], o0s[:], ALU.mult, ALU.add, r=["o0s", "o1s", "lam"], w=["o0s"])

                def epilogue2(qs, qe):
                    cols = slice(qs * 128, qe * 128)
                    ACT(sqb[:], o0s[:], AF.Square, r=["o0s"], w=["sqb"])
                    eb = sctr[0] % 4
                    sctr[0] += 1
                    MM(ps[eb][:], ones_bf[:], sqb[:], True, True, r=["sqb", "ones_bf"], w=[P(eb)])
                    ACT(rsd[:], ps[eb][:], AF.Sqrt, r=[P(eb)], w=["l0s"], bias=EPS, scale=1.0 / 128)
                    RCP(rsd[:], rsd[:], r=["l0s"], w=["l0s"])
                    STT(o0s[:], o0s[:], dg08[:, h:h + 1], rsd[:], ALU.mult, ALU.mult, r=["o0s", "dg08", "l0s"], w=["o0s"])
                    TT("dve", ydT[:, h, cols], o0s[:], zds[:, cols], ALU.mult, r=["o0s", "zd"], w=["ydT"])

                for i in range(len(steps) + LA):
                    if i < len(steps):
                        emit_qk(i)
                    if i >= LA:
                        emit_pv(i - LA)
                    while pending and pending[0][0] <= i - LA:
                        _, pqs, pqe = pending.pop(0)
                        epilogue2(pqs, pqe)
                while pending:
                    _, pqs, pqe = pending.pop(0)
                    epilogue2(pqs, pqe)
            if dbg and dbg >= 4:
                dump(es, "ydT", ydT[:, :, 0:1024], [128, 4, 1024], BF16)
                dump(es, "lamt", lamt[:], [128, 8])
            finish(blk2)
        if stop_after <= 2:
            return nc

        ycT = sb(top, "ycT", [128, 4, S_LEN], BF16)
        with nc.Block() as blk3, ExitStack() as es:
            wcb = [sb(es, "wc%d" % i, [128, 2, 8, 128], BF16) for i in range(2)]
            qcT = sb(es, "qcT", [128, S_LEN], BF16)
            zcs = sb(es, "zcs", [128, S_LEN], BF16)
            ptb = [sb(es, "ptc%d" % i, [128, 512], BF16) for i in range(4)]
            rlb = [sb(es, "rl%d" % i, [128, 512], F32) for i in range(2)]
            onb = [sb(es, "on%d" % i, [128, 512], F32) for i in range(2)]
            for h in range(4):
                wc = wcb[h % 2]
                for j, off, kk in ((0, 3584, "qc_w"), (1, 4096, "zc_w")):
                    DMA("pool", wc[:, j, :, :], win_v[:, :, off + h * 128:off + (h + 1) * 128], r=[], w=[kk])

                def c_qk(j):
                    cols = slice(j * 512, (j + 1) * 512)
                    for mc in range(2):
                        sbk = 2 * (j % 2) + mc
                        sb_ = 2 + mc
                        MM(ps[sb_][:], kmT[:, h, mc * 128:(mc + 1) * 128], qcT[:, cols], True, True, r=["kmT", ("qc", j)], w=[P(sb_)])
                        ACT(ptb[sbk][:], ps[sb_][:], AF.Exp, r=[P(sb_)], w=[("ptc", sbk)])

                def c_pv(j):
                    cols = slice(j * 512, (j + 1) * 512)
                    p = j % 2
                    for mc in range(2):
                        sbk = 2 * p + mc
                        MM(ps[4 + p][:], vm1[:, mc, h, 0:128], ptb[sbk][:], mc == 0, mc == 1, r=[("ptc", sbk), "vm1"], w=[P(4 + p)])
                    for mc in range(2):
                        sbk = 2 * p + mc
                        MM(ps[6 + p][:], ones_bf[:], ptb[sbk][:], mc == 0, mc == 1, r=[("ptc", sbk), "ones_bf"], w=[P(6 + p)])
                    RCP(rlb[p][:], ps[6 + p][:], r=[P(6 + p)], w=[("rl", p)])
                    TT("dve", onb[p][:], ps[4 + p][:], rlb[p][:], ALU.mult, r=[P(4 + p), ("rl", p)], w=[("on", p)])
                    TT("pool", ycT[:, h, cols], onb[p][:], zcs[:, cols], ALU.mult, r=[("on", p), ("zc", j)], w=["ycT"])

                for j in range(NB + 1):
                    if j < NB:
                        proj_fm(wc, 0, qcT, j, AF.Copy, None, "qc", perblk=True)
                        proj_fm(wc, 1, zcs, j, AF.Silu, None, "zc", perblk=True)
                        c_qk(j)
                    if j >= 1:
                        c_pv(j - 1)
            if dbg and dbg >= 5:
                dump(es, "ycT", ycT[:, :, 0:1024], [128, 4, 1024], BF16)
            finish(blk3)
        if stop_after <= 3:
            return nc

        with nc.Block() as blk4, ExitStack() as es:
            hflat = hT[:].rearrange("p c t -> p (c t)")
            wout_bf = hflat[:, 0:12 * D].rearrange("p (c f) -> p c f", f=D)
            wout_v = wout_d.rearrange("(c p) f -> p c f", p=128)
            for c in range(12):
                DMA("pool", wout_bf[:, c, :], wout_v[:, c, :], r=[], w=[("wout", c)])
            hf32 = hflat[:, 12 * D:].bitcast(F32)
            NXB = 8
            xtb = [hf32[:, i * D:(i + 1) * D] for i in range(NXB)]
            fgt = hf32[:, NXB * D:(NXB + 1) * D]
            DMA("sp", fgt, fg_d[:, :], r=[], w=["fg"])
            junk = sb(es, "junk4", [128, D], BF16)
            ss4 = sb(es, "ss4", [128, 2 * NXB], F32)
            ysrc = [(ymT, hh) for hh in range(4)] + [(ydT, hh) for hh in range(4)] + [(ycT, hh) for hh in range(4)]

            def load_x(i):
                DMA("sp", xtb[i % NXB], x_d[i * 128:(i + 1) * 128, :], r=[], w=[("xo", i % NXB)])

            PF = 6
            for i in range(PF):
                load_x(i)
            for i in range(NT):
                p2 = i % NXB
                xt = xtb[p2]
                kx = ("xo", p2)
                tcs = slice(i * 128, (i + 1) * 128)
                if i + PF < NT:
                    load_x(i + PF)
                for half in range(2):
                    b = 2 * (i % 4) + half
                    hc = slice(half * 512, (half + 1) * 512)
                    for ch, (src, hh) in enumerate(ysrc):
                        MM(ps[b][:], src[:, hh, tcs], wout_bf[:, ch, hc], ch == 0, ch == 11, r=[("wout", ch), "ycat"], w=[P(b)])
                    TT("dve", xt[:, hc], ps[b][:], xt[:, hc], ALU.add, r=[P(b), kx], w=[kx])
                ss = ss4[:, 2 * p2:2 * p2 + 1]
                rs = ss4[:, 2 * p2 + 1:2 * p2 + 2]
                ACT(junk[:], xt, AF.Square, r=[kx], w=["junk4", ("ss4", p2)], accum=ss)
                ACT(rs, ss, AF.Sqrt, r=[("ss4", p2)], w=[("ss4", p2)], bias=EPS, scale=1.0 / D)
                RCP(rs, rs, r=[("ss4", p2)], w=[("ss4", p2)])
                STT(xt, xt, rs, fgt, ALU.mult, ALU.mult, r=[kx, ("ss4", p2), "fg"], w=[kx])
                DMA("sp", out_d[i * 128:(i + 1) * 128, :], xt, r=[kx], w=[])
            finish(blk4)
    return nc


def make_consts():
    cst = np.zeros((128, 512), np.float32)
    cst[:, 0:128] = np.eye(128, dtype=np.float32)
    cst[:, 128:256] = np.triu(np.ones((128, 128), np.float32))
    p = np.arange(128)
    c, h = p // 4, p % 4
    cst[:, 256:384] = ((h[:, None] == h[None, :]) & (c[:, None] < c[None, :])).astype(np.float32)
    cst[:, 384:512] = 1.0
    return cst


def make_in_maps(inp):
    f = lambda a: np.ascontiguousarray(a, dtype=np.float32)
    cst = make_consts()
    par = np.zeros((128, 320), np.float32)
    par[:, 0:8] = inp["norm_g"][0].reshape(8, 128).T
    par[:, 8:16] = inp["mem_norm_g"][0].reshape(8, 128).T
    par[:, 16:32] = inp["conv_w"][0].reshape(4, 4, 128).transpose(2, 1, 0).reshape(128, 16)
    par[:, 32:36] = inp["conv_b"][0].reshape(4, 128).T
    par[:, 36:40] = inp["mnorm_g"][0].reshape(4, 128).T
    par[:, 40:44] = inp["skip_m"][0].reshape(4, 128).T
    par[:, 44:48] = inp["dnorm_g"][0].reshape(4, 128).T
    par[:, 48] = np.tile(inp["b_if"][0][0:4], 32)
    par[:, 49] = np.tile(inp["b_if"][0][4:8], 32)
    par[:, 64:128] = inp["lam_q1"][0][None, :]
    par[:, 128:192] = inp["lam_k1"][0][None, :]
    par[:, 192:256] = inp["lam_q2"][0][None, :]
    par[:, 256:320] = inp["lam_k2"][0][None, :]
    fg = np.ascontiguousarray(np.broadcast_to(inp["final_g"][None, :], (128, D)), dtype=np.float32)
    shared = {
        "w_in": f(inp["w_in"][0]), "w_kv": f(inp["w_mem_kv"][0]), "w_out": f(inp["w_out"][0]),
        "wq": f(inp["wq_m"][0]), "wk": f(inp["wk_m"][0]), "wv": f(inp["wv_m"][0]), "w_if": f(inp["w_if"][0]),
        "cst": cst, "par": par, "fg": fg,
    }
    maps = []
    for b in range(8):
        m = dict(shared)
        m["x"] = f(inp["x"][b])
        m["mem"] = f(inp["mem"][b])
        maps.append(m)
    return maps


_NC_CACHE = {}


def kernel(**inputs):
    if "nc" not in _NC_CACHE:
        _NC_CACHE["nc"] = build_nc()
    nc = _NC_CACHE["nc"]
    maps = make_in_maps(inputs)
    res = run_bass_kernel_spmd(nc, maps, core_ids=list(range(8)))
    return np.stack([np.asarray(r["out"], dtype=np.float32) for r in res.results], axis=0)
```
